# Optimizing a Trainium2 kernel written in Bass

```python
import math
import jax, jax.numpy as jnp
from jax import lax
import numpy as np

D_MODEL = 1024
BATCH = 8
SEQ = 8192
DEPTH = 2

CTX_LEN = 256
GRID_W = 64
HEAD_DIM = 64
ATTN_WIDTH = D_MODEL // 2
SGU_WIDTH = D_MODEL // 4
FOURIER_WIDTH = D_MODEL // 4
MIX_WIDTH = ATTN_WIDTH + SGU_WIDTH + FOURIER_WIDTH
N_Q_HEADS = ATTN_WIDTH // HEAD_DIM
N_KV_HEADS = N_Q_HEADS // 4
KV_WIDTH = N_KV_HEADS * HEAD_DIM
SGU_HEAD_DIM = 64
N_SGU_HEADS = SGU_WIDTH // SGU_HEAD_DIM
FOURIER_GROUP_DIM = 64
N_FOURIER_GROUPS = FOURIER_WIDTH // FOURIER_GROUP_DIM
IN_WIDTH = ATTN_WIDTH + 2 * KV_WIDTH + 2 * SGU_WIDTH + FOURIER_WIDTH
Q_END = ATTN_WIDTH
K_END = Q_END + KV_WIDTH
V_END = K_END + KV_WIDTH
U_END = V_END + SGU_WIDTH
G_END = U_END + SGU_WIDTH
WINDOW = 128
BLOCK = 128
CHUNK = 128
D_FF = 4 * D_MODEL
ROPE_THETA = 10000.0
EPS = 1e-6
NEG_INF = -1e30

kernel_name = "hymba_style_hybrid_dit_block"


def rmsnorm(x, g):
    xf = x.astype(jnp.float32)
    y = xf * lax.rsqrt(jnp.mean(xf * xf, axis=-1, keepdims=True) + EPS)
    return (y * g.astype(jnp.float32)).astype(x.dtype)


def axial_rope_tables(rows):
    row = jnp.repeat(jnp.arange(rows), GRID_W).astype(jnp.float32)
    col = jnp.tile(jnp.arange(GRID_W), rows).astype(jnp.float32)
    n_freq = HEAD_DIM // 4
    inv_freq = ROPE_THETA ** (-jnp.arange(n_freq, dtype=jnp.float32) / n_freq)
    ang_r = row[:, None] * inv_freq[None, :]
    ang_c = col[:, None] * inv_freq[None, :]
    ang = jnp.concatenate([ang_r, ang_r, ang_c, ang_c], axis=-1)
    return jnp.cos(ang), jnp.sin(ang)


def apply_rope(x, cos, sin):
    x1, x2, x3, x4 = jnp.split(x, 4, axis=-1)
    rot = jnp.concatenate([-x2, x1, -x4, x3], axis=-1)
    out = x * cos[None, :, None, :] + rot * sin[None, :, None, :]
    return out.astype(x.dtype)


def neighbour_blocks(t):
    z = jnp.zeros_like(t[:, :1])
    prev = jnp.concatenate([z, t[:, :-1]], axis=1)
    nxt = jnp.concatenate([t[:, 1:], z], axis=1)
    return jnp.concatenate([prev, t, nxt], axis=2)


def latent_window_attention(q, k, v, ck, cv, sink):
    B, S, H, hd = q.shape
    G = H // N_KV_HEADS
    nb = S // BLOCK
    scale = hd ** -0.5
    qb = q.reshape(B, nb, BLOCK, N_KV_HEADS, G, hd)
    kw = neighbour_blocks(k.reshape(B, nb, BLOCK, N_KV_HEADS, hd))
    vw = neighbour_blocks(v.reshape(B, nb, BLOCK, N_KV_HEADS, hd))
    s_loc = jnp.einsum('bnqhgd,bnkhd->bnhgqk', qb, kw, preferred_element_type=jnp.float32) * scale
    qpos = jnp.arange(BLOCK)[:, None]
    kpos = jnp.arange(3 * BLOCK)[None, :] - BLOCK
    band = jnp.abs(kpos - qpos) <= WINDOW
    kabs = jnp.arange(nb)[:, None, None] * BLOCK + kpos[None]
    valid = band[None] & (kabs >= 0) & (kabs < S)
    s_loc = jnp.where(valid[None, :, None, None], s_loc, NEG_INF)
    s_ctx = jnp.einsum('bnqhgd,bchd->bnhgqc', qb, ck, preferred_element_type=jnp.float32) * scale
    s_sink = jnp.broadcast_to(sink.astype(jnp.float32).reshape(1, 1, N_KV_HEADS, G, 1, 1),
                              s_loc.shape[:-1] + (1,))
    p = jax.nn.softmax(jnp.concatenate([s_loc, s_ctx, s_sink], axis=-1), axis=-1)
    n_loc = 3 * BLOCK
    n_ctx = ck.shape[1]
    p_loc = p[..., :n_loc].astype(v.dtype)
    p_ctx = p[..., n_loc:n_loc + n_ctx].astype(v.dtype)
    out = (jnp.einsum('bnhgqk,bnkhd->bnqhgd', p_loc, vw)
           + jnp.einsum('bnhgqc,bchd->bnqhgd', p_ctx, cv))
    return out.reshape(B, S, H * hd)


def context_attention(cq, ck, cv, sink):
    B, L, H, hd = cq.shape
    G = H // N_KV_HEADS
    qg = cq.reshape(B, L, N_KV_HEADS, G, hd)
    s = jnp.einsum('bqhgd,bkhd->bhgqk', qg, ck, preferred_element_type=jnp.float32) * (hd ** -0.5)
    s_sink = jnp.broadcast_to(sink.astype(jnp.float32).reshape(1, N_KV_HEADS, G, 1, 1),
                              s.shape[:-1] + (1,))
    p = jax.nn.softmax(jnp.concatenate([s, s_sink], axis=-1), axis=-1)[..., :L]
    out = jnp.einsum('bhgqk,bkhd->bqhgd', p.astype(cv.dtype), cv)
    return out.reshape(B, L, H * hd)


def spatial_gating(u, v, w_s, b_s, g_s):
    B, S, _ = u.shape
    vh = rmsnorm(v.reshape(B, S, N_SGU_HEADS, SGU_HEAD_DIM), g_s)
    vc = vh.reshape(B, S // CHUNK, CHUNK, N_SGU_HEADS, SGU_HEAD_DIM)
    mixed = jnp.einsum('hpq,bnqhd->bnphd', w_s, vc) + b_s.T[None, None, :, :, None]
    return u * mixed.reshape(B, S, SGU_WIDTH)


def fourier_mix(f):
    B, S, _ = f.shape
    fg = f.reshape(B, S, N_FOURIER_GROUPS, FOURIER_GROUP_DIM).astype(jnp.float32)
    y = jnp.real(jnp.fft.fft2(fg, axes=(1, 3), norm='ortho'))
    return y.reshape(B, S, FOURIER_WIDTH).astype(f.dtype)


def merge_heads(attn, sgu, four, g_mix, w_out):
    a = rmsnorm(attn, g_mix[:ATTN_WIDTH])
    s = rmsnorm(sgu, g_mix[ATTN_WIDTH:ATTN_WIDTH + SGU_WIDTH])
    f = rmsnorm(four, g_mix[ATTN_WIDTH + SGU_WIDTH:])
    return jnp.concatenate([a, s, f], axis=-1) @ w_out


def squared_relu_mlp(h, w1, w2):
    return jnp.square(jax.nn.relu(h @ w1)) @ w2


def setup_inputs(seed: int = 0) -> dict:
    key = jax.random.key(seed)
    ks = jax.random.split(key, 24)
    f32 = jnp.float32
    nrm = lambda k, shape, s: jax.random.normal(k, shape, f32) * s
    gain = lambda k, shape: 1.0 + 0.05 * jax.random.normal(k, shape, f32)
    return {
        "x": nrm(ks[0], (BATCH, SEQ, D_MODEL), 1.0),
        "c": nrm(ks[1], (BATCH, D_MODEL), 1.0),
        "ctx": nrm(ks[2], (BATCH, CTX_LEN, D_MODEL), 1.0),
        "c_ctx": nrm(ks[3], (D_MODEL,), 1.0),
        "w_ada": nrm(ks[4], (DEPTH, D_MODEL, 6 * D_MODEL), 0.5 * D_MODEL ** -0.5),
        "b_ada": nrm(ks[5], (DEPTH, 6 * D_MODEL), 0.02),
        "g_pre_mix": gain(ks[6], (DEPTH, D_MODEL)),
        "w_in": nrm(ks[7], (DEPTH, D_MODEL, IN_WIDTH), D_MODEL ** -0.5),
        "sink": nrm(ks[8], (DEPTH, N_Q_HEADS), 0.5),
        "w_sgu": nrm(ks[9], (DEPTH, N_SGU_HEADS, CHUNK, CHUNK), 0.5 * CHUNK ** -0.5),
        "b_sgu": gain(ks[10], (DEPTH, N_SGU_HEADS, CHUNK)),
        "g_sgu": gain(ks[11], (DEPTH, N_SGU_HEADS, SGU_HEAD_DIM)),
        "g_mix": gain(ks[12], (DEPTH, MIX_WIDTH)),
        "w_out": nrm(ks[13], (DEPTH, MIX_WIDTH, D_MODEL), MIX_WIDTH ** -0.5),
        "g_post_mix": gain(ks[14], (DEPTH, D_MODEL)),
        "g_pre_ff": gain(ks[15], (DEPTH, D_MODEL)),
        "w_ff1": nrm(ks[16], (DEPTH, D_MODEL, D_FF), D_MODEL ** -0.5),
        "w_ff2": nrm(ks[17], (DEPTH, D_FF, D_MODEL), D_FF ** -0.5),
        "g_post_ff": gain(ks[18], (DEPTH, D_MODEL)),
    }


def reference(x, c, ctx, c_ctx, w_ada, b_ada, g_pre_mix, w_in, sink, w_sgu, b_sgu, g_sgu,
              g_mix, w_out, g_post_mix, g_pre_ff, w_ff1, w_ff2, g_post_ff):
    B, S, _ = x.shape
    ROWS = S // GRID_W
    cos, sin = axial_rope_tables(ROWS)
    h_ctx = ctx
    for l in range(DEPTH):
        last = l == DEPTH - 1
        mod_x = (jax.nn.silu(c) @ w_ada[l] + b_ada[l])[:, None, :]
        sh1, sc1, g1, sh2, sc2, g2 = jnp.split(mod_x, 6, axis=-1)
        mod_c = jax.nn.silu(c_ctx) @ w_ada[l] + b_ada[l]
        csh1, csc1, cg1, csh2, csc2, cg2 = jnp.split(mod_c, 6, axis=-1)

        hc = rmsnorm(h_ctx, g_pre_mix[l]) * (1.0 + csc1) + csh1
        if last:
            ckv = hc @ w_in[l][:, Q_END:V_END]
            ck, cv = jnp.split(ckv, 2, axis=-1)
        else:
            pc = hc @ w_in[l]
            cq, ck, cv, cu, cg, cf = jnp.split(pc, [Q_END, K_END, V_END, U_END, G_END], axis=-1)
        ck = ck.reshape(B, CTX_LEN, N_KV_HEADS, HEAD_DIM)
        cv = cv.reshape(B, CTX_LEN, N_KV_HEADS, HEAD_DIM)
        if not last:
            c_attn = context_attention(cq.reshape(B, CTX_LEN, N_Q_HEADS, HEAD_DIM), ck, cv, sink[l])
            c_sgu = spatial_gating(jax.nn.gelu(cu), jax.nn.gelu(cg), w_sgu[l], b_sgu[l], g_sgu[l])
            c_four = fourier_mix(cf)
            yc = merge_heads(c_attn, c_sgu, c_four, g_mix[l], w_out[l])
            h_ctx = h_ctx + cg1 * rmsnorm(yc, g_post_mix[l])
            hc2 = rmsnorm(h_ctx, g_pre_ff[l]) * (1.0 + csc2) + csh2
            h_ctx = h_ctx + cg2 * rmsnorm(squared_relu_mlp(hc2, w_ff1[l], w_ff2[l]), g_post_ff[l])

        hx = rmsnorm(x, g_pre_mix[l]) * (1.0 + sc1) + sh1
        px = hx @ w_in[l]
        q, k, v, u, gv, f = jnp.split(px, [Q_END, K_END, V_END, U_END, G_END], axis=-1)
        q = apply_rope(q.reshape(B, S, N_Q_HEADS, HEAD_DIM), cos, sin)
        k = apply_rope(k.reshape(B, S, N_KV_HEADS, HEAD_DIM), cos, sin)
        v = v.reshape(B, S, N_KV_HEADS, HEAD_DIM)
        attn = latent_window_attention(q, k, v, ck, cv, sink[l])
        sgu = spatial_gating(jax.nn.gelu(u), jax.nn.gelu(gv), w_sgu[l], b_sgu[l], g_sgu[l])
        four = fourier_mix(f)
        y = merge_heads(attn, sgu, four, g_mix[l], w_out[l])
        x = x + g1 * rmsnorm(y, g_post_mix[l])
        hx2 = rmsnorm(x, g_pre_ff[l]) * (1.0 + sc2) + sh2
        x = x + g2 * rmsnorm(squared_relu_mlp(hx2, w_ff1[l], w_ff2[l]), g_post_ff[l])
    return x
```

```python
import contextlib
import numpy as np
import ml_dtypes
import concourse.bass as bass
import concourse.mybir as mybir
from concourse.bass_utils import run_bass_kernel_spmd

F32 = mybir.dt.float32
BF16 = mybir.dt.bfloat16
AF = mybir.ActivationFunctionType
ALU = mybir.AluOpType
AX = mybir.AxisListType
NPBF = ml_dtypes.bfloat16

COMPUTE = ("pe", "act", "dve", "pool")
NDMA_SEMS = 20

D = 1024
S = 8192
LCTX = 256
NTOK = S + LCTX
DFF = 4096
NCOL = 2176
EPS = 1e-6


class _Op:
    __slots__ = ("eng", "fn", "deps", "signal", "sigval", "is_dma", "sem_idx", "dma_val", "prev_same_sem")

    def __init__(self, eng, fn, is_dma):
        self.eng = eng
        self.fn = fn
        self.deps = []
        self.signal = False
        self.sigval = 0
        self.is_dma = is_dma
        self.sem_idx = None
        self.dma_val = 0
        self.prev_same_sem = None


class Prog:
    def __init__(self, nc):
        self.nc = nc
        self.ops = {e: [] for e in ("pe", "act", "dve", "pool", "sp")}
        self.last_w = {}
        self.readers = {}
        self.dma_count = {"sp": 0, "pool": 0, "act": 0}
        self.dma_last_on_sem = {}
        self.stacks = [contextlib.ExitStack()]
        self.last_op = {}

    def push(self):
        self.stacks.append(contextlib.ExitStack())

    def pop(self):
        self.stacks.pop().close()

    def sb(self, name, shape, dt):
        self.uid = getattr(self, "uid", 0) + 1
        return self.stacks[-1].enter_context(self.nc.sbuf_tensor("%s_u%d" % (name, self.uid), list(shape), dt))

    def ps(self, name, shape, dt):
        return self.stacks[-1].enter_context(self.nc.psum_tensor(name, list(shape), dt))

    def _track(self, op, r, w):
        ps_keys = [("ps", k[1]) for k in list(r) + list(w) if isinstance(k, tuple) and k[0] == "ps"]
        r = [k for k in r if not (isinstance(k, tuple) and k[0] == "ps")]
        w = [k for k in w if not (isinstance(k, tuple) and k[0] == "ps")] + list(dict.fromkeys(ps_keys))
        deps = set()
        for k in r:
            lw = self.last_w.get(k)
            if lw is not None:
                deps.add(lw)
        for k in w:
            lw = self.last_w.get(k)
            if lw is not None and (op.is_dma or lw.is_dma or lw.eng != op.eng):
                deps.add(lw)
            for rd in self.readers.get(k, ()):
                if rd is op:
                    continue
                if op.is_dma or rd.is_dma or rd.eng != op.eng:
                    deps.add(rd)
        for d in deps:
            if d is op:
                continue
            d.signal = True
            op.deps.append(d)
        for k in r:
            self.readers.setdefault(k, []).append(op)
        for k in w:
            self.last_w[k] = op
            self.readers[k] = []

    def op(self, eng, fn, r=(), w=()):
        o = _Op(eng, fn, False)
        self._track(o, r, w)
        self.ops[eng].append(o)
        self.last_op[eng] = o
        return o

    def wait_all(self, eng, r=()):
        o = _Op(eng, None, False)
        self._track(o, r, ())
        self.ops[eng].append(o)
        return o

    def dma(self, q, out, in_, r=(), w=(), **kw):
        o = _Op(q, (out, in_, kw), True)
        n = self.dma_count[q]
        self.dma_count[q] = n + 1
        nsem = NDMA_SEMS if q != "pool" else 6
        o.sem_idx = (q, n % nsem)
        prev = self.dma_last_on_sem.get(o.sem_idx)
        o.prev_same_sem = prev
        o.dma_val = (prev.dma_val if prev is not None else 0) + 16
        self.dma_last_on_sem[o.sem_idx] = o
        self._track(o, r, w)
        self.ops[q].append(o)
        return o

    def barrier(self):
        srcs = [o for o in self.last_op.values()] + list(self.dma_last_on_sem.values())
        for e in ("pe", "act", "dve", "pool", "sp"):
            o = _Op(e, None, False)
            for s_ in srcs:
                if (not s_.is_dma) and s_.eng == e:
                    continue
                s_.signal = True
                o.deps.append(s_)
            self.ops[e].append(o)
        self.last_w = {}
        self.readers = {}

    def emit(self):
        nc = self.nc
        st = self.stacks[0]
        esem = {e: st.enter_context(nc.semaphore("sem_" + e)) for e in COMPUTE}
        dsem = {}
        for q in ("sp", "pool", "act"):
            for i in range(min(NDMA_SEMS, self.dma_count[q])):
                dsem[(q, i)] = st.enter_context(nc.semaphore("dsem_%s_%d" % (q, i)))
        for e in COMPUTE:
            c = 0
            for o in self.ops[e]:
                if (not o.is_dma) and o.signal:
                    c += 1
                    o.sigval = c

        def src_of(d):
            if d.is_dma:
                return dsem[d.sem_idx], d.dma_val, ("d",) + d.sem_idx
            return esem[d.eng], d.sigval, ("e", d.eng)

        def run(engname, eng):
            seen = {}
            for o in self.ops[engname]:
                waits = {}
                deps = list(o.deps)
                if o.is_dma and o.prev_same_sem is not None:
                    deps.append(o.prev_same_sem)
                for d in deps:
                    sem, val, key = src_of(d)
                    if seen.get(key, 0) >= val:
                        continue
                    if key not in waits or waits[key][1] < val:
                        waits[key] = (sem, val)
                for key, (sem, val) in waits.items():
                    eng.wait_ge(sem, val)
                    seen[key] = val
                if o.is_dma:
                    out, in_, kw = o.fn
                    eng.dma_start(out=out, in_=in_, **kw).then_inc(dsem[o.sem_idx], 16)
                elif o.fn is None:
                    pass
                else:
                    ins = o.fn(eng)
                    if o.signal:
                        ins.then_inc(esem[engname], 1)

        with nc.Block() as block:
            @block.tensor
            def _(e):
                run("pe", e)

            @block.scalar
            def _(e):
                run("act", e)

            @block.vector
            def _(e):
                run("dve", e)

            @block.gpsimd
            def _(e):
                run("pool", e)

            @block.sync
            def _(e):
                run("sp", e)

    def close(self):
        while self.stacks:
            self.stacks.pop().close()


class Ring:
    def __init__(self, items):
        self.items = items
        self.i = 0

    def next(self):
        it = self.items[self.i % len(self.items)]
        self.i += 1
        return it


def _perm_cols():
    def rotp(d):
        if d < 16:
            return d + 16
        if d < 32:
            return d - 16
        if d < 48:
            return d + 16
        return d - 16
    cols = []
    for c in range(4):
        for h in (c, 4 + c):
            cols += [h * 64 + d for d in range(64)]
    for c in range(4):
        for h in (c, 4 + c):
            cols += [h * 64 + rotp(d) for d in range(64)]
    cols += [512 + j for j in range(128)]
    cols += [512 + h * 64 + rotp(d) for h in range(2) for d in range(64)]
    cols += [1280 + j for j in range(256)]
    cols += [640 + j for j in range(128)]
    cols += [768 + j for j in range(256)]
    cols += [1024 + j for j in range(256)]
    assert len(cols) == NCOL
    return np.array(cols)


def _attn_row_perm():
    rows = []
    for c in range(4):
        for h in (c, 4 + c):
            rows += [h * 64 + d for d in range(64)]
    return np.array(rows + list(range(512, 1024)))


def _constants():
    k = {}
    k["ident"] = np.eye(128, dtype=np.float32).astype(NPBF)
    n_freq = 16
    inv_freq = (np.float32(10000.0) ** (-np.arange(n_freq, dtype=np.float32) / np.float32(n_freq))).astype(np.float32)
    t = np.arange(S)
    row = (t // 64).astype(np.float32)
    col = (t % 64).astype(np.float32)
    ang = np.zeros((64, S), np.float32)
    ang[0:16] = row[None, :] * inv_freq[:, None]
    ang[16:32] = ang[0:16]
    ang[32:48] = col[None, :] * inv_freq[:, None]
    ang[48:64] = ang[32:48]
    cos = np.cos(ang).astype(np.float32)
    sin = np.sin(ang).astype(np.float32)
    sin[0:16] *= -1
    sin[32:48] *= -1
    cosT = np.ones((128, NTOK), np.float32)
    sinT = np.zeros((128, NTOK), np.float32)
    cosT[0:64, :S] = cos
    cosT[64:128, :S] = cos
    sinT[0:64, :S] = sin
    sinT[64:128, :S] = sin
    k["cosT"] = cosT
    k["sinT"] = sinT
    kk = np.arange(128)[:, None]
    qq = np.arange(128)[None, :]
    k["mask_next"] = (kk <= qq).astype(np.float32).astype(NPBF)
    k["mask_prev"] = (kk >= qq).astype(np.float32).astype(NPBF)
    k["ones"] = np.ones((128, 128), np.float32).astype(NPBF)
    cidx = np.arange(64)
    c64 = np.cos(2 * np.pi * np.outer(cidx, cidx) / 64.0) / 8.0
    s64 = np.sin(2 * np.pi * np.outer(cidx, cidx) / 64.0) / 8.0
    csblk = np.zeros((128, 256), np.float64)
    for g in range(2):
        csblk[g * 64:(g + 1) * 64, g * 64:(g + 1) * 64] = c64
        csblk[g * 64:(g + 1) * 64, 128 + g * 64:128 + (g + 1) * 64] = s64
    k["csblk"] = csblk.astype(np.float32).astype(NPBF)
    t1 = np.zeros((128, 128), np.float64)
    t1[0:64, 0:64] = c64
    t1[0:64, 64:128] = -s64
    t1[64:128, 0:64] = -s64
    t1[64:128, 64:128] = -c64
    k["t1tab"] = t1.astype(np.float32).astype(NPBF)
    t2 = np.arange(128)[:, None, None].astype(np.float64)
    k1 = np.arange(64)[None, :, None].astype(np.float64)
    k2 = np.arange(128)[None, None, :].astype(np.float64)
    phi = 2 * np.pi * ((t2 * (k1 + 64 * k2)) % 8192) / 8192.0
    k["mc"] = (np.cos(phi) / np.sqrt(128.0)).astype(np.float32).astype(NPBF).reshape(128, 64 * 128)
    k["ms"] = (np.sin(phi) / np.sqrt(128.0)).astype(np.float32).astype(NPBF).reshape(128, 64 * 128)
    tt = (np.arange(2)[None, :, None] * 128 + np.arange(128)[:, None, None]).astype(np.float64)
    kk2 = np.arange(256)[None, None, :].astype(np.float64)
    ph = 2 * np.pi * ((tt * kk2) % 256) / 256.0
    k["c256"] = (np.cos(ph) / 16.0).astype(np.float32).astype(NPBF).reshape(128, 512)
    k["s256n"] = (-np.sin(ph) / 16.0).astype(np.float32).astype(NPBF).reshape(128, 512)
    sel = np.zeros((2, 2, 128), np.float32)
    sel[0, 0, :] = 1
    sel[1, 1, :] = 1
    k["sel"] = sel.reshape(2, 256)
    k["i2"] = np.eye(2, dtype=np.float32)
    return k


CONST_SPECS = [
    ("ident", [128, 128], BF16), ("cosT", [128, NTOK], F32), ("sinT", [128, NTOK], F32),
    ("mask_next", [128, 128], BF16), ("mask_prev", [128, 128], BF16), ("ones", [128, 128], BF16),
    ("csblk", [128, 256], BF16), ("t1tab", [128, 128], BF16), ("mc", [128, 8192], BF16), ("ms", [128, 8192], BF16),
    ("c256", [128, 512], BF16), ("s256n", [128, 512], BF16), ("sel", [2, 256], F32), ("i2", [2, 2], F32),
]

IN_SPECS = [
    ("x", [S, D]), ("ctx", [LCTX, D]), ("csT", [128, 16]),
    ("w_ada", [2, D, 6 * D]), ("b_ada", [2, 6 * D]),
    ("g_pre_mix", [2, D]), ("g_post_mix", [2, D]), ("g_pre_ff", [2, D]), ("g_post_ff", [2, D]),
    ("w_inp", [2, D, NCOL]), ("sinkT", [2, 128, 4]), ("w_sguT", [2, 128, 512]), ("b_sguT", [2, 128, 4]),
    ("g_sgu", [2, 256]), ("g_mixT", [2, 128, 8]), ("w_outp", [2, D, D]),
    ("w_ff1", [2, D, DFF]), ("w_ff2", [2, DFF, D]),
]


class Builder:
    def __init__(self, nlayers=2, stop_after=None, debug=()):
        self.nlayers = nlayers
        self.stop_after = stop_after
        self.debug = set(debug)
        nc = bass.Bass("TRN2", target_bir_lowering=False)
        self.nc = nc
        self.I = {}
        for name, shape in IN_SPECS:
            self.I[name] = nc.dram_tensor(name, shape, F32, kind="ExternalInput").ap()
        self.C = {}
        for name, shape, dt in CONST_SPECS:
            self.C[name] = nc.dram_tensor("c_" + name, shape, dt, kind="ExternalInput").ap()
        self.out = nc.dram_tensor("out", [S, D], F32, kind="ExternalOutput").ap()
        kw = dict(kind="ExternalOutput") if "scratch_out" in self.debug else {}
        self.qT_d = nc.dram_tensor("qT_d", [4, 128, NTOK], BF16, **kw).ap()
        self.sguT_d = nc.dram_tensor("sguT_d", [2, 128, NTOK], BF16, **kw).ap()
        self.AB_d = nc.dram_tensor("AB_d", [NTOK, 512], BF16, **kw).ap()
        self.kT_d = nc.dram_tensor("kT_d", [128, NTOK], BF16, **kw).ap()
        self.vt_d = nc.dram_tensor("vt_d", [128, 66, 128], BF16, **kw).ap()
        self.fourT_d = nc.dram_tensor("fourT_d", [128, 2, NTOK], BF16, **kw).ap()
        self.xmid_d = nc.dram_tensor("xmid_d", [NTOK, D], F32, **kw).ap()
        self.x1_d = nc.dram_tensor("x1_d", [NTOK, D], F32, **kw).ap()
        self.G_d = nc.dram_tensor("G_d", [2, 4, D], F32, **kw).ap()
        self.modT_d = nc.dram_tensor("modT_d", [128, 64], F32, **kw).ap()
        self.dbg = {}
        self.P = Prog(nc)

    def act(self, out, in_, func, r, w, **kw):
        return self.P.op("act", lambda e: e.activation(out=out, in_=in_, func=func, **kw), r, w)

    def tt(self, eng, out, in0, in1, op, r, w):
        return self.P.op(eng, lambda e: e.tensor_tensor(out=out, in0=in0, in1=in1, op=op), r, w)

    def ts(self, eng, out, in0, s1, s2, op0, op1, r, w):
        if s2 is None:
            return self.P.op(eng, lambda e: e.tensor_scalar(out=out, in0=in0, scalar1=s1, scalar2=None, op0=op0), r, w)
        return self.P.op(eng, lambda e: e.tensor_scalar(out=out, in0=in0, scalar1=s1, scalar2=s2, op0=op0, op1=op1), r, w)

    def stt(self, out, in0, scalar, in1, op0, op1, r, w):
        return self.P.op("dve", lambda e: e.scalar_tensor_tensor(out=out, in0=in0, scalar=scalar, in1=in1, op0=op0, op1=op1), r, w)

    def cp(self, eng, out, in_, r, w):
        if eng == "act":
            return self.P.op("act", lambda e: e.activation(out=out, in_=in_, func=AF.Copy), r, w)
        return self.P.op(eng, lambda e: e.tensor_copy(out=out, in_=in_), r, w)

    def recip(self, out, in_, r, w):
        return self.P.op("dve", lambda e: e.reciprocal(out=out, in_=in_), r, w)

    def rstd(self, out, ss, n, key_in, key_out, tmpkey):
        self.act(out, ss, AF.Sqrt, r=key_in, w=[tmpkey], scale=1.0 / n, bias=self.eps_t[:, 0:1])
        self.recip(out, out, r=[tmpkey], w=key_out)

    def mm(self, specs, r, w):
        def fn(e):
            ins = None
            for sp_ in specs:
                ins = e.matmul(out=sp_[0], lhsT=sp_[1], rhs=sp_[2], start=sp_[3], stop=sp_[4], skip_group_check=True)
            return ins
        return self.P.op("pe", fn, r, w)

    def tr(self, specs, r, w):
        def fn(e):
            ins = None
            for o_, i_ in specs:
                ins = e.transpose(out=o_, in_=i_, identity=self.ident[:])
            return ins
        return self.P.op("pe", fn, r, w)

    def build(self):
        P = self.P
        nc = self.nc
        self.ident = P.sb("ident", [128, 128], BF16)
        self.ones = P.sb("ones", [128, 128], BF16)
        self.eps_t = P.sb("eps_t", [128, 1], F32)
        self.csil = P.sb("csil", [128, 8, 2], F32)
        self.modT = P.sb("modT", [128, 4, 8, 2], F32)
        self.psum = P.ps("psum", [128, 8, 512], F32)
        self.banks = [self.psum[:, i, :] for i in range(8)]
        P.dma("sp", self.ident[:], self.C["ident"][:, :], w=["ident"])
        P.dma("sp", self.ones[:], self.C["ones"][:, :], w=["ones"])
        P.op("dve", lambda e: e.memset(self.eps_t[:], EPS), w=["eps"])
        P.dma("sp", self.csil[:].rearrange("p c r -> p (c r)"), self.I["csT"][:, :], w=["csil"])
        self.act(self.csil[:], self.csil[:], AF.Silu, r=["csil"], w=["csil"])
        P.barrier()
        for l in range(self.nlayers):
            self.layer(l)
        P.barrier()
        P.emit()
        P.close()
        return nc

    def src_rows(self, l, tok0, n):
        if l == 0:
            if tok0 >= S:
                return self.I["ctx"][tok0 - S:tok0 - S + n, :]
            return self.I["x"][tok0:tok0 + n, :]
        return self.x1_d[tok0:tok0 + n, :]

    def dst_rows(self, l, tok0, n):
        if l == self.nlayers - 1 and self.nlayers == 2:
            assert tok0 < S
            return self.out[tok0:tok0 + n, :]
        if self.nlayers == 1 and tok0 < S:
            return self.out[tok0:tok0 + n, :]
        return self.x1_d[tok0:tok0 + n, :]

    def layer(self, l):
        P = self.P
        last = (l == 1)
        self.l = l
        self.ctx_full = not last
        P.push()
        self.gmixT = P.sb("gmixT", [128, 8], F32)
        self.esink = P.sb("esink", [128, 4], F32)
        self.bsT = P.sb("bsT", [128, 4], F32)
        self.fourT = P.sb("fourT", [128, 2, NTOK], BF16)
        P.dma("sp", self.gmixT[:], self.I["g_mixT"][l], w=["gmixT"])
        P.dma("sp", self.esink[:], self.I["sinkT"][l], w=["esink"])
        P.dma("sp", self.bsT[:], self.I["b_sguT"][l], w=["bsT"])
        self.act(self.esink[:], self.esink[:], AF.Exp, r=["esink"], w=["esink"])
        self.adaln(l)
        if self.stop_after == "adaln":
            P.pop()
            return
        self.phaseA(l)
        if self.stop_after == "A":
            P.pop()
            return
        self.phaseF(l)
        if self.stop_after == "F":
            P.pop()
            return
        self.phaseC(l)
        if self.stop_after == "C":
            P.pop()
            return
        P.pop()
        self.phaseD(l)

    def adaln(self, l):
        P = self.P
        I = self.I
        P.push()
        wblk = [P.sb("wada%d" % i, [128, 8, 512], F32) for i in range(2)]
        modrow = P.sb("modrow", [2, 6 * D], F32)
        brow = P.sb("brow", [2, 6 * D], F32)
        gains = P.sb("gains", [2, 4, D], F32)
        rows = P.sb("rows", [2, 6, D], F32)
        sel = P.sb("sel", [2, 2, 128], F32)
        i2 = P.sb("i2", [2, 2], F32)
        P.dma("sp", brow[:], I["b_ada"][l:l + 1, :].partition_broadcast(2), w=["brow"])
        for i, nm in enumerate(("g_pre_mix", "g_post_mix", "g_pre_ff", "g_post_ff")):
            P.dma("sp", gains[:, i, :], I[nm][l:l + 1, :].partition_broadcast(2), w=[("gains", i)])
        P.dma("sp", sel[:].rearrange("k r m -> k (r m)"), self.C["sel"][:, :], w=["sel"])
        P.dma("sp", i2[:], self.C["i2"][:, :], w=["i2"])
        for nb in range(12):
            wb = wblk[nb % 2]
            P.dma("sp", wb[:], I["w_ada"][l, :, nb * 512:(nb + 1) * 512].rearrange("(c p) n -> p c n", p=128),
                  w=[("wada", nb % 2)])
            bk = self.banks[nb % 2]
            self.mm([(bk[0:2, :], self.csil[:, c, :], wb[:, c, :], c == 0, c == 7) for c in range(8)],
                    r=[("wada", nb % 2), "csil"], w=[("ps", nb % 2)])
            self.tt("dve", modrow[:, nb * 512:(nb + 1) * 512], bk[0:2, :], brow[:, nb * 512:(nb + 1) * 512], ALU.add,
                    r=[("ps", nb % 2), "brow"], w=[("modrow", nb)])
        mr = lambda i: modrow[:, i * D:(i + 1) * D]
        mk = lambda i: [("modrow", 2 * i), ("modrow", 2 * i + 1)]
        self.stt(rows[:, 0, :], mr(1), 1.0, gains[:, 0, :], ALU.add, ALU.mult, r=mk(1) + [("gains", 0)], w=[("rows", 0)])
        self.cp("dve", rows[:, 1, :], mr(0), r=mk(0), w=[("rows", 1)])
        self.stt(rows[:, 2, :], mr(4), 1.0, gains[:, 2, :], ALU.add, ALU.mult, r=mk(4) + [("gains", 2)], w=[("rows", 2)])
        self.cp("dve", rows[:, 3, :], mr(3), r=mk(3), w=[("rows", 3)])
        self.tt("dve", rows[:, 4, :], mr(2), gains[:, 1, :], ALU.mult, r=mk(2) + [("gains", 1)], w=[("rows", 4)])
        self.tt("dve", rows[:, 5, :], mr(5), gains[:, 3, :], ALU.mult, r=mk(5) + [("gains", 3)], w=[("rows", 5)])
        bk = self.banks[2]
        specs = []
        for v in range(4):
            for c in range(8):
                col = (v * 8 + c) * 2
                specs.append((bk[:, col:col + 2], rows[:, v, c * 128:(c + 1) * 128], i2[:, :], True, True))
        self.mm(specs, r=[("rows", v) for v in range(4)] + ["i2"], w=[("ps", 2)])
        self.cp("dve", self.modT[:].rearrange("p v c r -> p (v c r)"), bk[:, 0:64], r=[("ps", 2)], w=["modT"])
        if "scratch_out" in self.debug:
            P.dma("pool", self.modT_d[:, :], self.modT[:].rearrange("p v c r -> p (v c r)"), r=["modT"], w=["modT_d"])
        P.dma("pool", self.G_d[:, 0, :], rows[:, 4, :], r=[("rows", 4)], w=["G_d0"])
        P.dma("pool", self.G_d[:, 1, :], rows[:, 5, :], r=[("rows", 5)], w=["G_d1"])
        P.barrier()
        P.pop()

    def new_staging(self):
        self._stg = [self.P.sb("stg%d" % i, [128, 1024], F32) for i in range(2)]
        self._stg_n = 0

    def load_weight(self, dst, src_fn, nchunks, width, tag):
        P = self.P
        piece = 1024
        keys = []
        for c in range(nchunks):
            for o in range(0, width, piece):
                wd = min(piece, width - o)
                n = self._stg_n
                self._stg_n += 1
                s_ = self._stg[n % 2]
                P.dma("sp", s_[:, 0:wd], src_fn(c)[:, o:o + wd], w=[("stg", n % 2)])
                eng = ("pool", "dve", "act")[n % 3]
                self.cp(eng, dst[:, c, o:o + wd], s_[:, 0:wd], r=[("stg", n % 2)], w=[(tag, c, o)])
                keys.append((tag, c, o))
        return keys

    def phaseA(self, l):
        P = self.P
        I = self.I
        C = self.C
        P.push()
        self.winb = P.sb("winb", [128, 8, NCOL], BF16)
        self.new_staging()
        self.wink = self.load_weight(self.winb, lambda c: I["w_inp"][l, c * 128:(c + 1) * 128, :], 8, NCOL, "winb")
        A = type("A", (), {})()
        self.A = A
        A.xt = [P.sb("xt%d" % i, [128, D], F32) for i in range(4)]
        A.junk = P.sb("junk", [128, D], BF16)
        A.ss = P.sb("ssA", [128, 8], F32)
        A.rs = P.sb("rsA", [128, 8], F32)
        A.xn = [P.sb("xn%d" % i, [128, D], BF16) for i in range(4)]
        A.hxT = [P.sb("hxT%d" % i, [128, 8, 512], BF16) for i in range(2)]
        A.cos = [P.sb("cos%d" % i, [128, 512], F32) for i in range(2)]
        A.sin = [P.sb("sin%d" % i, [128, 512], F32) for i in range(2)]
        A.t1 = [P.sb("t1_%d" % i, [128, 512], F32) for i in range(2)]
        A.t2 = [P.sb("t2_%d" % i, [128, 512], F32) for i in range(2)]
        A.qT = [P.sb("qTt%d" % i, [128, 4, 512], BF16) for i in range(2)]
        A.fT = P.sb("fTt", [128, 2, 512], BF16)
        A.kTt = [P.sb("kTt%d" % i, [128, 512], BF16) for i in range(2)]
        A.vtt = [P.sb("vtt%d" % i, [128, 4, 128], BF16) for i in range(2)]
        A.ABt = [P.sb("ABt%d" % i, [128, 4, 512], BF16) for i in range(2)]
        A.ug = [P.sb("ug%d" % i, [128, 4, 256], F32) for i in range(2)]
        A.gg = [P.sb("gg%d" % i, [128, 4, 256], F32) for i in range(2)]
        A.sq = P.sb("sqA", [128, 4, 256], F32)
        A.ssh = P.sb("ssh", [128, 16], F32)
        A.rsh = P.sb("rsh", [128, 16], F32)
        A.vc = P.sb("vc", [128, 4, 256], BF16)
        A.so = P.sb("so", [128, 4, 256], F32)
        A.ss2 = P.sb("ss2", [128, 4], F32)
        A.rs2 = P.sb("rs2", [128, 4], F32)
        A.son = P.sb("son", [128, 4, 256], BF16)
        A.sgt = [P.sb("sgt%d" % i, [128, 2, 512], BF16) for i in range(2)]
        A.gsg = P.sb("gsg", [128, 256], F32)
        A.wsT = P.sb("wsT", [128, 4, 128], BF16)
        A.wsf = P.sb("wsf", [128, 512], F32)
        A.csblk = P.sb("csblk", [128, 256], BF16)
        P.dma("sp", A.gsg[:], I["g_sgu"][l:l + 1, :].partition_broadcast(128), w=["gsg"])
        P.dma("sp", A.wsf[:], I["w_sguT"][l], w=["wsf"])
        self.cp("dve", A.wsT[:].rearrange("p h q -> p (h q)"), A.wsf[:], r=["wsf"], w=["wsT"])
        P.dma("sp", A.csblk[:], C["csblk"][:, :], w=["csblk"])
        A.trbanks = Ring([(self.banks[i].bitcast(BF16), ("ps", i)) for i in range(2)])
        A.pbanks = Ring([(self.banks[i], ("ps", i)) for i in range(2, 8)])
        A.xi = 0
        tiles = [(S, 2, True)] + [(i * 512, 4, False) for i in range(S // 512)]
        if "A_tiles" in self.debug:
            tiles = tiles[:3]
        self.A_front(l, 0, *tiles[0])
        for i, t in enumerate(tiles):
            if i + 1 < len(tiles):
                self.A_front(l, i + 1, *tiles[i + 1])
            self.A_back(l, i, *t)
            if i > 0:
                self.A_back2(l, i - 1, *tiles[i - 1])
        self.A_back2(l, len(tiles) - 1, *tiles[-1])
        P.barrier()
        P.pop()

    def A_front(self, l, it, tok0, nsub, is_ctx):
        P = self.P
        A = self.A
        mi = 1 if is_ctx else 0
        T = nsub * 128
        hx = A.hxT[it % 2]
        hk = ("hxT", it % 2)
        for s in range(nsub):
            xi = A.xi % 4
            A.xi += 1
            P.dma("sp", A.xt[xi][:], self.src_rows(l, tok0 + s * 128, 128), w=[("xt", xi)])
            self.act(A.junk[:], A.xt[xi][:], AF.Square, r=[("xt", xi)], w=["junk", ("ssA", s)], accum_out=A.ss[:, s:s + 1])
            self.rstd(A.rs[:, s:s + 1], A.ss[:, s:s + 1], float(D), [("ssA", s)], [("rsA", s)], ("rsAt", s))
            self.ts("dve", A.xn[xi][:], A.xt[xi][:], A.rs[:, s:s + 1], None, ALU.mult, None,
                    r=[("xt", xi), ("rsA", s)], w=[("xn", xi)])
        xis = [(A.xi - nsub + s) % 4 for s in range(nsub)]
        for cp_ in range(4):
            tb, sk = A.trbanks.next()
            self.tr([(tb[:, j * 512 + s * 128: j * 512 + (s + 1) * 128], A.xn[xis[s]][:, (2 * cp_ + j) * 128:(2 * cp_ + j + 1) * 128])
                     for j in range(2) for s in range(nsub)],
                    r=[("xn", xi) for xi in xis] + ["ident"], w=[sk])
            for j in range(2):
                c = 2 * cp_ + j
                slot = tb[:, j * 512:(j + 1) * 512]
                if cp_ % 2 == 0:
                    self.act(hx[:, c, 0:T], slot[:, 0:T], AF.Identity, r=[sk, "modT"], w=[hk + (c,)],
                             scale=self.modT[:, 0, c, mi:mi + 1], bias=self.modT[:, 1, c, mi:mi + 1])
                else:
                    self.ts("dve", hx[:, c, 0:T], slot[:, 0:T], self.modT[:, 0, c, mi:mi + 1], self.modT[:, 1, c, mi:mi + 1],
                            ALU.mult, ALU.add, r=[sk, "modT"], w=[hk + (c,)])

    def A_back(self, l, it, tok0, nsub, is_ctx):
        P = self.P
        A = self.A
        C = self.C
        if "A_nob" in self.debug:
            return
        T = nsub * 128
        hx = A.hxT[it % 2]
        hkeys = [("hxT", it % 2, c) for c in range(8)]
        W = self.winb
        full = (not is_ctx) or self.ctx_full
        sl = it % 2
        P.dma("sp", A.cos[sl][:, 0:T], C["cosT"][:, tok0:tok0 + T], w=[("cos", sl)])
        P.dma("sp", A.sin[sl][:, 0:T], C["sinT"][:, tok0:tok0 + T], w=[("sin", sl)])

        def fm_chunk(fc):
            bk, bkk = A.pbanks.next()
            self.mm([(bk[:, 0:T], W[:, dc, fc * 128:(fc + 1) * 128], hx[:, dc, 0:T], dc == 0, dc == 7) for dc in range(8)],
                    r=hkeys + self.wink, w=[bkk])
            return bk, bkk

        def rope(fc, fcr, out_ap, okey, j):
            bq, kq = fm_chunk(fc)
            br, kr = fm_chunk(fcr)
            self.tt("dve", A.t1[j % 2][:, 0:T], bq[:, 0:T], A.cos[sl][:, 0:T], ALU.mult, r=[kq, ("cos", sl)], w=[("t1", j % 2)])
            self.tt("dve", A.t2[j % 2][:, 0:T], br[:, 0:T], A.sin[sl][:, 0:T], ALU.mult, r=[kr, ("sin", sl)], w=[("t2", j % 2)])
            self.tt("pool", out_ap, A.t1[j % 2][:, 0:T], A.t2[j % 2][:, 0:T], ALU.add, r=[("t1", j % 2), ("t2", j % 2)], w=okey)

        rope(8, 9, A.kTt[sl][:, 0:T], [("kTt", sl)], 0)
        P.dma("pool", self.kT_d[:, tok0:tok0 + T], A.kTt[sl][:, 0:T], r=[("kTt", sl)], w=[("kT_d", tok0)])
        if full:
            for c in range(4):
                rope(c, 4 + c, A.qT[sl][:, c, 0:T], [("qTt", sl, c)], c + 1)
            P.dma("pool", self.qT_d[:, :, tok0:tok0 + T].rearrange("c p t -> p c t"), A.qT[sl][:, :, 0:T],
                  r=[("qTt", sl, c) for c in range(4)], w=[("qT_d", tok0)])
            for cf in range(2):
                bk, bkk = fm_chunk(10 + cf)
                self.cp("act", A.fT[:, cf, 0:T], bk[:, 0:T], r=[bkk], w=[("fT", cf)])
        if "A_nob2" in self.debug:
            return
        for s in range(nsub):
            b1, k1 = A.pbanks.next()
            ncol = 384 if full else 128
            self.mm([(b1[:, 0:ncol], hx[:, dc, s * 128:(s + 1) * 128], W[:, dc, 1536:1536 + ncol], dc == 0, dc == 7) for dc in range(8)],
                    r=hkeys + self.wink, w=[k1])
            self.cp("act", A.vtt[sl][:, s, :], b1[:, 0:128], r=[k1], w=[("vtt", sl, s)])
            if full:
                self.act(A.ug[sl][:, s, :], b1[:, 128:384], AF.Gelu_apprx_tanh, r=[k1], w=[("ug", sl, s)])
                b2, k2 = A.pbanks.next()
                self.mm([(b2[:, 0:256], hx[:, dc, s * 128:(s + 1) * 128], W[:, dc, 1920:2176], dc == 0, dc == 7) for dc in range(8)],
                        r=hkeys + self.wink, w=[k2])
                self.act(A.gg[sl][:, s, :], b2[:, 0:256], AF.Gelu_apprx_tanh, r=[k2], w=[("gg", sl, s)])
        blk0 = tok0 // 128
        P.dma("pool", self.vt_d[:, blk0:blk0 + nsub, :], A.vtt[sl][:, 0:nsub, :], r=[("vtt", sl, s) for s in range(nsub)],
              w=[("vt_d", tok0)])
        if not full or "A_nob3" in self.debug:
            return
        ns = nsub
        for s in range(ns):
            bk, bkk = A.pbanks.next()
            specs = []
            for cf in range(2):
                specs.append((bk[:, cf * 256:(cf + 1) * 256], A.fT[:, cf, s * 128:(s + 1) * 128], A.csblk[:, :], True, True))
            self.mm(specs, r=[("fT", 0), ("fT", 1), "csblk"], w=[bkk])
            self.cp("act", A.ABt[sl][:, s, :], bk[:, :], r=[bkk], w=[("ABt", sl, s)])
        P.dma("pool", self.AB_d[tok0:tok0 + T, :].rearrange("(s p) c -> p s c", p=128), A.ABt[sl][:, 0:ns, :],
              r=[("ABt", sl, s) for s in range(ns)], w=[("AB_d", tok0)])
        return

    def A_back2(self, l, it, tok0, nsub, is_ctx):
        P = self.P
        A = self.A
        full = (not is_ctx) or self.ctx_full
        if not full or "A_nob" in self.debug or "A_nob2" in self.debug or "A_nob3" in self.debug:
            return
        ns = nsub
        T = nsub * 128
        sl = it % 2
        ggk = [("gg", sl, s) for s in range(ns)]
        ugk = [("ug", sl, s) for s in range(ns)]
        self.tt("pool", A.sq[:, 0:ns, :], A.gg[sl][:, 0:ns, :], A.gg[sl][:, 0:ns, :], ALU.mult, r=ggk, w=["sq"])
        P.op("dve", lambda e: e.tensor_reduce(out=A.ssh[:, 0:ns * 4], in_=A.sq[:, 0:ns, :].rearrange("p s (h d) -> p (s h) d", h=4),
                                              axis=AX.X, op=ALU.add), r=["sq"], w=["ssh"])
        self.rstd(A.rsh[:, 0:ns * 4], A.ssh[:, 0:ns * 4], 64.0, ["ssh"], ["rsh"], "rsht")
        self.tt("dve", A.sq[:, 0:ns, :].rearrange("p s (h d) -> p (s h) d", h=4),
                A.gg[sl][:, 0:ns, :].rearrange("p s (h d) -> p (s h) d", h=4),
                A.rsh[:, 0:ns * 4].unsqueeze(2).to_broadcast([128, ns * 4, 64]), ALU.mult, r=ggk + ["rsh"], w=["sq"])
        self.tt("dve", A.vc[:, 0:ns, :], A.sq[:, 0:ns, :], A.gsg[:, :].unsqueeze(1).to_broadcast([128, ns, 256]), ALU.mult,
                r=["sq", "gsg"], w=["vc"])
        for hp in range(2):
            bk, bkk = A.pbanks.next()
            specs = []
            for hh in range(2):
                h = hp * 2 + hh
                o = bk[:, hh * 256:hh * 256 + ns * 64].rearrange("p (s d) -> p s d", d=64)
                specs.append((o, A.wsT[:, h, :], A.vc[:, 0:ns, h * 64:(h + 1) * 64], True, True))
            self.mm(specs, r=["vc", "wsT"], w=[bkk])
            for hh in range(2):
                h = hp * 2 + hh
                self.stt(A.so[:, 0:ns, h * 64:(h + 1) * 64], bk[:, hh * 256:hh * 256 + ns * 64].rearrange("p (s d) -> p s d", d=64),
                         self.bsT[:, h:h + 1], A.ug[sl][:, 0:ns, h * 64:(h + 1) * 64], ALU.add, ALU.mult,
                         r=[bkk, "bsT"] + ugk, w=[("so", h)])
        sok = [("so", h) for h in range(4)]
        self.tt("pool", A.sq[:, 0:ns, :], A.so[:, 0:ns, :], A.so[:, 0:ns, :], ALU.mult, r=sok, w=["sq"])
        P.op("dve", lambda e: e.tensor_reduce(out=A.ss2[:, 0:ns], in_=A.sq[:, 0:ns, :], axis=AX.X, op=ALU.add), r=["sq"], w=["ss2"])
        self.rstd(A.rs2[:, 0:ns], A.ss2[:, 0:ns], 256.0, ["ss2"], ["rs2"], "rs2t")
        self.tt("dve", A.son[:, 0:ns, :], A.so[:, 0:ns, :], A.rs2[:, 0:ns].unsqueeze(2).to_broadcast([128, ns, 256]), ALU.mult,
                r=sok + ["rs2"], w=["son"])
        tb, sk = A.trbanks.next()
        self.tr([(tb[:, cc * 512 + s * 128: cc * 512 + (s + 1) * 128], A.son[:, s, cc * 128:(cc + 1) * 128])
                 for cc in range(2) for s in range(ns)], r=["son", "ident"], w=[sk])
        for cc in range(2):
            self.act(A.sgt[sl][:, cc, 0:T], tb[:, cc * 512: cc * 512 + T], AF.Identity, r=[sk, "gmixT"], w=[("sgt", sl, cc)],
                     scale=self.gmixT[:, 4 + cc:5 + cc])
        P.dma("pool", self.sguT_d[:, :, tok0:tok0 + T].rearrange("c p t -> p c t"), A.sgt[sl][:, :, 0:T],
              r=[("sgt", sl, 0), ("sgt", sl, 1)], w=[("sguT_d", tok0)])

    def four_finish(self, ybank_ap, ykey, nk, tok_of, g):
        P = self.P
        F = self.F
        i = F.fi % 2
        F.fi += 1
        ysb = F.ysb[i]
        self.cp("act", ysb[:, 0:nk, :], ybank_ap, r=[ykey], w=[("ysb", i)])
        self.tt("pool", F.ysq[:, 0:nk, :], ysb[:, 0:nk, :], ysb[:, 0:nk, :], ALU.mult, r=[("ysb", i)], w=["ysq"])
        P.op("dve", lambda e: e.tensor_reduce(out=F.yss[:, 0:nk], in_=F.ysq[:, 0:nk, :], axis=AX.X, op=ALU.add), r=["ysq"], w=["yss"])
        self.rstd(F.yrs[:, 0:nk], F.yss[:, 0:nk], 256.0, ["yss"], ["yrs"], "yrst")
        self.tt("dve", F.ynb[i][:, 0:nk, :], ysb[:, 0:nk, :], F.yrs[:, 0:nk].unsqueeze(2).to_broadcast([128, nk, 256]), ALU.mult,
                r=[("ysb", i), "yrs"], w=[("ynb", i)])
        tb, sk = F.trbanks.next()
        self.tr([(tb[:, (j * 2 + cc) * 128:(j * 2 + cc + 1) * 128], F.ynb[i][:, j, cc * 128:(cc + 1) * 128])
                 for j in range(nk) for cc in range(2)], r=[("ynb", i), "ident"], w=[sk])
        for j in range(nk):
            for cc in range(2):
                self.act(tok_of(j, cc), tb[:, (j * 2 + cc) * 128:(j * 2 + cc + 1) * 128], AF.Identity, r=[sk, "gmixT"],
                         w=[("fourT", g, j, cc)], scale=self.gmixT[:, 6 + cc:7 + cc])

    def phaseF(self, l):
        P = self.P
        C = self.C
        P.push()
        F = type("F", (), {})()
        self.F = F
        F.fi = 0
        F.ab1 = P.sb("ab1", [128, 128, 128], BF16)
        F.p1 = P.sb("p1", [128, 64, 2, 128], BF16)
        F.mc = P.sb("mc", [128, 64, 128], BF16)
        F.ms = P.sb("ms", [128, 64, 128], BF16)
        F.t1tab = P.sb("t1tab", [128, 128], BF16)
        F.ysb = [P.sb("ysb%d" % i, [128, 4, 256], F32) for i in range(2)]
        F.ysq = P.sb("ysq", [128, 4, 256], F32)
        F.yss = P.sb("yss", [128, 4], F32)
        F.yrs = P.sb("yrs", [128, 4], F32)
        F.ynb = [P.sb("ynb%d" % i, [128, 4, 256], BF16) for i in range(2)]
        F.trbanks = Ring([(self.banks[i].bitcast(BF16), ("ps", i)) for i in range(2)])
        P.dma("sp", F.mc[:].rearrange("p a b -> p (a b)"), C["mc"][:, :], w=["mc"])
        P.dma("sp", F.ms[:].rearrange("p a b -> p (a b)"), C["ms"][:, :], w=["ms"])
        P.dma("sp", F.t1tab[:], C["t1tab"][:, :], w=["t1tab"])
        pb = Ring([(self.banks[i], ("ps", i)) for i in range(2, 5)])
        yb = Ring([(self.banks[i], ("ps", i)) for i in range(5, 8)])
        F.p1b = P.sb("p1b", [128, 64, 2, 128], BF16)
        p1s = [F.p1, F.p1b]
        for hh in range(2):
            for ab in range(2):
                src = self.AB_d[0:S, hh * 256 + ab * 128: hh * 256 + ab * 128 + 128].rearrange("(t1 t2) c -> t1 t2 c", t2=128)
                P.dma("sp", F.ab1[ab * 64:(ab + 1) * 64, :, :], src, w=[("ab1", ab)])
            for c4 in range(32):
                bk, bkk = pb.next()
                specs = []
                for j in range(4):
                    ch = c4 * 4 + j
                    specs.append((bk[:, j * 128:(j + 1) * 128], F.ab1[:, :, ch], F.t1tab[:, :], True, True))
                self.mm(specs, r=[("ab1", 0), ("ab1", 1), "t1tab"], w=[bkk])
                o = p1s[hh][:, :, :, c4 * 4:(c4 + 1) * 4].rearrange("p k r c -> p c r k")
                i_ = bk[:, :].rearrange("p (c r k) -> p c r k", c=4, r=2)
                if c4 % 2 == 0:
                    self.cp("act", o, i_, r=[bkk], w=[("p1", hh, c4)])
                else:
                    self.cp("dve", o, i_, r=[bkk], w=[("p1", hh, c4)])
        p1keys = [("p1", hh, c4) for hh in range(2) for c4 in range(32)]
        for kg in range(16):
            bka, kka = yb.next()
            for half2 in range(2):
                if half2 == 1:
                    bka, kka = yb.next()
                specs = []
                for j in range(2):
                    k1 = kg * 4 + half2 * 2 + j
                    for hh in range(2):
                        o = bka[:, j * 256 + hh * 128: j * 256 + hh * 128 + 128]
                        specs.append((o, F.mc[:, k1, :], p1s[hh][:, k1, 0, :], True, False))
                        specs.append((o, F.ms[:, k1, :], p1s[hh][:, k1, 1, :], False, True))
                self.mm(specs, r=p1keys + ["mc", "ms"], w=[kka])
                k1base = kg * 4 + half2 * 2

                def tok_of(j, cc, k1base=k1base):
                    return self.fourT[:, cc, 0:S].rearrange("p (k2 k1) -> p k1 k2", k1=64)[:, k1base + j, :]
                self.four_finish(bka[:, :].rearrange("p (j c) -> p j c", j=2), kka, 2, tok_of, ("L", kg, half2))
        if self.ctx_full:
            cab = P.sb("cab", [128, 2, 512], BF16)
            c256 = P.sb("c256", [128, 2, 256], BF16)
            s256 = P.sb("s256n", [128, 2, 256], BF16)
            P.dma("sp", cab[:], self.AB_d[S:S + 256, :].rearrange("(tb p) c -> p tb c", p=128), w=["cab"])
            P.dma("sp", c256[:].rearrange("p a b -> p (a b)"), C["c256"][:, :], w=["c256"])
            P.dma("sp", s256[:].rearrange("p a b -> p (a b)"), C["s256n"][:, :], w=["s256"])
            for kb in range(2):
                bk, bkk = yb.next()
                specs = []
                for hh in range(2):
                    o = bk[:, hh * 128:(hh + 1) * 128]
                    for tb in range(2):
                        specs.append((o, c256[:, tb, kb * 128:(kb + 1) * 128], cab[:, tb, hh * 256:hh * 256 + 128], tb == 0, False))
                        specs.append((o, s256[:, tb, kb * 128:(kb + 1) * 128], cab[:, tb, hh * 256 + 128:hh * 256 + 256], False, tb == 1))
                self.mm(specs, r=["cab", "c256", "s256"], w=[bkk])

                def tok_of(j, cc, kb=kb):
                    return self.fourT[:, cc, S + kb * 128:S + (kb + 1) * 128]
                self.four_finish(bk[:, 0:256].rearrange("p (j c) -> p j c", j=1), bkk, 1, tok_of, ("C", kb))
        if "scratch_out" in self.debug and l == 0:
            P.barrier()
            P.dma("pool", self.fourT_d[:, :, :], self.fourT[:], w=["fourT_d"])
        P.barrier()
        P.pop()

    def phaseC(self, l):
        P = self.P
        I = self.I
        C = self.C
        P.push()
        self.woutb = P.sb("woutb", [128, 8, D], BF16)
        self.new_staging()
        self.woutk = self.load_weight(self.woutb, lambda c: I["w_outp"][l, c * 128:(c + 1) * 128, :], 8, D, "woutb")
        K = type("K", (), {})()
        self.K = K
        self.kT_all = P.sb("kT_all", [128, NTOK], BF16)
        self.vt_all = P.sb("vt_all", [128, 66, 128], BF16)
        P.dma("sp", self.kT_all[:], self.kT_d[:, :], w=["kT_all"])
        P.dma("sp", self.vt_all[:], self.vt_d[:, :, :], w=["vt_all"])
        K.qt = [P.sb("qt%d" % i, [128, 4, 512], BF16) for i in range(2)]
        K.sgt = [P.sb("sgtC%d" % i, [128, 2, 512], BF16) for i in range(3)]
        K.xt = [P.sb("xtC%d" % i, [128, D], F32) for i in range(4)]
        K.PT = [P.sb("PT%d" % i, [128, 2, 512], BF16) for i in range(6)]
        K.oc = [P.sb("oc%d" % i, [128, 512], F32) for i in range(2)]
        K.dd = [P.sb("dd%d" % i, [128, 512], F32) for i in range(2)]
        K.ar = [P.sb("ar%d" % i, [128, 512], F32) for i in range(2)]
        K.sqT = P.sb("sqT", [128, 4, 512], BF16)
        K.aT = [P.sb("aT%d" % i, [128, 4, 512], BF16) for i in range(2)]
        K.ys = P.sb("ysC", [128, D], F32)
        K.yc = P.sb("ycC", [128, D], F32)
        K.tmp = P.sb("tmpC", [128, D], F32)
        K.junk = P.sb("junkC", [128, D], BF16)
        K.ssa = P.sb("ssa", [128, 4], F32)
        K.rsa = [P.sb("rsa%d" % i, [128, 4], F32) for i in range(2)]
        K.ssy = P.sb("ssy", [128, 4], F32)
        K.rsy = P.sb("rsy", [128, 4], F32)
        K.G1 = P.sb("G1bc", [128, 2, D], F32)
        K.mnext = P.sb("mnext", [128, 128], BF16)
        K.mprev = P.sb("mprev", [128, 128], BF16)
        P.dma("sp", K.mnext[:], C["mask_next"][:, :], w=["mnext"])
        P.dma("sp", K.mprev[:], C["mask_prev"][:, :], w=["mprev"])
        for r_ in range(2):
            P.dma("sp", K.G1[:, r_, :], self.G_d[r_:r_ + 1, 0, :].partition_broadcast(128), w=[("G1", r_)])
        K.spairs = Ring([(self.psum[:, 2 * i:2 * i + 2, :], [("ps", 2 * i), ("ps", 2 * i + 1)]) for i in range(2)])
        K.pti = 0
        K.xi = 0
        tiles = [(i * 512, 4, False) for i in range(S // 512)]
        if self.ctx_full:
            tiles = tiles + [(S, 2, True)]
        if "C_tiles" in self.debug:
            tiles = tiles[:2] + tiles[-1:]
        prev = None
        for it, t in enumerate(tiles):
            self.C_attn(l, it, t[0], t[1], t[2], prev)
            prev = (it, t[0], t[1], t[2])
        for s_ in range(prev[2]):
            self.C_tail(l, prev, s_, 0)
            self.C_tail(l, prev, s_, 1)
        P.barrier()
        P.pop()

    def C_attn(self, l, it, tok0, nsub, is_ctx, prev):
        P = self.P
        K = self.K
        T = nsub * 128
        sl = it % 2
        qt = K.qt[sl]
        sgt = K.sgt[it % 3]
        P.dma("sp", qt[:, :, 0:T], self.qT_d[:, :, tok0:tok0 + T].rearrange("c p t -> p c t"), w=[("qt", sl)])
        P.dma("sp", sgt[:, :, 0:T], self.sguT_d[:, :, tok0:tok0 + T].rearrange("c p t -> p c t"), w=[("sgtC", it % 3)])
        items = []
        if not is_ctx:
            n0 = tok0 // 128
            for j in range(n0 - 1, n0 + nsub + 1):
                if j < 0 or j >= S // 128:
                    continue
                qlo = max(j - 1, n0) - n0
                qhi = min(j + 1, n0 + nsub - 1) - n0
                mnext = (j - 1 - n0) if (j - 1 >= n0) else None
                mprev = (j + 1 - n0) if (j + 1 <= n0 + nsub - 1) else None
                items.append((j, qlo * 128, (qhi + 1) * 128, mnext, mprev))
        items.append((64, 0, T, None, None))
        items.append((65, 0, T, None, None))
        LOOK = 1
        bo, ko = self.banks[6], ("ps", 6)
        bd, kd = self.banks[7], ("ps", 7)
        aT = K.aT[sl]
        nit = len(items)
        for c in range(4):
            pend = []
            first = True
            do_tail = prev is not None and c < prev[2]
            for idx in range(nit + LOOK):
                if idx < nit:
                    j, lo, hi, mn, mp = items[idx]
                    n = hi - lo
                    pair, pkeys = K.spairs.next()
                    kcols = slice(j * 128, (j + 1) * 128)
                    self.mm([(pair[:, 0, 0:n], self.kT_all[0:64, kcols], qt[0:64, c, lo:hi], True, True),
                             (pair[:, 1, 0:n], self.kT_all[64:128, kcols], qt[64:128, c, lo:hi], True, True)],
                            r=[("qt", sl), "kT_all"], w=pkeys)
                    pi = K.pti % len(K.PT)
                    K.pti += 1
                    pt = K.PT[pi]
                    pk = [("PT", pi)]
                    self.act(pt[:, :, 0:n], pair[:, :, 0:n], AF.Exp, r=pkeys, w=pk, scale=0.125)
                    if mn is not None:
                        o = mn * 128 - lo
                        self.tt("dve", pt[:, :, o:o + 128], pt[:, :, o:o + 128],
                                K.mnext[:, :].unsqueeze(1).to_broadcast([128, 2, 128]), ALU.mult, r=pk + ["mnext"], w=pk)
                    if mp is not None:
                        o = mp * 128 - lo
                        self.tt("dve", pt[:, :, o:o + 128], pt[:, :, o:o + 128],
                                K.mprev[:, :].unsqueeze(1).to_broadcast([128, 2, 128]), ALU.mult, r=pk + ["mprev"], w=pk)
                    pend.append((j, lo, hi, pt, pk))
                if idx >= LOOK:
                    j, lo, hi, pt, pk = pend.pop(0)
                    n = hi - lo
                    last = (idx == nit + LOOK - 1)
                    self.mm([(bo[0:64, lo:hi], self.vt_all[:, j, 0:64], pt[:, 0, 0:n], first, last),
                             (bo[64:128, lo:hi], self.vt_all[:, j, 64:128], pt[:, 1, 0:n], first, last),
                             (bd[0:64, lo:hi], self.ones[:, 0:64], pt[:, 0, 0:n], first, last),
                             (bd[64:128, lo:hi], self.ones[:, 0:64], pt[:, 1, 0:n], first, last)],
                            r=pk + ["ones", "vt_all"], w=[ko, kd])
                    first = False
                if do_tail and idx == nit // 2 - 1:
                    self.C_tail(l, prev, c, 0)
                if do_tail and idx == nit - 1:
                    self.C_tail(l, prev, c, 1)
            d2 = c % 2
            self.cp("act", K.oc[d2][:, 0:T], bo[:, 0:T], r=[ko], w=[("oc", d2)])
            self.ts("dve", K.dd[d2][:, 0:T], bd[:, 0:T], self.esink[:, c:c + 1], None, ALU.add, None, r=[kd, "esink"], w=[("dd", d2)])
            self.recip(K.dd[d2][:, 0:T], K.dd[d2][:, 0:T], r=[("dd", d2)], w=[("dd", d2)])
            self.tt("pool", K.ar[d2][:, 0:T], K.oc[d2][:, 0:T], K.dd[d2][:, 0:T], ALU.mult, r=[("oc", d2), ("dd", d2)], w=[("ar", d2)])
            self.act(K.sqT[:, c, 0:T], K.ar[d2][:, 0:T], AF.Square, r=[("ar", d2)], w=[("sqT", c)])
            self.ts("pool", aT[:, c, 0:T], K.ar[d2][:, 0:T], self.gmixT[:, c:c + 1], 1.0, ALU.mult, ALU.mult,
                    r=[("ar", d2), "gmixT"], w=[("aT", sl, c)])
        bs_, ks_ = self.banks[4], ("ps", 4)
        specs = []
        for s in range(nsub):
            for c in range(4):
                specs.append((bs_[:, s:s + 1], K.sqT[:, c, s * 128:(s + 1) * 128], self.ones[:, 0:1], c == 0, c == 3))
        self.mm(specs, r=[("sqT", c) for c in range(4)] + ["ones"], w=[ks_])
        self.rstd(K.rsa[sl][:, 0:nsub], bs_[:, 0:nsub], 512.0, [ks_], [("rsa", sl)], ("rsat", sl))

    def C_tail(self, l, tile, s, half):
        P = self.P
        K = self.K
        it, tok0, nsub, is_ctx = tile
        sl = it % 2
        mi = 1 if is_ctx else 0
        sgt = K.sgt[it % 3]
        aT = K.aT[sl]
        cols = slice(s * 128, (s + 1) * 128)
        hc = slice(half * 512, (half + 1) * 512)
        xi = (it * 4 + s) % 4
        if half == 0:
            P.dma("sp", K.xt[xi][:], self.src_rows(l, tok0 + s * 128, 128), w=[("xtC", xi)])
        pr = self.psum[:, 4:6, :]
        ka, kb = ("ps", 4), ("ps", 5)
        aTk = [("aT", sl, c) for c in range(4)]
        self.mm([(pr[:, 0, :], aT[:, c, cols], self.woutb[:, c, hc], c == 0, c == 3) for c in range(4)],
                r=aTk + self.woutk, w=[ka])
        specs = []
        for cc in range(2):
            specs.append((pr[:, 1, :], sgt[:, cc, cols], self.woutb[:, 4 + cc, hc], cc == 0, False))
        for cc in range(2):
            specs.append((pr[:, 1, :], self.fourT[:, cc, tok0 + s * 128: tok0 + (s + 1) * 128], self.woutb[:, 6 + cc, hc], False, cc == 1))
        self.mm(specs, r=[("sgtC", it % 3)] + self.woutk, w=[kb])
        self.cp("act", K.ys[:, hc], pr[:, 1, :], r=[kb], w=[("ysC", half)])
        self.stt(K.yc[:, hc], pr[:, 0, :], K.rsa[sl][:, s:s + 1], K.ys[:, hc], ALU.mult, ALU.add,
                 r=[ka, ("rsa", sl), ("ysC", half)], w=[("ycC", half)])
        if half == 0:
            return
        yck = [("ycC", 0), ("ycC", 1)]
        self.act(K.junk[:], K.yc[:], AF.Square, r=yck, w=["junkC", "ssy"], accum_out=K.ssy[:, 0:1])
        self.rstd(K.rsy[:, 0:1], K.ssy[:, 0:1], float(D), ["ssy"], ["rsy"], "rsyt")
        self.tt("pool", K.tmp[:], K.yc[:], K.G1[:, mi, :], ALU.mult, r=yck + [("G1", mi)], w=["tmpC"])
        self.stt(K.xt[xi][:], K.tmp[:], K.rsy[:, 0:1], K.xt[xi][:], ALU.mult, ALU.add,
                 r=["tmpC", "rsy", ("xtC", xi)], w=[("xtC", xi)])
        P.dma("pool", self.xmid_d[tok0 + s * 128: tok0 + (s + 1) * 128, :], K.xt[xi][:], r=[("xtC", xi)], w=[("xmid", tok0, s)])

    def phaseD(self, l):
        P = self.P
        I = self.I
        P.push()
        self.w1b = P.sb("w1b", [128, 8, DFF], BF16)
        self.w2b = P.sb("w2b", [128, 32, D], BF16)
        self.new_staging()
        self.w1k = self.load_weight(self.w1b, lambda c: I["w_ff1"][l, c * 128:(c + 1) * 128, :], 8, DFF, "w1b")
        self.w2k = self.load_weight(self.w2b, lambda c: I["w_ff2"][l, c * 128:(c + 1) * 128, :], 32, D, "w2b")
        Dd = type("Dd", (), {})()
        self.Dd = Dd
        Dd.xm = [P.sb("xm%d" % i, [128, D], F32) for i in range(4)]
        Dd.xn = [P.sb("xnD%d" % i, [128, D], BF16) for i in range(2)]
        Dd.hx = [P.sb("hx2T%d" % i, [128, 8, 256], BF16) for i in range(2)]
        Dd.hT = P.sb("hT", [128, 32, 256], BF16)
        Dd.rl = [P.sb("rl%d" % i, [128, 512], F32) for i in range(3)]
        Dd.tmp = P.sb("tmpD", [128, D], F32)
        Dd.junk = P.sb("junkD", [128, D], BF16)
        Dd.ss = P.sb("ssD", [128, 4], F32)
        Dd.rs = P.sb("rsD", [128, 4], F32)
        Dd.ssy = P.sb("ssyD", [128, 4], F32)
        Dd.rsy = P.sb("rsyD", [128, 2], F32)
        Dd.G2 = P.sb("G2bc", [128, 2, D], F32)
        for r_ in range(2):
            P.dma("sp", Dd.G2[:, r_, :], self.G_d[r_:r_ + 1, 1, :].partition_broadcast(128), w=[("G2", r_)])
        Dd.trbanks = Ring([(self.banks[i].bitcast(BF16), ("ps", i)) for i in range(2)])
        Dd.fbanks = Ring([(self.banks[i], ("ps", i)) for i in range(2, 5)])
        Dd.yring = Ring([(self.banks[i], ("ps", i)) for i in range(5, 8)])
        Dd.xi = 0
        Dd.rli = 0
        tiles = [(i * 256, False) for i in range(S // 256)]
        if self.ctx_full:
            tiles = tiles + [(S, True)]
        if "D_tiles" in self.debug:
            tiles = tiles[:2] + tiles[-1:]
        self.D_front(l, 0, *tiles[0])
        for it, t in enumerate(tiles):
            if it + 1 < len(tiles):
                self.D_front(l, it + 1, *tiles[it + 1])
            self.D_back(l, it, *t)
        P.barrier()
        P.pop()

    def D_front(self, l, it, tok0, is_ctx):
        P = self.P
        Dd = self.Dd
        mi = 1 if is_ctx else 0
        hx = Dd.hx[it % 2]
        xis = []
        for s in range(2):
            xi = Dd.xi % 4
            Dd.xi += 1
            xis.append(xi)
            P.dma("sp", Dd.xm[xi][:], self.xmid_d[tok0 + s * 128: tok0 + (s + 1) * 128, :], w=[("xm", xi)])
            self.act(Dd.junk[:], Dd.xm[xi][:], AF.Square, r=[("xm", xi)], w=["junkD", ("ssD", s)], accum_out=Dd.ss[:, s:s + 1])
            self.rstd(Dd.rs[:, s:s + 1], Dd.ss[:, s:s + 1], float(D), [("ssD", s)], [("rsD", s)], ("rsDt", s))
            self.ts("dve", Dd.xn[s][:], Dd.xm[xi][:], Dd.rs[:, s:s + 1], None, ALU.mult, None, r=[("xm", xi), ("rsD", s)], w=[("xnD", s)])
        Dd.cur_xis = getattr(Dd, "cur_xis", {})
        Dd.cur_xis[it] = xis
        for cq in range(2):
            tb, sk = Dd.trbanks.next()
            self.tr([(tb[:, j * 256 + s * 128: j * 256 + (s + 1) * 128], Dd.xn[s][:, (4 * cq + j) * 128:(4 * cq + j + 1) * 128])
                     for j in range(4) for s in range(2)], r=[("xnD", 0), ("xnD", 1), "ident"], w=[sk])
            for j in range(4):
                c = 4 * cq + j
                slot = tb[:, j * 256:(j + 1) * 256]
                if cq == 0:
                    self.act(hx[:, c, :], slot, AF.Identity, r=[sk, "modT"], w=[("hx2T", it % 2, c)],
                             scale=self.modT[:, 2, c, mi:mi + 1], bias=self.modT[:, 3, c, mi:mi + 1])
                else:
                    self.ts("dve", hx[:, c, :], slot, self.modT[:, 2, c, mi:mi + 1], self.modT[:, 3, c, mi:mi + 1], ALU.mult, ALU.add,
                            r=[sk, "modT"], w=[("hx2T", it % 2, c)])

    def D_back(self, l, it, tok0, is_ctx):
        P = self.P
        Dd = self.Dd
        mi = 1 if is_ctx else 0
        hx = Dd.hx[it % 2]
        hk = [("hx2T", it % 2, c) for c in range(8)]
        xis = Dd.cur_xis[it]
        for fp in range(16):
            fb, sk = Dd.fbanks.next()
            specs = []
            for j in range(2):
                fc = 2 * fp + j
                for dc in range(8):
                    specs.append((fb[:, j * 256:(j + 1) * 256], self.w1b[:, dc, fc * 128:(fc + 1) * 128], hx[:, dc, :], dc == 0, dc == 7))
            self.mm(specs, r=hk + self.w1k, w=[sk])
            ri = Dd.rli % 3
            Dd.rli += 1
            self.act(Dd.rl[ri][:], fb[:, :], AF.Relu, r=[sk], w=[("rl", ri)])
            self.tt("pool" if fp % 2 == 0 else "dve", Dd.hT[:, 2 * fp:2 * fp + 2, :].rearrange("p a t -> p (a t)"), Dd.rl[ri][:], Dd.rl[ri][:],
                    ALU.mult, r=[("rl", ri)], w=[("hT", 2 * fp), ("hT", 2 * fp + 1)])
        hTk = [("hT", fc) for fc in range(32)]
        for s in range(2):
            xi = xis[s]
            ybs = []
            for half in range(2):
                b_, k_ = Dd.yring.next()
                self.mm([(b_[:, :], Dd.hT[:, fc, s * 128:(s + 1) * 128], self.w2b[:, fc, half * 512:(half + 1) * 512], fc == 0, fc == 31)
                         for fc in range(32)], r=hTk + self.w2k, w=[k_])
                ybs.append((b_, k_))
            for half in range(2):
                self.act(Dd.junk[:, half * 512:(half + 1) * 512], ybs[half][0][:, :], AF.Square, r=[ybs[half][1]],
                         w=[("junkD", half), ("ssyD", s, half)], accum_out=Dd.ssy[:, s * 2 + half:s * 2 + half + 1])
            self.tt("dve", Dd.ssy[:, s * 2:s * 2 + 1], Dd.ssy[:, s * 2:s * 2 + 1], Dd.ssy[:, s * 2 + 1:s * 2 + 2], ALU.add,
                    r=[("ssyD", s, 0), ("ssyD", s, 1)], w=[("ssyDs", s)])
            self.rstd(Dd.rsy[:, s:s + 1], Dd.ssy[:, s * 2:s * 2 + 1], float(D), [("ssyDs", s)], [("rsyD", s)], ("rsyDt", s))
            for half in range(2):
                hc = slice(half * 512, (half + 1) * 512)
                self.tt("dve", Dd.tmp[:, hc], ybs[half][0][:, :], Dd.G2[:, mi, hc], ALU.mult, r=[ybs[half][1], ("G2", mi)], w=[("tmpD", half)])
            self.stt(Dd.xm[xi][:], Dd.tmp[:], Dd.rsy[:, s:s + 1], Dd.xm[xi][:], ALU.mult, ALU.add,
                     r=[("tmpD", 0), ("tmpD", 1), ("rsyD", s), ("xm", xi)], w=[("xm", xi)])
            if is_ctx and l == self.nlayers - 1:
                continue
            P.dma("pool", self.dst_rows(l, tok0 + s * 128, 128), Dd.xm[xi][:], r=[("xm", xi)], w=[("xout", tok0, s)])


def _prep_shared(inp):
    sh = {}
    f = lambda a: np.ascontiguousarray(a, dtype=np.float32)
    sh["w_ada"] = f(inp["w_ada"])
    sh["b_ada"] = f(inp["b_ada"])
    for nm in ("g_pre_mix", "g_post_mix", "g_pre_ff", "g_post_ff"):
        sh[nm] = f(inp[nm])
    sh["w_inp"] = f(inp["w_in"][:, :, _perm_cols()])
    sink = inp["sink"]
    sinkT = np.zeros((2, 128, 4), np.float32)
    for c in range(4):
        sinkT[:, 0:64, c] = sink[:, c][:, None]
        sinkT[:, 64:128, c] = sink[:, 4 + c][:, None]
    sh["sinkT"] = sinkT
    sh["w_sguT"] = f(np.transpose(inp["w_sgu"], (0, 3, 1, 2)).reshape(2, 128, 512))
    sh["b_sguT"] = f(np.transpose(inp["b_sgu"], (0, 2, 1)))
    sh["g_sgu"] = f(inp["g_sgu"].reshape(2, 256))
    rp = _attn_row_perm()
    gm = inp["g_mix"][:, rp]
    sh["g_mixT"] = f(np.transpose(gm.reshape(2, 8, 128), (0, 2, 1)))
    sh["w_outp"] = f(inp["w_out"][:, rp, :])
    sh["w_ff1"] = f(inp["w_ff1"])
    sh["w_ff2"] = f(inp["w_ff2"])
    return sh


def _core_inputs(inp, sh, consts, b):
    m = dict(sh)
    m["x"] = np.ascontiguousarray(inp["x"][b], dtype=np.float32)
    m["ctx"] = np.ascontiguousarray(inp["ctx"][b], dtype=np.float32)
    cs = np.stack([inp["c"][b], inp["c_ctx"]], axis=0).astype(np.float32)
    m["csT"] = np.ascontiguousarray(np.transpose(cs.reshape(2, 8, 128), (2, 1, 0)).reshape(128, 16))
    for k, v in consts.items():
        m["c_" + k] = v
    return m


_CACHE = {}


def kernel(**inputs):
    inp = {k: np.asarray(v) for k, v in inputs.items()}
    if "nc" not in _CACHE:
        _CACHE["nc"] = Builder().build()
        _CACHE["consts"] = _constants()
    nc = _CACHE["nc"]
    consts = _CACHE["consts"]
    sh = _prep_shared(inp)
    in_maps = [_core_inputs(inp, sh, consts, b) for b in range(8)]
    res = run_bass_kernel_spmd(nc, in_maps, core_ids=list(range(8)))
    out = np.stack([np.asarray(res.results[b]["out"], dtype=np.float32) for b in range(8)], axis=0)
    return out
```

```python
import contextlib
import numpy as np
import ml_dtypes
import concourse.bass as bass
import concourse.mybir as mybir
from concourse.bass_utils import run_bass_kernel_spmd

F32 = mybir.dt.float32
BF16 = mybir.dt.bfloat16
AF = mybir.ActivationFunctionType
ALU = mybir.AluOpType
AX = mybir.AxisListType
NPBF = ml_dtypes.bfloat16

COMPUTE = ("pe", "act", "dve", "pool")
NDMA_SEMS = 20

D = 1024
S = 8192
LCTX = 256
NTOK = S + LCTX
DFF = 4096
NCOL = 2176
EPS = 1e-6


class _Op:
    __slots__ = ("eng", "fn", "deps", "signal", "sigval", "is_dma", "sem_idx", "dma_val", "prev_same_sem")

    def __init__(self, eng, fn, is_dma):
        self.eng = eng
        self.fn = fn
        self.deps = []
        self.signal = False
        self.sigval = 0
        self.is_dma = is_dma
        self.sem_idx = None
        self.dma_val = 0
        self.prev_same_sem = None


class Prog:
    def __init__(self, nc):
        self.nc = nc
        self.ops = {e: [] for e in ("pe", "act", "dve", "pool", "sp")}
        self.last_w = {}
        self.readers = {}
        self.dma_count = {"sp": 0, "pool": 0, "act": 0}
        self.dma_last_on_sem = {}
        self.stacks = [contextlib.ExitStack()]
        self.last_op = {}

    def push(self):
        self.stacks.append(contextlib.ExitStack())

    def pop(self):
        self.stacks.pop().close()

    def sb(self, name, shape, dt):
        self.uid = getattr(self, "uid", 0) + 1
        return self.stacks[-1].enter_context(self.nc.sbuf_tensor("%s_u%d" % (name, self.uid), list(shape), dt))

    def ps(self, name, shape, dt):
        return self.stacks[-1].enter_context(self.nc.psum_tensor(name, list(shape), dt))

    def _track(self, op, r, w):
        ps_keys = [("ps", k[1]) for k in list(r) + list(w) if isinstance(k, tuple) and k[0] == "ps"]
        r = [k for k in r if not (isinstance(k, tuple) and k[0] == "ps")]
        w = [k for k in w if not (isinstance(k, tuple) and k[0] == "ps")] + list(dict.fromkeys(ps_keys))
        deps = set()
        for k in r:
            lw = self.last_w.get(k)
            if lw is not None:
                deps.add(lw)
        for k in w:
            lw = self.last_w.get(k)
            if lw is not None and (op.is_dma or lw.is_dma or lw.eng != op.eng):
                deps.add(lw)
            for rd in self.readers.get(k, ()):
                if rd is op:
                    continue
                if op.is_dma or rd.is_dma or rd.eng != op.eng:
                    deps.add(rd)
        for d in deps:
            if d is op:
                continue
            d.signal = True
            op.deps.append(d)
        for k in r:
            self.readers.setdefault(k, []).append(op)
        for k in w:
            self.last_w[k] = op
            self.readers[k] = []

    def op(self, eng, fn, r=(), w=()):
        o = _Op(eng, fn, False)
        self._track(o, r, w)
        self.ops[eng].append(o)
        self.last_op[eng] = o
        return o

    def wait_all(self, eng, r=()):
        o = _Op(eng, None, False)
        self._track(o, r, ())
        self.ops[eng].append(o)
        return o

    def dma(self, q, out, in_, r=(), w=(), **kw):
        o = _Op(q, (out, in_, kw), True)
        n = self.dma_count[q]
        self.dma_count[q] = n + 1
        nsem = NDMA_SEMS if q != "pool" else 6
        o.sem_idx = (q, n % nsem)
        prev = self.dma_last_on_sem.get(o.sem_idx)
        o.prev_same_sem = prev
        o.dma_val = (prev.dma_val if prev is not None else 0) + 16
        self.dma_last_on_sem[o.sem_idx] = o
        self._track(o, r, w)
        self.ops[q].append(o)
        return o

    def barrier(self):
        srcs = [o for o in self.last_op.values()] + list(self.dma_last_on_sem.values())
        for e in ("pe", "act", "dve", "pool", "sp"):
            o = _Op(e, None, False)
            for s_ in srcs:
                if (not s_.is_dma) and s_.eng == e:
                    continue
                s_.signal = True
                o.deps.append(s_)
            self.ops[e].append(o)
        self.last_w = {}
        self.readers = {}

    def emit(self):
        nc = self.nc
        st = self.stacks[0]
        esem = {e: st.enter_context(nc.semaphore("sem_" + e)) for e in COMPUTE}
        dsem = {}
        for q in ("sp", "pool", "act"):
            for i in range(min(NDMA_SEMS, self.dma_count[q])):
                dsem[(q, i)] = st.enter_context(nc.semaphore("dsem_%s_%d" % (q, i)))
        for e in COMPUTE:
            c = 0
            for o in self.ops[e]:
                if (not o.is_dma) and o.signal:
                    c += 1
                    o.sigval = c

        def src_of(d):
            if d.is_dma:
                return dsem[d.sem_idx], d.dma_val, ("d",) + d.sem_idx
            return esem[d.eng], d.sigval, ("e", d.eng)

        def run(engname, eng):
            seen = {}
            for o in self.ops[engname]:
                waits = {}
                deps = list(o.deps)
                if o.is_dma and o.prev_same_sem is not None:
                    deps.append(o.prev_same_sem)
                for d in deps:
                    sem, val, key = src_of(d)
                    if seen.get(key, 0) >= val:
                        continue
                    if key not in waits or waits[key][1] < val:
                        waits[key] = (sem, val)
                for key, (sem, val) in waits.items():
                    eng.wait_ge(sem, val)
                    seen[key] = val
                if o.is_dma:
                    out, in_, kw = o.fn
                    eng.dma_start(out=out, in_=in_, **kw).then_inc(dsem[o.sem_idx], 16)
                elif o.fn is None:
                    pass
                else:
                    ins = o.fn(eng)
                    if o.signal:
                        ins.then_inc(esem[engname], 1)

        with nc.Block() as block:
            @block.tensor
            def _(e):
                run("pe", e)

            @block.scalar
            def _(e):
                run("act", e)

            @block.vector
            def _(e):
                run("dve", e)

            @block.gpsimd
            def _(e):
                run("pool", e)

            @block.sync
            def _(e):
                run("sp", e)

    def close(self):
        while self.stacks:
            self.stacks.pop().close()


class Ring:
    def __init__(self, items):
        self.items = items
        self.i = 0

    def next(self):
        it = self.items[self.i % len(self.items)]
        self.i += 1
        return it


def _perm_cols():
    def rotp(d):
        if d < 16:
            return d + 16
        if d < 32:
            return d - 16
        if d < 48:
            return d + 16
        return d - 16
    cols = []
    for c in range(4):
        for h in (c, 4 + c):
            cols += [h * 64 + d for d in range(64)]
    for c in range(4):
        for h in (c, 4 + c):
            cols += [h * 64 + rotp(d) for d in range(64)]
    cols += [512 + j for j in range(128)]
    cols += [512 + h * 64 + rotp(d) for h in range(2) for d in range(64)]
    cols += [1280 + j for j in range(256)]
    cols += [640 + j for j in range(128)]
    cols += [768 + j for j in range(256)]
    cols += [1024 + j for j in range(256)]
    assert len(cols) == NCOL
    return np.array(cols)


def _attn_row_perm():
    rows = []
    for c in range(4):
        for h in (c, 4 + c):
            rows += [h * 64 + d for d in range(64)]
    return np.array(rows + list(range(512, 1024)))


def _constants():
    k = {}
    k["ident"] = np.eye(128, dtype=np.float32).astype(NPBF)
    n_freq = 16
    inv_freq = (np.float32(10000.0) ** (-np.arange(n_freq, dtype=np.float32) / np.float32(n_freq))).astype(np.float32)
    t = np.arange(S)
    row = (t // 64).astype(np.float32)
    col = (t % 64).astype(np.float32)
    ang = np.zeros((64, S), np.float32)
    ang[0:16] = row[None, :] * inv_freq[:, None]
    ang[16:32] = ang[0:16]
    ang[32:48] = col[None, :] * inv_freq[:, None]
    ang[48:64] = ang[32:48]
    cos = np.cos(ang).astype(np.float32)
    sin = np.sin(ang).astype(np.float32)
    sin[0:16] *= -1
    sin[32:48] *= -1
    cosT = np.ones((128, NTOK), np.float32)
    sinT = np.zeros((128, NTOK), np.float32)
    cosT[0:64, :S] = cos
    cosT[64:128, :S] = cos
    sinT[0:64, :S] = sin
    sinT[64:128, :S] = sin
    k["cosT"] = cosT
    k["sinT"] = sinT
    kk = np.arange(128)[:, None]
    qq = np.arange(128)[None, :]
    k["mask_next"] = (kk <= qq).astype(np.float32).astype(NPBF)
    k["mask_prev"] = (kk >= qq).astype(np.float32).astype(NPBF)
    k["ones"] = np.ones((128, 128), np.float32).astype(NPBF)
    cidx = np.arange(64)
    c64 = np.cos(2 * np.pi * np.outer(cidx, cidx) / 64.0) / 8.0
    s64 = np.sin(2 * np.pi * np.outer(cidx, cidx) / 64.0) / 8.0
    csblk = np.zeros((128, 256), np.float64)
    for g in range(2):
        csblk[g * 64:(g + 1) * 64, g * 64:(g + 1) * 64] = c64
        csblk[g * 64:(g + 1) * 64, 128 + g * 64:128 + (g + 1) * 64] = s64
    k["csblk"] = csblk.astype(np.float32).astype(NPBF)
    t1 = np.zeros((128, 128), np.float64)
    t1[0:64, 0:64] = c64
    t1[0:64, 64:128] = -s64
    t1[64:128, 0:64] = -s64
    t1[64:128, 64:128] = -c64
    k["t1tab"] = t1.astype(np.float32).astype(NPBF)
    t2 = np.arange(128)[:, None, None].astype(np.float64)
    k1 = np.arange(64)[None, :, None].astype(np.float64)
    k2 = np.arange(128)[None, None, :].astype(np.float64)
    phi = 2 * np.pi * ((t2 * (k1 + 64 * k2)) % 8192) / 8192.0
    k["mc"] = (np.cos(phi) / np.sqrt(128.0)).astype(np.float32).astype(NPBF).reshape(128, 64 * 128)
    k["ms"] = (np.sin(phi) / np.sqrt(128.0)).astype(np.float32).astype(NPBF).reshape(128, 64 * 128)
    tt = (np.arange(2)[None, :, None] * 128 + np.arange(128)[:, None, None]).astype(np.float64)
    kk2 = np.arange(256)[None, None, :].astype(np.float64)
    ph = 2 * np.pi * ((tt * kk2) % 256) / 256.0
    k["c256"] = (np.cos(ph) / 16.0).astype(np.float32).astype(NPBF).reshape(128, 512)
    k["s256n"] = (-np.sin(ph) / 16.0).astype(np.float32).astype(NPBF).reshape(128, 512)
    sel = np.zeros((2, 2, 128), np.float32)
    sel[0, 0, :] = 1
    sel[1, 1, :] = 1
    k["sel"] = sel.reshape(2, 256)
    k["i2"] = np.eye(2, dtype=np.float32)
    return k


CONST_SPECS = [
    ("ident", [128, 128], BF16), ("cosT", [128, NTOK], F32), ("sinT", [128, NTOK], F32),
    ("mask_next", [128, 128], BF16), ("mask_prev", [128, 128], BF16), ("ones", [128, 128], BF16),
    ("csblk", [128, 256], BF16), ("t1tab", [128, 128], BF16), ("mc", [128, 8192], BF16), ("ms", [128, 8192], BF16),
    ("c256", [128, 512], BF16), ("s256n", [128, 512], BF16), ("sel", [2, 256], F32), ("i2", [2, 2], F32),
]

IN_SPECS = [
    ("x", [S, D]), ("ctx", [LCTX, D]), ("csT", [128, 16]),
    ("w_ada", [2, D, 6 * D]), ("b_ada", [2, 6 * D]),
    ("g_pre_mix", [2, D]), ("g_post_mix", [2, D]), ("g_pre_ff", [2, D]), ("g_post_ff", [2, D]),
    ("w_inp", [2, D, NCOL]), ("sinkT", [2, 128, 4]), ("w_sguT", [2, 128, 512]), ("b_sguT", [2, 128, 4]),
    ("g_sgu", [2, 256]), ("g_mixT", [2, 128, 8]), ("w_outp", [2, D, D]),
    ("w_ff1", [2, D, DFF]), ("w_ff2", [2, DFF, D]),
]


class Builder:
    def __init__(self, nlayers=2, stop_after=None, debug=()):
        self.nlayers = nlayers
        self.stop_after = stop_after
        self.debug = set(debug)
        nc = bass.Bass("TRN2", target_bir_lowering=False)
        self.nc = nc
        self.I = {}
        for name, shape in IN_SPECS:
            self.I[name] = nc.dram_tensor(name, shape, F32, kind="ExternalInput").ap()
        self.C = {}
        for name, shape, dt in CONST_SPECS:
            self.C[name] = nc.dram_tensor("c_" + name, shape, dt, kind="ExternalInput").ap()
        self.out = nc.dram_tensor("out", [S, D], F32, kind="ExternalOutput").ap()
        kw = dict(kind="ExternalOutput") if "scratch_out" in self.debug else {}
        self.qT_d = nc.dram_tensor("qT_d", [4, 128, NTOK], BF16, **kw).ap()
        self.sguT_d = nc.dram_tensor("sguT_d", [2, 128, NTOK], BF16, **kw).ap()
        self.AB_d = nc.dram_tensor("AB_d", [NTOK, 512], BF16, **kw).ap()
        self.kT_d = nc.dram_tensor("kT_d", [128, NTOK], BF16, **kw).ap()
        self.vt_d = nc.dram_tensor("vt_d", [128, 66, 128], BF16, **kw).ap()
        self.fourT_d = nc.dram_tensor("fourT_d", [128, 2, NTOK], BF16, **kw).ap()
        self.xmid_d = nc.dram_tensor("xmid_d", [NTOK, D], F32, **kw).ap()
        self.x1_d = nc.dram_tensor("x1_d", [NTOK, D], F32, **kw).ap()
        self.G_d = nc.dram_tensor("G_d", [2, 4, D], F32, **kw).ap()
        self.modT_d = nc.dram_tensor("modT_d", [128, 64], F32, **kw).ap()
        self.dbg = {}
        self.P = Prog(nc)

    def act(self, out, in_, func, r, w, **kw):
        return self.P.op("act", lambda e: e.activation(out=out, in_=in_, func=func, **kw), r, w)

    def tt(self, eng, out, in0, in1, op, r, w):
        return self.P.op(eng, lambda e: e.tensor_tensor(out=out, in0=in0, in1=in1, op=op), r, w)

    def ts(self, eng, out, in0, s1, s2, op0, op1, r, w):
        if s2 is None:
            return self.P.op(eng, lambda e: e.tensor_scalar(out=out, in0=in0, scalar1=s1, scalar2=None, op0=op0), r, w)
        return self.P.op(eng, lambda e: e.tensor_scalar(out=out, in0=in0, scalar1=s1, scalar2=s2, op0=op0, op1=op1), r, w)

    def stt(self, out, in0, scalar, in1, op0, op1, r, w):
        return self.P.op("dve", lambda e: e.scalar_tensor_tensor(out=out, in0=in0, scalar=scalar, in1=in1, op0=op0, op1=op1), r, w)

    def cp(self, eng, out, in_, r, w):
        if eng == "act":
            return self.P.op("act", lambda e: e.activation(out=out, in_=in_, func=AF.Copy), r, w)
        return self.P.op(eng, lambda e: e.tensor_copy(out=out, in_=in_), r, w)

    def recip(self, out, in_, r, w):
        return self.P.op("dve", lambda e: e.reciprocal(out=out, in_=in_), r, w)

    def rstd(self, out, ss, n, key_in, key_out, tmpkey):
        self.act(out, ss, AF.Sqrt, r=key_in, w=[tmpkey], scale=1.0 / n, bias=self.eps_t[:, 0:1])
        self.recip(out, out, r=[tmpkey], w=key_out)

    def mm(self, specs, r, w):
        def fn(e):
            ins = None
            for sp_ in specs:
                ins = e.matmul(out=sp_[0], lhsT=sp_[1], rhs=sp_[2], start=sp_[3], stop=sp_[4], skip_group_check=True)
            return ins
        return self.P.op("pe", fn, r, w)

    def tr(self, specs, r, w):
        def fn(e):
            ins = None
            for o_, i_ in specs:
                ins = e.transpose(out=o_, in_=i_, identity=self.ident[:])
            return ins
        return self.P.op("pe", fn, r, w)

    def build(self):
        P = self.P
        nc = self.nc
        self.ident = P.sb("ident", [128, 128], BF16)
        self.ones = P.sb("ones", [128, 128], BF16)
        self.eps_t = P.sb("eps_t", [128, 1], F32)
        self.csil = P.sb("csil", [128, 8, 2], F32)
        self.modT = P.sb("modT", [128, 4, 8, 2], F32)
        self.psum = P.ps("psum", [128, 8, 512], F32)
        self.banks = [self.psum[:, i, :] for i in range(8)]
        P.dma("sp", self.ident[:], self.C["ident"][:, :], w=["ident"])
        P.dma("sp", self.ones[:], self.C["ones"][:, :], w=["ones"])
        P.op("dve", lambda e: e.memset(self.eps_t[:], EPS), w=["eps"])
        P.dma("sp", self.csil[:].rearrange("p c r -> p (c r)"), self.I["csT"][:, :], w=["csil"])
        self.act(self.csil[:], self.csil[:], AF.Silu, r=["csil"], w=["csil"])
        P.barrier()
        for l in range(self.nlayers):
            self.layer(l)
        P.barrier()
        P.emit()
        P.close()
        return nc

    def src_rows(self, l, tok0, n):
        if l == 0:
            if tok0 >= S:
                return self.I["ctx"][tok0 - S:tok0 - S + n, :]
            return self.I["x"][tok0:tok0 + n, :]
        return self.x1_d[tok0:tok0 + n, :]

    def dst_rows(self, l, tok0, n):
        if l == self.nlayers - 1 and self.nlayers == 2:
            assert tok0 < S
            return self.out[tok0:tok0 + n, :]
        if self.nlayers == 1 and tok0 < S:
            return self.out[tok0:tok0 + n, :]
        return self.x1_d[tok0:tok0 + n, :]

    def layer(self, l):
        P = self.P
        last = (l == 1)
        self.l = l
        self.ctx_full = not last
        P.push()
        self.gmixT = P.sb("gmixT", [128, 8], F32)
        self.esink = P.sb("esink", [128, 4], F32)
        self.bsT = P.sb("bsT", [128, 4], F32)
        self.fourT = P.sb("fourT", [128, 2, NTOK], BF16)
        P.dma("sp", self.gmixT[:], self.I["g_mixT"][l], w=["gmixT"])
        P.dma("sp", self.esink[:], self.I["sinkT"][l], w=["esink"])
        P.dma("sp", self.bsT[:], self.I["b_sguT"][l], w=["bsT"])
        self.act(self.esink[:], self.esink[:], AF.Exp, r=["esink"], w=["esink"])
        self.adaln(l)
        if self.stop_after == "adaln":
            P.pop()
            return
        self.phaseA(l)
        if self.stop_after == "A":
            P.pop()
            return
        self.phaseF(l)
        if self.stop_after == "F":
            P.pop()
            return
        self.phaseC(l)
        if self.stop_after == "C":
            P.pop()
            return
        P.pop()
        self.phaseD(l)

    def adaln(self, l):
        P = self.P
        I = self.I
        P.push()
        wblk = [P.sb("wada%d" % i, [128, 8, 512], F32) for i in range(2)]
        modrow = P.sb("modrow", [2, 6 * D], F32)
        brow = P.sb("brow", [2, 6 * D], F32)
        gains = P.sb("gains", [2, 4, D], F32)
        rows = P.sb("rows", [2, 6, D], F32)
        sel = P.sb("sel", [2, 2, 128], F32)
        i2 = P.sb("i2", [2, 2], F32)
        P.dma("sp", brow[:], I["b_ada"][l:l + 1, :].partition_broadcast(2), w=["brow"])
        for i, nm in enumerate(("g_pre_mix", "g_post_mix", "g_pre_ff", "g_post_ff")):
            P.dma("sp", gains[:, i, :], I[nm][l:l + 1, :].partition_broadcast(2), w=[("gains", i)])
        P.dma("sp", sel[:].rearrange("k r m -> k (r m)"), self.C["sel"][:, :], w=["sel"])
        P.dma("sp", i2[:], self.C["i2"][:, :], w=["i2"])
        for nb in range(12):
            wb = wblk[nb % 2]
            P.dma("sp", wb[:], I["w_ada"][l, :, nb * 512:(nb + 1) * 512].rearrange("(c p) n -> p c n", p=128),
                  w=[("wada", nb % 2)])
            bk = self.banks[nb % 2]
            self.mm([(bk[0:2, :], self.csil[:, c, :], wb[:, c, :], c == 0, c == 7) for c in range(8)],
                    r=[("wada", nb % 2), "csil"], w=[("ps", nb % 2)])
            self.tt("dve", modrow[:, nb * 512:(nb + 1) * 512], bk[0:2, :], brow[:, nb * 512:(nb + 1) * 512], ALU.add,
                    r=[("ps", nb % 2), "brow"], w=[("modrow", nb)])
        mr = lambda i: modrow[:, i * D:(i + 1) * D]
        mk = lambda i: [("modrow", 2 * i), ("modrow", 2 * i + 1)]
        self.stt(rows[:, 0, :], mr(1), 1.0, gains[:, 0, :], ALU.add, ALU.mult, r=mk(1) + [("gains", 0)], w=[("rows", 0)])
        self.cp("dve", rows[:, 1, :], mr(0), r=mk(0), w=[("rows", 1)])
        self.stt(rows[:, 2, :], mr(4), 1.0, gains[:, 2, :], ALU.add, ALU.mult, r=mk(4) + [("gains", 2)], w=[("rows", 2)])
        self.cp("dve", rows[:, 3, :], mr(3), r=mk(3), w=[("rows", 3)])
        self.tt("dve", rows[:, 4, :], mr(2), gains[:, 1, :], ALU.mult, r=mk(2) + [("gains", 1)], w=[("rows", 4)])
        self.tt("dve", rows[:, 5, :], mr(5), gains[:, 3, :], ALU.mult, r=mk(5) + [("gains", 3)], w=[("rows", 5)])
        bk = self.banks[2]
        specs = []
        for v in range(4):
            for c in range(8):
                col = (v * 8 + c) * 2
                specs.append((bk[:, col:col + 2], rows[:, v, c * 128:(c + 1) * 128], i2[:, :], True, True))
        self.mm(specs, r=[("rows", v) for v in range(4)] + ["i2"], w=[("ps", 2)])
        self.cp("dve", self.modT[:].rearrange("p v c r -> p (v c r)"), bk[:, 0:64], r=[("ps", 2)], w=["modT"])
        if "scratch_out" in self.debug:
            P.dma("pool", self.modT_d[:, :], self.modT[:].rearrange("p v c r -> p (v c r)"), r=["modT"], w=["modT_d"])
        P.dma("pool", self.G_d[:, 0, :], rows[:, 4, :], r=[("rows", 4)], w=["G_d0"])
        P.dma("pool", self.G_d[:, 1, :], rows[:, 5, :], r=[("rows", 5)], w=["G_d1"])
        P.barrier()
        P.pop()

    def new_staging(self):
        self._stg = [self.P.sb("stg%d" % i, [128, 1024], F32) for i in range(2)]
        self._stg_n = 0

    def load_weight(self, dst, src_fn, nchunks, width, tag):
        P = self.P
        piece = 1024
        keys = []
        for c in range(nchunks):
            for o in range(0, width, piece):
                wd = min(piece, width - o)
                n = self._stg_n
                self._stg_n += 1
                s_ = self._stg[n % 2]
                P.dma("sp", s_[:, 0:wd], src_fn(c)[:, o:o + wd], w=[("stg", n % 2)])
                eng = ("pool", "dve", "act")[n % 3]
                self.cp(eng, dst[:, c, o:o + wd], s_[:, 0:wd], r=[("stg", n % 2)], w=[(tag, c, o)])
                keys.append((tag, c, o))
        return keys

    def phaseA(self, l):
        P = self.P
        I = self.I
        C = self.C
        P.push()
        self.winb = P.sb("winb", [128, 8, NCOL], BF16)
        self.new_staging()
        self.wink = self.load_weight(self.winb, lambda c: I["w_inp"][l, c * 128:(c + 1) * 128, :], 8, NCOL, "winb")
        A = type("A", (), {})()
        self.A = A
        A.xt = [P.sb("xt%d" % i, [128, D], F32) for i in range(4)]
        A.junk = P.sb("junk", [128, D], BF16)
        A.ss = P.sb("ssA", [128, 8], F32)
        A.rs = P.sb("rsA", [128, 8], F32)
        A.xn = [P.sb("xn%d" % i, [128, D], BF16) for i in range(4)]
        A.hxT = [P.sb("hxT%d" % i, [128, 8, 512], BF16) for i in range(2)]
        A.cos = [P.sb("cos%d" % i, [128, 512], F32) for i in range(2)]
        A.sin = [P.sb("sin%d" % i, [128, 512], F32) for i in range(2)]
        A.t1 = [P.sb("t1_%d" % i, [128, 512], F32) for i in range(2)]
        A.t2 = [P.sb("t2_%d" % i, [128, 512], F32) for i in range(2)]
        A.qT = [P.sb("qTt%d" % i, [128, 4, 512], BF16) for i in range(2)]
        A.fT = P.sb("fTt", [128, 2, 512], BF16)
        A.kTt = [P.sb("kTt%d" % i, [128, 512], BF16) for i in range(2)]
        A.vtt = [P.sb("vtt%d" % i, [128, 4, 128], BF16) for i in range(2)]
        A.ABt = [P.sb("ABt%d" % i, [128, 4, 512], BF16) for i in range(2)]
        A.ug = P.sb("ug", [128, 4, 256], F32)
        A.gg = P.sb("gg", [128, 4, 256], F32)
        A.sq = P.sb("sqA", [128, 4, 256], F32)
        A.ssh = P.sb("ssh", [128, 16], F32)
        A.rsh = P.sb("rsh", [128, 16], F32)
        A.vc = P.sb("vc", [128, 4, 256], BF16)
        A.so = P.sb("so", [128, 4, 256], F32)
        A.ss2 = P.sb("ss2", [128, 4], F32)
        A.rs2 = P.sb("rs2", [128, 4], F32)
        A.son = P.sb("son", [128, 4, 256], BF16)
        A.sgt = [P.sb("sgt%d" % i, [128, 2, 512], BF16) for i in range(2)]
        A.gsg = P.sb("gsg", [128, 256], F32)
        A.wsT = P.sb("wsT", [128, 4, 128], BF16)
        A.wsf = P.sb("wsf", [128, 512], F32)
        A.csblk = P.sb("csblk", [128, 256], BF16)
        P.dma("sp", A.gsg[:], I["g_sgu"][l:l + 1, :].partition_broadcast(128), w=["gsg"])
        P.dma("sp", A.wsf[:], I["w_sguT"][l], w=["wsf"])
        self.cp("dve", A.wsT[:].rearrange("p h q -> p (h q)"), A.wsf[:], r=["wsf"], w=["wsT"])
        P.dma("sp", A.csblk[:], C["csblk"][:, :], w=["csblk"])
        A.trbanks = Ring([(self.banks[i].bitcast(BF16), ("ps", i)) for i in range(2)])
        A.pbanks = Ring([(self.banks[i], ("ps", i)) for i in range(2, 8)])
        A.xi = 0
        tiles = [(S, 2, True)] + [(i * 512, 4, False) for i in range(S // 512)]
        if "A_tiles" in self.debug:
            tiles = tiles[:3]
        self.A_front(l, 0, *tiles[0])
        for i, t in enumerate(tiles):
            if i + 1 < len(tiles):
                self.A_front(l, i + 1, *tiles[i + 1])
            self.A_back(l, i, *t)
        P.barrier()
        P.pop()

    def A_front(self, l, it, tok0, nsub, is_ctx):
        P = self.P
        A = self.A
        mi = 1 if is_ctx else 0
        T = nsub * 128
        hx = A.hxT[it % 2]
        hk = ("hxT", it % 2)
        for s in range(nsub):
            xi = A.xi % 4
            A.xi += 1
            P.dma("sp", A.xt[xi][:], self.src_rows(l, tok0 + s * 128, 128), w=[("xt", xi)])
            self.act(A.junk[:], A.xt[xi][:], AF.Square, r=[("xt", xi)], w=["junk", ("ssA", s)], accum_out=A.ss[:, s:s + 1])
            self.rstd(A.rs[:, s:s + 1], A.ss[:, s:s + 1], float(D), [("ssA", s)], [("rsA", s)], ("rsAt", s))
            self.ts("dve", A.xn[xi][:], A.xt[xi][:], A.rs[:, s:s + 1], None, ALU.mult, None,
                    r=[("xt", xi), ("rsA", s)], w=[("xn", xi)])
        xis = [(A.xi - nsub + s) % 4 for s in range(nsub)]
        for cp_ in range(4):
            tb, sk = A.trbanks.next()
            self.tr([(tb[:, j * 512 + s * 128: j * 512 + (s + 1) * 128], A.xn[xis[s]][:, (2 * cp_ + j) * 128:(2 * cp_ + j + 1) * 128])
                     for j in range(2) for s in range(nsub)],
                    r=[("xn", xi) for xi in xis] + ["ident"], w=[sk])
            for j in range(2):
                c = 2 * cp_ + j
                slot = tb[:, j * 512:(j + 1) * 512]
                if cp_ % 2 == 0:
                    self.act(hx[:, c, 0:T], slot[:, 0:T], AF.Identity, r=[sk, "modT"], w=[hk + (c,)],
                             scale=self.modT[:, 0, c, mi:mi + 1], bias=self.modT[:, 1, c, mi:mi + 1])
                else:
                    self.ts("dve", hx[:, c, 0:T], slot[:, 0:T], self.modT[:, 0, c, mi:mi + 1], self.modT[:, 1, c, mi:mi + 1],
                            ALU.mult, ALU.add, r=[sk, "modT"], w=[hk + (c,)])

    def A_back(self, l, it, tok0, nsub, is_ctx):
        P = self.P
        A = self.A
        C = self.C
        if "A_nob" in self.debug:
            return
        T = nsub * 128
        hx = A.hxT[it % 2]
        hkeys = [("hxT", it % 2, c) for c in range(8)]
        W = self.winb
        full = (not is_ctx) or self.ctx_full
        sl = it % 2
        P.dma("sp", A.cos[sl][:, 0:T], C["cosT"][:, tok0:tok0 + T], w=[("cos", sl)])
        P.dma("sp", A.sin[sl][:, 0:T], C["sinT"][:, tok0:tok0 + T], w=[("sin", sl)])

        def fm_chunk(fc):
            bk, bkk = A.pbanks.next()
            self.mm([(bk[:, 0:T], W[:, dc, fc * 128:(fc + 1) * 128], hx[:, dc, 0:T], dc == 0, dc == 7) for dc in range(8)],
                    r=hkeys + self.wink, w=[bkk])
            return bk, bkk

        def rope(fc, fcr, out_ap, okey, j):
            bq, kq = fm_chunk(fc)
            br, kr = fm_chunk(fcr)
            self.tt("dve", A.t1[j % 2][:, 0:T], bq[:, 0:T], A.cos[sl][:, 0:T], ALU.mult, r=[kq, ("cos", sl)], w=[("t1", j % 2)])
            self.tt("dve", A.t2[j % 2][:, 0:T], br[:, 0:T], A.sin[sl][:, 0:T], ALU.mult, r=[kr, ("sin", sl)], w=[("t2", j % 2)])
            self.tt("pool", out_ap, A.t1[j % 2][:, 0:T], A.t2[j % 2][:, 0:T], ALU.add, r=[("t1", j % 2), ("t2", j % 2)], w=okey)

        rope(8, 9, A.kTt[sl][:, 0:T], [("kTt", sl)], 0)
        P.dma("pool", self.kT_d[:, tok0:tok0 + T], A.kTt[sl][:, 0:T], r=[("kTt", sl)], w=[("kT_d", tok0)])
        if full:
            for c in range(4):
                rope(c, 4 + c, A.qT[sl][:, c, 0:T], [("qTt", sl, c)], c + 1)
            P.dma("pool", self.qT_d[:, :, tok0:tok0 + T].rearrange("c p t -> p c t"), A.qT[sl][:, :, 0:T],
                  r=[("qTt", sl, c) for c in range(4)], w=[("qT_d", tok0)])
            for cf in range(2):
                bk, bkk = fm_chunk(10 + cf)
                self.cp("act", A.fT[:, cf, 0:T], bk[:, 0:T], r=[bkk], w=[("fT", cf)])
        if "A_nob2" in self.debug:
            return
        for s in range(nsub):
            b1, k1 = A.pbanks.next()
            ncol = 384 if full else 128
            self.mm([(b1[:, 0:ncol], hx[:, dc, s * 128:(s + 1) * 128], W[:, dc, 1536:1536 + ncol], dc == 0, dc == 7) for dc in range(8)],
                    r=hkeys + self.wink, w=[k1])
            self.cp("act", A.vtt[sl][:, s, :], b1[:, 0:128], r=[k1], w=[("vtt", sl, s)])
            if full:
                self.act(A.ug[:, s, :], b1[:, 128:384], AF.Gelu_apprx_tanh, r=[k1], w=[("ug", s)])
                b2, k2 = A.pbanks.next()
                self.mm([(b2[:, 0:256], hx[:, dc, s * 128:(s + 1) * 128], W[:, dc, 1920:2176], dc == 0, dc == 7) for dc in range(8)],
                        r=hkeys + self.wink, w=[k2])
                self.act(A.gg[:, s, :], b2[:, 0:256], AF.Gelu_apprx_tanh, r=[k2], w=[("gg", s)])
        blk0 = tok0 // 128
        P.dma("pool", self.vt_d[:, blk0:blk0 + nsub, :], A.vtt[sl][:, 0:nsub, :], r=[("vtt", sl, s) for s in range(nsub)],
              w=[("vt_d", tok0)])
        if not full or "A_nob3" in self.debug:
            return
        ns = nsub
        ggk = [("gg", s) for s in range(ns)]
        ugk = [("ug", s) for s in range(ns)]
        for s in range(ns):
            bk, bkk = A.pbanks.next()
            specs = []
            for cf in range(2):
                specs.append((bk[:, cf * 256:(cf + 1) * 256], A.fT[:, cf, s * 128:(s + 1) * 128], A.csblk[:, :], True, True))
            self.mm(specs, r=[("fT", 0), ("fT", 1), "csblk"], w=[bkk])
            self.cp("act", A.ABt[sl][:, s, :], bk[:, :], r=[bkk], w=[("ABt", sl, s)])
        P.dma("pool", self.AB_d[tok0:tok0 + T, :].rearrange("(s p) c -> p s c", p=128), A.ABt[sl][:, 0:ns, :],
              r=[("ABt", sl, s) for s in range(ns)], w=[("AB_d", tok0)])
        if "A_nob4" in self.debug:
            return
        self.tt("pool", A.sq[:, 0:ns, :], A.gg[:, 0:ns, :], A.gg[:, 0:ns, :], ALU.mult, r=ggk, w=["sq"])
        P.op("dve", lambda e: e.tensor_reduce(out=A.ssh[:, 0:ns * 4], in_=A.sq[:, 0:ns, :].rearrange("p s (h d) -> p (s h) d", h=4),
                                              axis=AX.X, op=ALU.add), r=["sq"], w=["ssh"])
        self.rstd(A.rsh[:, 0:ns * 4], A.ssh[:, 0:ns * 4], 64.0, ["ssh"], ["rsh"], "rsht")
        self.tt("dve", A.sq[:, 0:ns, :].rearrange("p s (h d) -> p (s h) d", h=4),
                A.gg[:, 0:ns, :].rearrange("p s (h d) -> p (s h) d", h=4),
                A.rsh[:, 0:ns * 4].unsqueeze(2).to_broadcast([128, ns * 4, 64]), ALU.mult, r=ggk + ["rsh"], w=["sq"])
        self.tt("dve", A.vc[:, 0:ns, :], A.sq[:, 0:ns, :], A.gsg[:, :].unsqueeze(1).to_broadcast([128, ns, 256]), ALU.mult,
                r=["sq", "gsg"], w=["vc"])
        for hp in range(2):
            bk, bkk = A.pbanks.next()
            specs = []
            for hh in range(2):
                h = hp * 2 + hh
                o = bk[:, hh * 256:hh * 256 + ns * 64].rearrange("p (s d) -> p s d", d=64)
                specs.append((o, A.wsT[:, h, :], A.vc[:, 0:ns, h * 64:(h + 1) * 64], True, True))
            self.mm(specs, r=["vc", "wsT"], w=[bkk])
            for hh in range(2):
                h = hp * 2 + hh
                self.stt(A.so[:, 0:ns, h * 64:(h + 1) * 64], bk[:, hh * 256:hh * 256 + ns * 64].rearrange("p (s d) -> p s d", d=64),
                         self.bsT[:, h:h + 1], A.ug[:, 0:ns, h * 64:(h + 1) * 64], ALU.add, ALU.mult,
                         r=[bkk, "bsT"] + ugk, w=[("so", h)])
        sok = [("so", h) for h in range(4)]
        self.tt("pool", A.sq[:, 0:ns, :], A.so[:, 0:ns, :], A.so[:, 0:ns, :], ALU.mult, r=sok, w=["sq"])
        P.op("dve", lambda e: e.tensor_reduce(out=A.ss2[:, 0:ns], in_=A.sq[:, 0:ns, :], axis=AX.X, op=ALU.add), r=["sq"], w=["ss2"])
        self.rstd(A.rs2[:, 0:ns], A.ss2[:, 0:ns], 256.0, ["ss2"], ["rs2"], "rs2t")
        self.tt("dve", A.son[:, 0:ns, :], A.so[:, 0:ns, :], A.rs2[:, 0:ns].unsqueeze(2).to_broadcast([128, ns, 256]), ALU.mult,
                r=sok + ["rs2"], w=["son"])
        tb, sk = A.trbanks.next()
        self.tr([(tb[:, cc * 512 + s * 128: cc * 512 + (s + 1) * 128], A.son[:, s, cc * 128:(cc + 1) * 128])
                 for cc in range(2) for s in range(ns)], r=["son", "ident"], w=[sk])
        for cc in range(2):
            self.act(A.sgt[sl][:, cc, 0:T], tb[:, cc * 512: cc * 512 + T], AF.Identity, r=[sk, "gmixT"], w=[("sgt", sl, cc)],
                     scale=self.gmixT[:, 4 + cc:5 + cc])
        P.dma("pool", self.sguT_d[:, :, tok0:tok0 + T].rearrange("c p t -> p c t"), A.sgt[sl][:, :, 0:T],
              r=[("sgt", sl, 0), ("sgt", sl, 1)], w=[("sguT_d", tok0)])

    def four_finish(self, ybank_ap, ykey, nk, tok_of, g):
        P = self.P
        F = self.F
        i = F.fi % 2
        F.fi += 1
        ysb = F.ysb[i]
        self.cp("act", ysb[:, 0:nk, :], ybank_ap, r=[ykey], w=[("ysb", i)])
        self.tt("pool", F.ysq[:, 0:nk, :], ysb[:, 0:nk, :], ysb[:, 0:nk, :], ALU.mult, r=[("ysb", i)], w=["ysq"])
        P.op("dve", lambda e: e.tensor_reduce(out=F.yss[:, 0:nk], in_=F.ysq[:, 0:nk, :], axis=AX.X, op=ALU.add), r=["ysq"], w=["yss"])
        self.rstd(F.yrs[:, 0:nk], F.yss[:, 0:nk], 256.0, ["yss"], ["yrs"], "yrst")
        self.tt("dve", F.ynb[i][:, 0:nk, :], ysb[:, 0:nk, :], F.yrs[:, 0:nk].unsqueeze(2).to_broadcast([128, nk, 256]), ALU.mult,
                r=[("ysb", i), "yrs"], w=[("ynb", i)])
        tb, sk = F.trbanks.next()
        self.tr([(tb[:, (j * 2 + cc) * 128:(j * 2 + cc + 1) * 128], F.ynb[i][:, j, cc * 128:(cc + 1) * 128])
                 for j in range(nk) for cc in range(2)], r=[("ynb", i), "ident"], w=[sk])
        for j in range(nk):
            for cc in range(2):
                self.act(tok_of(j, cc), tb[:, (j * 2 + cc) * 128:(j * 2 + cc + 1) * 128], AF.Identity, r=[sk, "gmixT"],
                         w=[("fourT", g, j, cc)], scale=self.gmixT[:, 6 + cc:7 + cc])

    def phaseF(self, l):
        P = self.P
        C = self.C
        P.push()
        F = type("F", (), {})()
        self.F = F
        F.fi = 0
        F.ab1 = P.sb("ab1", [128, 128, 128], BF16)
        F.p1 = P.sb("p1", [128, 64, 2, 128], BF16)
        F.mc = P.sb("mc", [128, 64, 128], BF16)
        F.ms = P.sb("ms", [128, 64, 128], BF16)
        F.t1tab = P.sb("t1tab", [128, 128], BF16)
        F.ysb = [P.sb("ysb%d" % i, [128, 4, 256], F32) for i in range(2)]
        F.ysq = P.sb("ysq", [128, 4, 256], F32)
        F.yss = P.sb("yss", [128, 4], F32)
        F.yrs = P.sb("yrs", [128, 4], F32)
        F.ynb = [P.sb("ynb%d" % i, [128, 4, 256], BF16) for i in range(2)]
        F.trbanks = Ring([(self.banks[i].bitcast(BF16), ("ps", i)) for i in range(2)])
        P.dma("sp", F.mc[:].rearrange("p a b -> p (a b)"), C["mc"][:, :], w=["mc"])
        P.dma("sp", F.ms[:].rearrange("p a b -> p (a b)"), C["ms"][:, :], w=["ms"])
        P.dma("sp", F.t1tab[:], C["t1tab"][:, :], w=["t1tab"])
        pb = Ring([(self.banks[i], ("ps", i)) for i in range(2, 5)])
        yb = Ring([(self.banks[i], ("ps", i)) for i in range(5, 8)])
        F.p1b = P.sb("p1b", [128, 64, 2, 128], BF16)
        p1s = [F.p1, F.p1b]
        for hh in range(2):
            for ab in range(2):
                src = self.AB_d[0:S, hh * 256 + ab * 128: hh * 256 + ab * 128 + 128].rearrange("(t1 t2) c -> t1 t2 c", t2=128)
                P.dma("sp", F.ab1[ab * 64:(ab + 1) * 64, :, :], src, w=[("ab1", ab)])
            for c4 in range(32):
                bk, bkk = pb.next()
                specs = []
                for j in range(4):
                    ch = c4 * 4 + j
                    specs.append((bk[:, j * 128:(j + 1) * 128], F.ab1[:, :, ch], F.t1tab[:, :], True, True))
                self.mm(specs, r=[("ab1", 0), ("ab1", 1), "t1tab"], w=[bkk])
                o = p1s[hh][:, :, :, c4 * 4:(c4 + 1) * 4].rearrange("p k r c -> p c r k")
                i_ = bk[:, :].rearrange("p (c r k) -> p c r k", c=4, r=2)
                if c4 % 2 == 0:
                    self.cp("act", o, i_, r=[bkk], w=[("p1", hh, c4)])
                else:
                    self.cp("dve", o, i_, r=[bkk], w=[("p1", hh, c4)])
        p1keys = [("p1", hh, c4) for hh in range(2) for c4 in range(32)]
        for kg in range(16):
            bka, kka = yb.next()
            for half2 in range(2):
                if half2 == 1:
                    bka, kka = yb.next()
                specs = []
                for j in range(2):
                    k1 = kg * 4 + half2 * 2 + j
                    for hh in range(2):
                        o = bka[:, j * 256 + hh * 128: j * 256 + hh * 128 + 128]
                        specs.append((o, F.mc[:, k1, :], p1s[hh][:, k1, 0, :], True, False))
                        specs.append((o, F.ms[:, k1, :], p1s[hh][:, k1, 1, :], False, True))
                self.mm(specs, r=p1keys + ["mc", "ms"], w=[kka])
                k1base = kg * 4 + half2 * 2

                def tok_of(j, cc, k1base=k1base):
                    return self.fourT[:, cc, 0:S].rearrange("p (k2 k1) -> p k1 k2", k1=64)[:, k1base + j, :]
                self.four_finish(bka[:, :].rearrange("p (j c) -> p j c", j=2), kka, 2, tok_of, ("L", kg, half2))
        if self.ctx_full:
            cab = P.sb("cab", [128, 2, 512], BF16)
            c256 = P.sb("c256", [128, 2, 256], BF16)
            s256 = P.sb("s256n", [128, 2, 256], BF16)
            P.dma("sp", cab[:], self.AB_d[S:S + 256, :].rearrange("(tb p) c -> p tb c", p=128), w=["cab"])
            P.dma("sp", c256[:].rearrange("p a b -> p (a b)"), C["c256"][:, :], w=["c256"])
            P.dma("sp", s256[:].rearrange("p a b -> p (a b)"), C["s256n"][:, :], w=["s256"])
            for kb in range(2):
                bk, bkk = yb.next()
                specs = []
                for hh in range(2):
                    o = bk[:, hh * 128:(hh + 1) * 128]
                    for tb in range(2):
                        specs.append((o, c256[:, tb, kb * 128:(kb + 1) * 128], cab[:, tb, hh * 256:hh * 256 + 128], tb == 0, False))
                        specs.append((o, s256[:, tb, kb * 128:(kb + 1) * 128], cab[:, tb, hh * 256 + 128:hh * 256 + 256], False, tb == 1))
                self.mm(specs, r=["cab", "c256", "s256"], w=[bkk])

                def tok_of(j, cc, kb=kb):
                    return self.fourT[:, cc, S + kb * 128:S + (kb + 1) * 128]
                self.four_finish(bk[:, 0:256].rearrange("p (j c) -> p j c", j=1), bkk, 1, tok_of, ("C", kb))
        if "scratch_out" in self.debug and l == 0:
            P.barrier()
            P.dma("pool", self.fourT_d[:, :, :], self.fourT[:], w=["fourT_d"])
        P.barrier()
        P.pop()

    def phaseC(self, l):
        P = self.P
        I = self.I
        C = self.C
        P.push()
        self.woutb = P.sb("woutb", [128, 8, D], BF16)
        self.new_staging()
        self.woutk = self.load_weight(self.woutb, lambda c: I["w_outp"][l, c * 128:(c + 1) * 128, :], 8, D, "woutb")
        K = type("K", (), {})()
        self.K = K
        self.kT_all = P.sb("kT_all", [128, NTOK], BF16)
        self.vt_all = P.sb("vt_all", [128, 66, 128], BF16)
        P.dma("sp", self.kT_all[:], self.kT_d[:, :], w=["kT_all"])
        P.dma("sp", self.vt_all[:], self.vt_d[:, :, :], w=["vt_all"])
        K.qt = [P.sb("qt%d" % i, [128, 4, 512], BF16) for i in range(2)]
        K.sgt = [P.sb("sgtC%d" % i, [128, 2, 512], BF16) for i in range(2)]
        K.xt = [P.sb("xtC%d" % i, [128, D], F32) for i in range(4)]
        K.PT = [P.sb("PT%d" % i, [128, 2, 512], BF16) for i in range(6)]
        K.oc = [P.sb("oc%d" % i, [128, 512], F32) for i in range(2)]
        K.dd = [P.sb("dd%d" % i, [128, 512], F32) for i in range(2)]
        K.ar = [P.sb("ar%d" % i, [128, 512], F32) for i in range(2)]
        K.sqT = P.sb("sqT", [128, 4, 512], BF16)
        K.aT = P.sb("aT", [128, 4, 512], BF16)
        K.ys2 = [P.sb("ysC%d" % i, [128, D], F32) for i in range(2)]
        K.yc2 = [P.sb("ycC%d" % i, [128, D], F32) for i in range(2)]
        K.tmp = P.sb("tmpC", [128, D], F32)
        K.junk = P.sb("junkC", [128, D], BF16)
        K.ssa = P.sb("ssa", [128, 4], F32)
        K.rsa = P.sb("rsa", [128, 4], F32)
        K.ssy = P.sb("ssy", [128, 4], F32)
        K.rsy = P.sb("rsy", [128, 4], F32)
        K.G1 = P.sb("G1bc", [128, 2, D], F32)
        K.mnext = P.sb("mnext", [128, 128], BF16)
        K.mprev = P.sb("mprev", [128, 128], BF16)
        P.dma("sp", K.mnext[:], C["mask_next"][:, :], w=["mnext"])
        P.dma("sp", K.mprev[:], C["mask_prev"][:, :], w=["mprev"])
        for r_ in range(2):
            P.dma("sp", K.G1[:, r_, :], self.G_d[r_:r_ + 1, 0, :].partition_broadcast(128), w=[("G1", r_)])
        K.spairs = Ring([(self.psum[:, 2 * i:2 * i + 2, :], [("ps", 2 * i), ("ps", 2 * i + 1)]) for i in range(3)])
        K.pti = 0
        K.xi = 0
        tiles = [(i * 512, 4, False) for i in range(S // 512)]
        if self.ctx_full:
            tiles = tiles + [(S, 2, True)]
        if "C_tiles" in self.debug:
            tiles = tiles[:2] + tiles[-1:]
        for it, t in enumerate(tiles):
            self.C_tile(l, it, *t)
        P.barrier()
        P.pop()

    def C_tile(self, l, it, tok0, nsub, is_ctx):
        P = self.P
        K = self.K
        T = nsub * 128
        sl = it % 2
        mi = 1 if is_ctx else 0
        qt = K.qt[sl]
        P.dma("sp", qt[:, :, 0:T], self.qT_d[:, :, tok0:tok0 + T].rearrange("c p t -> p c t"), w=[("qt", sl)])
        P.dma("sp", K.sgt[sl][:, :, 0:T], self.sguT_d[:, :, tok0:tok0 + T].rearrange("c p t -> p c t"), w=[("sgtC", sl)])
        xis = []
        for s in range(nsub):
            xi = K.xi % 4
            K.xi += 1
            xis.append(xi)
            P.dma("sp", K.xt[xi][:], self.src_rows(l, tok0 + s * 128, 128), w=[("xtC", xi)])
        items = []
        if not is_ctx:
            n0 = tok0 // 128
            for j in range(n0 - 1, n0 + nsub + 1):
                if j < 0 or j >= S // 128:
                    continue
                qlo = max(j - 1, n0) - n0
                qhi = min(j + 1, n0 + nsub - 1) - n0
                mnext = (j - 1 - n0) if (j - 1 >= n0) else None
                mprev = (j + 1 - n0) if (j + 1 <= n0 + nsub - 1) else None
                items.append((j, qlo * 128, (qhi + 1) * 128, mnext, mprev))
        items.append((64, 0, T, None, None))
        items.append((65, 0, T, None, None))
        LOOK = 2
        bo, ko = self.banks[6], ("ps", 6)
        bd, kd = self.banks[7], ("ps", 7)
        for c in range(4):
            pend = []
            first = True
            for idx in range(len(items) + LOOK):
                if idx < len(items):
                    j, lo, hi, mn, mp = items[idx]
                    n = hi - lo
                    pair, pkeys = K.spairs.next()
                    kcols = slice(j * 128, (j + 1) * 128)
                    self.mm([(pair[:, 0, 0:n], self.kT_all[0:64, kcols], qt[0:64, c, lo:hi], True, True),
                             (pair[:, 1, 0:n], self.kT_all[64:128, kcols], qt[64:128, c, lo:hi], True, True)],
                            r=[("qt", sl), "kT_all"], w=pkeys)
                    pi = K.pti % len(K.PT)
                    K.pti += 1
                    pt = K.PT[pi]
                    pk = [("PT", pi)]
                    self.act(pt[:, :, 0:n], pair[:, :, 0:n], AF.Exp, r=pkeys, w=pk, scale=0.125)
                    if mn is not None:
                        o = mn * 128 - lo
                        self.tt("dve", pt[:, :, o:o + 128], pt[:, :, o:o + 128],
                                K.mnext[:, :].unsqueeze(1).to_broadcast([128, 2, 128]), ALU.mult, r=pk + ["mnext"], w=pk)
                    if mp is not None:
                        o = mp * 128 - lo
                        self.tt("dve", pt[:, :, o:o + 128], pt[:, :, o:o + 128],
                                K.mprev[:, :].unsqueeze(1).to_broadcast([128, 2, 128]), ALU.mult, r=pk + ["mprev"], w=pk)
                    pend.append((j, lo, hi, pt, pk))
                if idx >= LOOK:
                    j, lo, hi, pt, pk = pend.pop(0)
                    n = hi - lo
                    last = (idx == len(items) + LOOK - 1)
                    self.mm([(bo[0:64, lo:hi], self.vt_all[:, j, 0:64], pt[:, 0, 0:n], first, last),
                             (bo[64:128, lo:hi], self.vt_all[:, j, 64:128], pt[:, 1, 0:n], first, last),
                             (bd[0:64, lo:hi], self.ones[:, 0:64], pt[:, 0, 0:n], first, last),
                             (bd[64:128, lo:hi], self.ones[:, 0:64], pt[:, 1, 0:n], first, last)],
                            r=pk + ["ones", "vt_all"], w=[ko, kd])
                    first = False
            d2 = c % 2
            self.cp("act", K.oc[d2][:, 0:T], bo[:, 0:T], r=[ko], w=[("oc", d2)])
            self.ts("dve", K.dd[d2][:, 0:T], bd[:, 0:T], self.esink[:, c:c + 1], None, ALU.add, None, r=[kd, "esink"], w=[("dd", d2)])
            self.recip(K.dd[d2][:, 0:T], K.dd[d2][:, 0:T], r=[("dd", d2)], w=[("dd", d2)])
            self.tt("pool", K.ar[d2][:, 0:T], K.oc[d2][:, 0:T], K.dd[d2][:, 0:T], ALU.mult, r=[("oc", d2), ("dd", d2)], w=[("ar", d2)])
            self.act(K.sqT[:, c, 0:T], K.ar[d2][:, 0:T], AF.Square, r=[("ar", d2)], w=[("sqT", c)])
            self.ts("pool", K.aT[:, c, 0:T], K.ar[d2][:, 0:T], self.gmixT[:, c:c + 1], 1.0, ALU.mult, ALU.mult,
                    r=[("ar", d2), "gmixT"], w=[("aT", c)])
        pair, pkeys = K.spairs.next()
        bs_, ks_ = pair[:, 0, :], pkeys[0]
        specs = []
        for s in range(nsub):
            for c in range(4):
                specs.append((bs_[:, s:s + 1], K.sqT[:, c, s * 128:(s + 1) * 128], self.ones[:, 0:1], c == 0, c == 3))
        self.mm(specs, r=[("sqT", c) for c in range(4)] + ["ones"], w=[ks_])
        self.rstd(K.rsa[:, 0:nsub], bs_[:, 0:nsub], 512.0, [ks_], ["rsa"], "rsat")
        aTk = [("aT", c) for c in range(4)]
        def st1(s):
            xi = xis[s]
            cols = slice(s * 128, (s + 1) * 128)
            b2 = s % 2
            ys, yc = K.ys2[b2], K.yc2[b2]
            pa, pak = K.spairs.next()
            pb_, pbk = K.spairs.next()
            for half in range(2):
                self.mm([(pa[:, half, :], K.aT[:, c, cols], self.woutb[:, c, half * 512:(half + 1) * 512], c == 0, c == 3) for c in range(4)],
                        r=aTk + self.woutk, w=[pak[half]])
            for half in range(2):
                specs = []
                for cc in range(2):
                    specs.append((pb_[:, half, :], K.sgt[sl][:, cc, cols], self.woutb[:, 4 + cc, half * 512:(half + 1) * 512], cc == 0, False))
                for cc in range(2):
                    specs.append((pb_[:, half, :], self.fourT[:, cc, tok0 + s * 128: tok0 + (s + 1) * 128],
                                  self.woutb[:, 6 + cc, half * 512:(half + 1) * 512], False, cc == 1))
                self.mm(specs, r=[("sgtC", sl)] + self.woutk, w=[pbk[half]])
            self.cp("act", ys[:].rearrange("p (h n) -> p h n", h=2), pb_[:, :, :], r=pbk, w=[("ysC", b2)])
            self.stt(yc[:].rearrange("p (h n) -> p h n", h=2), pa[:, :, :], K.rsa[:, s:s + 1], ys[:].rearrange("p (h n) -> p h n", h=2),
                     ALU.mult, ALU.add, r=pak + ["rsa", ("ysC", b2)], w=[("ycC", b2)])

        def st2(s):
            xi = xis[s]
            b2 = s % 2
            yc = K.yc2[b2]
            yck = [("ycC", b2)]
            self.act(K.junk[:], yc[:], AF.Square, r=yck, w=["junkC", ("ssy", s)], accum_out=K.ssy[:, s:s + 1])
            self.rstd(K.rsy[:, s:s + 1], K.ssy[:, s:s + 1], float(D), [("ssy", s)], [("rsy", s)], ("rsyt", s))
            self.tt("pool", K.tmp[:], yc[:], K.G1[:, mi, :], ALU.mult, r=yck + [("G1", mi)], w=["tmpC"])
            self.stt(K.xt[xi][:], K.tmp[:], K.rsy[:, s:s + 1], K.xt[xi][:], ALU.mult, ALU.add,
                     r=["tmpC", ("rsy", s), ("xtC", xi)], w=[("xtC", xi)])
            P.dma("pool", self.xmid_d[tok0 + s * 128: tok0 + (s + 1) * 128, :], K.xt[xi][:], r=[("xtC", xi)], w=[("xmid", tok0, s)])

        st1(0)
        for s in range(1, nsub):
            st1(s)
            st2(s - 1)
        st2(nsub - 1)

    def phaseD(self, l):
        P = self.P
        I = self.I
        P.push()
        self.w1b = P.sb("w1b", [128, 8, DFF], BF16)
        self.w2b = P.sb("w2b", [128, 32, D], BF16)
        self.new_staging()
        self.w1k = self.load_weight(self.w1b, lambda c: I["w_ff1"][l, c * 128:(c + 1) * 128, :], 8, DFF, "w1b")
        self.w2k = self.load_weight(self.w2b, lambda c: I["w_ff2"][l, c * 128:(c + 1) * 128, :], 32, D, "w2b")
        Dd = type("Dd", (), {})()
        self.Dd = Dd
        Dd.xm = [P.sb("xm%d" % i, [128, D], F32) for i in range(4)]
        Dd.xn = [P.sb("xnD%d" % i, [128, D], BF16) for i in range(2)]
        Dd.hx = [P.sb("hx2T%d" % i, [128, 8, 256], BF16) for i in range(2)]
        Dd.hT = P.sb("hT", [128, 32, 256], BF16)
        Dd.rl = [P.sb("rl%d" % i, [128, 512], F32) for i in range(3)]
        Dd.tmp = P.sb("tmpD", [128, D], F32)
        Dd.junk = P.sb("junkD", [128, D], BF16)
        Dd.ss = P.sb("ssD", [128, 4], F32)
        Dd.rs = P.sb("rsD", [128, 4], F32)
        Dd.ssy = P.sb("ssyD", [128, 4], F32)
        Dd.rsy = P.sb("rsyD", [128, 2], F32)
        Dd.G2 = P.sb("G2bc", [128, 2, D], F32)
        for r_ in range(2):
            P.dma("sp", Dd.G2[:, r_, :], self.G_d[r_:r_ + 1, 1, :].partition_broadcast(128), w=[("G2", r_)])
        Dd.trbanks = Ring([(self.banks[i].bitcast(BF16), ("ps", i)) for i in range(2)])
        Dd.fbanks = Ring([(self.banks[i], ("ps", i)) for i in range(2, 5)])
        Dd.yring = Ring([(self.banks[i], ("ps", i)) for i in range(5, 8)])
        Dd.xi = 0
        Dd.rli = 0
        tiles = [(i * 256, False) for i in range(S // 256)]
        if self.ctx_full:
            tiles = tiles + [(S, True)]
        if "D_tiles" in self.debug:
            tiles = tiles[:2] + tiles[-1:]
        self.D_front(l, 0, *tiles[0])
        for it, t in enumerate(tiles):
            if it + 1 < len(tiles):
                self.D_front(l, it + 1, *tiles[it + 1])
            self.D_back(l, it, *t)
        P.barrier()
        P.pop()

    def D_front(self, l, it, tok0, is_ctx):
        P = self.P
        Dd = self.Dd
        mi = 1 if is_ctx else 0
        hx = Dd.hx[it % 2]
        xis = []
        for s in range(2):
            xi = Dd.xi % 4
            Dd.xi += 1
            xis.append(xi)
            P.dma("sp", Dd.xm[xi][:], self.xmid_d[tok0 + s * 128: tok0 + (s + 1) * 128, :], w=[("xm", xi)])
            self.act(Dd.junk[:], Dd.xm[xi][:], AF.Square, r=[("xm", xi)], w=["junkD", ("ssD", s)], accum_out=Dd.ss[:, s:s + 1])
            self.rstd(Dd.rs[:, s:s + 1], Dd.ss[:, s:s + 1], float(D), [("ssD", s)], [("rsD", s)], ("rsDt", s))
            self.ts("dve", Dd.xn[s][:], Dd.xm[xi][:], Dd.rs[:, s:s + 1], None, ALU.mult, None, r=[("xm", xi), ("rsD", s)], w=[("xnD", s)])
        Dd.cur_xis = getattr(Dd, "cur_xis", {})
        Dd.cur_xis[it] = xis
        for cq in range(2):
            tb, sk = Dd.trbanks.next()
            self.tr([(tb[:, j * 256 + s * 128: j * 256 + (s + 1) * 128], Dd.xn[s][:, (4 * cq + j) * 128:(4 * cq + j + 1) * 128])
                     for j in range(4) for s in range(2)], r=[("xnD", 0), ("xnD", 1), "ident"], w=[sk])
            for j in range(4):
                c = 4 * cq + j
                slot = tb[:, j * 256:(j + 1) * 256]
                if cq == 0:
                    self.act(hx[:, c, :], slot, AF.Identity, r=[sk, "modT"], w=[("hx2T", it % 2, c)],
                             scale=self.modT[:, 2, c, mi:mi + 1], bias=self.modT[:, 3, c, mi:mi + 1])
                else:
                    self.ts("dve", hx[:, c, :], slot, self.modT[:, 2, c, mi:mi + 1], self.modT[:, 3, c, mi:mi + 1], ALU.mult, ALU.add,
                            r=[sk, "modT"], w=[("hx2T", it % 2, c)])

    def D_back(self, l, it, tok0, is_ctx):
        P = self.P
        Dd = self.Dd
        mi = 1 if is_ctx else 0
        hx = Dd.hx[it % 2]
        hk = [("hx2T", it % 2, c) for c in range(8)]
        xis = Dd.cur_xis[it]
        for fp in range(16):
            fb, sk = Dd.fbanks.next()
            specs = []
            for j in range(2):
                fc = 2 * fp + j
                for dc in range(8):
                    specs.append((fb[:, j * 256:(j + 1) * 256], self.w1b[:, dc, fc * 128:(fc + 1) * 128], hx[:, dc, :], dc == 0, dc == 7))
            self.mm(specs, r=hk + self.w1k, w=[sk])
            ri = Dd.rli % 3
            Dd.rli += 1
            self.act(Dd.rl[ri][:], fb[:, :], AF.Relu, r=[sk], w=[("rl", ri)])
            self.tt("pool" if fp % 2 == 0 else "dve", Dd.hT[:, 2 * fp:2 * fp + 2, :].rearrange("p a t -> p (a t)"), Dd.rl[ri][:], Dd.rl[ri][:],
                    ALU.mult, r=[("rl", ri)], w=[("hT", 2 * fp), ("hT", 2 * fp + 1)])
        hTk = [("hT", fc) for fc in range(32)]
        for s in range(2):
            xi = xis[s]
            ybs = []
            for half in range(2):
                b_, k_ = Dd.yring.next()
                self.mm([(b_[:, :], Dd.hT[:, fc, s * 128:(s + 1) * 128], self.w2b[:, fc, half * 512:(half + 1) * 512], fc == 0, fc == 31)
                         for fc in range(32)], r=hTk + self.w2k, w=[k_])
                ybs.append((b_, k_))
            for half in range(2):
                self.act(Dd.junk[:, half * 512:(half + 1) * 512], ybs[half][0][:, :], AF.Square, r=[ybs[half][1]],
                         w=[("junkD", half), ("ssyD", s, half)], accum_out=Dd.ssy[:, s * 2 + half:s * 2 + half + 1])
            self.tt("dve", Dd.ssy[:, s * 2:s * 2 + 1], Dd.ssy[:, s * 2:s * 2 + 1], Dd.ssy[:, s * 2 + 1:s * 2 + 2], ALU.add,
                    r=[("ssyD", s, 0), ("ssyD", s, 1)], w=[("ssyDs", s)])
            self.rstd(Dd.rsy[:, s:s + 1], Dd.ssy[:, s * 2:s * 2 + 1], float(D), [("ssyDs", s)], [("rsyD", s)], ("rsyDt", s))
            for half in range(2):
                hc = slice(half * 512, (half + 1) * 512)
                self.tt("dve", Dd.tmp[:, hc], ybs[half][0][:, :], Dd.G2[:, mi, hc], ALU.mult, r=[ybs[half][1], ("G2", mi)], w=[("tmpD", half)])
            self.stt(Dd.xm[xi][:], Dd.tmp[:], Dd.rsy[:, s:s + 1], Dd.xm[xi][:], ALU.mult, ALU.add,
                     r=[("tmpD", 0), ("tmpD", 1), ("rsyD", s), ("xm", xi)], w=[("xm", xi)])
            if is_ctx and l == self.nlayers - 1:
                continue
            P.dma("pool", self.dst_rows(l, tok0 + s * 128, 128), Dd.xm[xi][:], r=[("xm", xi)], w=[("xout", tok0, s)])


def _prep_shared(inp):
    sh = {}
    f = lambda a: np.ascontiguousarray(a, dtype=np.float32)
    sh["w_ada"] = f(inp["w_ada"])
    sh["b_ada"] = f(inp["b_ada"])
    for nm in ("g_pre_mix", "g_post_mix", "g_pre_ff", "g_post_ff"):
        sh[nm] = f(inp[nm])
    sh["w_inp"] = f(inp["w_in"][:, :, _perm_cols()])
    sink = inp["sink"]
    sinkT = np.zeros((2, 128, 4), np.float32)
    for c in range(4):
        sinkT[:, 0:64, c] = sink[:, c][:, None]
        sinkT[:, 64:128, c] = sink[:, 4 + c][:, None]
    sh["sinkT"] = sinkT
    sh["w_sguT"] = f(np.transpose(inp["w_sgu"], (0, 3, 1, 2)).reshape(2, 128, 512))
    sh["b_sguT"] = f(np.transpose(inp["b_sgu"], (0, 2, 1)))
    sh["g_sgu"] = f(inp["g_sgu"].reshape(2, 256))
    rp = _attn_row_perm()
    gm = inp["g_mix"][:, rp]
    sh["g_mixT"] = f(np.transpose(gm.reshape(2, 8, 128), (0, 2, 1)))
    sh["w_outp"] = f(inp["w_out"][:, rp, :])
    sh["w_ff1"] = f(inp["w_ff1"])
    sh["w_ff2"] = f(inp["w_ff2"])
    return sh


def _core_inputs(inp, sh, consts, b):
    m = dict(sh)
    m["x"] = np.ascontiguousarray(inp["x"][b], dtype=np.float32)
    m["ctx"] = np.ascontiguousarray(inp["ctx"][b], dtype=np.float32)
    cs = np.stack([inp["c"][b], inp["c_ctx"]], axis=0).astype(np.float32)
    m["csT"] = np.ascontiguousarray(np.transpose(cs.reshape(2, 8, 128), (2, 1, 0)).reshape(128, 16))
    for k, v in consts.items():
        m["c_" + k] = v
    return m


_CACHE = {}


def kernel(**inputs):
    inp = {k: np.asarray(v) for k, v in inputs.items()}
    if "nc" not in _CACHE:
        _CACHE["nc"] = Builder().build()
        _CACHE["consts"] = _constants()
    nc = _CACHE["nc"]
    consts = _CACHE["consts"]
    sh = _prep_shared(inp)
    in_maps = [_core_inputs(inp, sh, consts, b) for b in range(8)]
    res = run_bass_kernel_spmd(nc, in_maps, core_ids=list(range(8)))
    out = np.stack([np.asarray(res.results[b]["out"], dtype=np.float32) for b in range(8)], axis=0)
    return out
```

```python
import contextlib
import numpy as np
import ml_dtypes
import concourse.bass as bass
import concourse.mybir as mybir
from concourse.bass_utils import run_bass_kernel_spmd

F32 = mybir.dt.float32
BF16 = mybir.dt.bfloat16
AF = mybir.ActivationFunctionType
ALU = mybir.AluOpType
AX = mybir.AxisListType
NPBF = ml_dtypes.bfloat16

COMPUTE = ("pe", "act", "dve", "pool")
NDMA_SEMS = 20

D = 1024
S = 8192
LCTX = 256
NTOK = S + LCTX
DFF = 4096
NCOL = 2176
EPS = 1e-6


class _Op:
    __slots__ = ("eng", "fn", "deps", "signal", "sigval", "is_dma", "sem_idx", "dma_val", "prev_same_sem")

    def __init__(self, eng, fn, is_dma):
        self.eng = eng
        self.fn = fn
        self.deps = []
        self.signal = False
        self.sigval = 0
        self.is_dma = is_dma
        self.sem_idx = None
        self.dma_val = 0
        self.prev_same_sem = None


class Prog:
    def __init__(self, nc):
        self.nc = nc
        self.ops = {e: [] for e in ("pe", "act", "dve", "pool", "sp")}
        self.last_w = {}
        self.readers = {}
        self.dma_count = {"sp": 0, "pool": 0, "act": 0}
        self.dma_last_on_sem = {}
        self.stacks = [contextlib.ExitStack()]
        self.last_op = {}

    def push(self):
        self.stacks.append(contextlib.ExitStack())

    def pop(self):
        self.stacks.pop().close()

    def sb(self, name, shape, dt):
        self.uid = getattr(self, "uid", 0) + 1
        return self.stacks[-1].enter_context(self.nc.sbuf_tensor("%s_u%d" % (name, self.uid), list(shape), dt))

    def ps(self, name, shape, dt):
        return self.stacks[-1].enter_context(self.nc.psum_tensor(name, list(shape), dt))

    def _track(self, op, r, w):
        ps_keys = [("ps", k[1]) for k in list(r) + list(w) if isinstance(k, tuple) and k[0] == "ps"]
        r = [k for k in r if not (isinstance(k, tuple) and k[0] == "ps")]
        w = [k for k in w if not (isinstance(k, tuple) and k[0] == "ps")] + list(dict.fromkeys(ps_keys))
        deps = set()
        for k in r:
            lw = self.last_w.get(k)
            if lw is not None:
                deps.add(lw)
        for k in w:
            lw = self.last_w.get(k)
            if lw is not None and (op.is_dma or lw.is_dma or lw.eng != op.eng):
                deps.add(lw)
            for rd in self.readers.get(k, ()):
                if rd is op:
                    continue
                if op.is_dma or rd.is_dma or rd.eng != op.eng:
                    deps.add(rd)
        for d in deps:
            if d is op:
                continue
            d.signal = True
            op.deps.append(d)
        for k in r:
            self.readers.setdefault(k, []).append(op)
        for k in w:
            self.last_w[k] = op
            self.readers[k] = []

    def op(self, eng, fn, r=(), w=()):
        o = _Op(eng, fn, False)
        self._track(o, r, w)
        self.ops[eng].append(o)
        self.last_op[eng] = o
        return o

    def wait_all(self, eng, r=()):
        o = _Op(eng, None, False)
        self._track(o, r, ())
        self.ops[eng].append(o)
        return o

    def dma(self, q, out, in_, r=(), w=(), **kw):
        o = _Op(q, (out, in_, kw), True)
        n = self.dma_count[q]
        self.dma_count[q] = n + 1
        nsem = NDMA_SEMS if q != "pool" else 6
        o.sem_idx = (q, n % nsem)
        prev = self.dma_last_on_sem.get(o.sem_idx)
        o.prev_same_sem = prev
        o.dma_val = (prev.dma_val if prev is not None else 0) + 16
        self.dma_last_on_sem[o.sem_idx] = o
        self._track(o, r, w)
        self.ops[q].append(o)
        return o

    def barrier(self):
        srcs = [o for o in self.last_op.values()] + list(self.dma_last_on_sem.values())
        for e in ("pe", "act", "dve", "pool", "sp"):
            o = _Op(e, None, False)
            for s_ in srcs:
                if (not s_.is_dma) and s_.eng == e:
                    continue
                s_.signal = True
                o.deps.append(s_)
            self.ops[e].append(o)
        self.last_w = {}
        self.readers = {}

    def emit(self):
        nc = self.nc
        st = self.stacks[0]
        esem = {e: st.enter_context(nc.semaphore("sem_" + e)) for e in COMPUTE}
        dsem = {}
        for q in ("sp", "pool", "act"):
            for i in range(min(NDMA_SEMS, self.dma_count[q])):
                dsem[(q, i)] = st.enter_context(nc.semaphore("dsem_%s_%d" % (q, i)))
        for e in COMPUTE:
            c = 0
            for o in self.ops[e]:
                if (not o.is_dma) and o.signal:
                    c += 1
                    o.sigval = c

        def src_of(d):
            if d.is_dma:
                return dsem[d.sem_idx], d.dma_val, ("d",) + d.sem_idx
            return esem[d.eng], d.sigval, ("e", d.eng)

        def run(engname, eng):
            seen = {}
            for o in self.ops[engname]:
                waits = {}
                deps = list(o.deps)
                if o.is_dma and o.prev_same_sem is not None:
                    deps.append(o.prev_same_sem)
                for d in deps:
                    sem, val, key = src_of(d)
                    if seen.get(key, 0) >= val:
                        continue
                    if key not in waits or waits[key][1] < val:
                        waits[key] = (sem, val)
                for key, (sem, val) in waits.items():
                    eng.wait_ge(sem, val)
                    seen[key] = val
                if o.is_dma:
                    out, in_, kw = o.fn
                    eng.dma_start(out=out, in_=in_, **kw).then_inc(dsem[o.sem_idx], 16)
                elif o.fn is None:
                    pass
                else:
                    ins = o.fn(eng)
                    if o.signal:
                        ins.then_inc(esem[engname], 1)

        with nc.Block() as block:
            @block.tensor
            def _(e):
                run("pe", e)

            @block.scalar
            def _(e):
                run("act", e)

            @block.vector
            def _(e):
                run("dve", e)

            @block.gpsimd
            def _(e):
                run("pool", e)

            @block.sync
            def _(e):
                run("sp", e)

    def close(self):
        while self.stacks:
            self.stacks.pop().close()


class Ring:
    def __init__(self, items):
        self.items = items
        self.i = 0

    def next(self):
        it = self.items[self.i % len(self.items)]
        self.i += 1
        return it


def _perm_cols():
    def rotp(d):
        if d < 16:
            return d + 16
        if d < 32:
            return d - 16
        if d < 48:
            return d + 16
        return d - 16
    cols = []
    for c in range(4):
        for h in (c, 4 + c):
            cols += [h * 64 + d for d in range(64)]
    for c in range(4):
        for h in (c, 4 + c):
            cols += [h * 64 + rotp(d) for d in range(64)]
    cols += [512 + j for j in range(128)]
    cols += [512 + h * 64 + rotp(d) for h in range(2) for d in range(64)]
    cols += [1280 + j for j in range(256)]
    cols += [640 + j for j in range(128)]
    cols += [768 + j for j in range(256)]
    cols += [1024 + j for j in range(256)]
    assert len(cols) == NCOL
    return np.array(cols)


def _attn_row_perm():
    rows = []
    for c in range(4):
        for h in (c, 4 + c):
            rows += [h * 64 + d for d in range(64)]
    return np.array(rows + list(range(512, 1024)))


def _constants():
    k = {}
    k["ident"] = np.eye(128, dtype=np.float32).astype(NPBF)
    n_freq = 16
    inv_freq = (np.float32(10000.0) ** (-np.arange(n_freq, dtype=np.float32) / np.float32(n_freq))).astype(np.float32)
    t = np.arange(S)
    row = (t // 64).astype(np.float32)
    col = (t % 64).astype(np.float32)
    ang = np.zeros((64, S), np.float32)
    ang[0:16] = row[None, :] * inv_freq[:, None]
    ang[16:32] = ang[0:16]
    ang[32:48] = col[None, :] * inv_freq[:, None]
    ang[48:64] = ang[32:48]
    cos = np.cos(ang).astype(np.float32)
    sin = np.sin(ang).astype(np.float32)
    sin[0:16] *= -1
    sin[32:48] *= -1
    cosT = np.ones((128, NTOK), np.float32)
    sinT = np.zeros((128, NTOK), np.float32)
    cosT[0:64, :S] = cos
    cosT[64:128, :S] = cos
    sinT[0:64, :S] = sin
    sinT[64:128, :S] = sin
    k["cosT"] = cosT
    k["sinT"] = sinT
    kk = np.arange(128)[:, None]
    qq = np.arange(128)[None, :]
    k["mask_next"] = (kk <= qq).astype(np.float32).astype(NPBF)
    k["mask_prev"] = (kk >= qq).astype(np.float32).astype(NPBF)
    k["ones"] = np.ones((128, 128), np.float32).astype(NPBF)
    cidx = np.arange(64)
    c64 = np.cos(2 * np.pi * np.outer(cidx, cidx) / 64.0) / 8.0
    s64 = np.sin(2 * np.pi * np.outer(cidx, cidx) / 64.0) / 8.0
    csblk = np.zeros((128, 256), np.float64)
    for g in range(2):
        csblk[g * 64:(g + 1) * 64, g * 64:(g + 1) * 64] = c64
        csblk[g * 64:(g + 1) * 64, 128 + g * 64:128 + (g + 1) * 64] = s64
    k["csblk"] = csblk.astype(np.float32).astype(NPBF)
    t1 = np.zeros((128, 128), np.float64)
    t1[0:64, 0:64] = c64
    t1[0:64, 64:128] = -s64
    t1[64:128, 0:64] = -s64
    t1[64:128, 64:128] = -c64
    k["t1tab"] = t1.astype(np.float32).astype(NPBF)
    t2 = np.arange(128)[:, None, None].astype(np.float64)
    k1 = np.arange(64)[None, :, None].astype(np.float64)
    k2 = np.arange(128)[None, None, :].astype(np.float64)
    phi = 2 * np.pi * ((t2 * (k1 + 64 * k2)) % 8192) / 8192.0
    k["mc"] = (np.cos(phi) / np.sqrt(128.0)).astype(np.float32).astype(NPBF).reshape(128, 64 * 128)
    k["ms"] = (np.sin(phi) / np.sqrt(128.0)).astype(np.float32).astype(NPBF).reshape(128, 64 * 128)
    tt = (np.arange(2)[None, :, None] * 128 + np.arange(128)[:, None, None]).astype(np.float64)
    kk2 = np.arange(256)[None, None, :].astype(np.float64)
    ph = 2 * np.pi * ((tt * kk2) % 256) / 256.0
    k["c256"] = (np.cos(ph) / 16.0).astype(np.float32).astype(NPBF).reshape(128, 512)
    k["s256n"] = (-np.sin(ph) / 16.0).astype(np.float32).astype(NPBF).reshape(128, 512)
    sel = np.zeros((2, 2, 128), np.float32)
    sel[0, 0, :] = 1
    sel[1, 1, :] = 1
    k["sel"] = sel.reshape(2, 256)
    k["i2"] = np.eye(2, dtype=np.float32)
    return k


CONST_SPECS = [
    ("ident", [128, 128], BF16), ("cosT", [128, NTOK], F32), ("sinT", [128, NTOK], F32),
    ("mask_next", [128, 128], BF16), ("mask_prev", [128, 128], BF16), ("ones", [128, 128], BF16),
    ("csblk", [128, 256], BF16), ("t1tab", [128, 128], BF16), ("mc", [128, 8192], BF16), ("ms", [128, 8192], BF16),
    ("c256", [128, 512], BF16), ("s256n", [128, 512], BF16), ("sel", [2, 256], F32), ("i2", [2, 2], F32),
]

IN_SPECS = [
    ("x", [S, D]), ("ctx", [LCTX, D]), ("csT", [128, 16]),
    ("w_ada", [2, D, 6 * D]), ("b_ada", [2, 6 * D]),
    ("g_pre_mix", [2, D]), ("g_post_mix", [2, D]), ("g_pre_ff", [2, D]), ("g_post_ff", [2, D]),
    ("w_inp", [2, D, NCOL]), ("sinkT", [2, 128, 4]), ("w_sguT", [2, 128, 512]), ("b_sguT", [2, 128, 4]),
    ("g_sgu", [2, 256]), ("g_mixT", [2, 128, 8]), ("w_outp", [2, D, D]),
    ("w_ff1", [2, D, DFF]), ("w_ff2", [2, DFF, D]),
]


class Builder:
    def __init__(self, nlayers=2, stop_after=None, debug=()):
        self.nlayers = nlayers
        self.stop_after = stop_after
        self.debug = set(debug)
        nc = bass.Bass("TRN2", target_bir_lowering=False)
        self.nc = nc
        self.I = {}
        for name, shape in IN_SPECS:
            self.I[name] = nc.dram_tensor(name, shape, F32, kind="ExternalInput").ap()
        self.C = {}
        for name, shape, dt in CONST_SPECS:
            self.C[name] = nc.dram_tensor("c_" + name, shape, dt, kind="ExternalInput").ap()
        self.out = nc.dram_tensor("out", [S, D], F32, kind="ExternalOutput").ap()
        kw = dict(kind="ExternalOutput") if "scratch_out" in self.debug else {}
        self.qT_d = nc.dram_tensor("qT_d", [4, 128, NTOK], BF16, **kw).ap()
        self.sguT_d = nc.dram_tensor("sguT_d", [2, 128, NTOK], BF16, **kw).ap()
        self.AB_d = nc.dram_tensor("AB_d", [NTOK, 512], BF16, **kw).ap()
        self.kT_d = nc.dram_tensor("kT_d", [128, NTOK], BF16, **kw).ap()
        self.vt_d = nc.dram_tensor("vt_d", [128, 66, 128], BF16, **kw).ap()
        self.fourT_d = nc.dram_tensor("fourT_d", [128, 2, NTOK], BF16, **kw).ap()
        self.xmid_d = nc.dram_tensor("xmid_d", [NTOK, D], F32, **kw).ap()
        self.x1_d = nc.dram_tensor("x1_d", [NTOK, D], F32, **kw).ap()
        self.G_d = nc.dram_tensor("G_d", [2, 4, D], F32, **kw).ap()
        self.modT_d = nc.dram_tensor("modT_d", [128, 64], F32, **kw).ap()
        self.dbg = {}
        self.P = Prog(nc)

    def act(self, out, in_, func, r, w, **kw):
        return self.P.op("act", lambda e: e.activation(out=out, in_=in_, func=func, **kw), r, w)

    def tt(self, eng, out, in0, in1, op, r, w):
        return self.P.op(eng, lambda e: e.tensor_tensor(out=out, in0=in0, in1=in1, op=op), r, w)

    def ts(self, eng, out, in0, s1, s2, op0, op1, r, w):
        if s2 is None:
            return self.P.op(eng, lambda e: e.tensor_scalar(out=out, in0=in0, scalar1=s1, scalar2=None, op0=op0), r, w)
        return self.P.op(eng, lambda e: e.tensor_scalar(out=out, in0=in0, scalar1=s1, scalar2=s2, op0=op0, op1=op1), r, w)

    def stt(self, out, in0, scalar, in1, op0, op1, r, w):
        return self.P.op("dve", lambda e: e.scalar_tensor_tensor(out=out, in0=in0, scalar=scalar, in1=in1, op0=op0, op1=op1), r, w)

    def cp(self, eng, out, in_, r, w):
        if eng == "act":
            return self.P.op("act", lambda e: e.activation(out=out, in_=in_, func=AF.Copy), r, w)
        return self.P.op(eng, lambda e: e.tensor_copy(out=out, in_=in_), r, w)

    def recip(self, out, in_, r, w):
        return self.P.op("dve", lambda e: e.reciprocal(out=out, in_=in_), r, w)

    def rstd(self, out, ss, n, key_in, key_out, tmpkey):
        self.act(out, ss, AF.Sqrt, r=key_in, w=[tmpkey], scale=1.0 / n, bias=self.eps_t[:, 0:1])
        self.recip(out, out, r=[tmpkey], w=key_out)

    def mm(self, specs, r, w):
        def fn(e):
            ins = None
            for sp_ in specs:
                ins = e.matmul(out=sp_[0], lhsT=sp_[1], rhs=sp_[2], start=sp_[3], stop=sp_[4], skip_group_check=True)
            return ins
        return self.P.op("pe", fn, r, w)

    def tr(self, specs, r, w):
        def fn(e):
            ins = None
            for o_, i_ in specs:
                ins = e.transpose(out=o_, in_=i_, identity=self.ident[:])
            return ins
        return self.P.op("pe", fn, r, w)

    def build(self):
        P = self.P
        nc = self.nc
        self.ident = P.sb("ident", [128, 128], BF16)
        self.ones = P.sb("ones", [128, 128], BF16)
        self.eps_t = P.sb("eps_t", [128, 1], F32)
        self.csil = P.sb("csil", [128, 8, 2], F32)
        self.modT = P.sb("modT", [128, 4, 8, 2], F32)
        self.psum = P.ps("psum", [128, 8, 512], F32)
        self.banks = [self.psum[:, i, :] for i in range(8)]
        P.dma("sp", self.ident[:], self.C["ident"][:, :], w=["ident"])
        P.dma("sp", self.ones[:], self.C["ones"][:, :], w=["ones"])
        P.op("dve", lambda e: e.memset(self.eps_t[:], EPS), w=["eps"])
        P.dma("sp", self.csil[:].rearrange("p c r -> p (c r)"), self.I["csT"][:, :], w=["csil"])
        self.act(self.csil[:], self.csil[:], AF.Silu, r=["csil"], w=["csil"])
        P.barrier()
        for l in range(self.nlayers):
            self.layer(l)
        P.barrier()
        P.emit()
        P.close()
        return nc

    def src_rows(self, l, tok0, n):
        if l == 0:
            if tok0 >= S:
                return self.I["ctx"][tok0 - S:tok0 - S + n, :]
            return self.I["x"][tok0:tok0 + n, :]
        return self.x1_d[tok0:tok0 + n, :]

    def dst_rows(self, l, tok0, n):
        if l == self.nlayers - 1 and self.nlayers == 2:
            assert tok0 < S
            return self.out[tok0:tok0 + n, :]
        if self.nlayers == 1 and tok0 < S:
            return self.out[tok0:tok0 + n, :]
        return self.x1_d[tok0:tok0 + n, :]

    def layer(self, l):
        P = self.P
        last = (l == 1)
        self.l = l
        self.ctx_full = not last
        P.push()
        self.gmixT = P.sb("gmixT", [128, 8], F32)
        self.esink = P.sb("esink", [128, 4], F32)
        self.bsT = P.sb("bsT", [128, 4], F32)
        self.fourT = P.sb("fourT", [128, 2, NTOK], BF16)
        P.dma("sp", self.gmixT[:], self.I["g_mixT"][l], w=["gmixT"])
        P.dma("sp", self.esink[:], self.I["sinkT"][l], w=["esink"])
        P.dma("sp", self.bsT[:], self.I["b_sguT"][l], w=["bsT"])
        self.act(self.esink[:], self.esink[:], AF.Exp, r=["esink"], w=["esink"])
        self.adaln(l)
        if self.stop_after == "adaln":
            P.pop()
            return
        self.phaseA(l)
        if self.stop_after == "A":
            P.pop()
            return
        self.phaseF(l)
        if self.stop_after == "F":
            P.pop()
            return
        self.phaseC(l)
        if self.stop_after == "C":
            P.pop()
            return
        P.pop()
        self.phaseD(l)

    def adaln(self, l):
        P = self.P
        I = self.I
        P.push()
        wblk = [P.sb("wada%d" % i, [128, 8, 512], F32) for i in range(2)]
        modrow = P.sb("modrow", [2, 6 * D], F32)
        brow = P.sb("brow", [2, 6 * D], F32)
        gains = P.sb("gains", [2, 4, D], F32)
        rows = P.sb("rows", [2, 6, D], F32)
        sel = P.sb("sel", [2, 2, 128], F32)
        i2 = P.sb("i2", [2, 2], F32)
        P.dma("sp", brow[:], I["b_ada"][l:l + 1, :].partition_broadcast(2), w=["brow"])
        for i, nm in enumerate(("g_pre_mix", "g_post_mix", "g_pre_ff", "g_post_ff")):
            P.dma("sp", gains[:, i, :], I[nm][l:l + 1, :].partition_broadcast(2), w=[("gains", i)])
        P.dma("sp", sel[:].rearrange("k r m -> k (r m)"), self.C["sel"][:, :], w=["sel"])
        P.dma("sp", i2[:], self.C["i2"][:, :], w=["i2"])
        for nb in range(12):
            wb = wblk[nb % 2]
            P.dma("sp", wb[:], I["w_ada"][l, :, nb * 512:(nb + 1) * 512].rearrange("(c p) n -> p c n", p=128),
                  w=[("wada", nb % 2)])
            bk = self.banks[nb % 2]
            self.mm([(bk[0:2, :], self.csil[:, c, :], wb[:, c, :], c == 0, c == 7) for c in range(8)],
                    r=[("wada", nb % 2), "csil"], w=[("ps", nb % 2)])
            self.tt("dve", modrow[:, nb * 512:(nb + 1) * 512], bk[0:2, :], brow[:, nb * 512:(nb + 1) * 512], ALU.add,
                    r=[("ps", nb % 2), "brow"], w=[("modrow", nb)])
        mr = lambda i: modrow[:, i * D:(i + 1) * D]
        mk = lambda i: [("modrow", 2 * i), ("modrow", 2 * i + 1)]
        self.stt(rows[:, 0, :], mr(1), 1.0, gains[:, 0, :], ALU.add, ALU.mult, r=mk(1) + [("gains", 0)], w=[("rows", 0)])
        self.cp("dve", rows[:, 1, :], mr(0), r=mk(0), w=[("rows", 1)])
        self.stt(rows[:, 2, :], mr(4), 1.0, gains[:, 2, :], ALU.add, ALU.mult, r=mk(4) + [("gains", 2)], w=[("rows", 2)])
        self.cp("dve", rows[:, 3, :], mr(3), r=mk(3), w=[("rows", 3)])
        self.tt("dve", rows[:, 4, :], mr(2), gains[:, 1, :], ALU.mult, r=mk(2) + [("gains", 1)], w=[("rows", 4)])
        self.tt("dve", rows[:, 5, :], mr(5), gains[:, 3, :], ALU.mult, r=mk(5) + [("gains", 3)], w=[("rows", 5)])
        bk = self.banks[2]
        specs = []
        for v in range(4):
            for c in range(8):
                col = (v * 8 + c) * 2
                specs.append((bk[:, col:col + 2], rows[:, v, c * 128:(c + 1) * 128], i2[:, :], True, True))
        self.mm(specs, r=[("rows", v) for v in range(4)] + ["i2"], w=[("ps", 2)])
        self.cp("dve", self.modT[:].rearrange("p v c r -> p (v c r)"), bk[:, 0:64], r=[("ps", 2)], w=["modT"])
        if "scratch_out" in self.debug:
            P.dma("pool", self.modT_d[:, :], self.modT[:].rearrange("p v c r -> p (v c r)"), r=["modT"], w=["modT_d"])
        P.dma("pool", self.G_d[:, 0, :], rows[:, 4, :], r=[("rows", 4)], w=["G_d0"])
        P.dma("pool", self.G_d[:, 1, :], rows[:, 5, :], r=[("rows", 5)], w=["G_d1"])
        P.barrier()
        P.pop()

    def new_staging(self):
        self._stg = [self.P.sb("stg%d" % i, [128, 1024], F32) for i in range(2)]
        self._stg_n = 0

    def load_weight(self, dst, src_fn, nchunks, width, tag):
        P = self.P
        piece = 1024
        keys = []
        for c in range(nchunks):
            for o in range(0, width, piece):
                wd = min(piece, width - o)
                n = self._stg_n
                self._stg_n += 1
                s_ = self._stg[n % 2]
                P.dma("sp", s_[:, 0:wd], src_fn(c)[:, o:o + wd], w=[("stg", n % 2)])
                eng = ("pool", "dve", "act")[n % 3]
                self.cp(eng, dst[:, c, o:o + wd], s_[:, 0:wd], r=[("stg", n % 2)], w=[(tag, c, o)])
                keys.append((tag, c, o))
        return keys

    def phaseA(self, l):
        P = self.P
        I = self.I
        C = self.C
        P.push()
        self.winb = P.sb("winb", [128, 8, NCOL], BF16)
        self.new_staging()
        self.wink = self.load_weight(self.winb, lambda c: I["w_inp"][l, c * 128:(c + 1) * 128, :], 8, NCOL, "winb")
        A = type("A", (), {})()
        self.A = A
        A.xt = [P.sb("xt%d" % i, [128, D], F32) for i in range(4)]
        A.junk = P.sb("junk", [128, D], BF16)
        A.ss = P.sb("ssA", [128, 8], F32)
        A.rs = P.sb("rsA", [128, 8], F32)
        A.xn = [P.sb("xn%d" % i, [128, D], BF16) for i in range(4)]
        A.hxT = [P.sb("hxT%d" % i, [128, 8, 512], BF16) for i in range(2)]
        A.cos = [P.sb("cos%d" % i, [128, 512], F32) for i in range(2)]
        A.sin = [P.sb("sin%d" % i, [128, 512], F32) for i in range(2)]
        A.t1 = [P.sb("t1_%d" % i, [128, 512], F32) for i in range(2)]
        A.t2 = [P.sb("t2_%d" % i, [128, 512], F32) for i in range(2)]
        A.qT = [P.sb("qTt%d" % i, [128, 4, 512], BF16) for i in range(2)]
        A.fT = P.sb("fTt", [128, 2, 512], BF16)
        A.kTt = [P.sb("kTt%d" % i, [128, 512], BF16) for i in range(2)]
        A.vtt = [P.sb("vtt%d" % i, [128, 4, 128], BF16) for i in range(2)]
        A.ABt = [P.sb("ABt%d" % i, [128, 4, 512], BF16) for i in range(2)]
        A.ug = P.sb("ug", [128, 4, 256], F32)
        A.gg = P.sb("gg", [128, 4, 256], F32)
        A.sq = P.sb("sqA", [128, 4, 256], F32)
        A.ssh = P.sb("ssh", [128, 16], F32)
        A.rsh = P.sb("rsh", [128, 16], F32)
        A.vc = P.sb("vc", [128, 4, 256], BF16)
        A.so = P.sb("so", [128, 4, 256], F32)
        A.ss2 = P.sb("ss2", [128, 4], F32)
        A.rs2 = P.sb("rs2", [128, 4], F32)
        A.son = P.sb("son", [128, 4, 256], BF16)
        A.sgt = [P.sb("sgt%d" % i, [128, 2, 512], BF16) for i in range(2)]
        A.gsg = P.sb("gsg", [128, 256], F32)
        A.wsT = P.sb("wsT", [128, 4, 128], BF16)
        A.wsf = P.sb("wsf", [128, 512], F32)
        A.csblk = P.sb("csblk", [128, 256], BF16)
        P.dma("sp", A.gsg[:], I["g_sgu"][l:l + 1, :].partition_broadcast(128), w=["gsg"])
        P.dma("sp", A.wsf[:], I["w_sguT"][l], w=["wsf"])
        self.cp("dve", A.wsT[:].rearrange("p h q -> p (h q)"), A.wsf[:], r=["wsf"], w=["wsT"])
        P.dma("sp", A.csblk[:], C["csblk"][:, :], w=["csblk"])
        A.trbanks = Ring([(self.banks[i].bitcast(BF16), ("ps", i)) for i in range(2)])
        A.pbanks = Ring([(self.banks[i], ("ps", i)) for i in range(2, 8)])
        A.xi = 0
        tiles = [(S, 2, True)] + [(i * 512, 4, False) for i in range(S // 512)]
        if "A_tiles" in self.debug:
            tiles = tiles[:3]
        self.A_front(l, 0, *tiles[0])
        for i, t in enumerate(tiles):
            if i + 1 < len(tiles):
                self.A_front(l, i + 1, *tiles[i + 1])
            self.A_back(l, i, *t)
        P.barrier()
        P.pop()

    def A_front(self, l, it, tok0, nsub, is_ctx):
        P = self.P
        A = self.A
        mi = 1 if is_ctx else 0
        T = nsub * 128
        hx = A.hxT[it % 2]
        hk = ("hxT", it % 2)
        for s in range(nsub):
            xi = A.xi % 4
            A.xi += 1
            P.dma("sp", A.xt[xi][:], self.src_rows(l, tok0 + s * 128, 128), w=[("xt", xi)])
            self.act(A.junk[:], A.xt[xi][:], AF.Square, r=[("xt", xi)], w=["junk", ("ssA", s)], accum_out=A.ss[:, s:s + 1])
            self.rstd(A.rs[:, s:s + 1], A.ss[:, s:s + 1], float(D), [("ssA", s)], [("rsA", s)], ("rsAt", s))
            self.ts("dve", A.xn[xi][:], A.xt[xi][:], A.rs[:, s:s + 1], None, ALU.mult, None,
                    r=[("xt", xi), ("rsA", s)], w=[("xn", xi)])
        xis = [(A.xi - nsub + s) % 4 for s in range(nsub)]
        for cp_ in range(4):
            tb, sk = A.trbanks.next()
            self.tr([(tb[:, j * 512 + s * 128: j * 512 + (s + 1) * 128], A.xn[xis[s]][:, (2 * cp_ + j) * 128:(2 * cp_ + j + 1) * 128])
                     for j in range(2) for s in range(nsub)],
                    r=[("xn", xi) for xi in xis] + ["ident"], w=[sk])
            for j in range(2):
                c = 2 * cp_ + j
                slot = tb[:, j * 512:(j + 1) * 512]
                if cp_ % 2 == 0:
                    self.act(hx[:, c, 0:T], slot[:, 0:T], AF.Identity, r=[sk, "modT"], w=[hk + (c,)],
                             scale=self.modT[:, 0, c, mi:mi + 1], bias=self.modT[:, 1, c, mi:mi + 1])
                else:
                    self.ts("dve", hx[:, c, 0:T], slot[:, 0:T], self.modT[:, 0, c, mi:mi + 1], self.modT[:, 1, c, mi:mi + 1],
                            ALU.mult, ALU.add, r=[sk, "modT"], w=[hk + (c,)])

    def A_back(self, l, it, tok0, nsub, is_ctx):
        P = self.P
        A = self.A
        C = self.C
        if "A_nob" in self.debug:
            return
        T = nsub * 128
        hx = A.hxT[it % 2]
        hkeys = [("hxT", it % 2, c) for c in range(8)]
        W = self.winb
        full = (not is_ctx) or self.ctx_full
        sl = it % 2
        P.dma("sp", A.cos[sl][:, 0:T], C["cosT"][:, tok0:tok0 + T], w=[("cos", sl)])
        P.dma("sp", A.sin[sl][:, 0:T], C["sinT"][:, tok0:tok0 + T], w=[("sin", sl)])

        def fm_chunk(fc):
            bk, bkk = A.pbanks.next()
            self.mm([(bk[:, 0:T], W[:, dc, fc * 128:(fc + 1) * 128], hx[:, dc, 0:T], dc == 0, dc == 7) for dc in range(8)],
                    r=hkeys + self.wink, w=[bkk])
            return bk, bkk

        def rope(fc, fcr, out_ap, okey, j):
            bq, kq = fm_chunk(fc)
            br, kr = fm_chunk(fcr)
            self.tt("dve", A.t1[j % 2][:, 0:T], bq[:, 0:T], A.cos[sl][:, 0:T], ALU.mult, r=[kq, ("cos", sl)], w=[("t1", j % 2)])
            self.tt("dve", A.t2[j % 2][:, 0:T], br[:, 0:T], A.sin[sl][:, 0:T], ALU.mult, r=[kr, ("sin", sl)], w=[("t2", j % 2)])
            self.tt("pool", out_ap, A.t1[j % 2][:, 0:T], A.t2[j % 2][:, 0:T], ALU.add, r=[("t1", j % 2), ("t2", j % 2)], w=okey)

        rope(8, 9, A.kTt[sl][:, 0:T], [("kTt", sl)], 0)
        P.dma("pool", self.kT_d[:, tok0:tok0 + T], A.kTt[sl][:, 0:T], r=[("kTt", sl)], w=[("kT_d", tok0)])
        if full:
            for c in range(4):
                rope(c, 4 + c, A.qT[sl][:, c, 0:T], [("qTt", sl, c)], c + 1)
            P.dma("pool", self.qT_d[:, :, tok0:tok0 + T].rearrange("c p t -> p c t"), A.qT[sl][:, :, 0:T],
                  r=[("qTt", sl, c) for c in range(4)], w=[("qT_d", tok0)])
            for cf in range(2):
                bk, bkk = fm_chunk(10 + cf)
                self.cp("act", A.fT[:, cf, 0:T], bk[:, 0:T], r=[bkk], w=[("fT", cf)])
        if "A_nob2" in self.debug:
            return
        for s in range(nsub):
            b1, k1 = A.pbanks.next()
            ncol = 384 if full else 128
            self.mm([(b1[:, 0:ncol], hx[:, dc, s * 128:(s + 1) * 128], W[:, dc, 1536:1536 + ncol], dc == 0, dc == 7) for dc in range(8)],
                    r=hkeys + self.wink, w=[k1])
            self.cp("act", A.vtt[sl][:, s, :], b1[:, 0:128], r=[k1], w=[("vtt", sl, s)])
            if full:
                self.act(A.ug[:, s, :], b1[:, 128:384], AF.Gelu_apprx_tanh, r=[k1], w=[("ug", s)])
                b2, k2 = A.pbanks.next()
                self.mm([(b2[:, 0:256], hx[:, dc, s * 128:(s + 1) * 128], W[:, dc, 1920:2176], dc == 0, dc == 7) for dc in range(8)],
                        r=hkeys + self.wink, w=[k2])
                self.act(A.gg[:, s, :], b2[:, 0:256], AF.Gelu_apprx_tanh, r=[k2], w=[("gg", s)])
        blk0 = tok0 // 128
        P.dma("pool", self.vt_d[:, blk0:blk0 + nsub, :], A.vtt[sl][:, 0:nsub, :], r=[("vtt", sl, s) for s in range(nsub)],
              w=[("vt_d", tok0)])
        if not full or "A_nob3" in self.debug:
            return
        ns = nsub
        ggk = [("gg", s) for s in range(ns)]
        ugk = [("ug", s) for s in range(ns)]
        for s in range(ns):
            bk, bkk = A.pbanks.next()
            specs = []
            for cf in range(2):
                specs.append((bk[:, cf * 256:(cf + 1) * 256], A.fT[:, cf, s * 128:(s + 1) * 128], A.csblk[:, :], True, True))
            self.mm(specs, r=[("fT", 0), ("fT", 1), "csblk"], w=[bkk])
            self.cp("act", A.ABt[sl][:, s, :], bk[:, :], r=[bkk], w=[("ABt", sl, s)])
        P.dma("pool", self.AB_d[tok0:tok0 + T, :].rearrange("(s p) c -> p s c", p=128), A.ABt[sl][:, 0:ns, :],
              r=[("ABt", sl, s) for s in range(ns)], w=[("AB_d", tok0)])
        if "A_nob4" in self.debug:
            return
        self.tt("pool", A.sq[:, 0:ns, :], A.gg[:, 0:ns, :], A.gg[:, 0:ns, :], ALU.mult, r=ggk, w=["sq"])
        P.op("dve", lambda e: e.tensor_reduce(out=A.ssh[:, 0:ns * 4], in_=A.sq[:, 0:ns, :].rearrange("p s (h d) -> p (s h) d", h=4),
                                              axis=AX.X, op=ALU.add), r=["sq"], w=["ssh"])
        self.rstd(A.rsh[:, 0:ns * 4], A.ssh[:, 0:ns * 4], 64.0, ["ssh"], ["rsh"], "rsht")
        self.tt("dve", A.sq[:, 0:ns, :].rearrange("p s (h d) -> p (s h) d", h=4),
                A.gg[:, 0:ns, :].rearrange("p s (h d) -> p (s h) d", h=4),
                A.rsh[:, 0:ns * 4].unsqueeze(2).to_broadcast([128, ns * 4, 64]), ALU.mult, r=ggk + ["rsh"], w=["sq"])
        self.tt("dve", A.vc[:, 0:ns, :], A.sq[:, 0:ns, :], A.gsg[:, :].unsqueeze(1).to_broadcast([128, ns, 256]), ALU.mult,
                r=["sq", "gsg"], w=["vc"])
        for hp in range(2):
            bk, bkk = A.pbanks.next()
            specs = []
            for hh in range(2):
                h = hp * 2 + hh
                o = bk[:, hh * 256:hh * 256 + ns * 64].rearrange("p (s d) -> p s d", d=64)
                specs.append((o, A.wsT[:, h, :], A.vc[:, 0:ns, h * 64:(h + 1) * 64], True, True))
            self.mm(specs, r=["vc", "wsT"], w=[bkk])
            for hh in range(2):
                h = hp * 2 + hh
                self.stt(A.so[:, 0:ns, h * 64:(h + 1) * 64], bk[:, hh * 256:hh * 256 + ns * 64].rearrange("p (s d) -> p s d", d=64),
                         self.bsT[:, h:h + 1], A.ug[:, 0:ns, h * 64:(h + 1) * 64], ALU.add, ALU.mult,
                         r=[bkk, "bsT"] + ugk, w=[("so", h)])
        sok = [("so", h) for h in range(4)]
        self.tt("pool", A.sq[:, 0:ns, :], A.so[:, 0:ns, :], A.so[:, 0:ns, :], ALU.mult, r=sok, w=["sq"])
        P.op("dve", lambda e: e.tensor_reduce(out=A.ss2[:, 0:ns], in_=A.sq[:, 0:ns, :], axis=AX.X, op=ALU.add), r=["sq"], w=["ss2"])
        self.rstd(A.rs2[:, 0:ns], A.ss2[:, 0:ns], 256.0, ["ss2"], ["rs2"], "rs2t")
        self.tt("dve", A.son[:, 0:ns, :], A.so[:, 0:ns, :], A.rs2[:, 0:ns].unsqueeze(2).to_broadcast([128, ns, 256]), ALU.mult,
                r=sok + ["rs2"], w=["son"])
        tb, sk = A.trbanks.next()
        self.tr([(tb[:, cc * 512 + s * 128: cc * 512 + (s + 1) * 128], A.son[:, s, cc * 128:(cc + 1) * 128])
                 for cc in range(2) for s in range(ns)], r=["son", "ident"], w=[sk])
        for cc in range(2):
            self.act(A.sgt[sl][:, cc, 0:T], tb[:, cc * 512: cc * 512 + T], AF.Identity, r=[sk, "gmixT"], w=[("sgt", sl, cc)],
                     scale=self.gmixT[:, 4 + cc:5 + cc])
        P.dma("pool", self.sguT_d[:, :, tok0:tok0 + T].rearrange("c p t -> p c t"), A.sgt[sl][:, :, 0:T],
              r=[("sgt", sl, 0), ("sgt", sl, 1)], w=[("sguT_d", tok0)])

    def four_finish(self, ybank_ap, ykey, nk, tok_of, g):
        P = self.P
        F = self.F
        i = F.fi % 2
        F.fi += 1
        ysb = F.ysb[i]
        self.cp("act", ysb[:, 0:nk, :], ybank_ap, r=[ykey], w=[("ysb", i)])
        self.tt("pool", F.ysq[:, 0:nk, :], ysb[:, 0:nk, :], ysb[:, 0:nk, :], ALU.mult, r=[("ysb", i)], w=["ysq"])
        P.op("dve", lambda e: e.tensor_reduce(out=F.yss[:, 0:nk], in_=F.ysq[:, 0:nk, :], axis=AX.X, op=ALU.add), r=["ysq"], w=["yss"])
        self.rstd(F.yrs[:, 0:nk], F.yss[:, 0:nk], 256.0, ["yss"], ["yrs"], "yrst")
        self.tt("dve", F.ynb[i][:, 0:nk, :], ysb[:, 0:nk, :], F.yrs[:, 0:nk].unsqueeze(2).to_broadcast([128, nk, 256]), ALU.mult,
                r=[("ysb", i), "yrs"], w=[("ynb", i)])
        tb, sk = F.trbanks.next()
        self.tr([(tb[:, (j * 2 + cc) * 128:(j * 2 + cc + 1) * 128], F.ynb[i][:, j, cc * 128:(cc + 1) * 128])
                 for j in range(nk) for cc in range(2)], r=[("ynb", i), "ident"], w=[sk])
        for j in range(nk):
            for cc in range(2):
                self.act(tok_of(j, cc), tb[:, (j * 2 + cc) * 128:(j * 2 + cc + 1) * 128], AF.Identity, r=[sk, "gmixT"],
                         w=[("fourT", g, j, cc)], scale=self.gmixT[:, 6 + cc:7 + cc])

    def phaseF(self, l):
        P = self.P
        C = self.C
        P.push()
        F = type("F", (), {})()
        self.F = F
        F.fi = 0
        F.ab1 = P.sb("ab1", [128, 128, 128], BF16)
        F.p1 = P.sb("p1", [128, 64, 2, 128], BF16)
        F.mc = P.sb("mc", [128, 64, 128], BF16)
        F.ms = P.sb("ms", [128, 64, 128], BF16)
        F.t1tab = P.sb("t1tab", [128, 128], BF16)
        F.ysb = [P.sb("ysb%d" % i, [128, 4, 256], F32) for i in range(2)]
        F.ysq = P.sb("ysq", [128, 4, 256], F32)
        F.yss = P.sb("yss", [128, 4], F32)
        F.yrs = P.sb("yrs", [128, 4], F32)
        F.ynb = [P.sb("ynb%d" % i, [128, 4, 256], BF16) for i in range(2)]
        F.trbanks = Ring([(self.banks[i].bitcast(BF16), ("ps", i)) for i in range(2)])
        P.dma("sp", F.mc[:].rearrange("p a b -> p (a b)"), C["mc"][:, :], w=["mc"])
        P.dma("sp", F.ms[:].rearrange("p a b -> p (a b)"), C["ms"][:, :], w=["ms"])
        P.dma("sp", F.t1tab[:], C["t1tab"][:, :], w=["t1tab"])
        pb = Ring([(self.banks[i], ("ps", i)) for i in range(2, 5)])
        yb = Ring([(self.banks[i], ("ps", i)) for i in range(5, 8)])
        F.p1b = P.sb("p1b", [128, 64, 2, 128], BF16)
        p1s = [F.p1, F.p1b]
        for hh in range(2):
            for ab in range(2):
                src = self.AB_d[0:S, hh * 256 + ab * 128: hh * 256 + ab * 128 + 128].rearrange("(t1 t2) c -> t1 t2 c", t2=128)
                P.dma("sp", F.ab1[ab * 64:(ab + 1) * 64, :, :], src, w=[("ab1", ab)])
            for c4 in range(32):
                bk, bkk = pb.next()
                specs = []
                for j in range(4):
                    ch = c4 * 4 + j
                    specs.append((bk[:, j * 128:(j + 1) * 128], F.ab1[:, :, ch], F.t1tab[:, :], True, True))
                self.mm(specs, r=[("ab1", 0), ("ab1", 1), "t1tab"], w=[bkk])
                o = p1s[hh][:, :, :, c4 * 4:(c4 + 1) * 4].rearrange("p k r c -> p c r k")
                i_ = bk[:, :].rearrange("p (c r k) -> p c r k", c=4, r=2)
                if c4 % 2 == 0:
                    self.cp("act", o, i_, r=[bkk], w=[("p1", hh, c4)])
                else:
                    self.cp("dve", o, i_, r=[bkk], w=[("p1", hh, c4)])
        p1keys = [("p1", hh, c4) for hh in range(2) for c4 in range(32)]
        for kg in range(16):
            bka, kka = yb.next()
            for half2 in range(2):
                if half2 == 1:
                    bka, kka = yb.next()
                specs = []
                for j in range(2):
                    k1 = kg * 4 + half2 * 2 + j
                    for hh in range(2):
                        o = bka[:, j * 256 + hh * 128: j * 256 + hh * 128 + 128]
                        specs.append((o, F.mc[:, k1, :], p1s[hh][:, k1, 0, :], True, False))
                        specs.append((o, F.ms[:, k1, :], p1s[hh][:, k1, 1, :], False, True))
                self.mm(specs, r=p1keys + ["mc", "ms"], w=[kka])
                k1base = kg * 4 + half2 * 2

                def tok_of(j, cc, k1base=k1base):
                    return self.fourT[:, cc, 0:S].rearrange("p (k2 k1) -> p k1 k2", k1=64)[:, k1base + j, :]
                self.four_finish(bka[:, :].rearrange("p (j c) -> p j c", j=2), kka, 2, tok_of, ("L", kg, half2))
        if self.ctx_full:
            cab = P.sb("cab", [128, 2, 512], BF16)
            c256 = P.sb("c256", [128, 2, 256], BF16)
            s256 = P.sb("s256n", [128, 2, 256], BF16)
            P.dma("sp", cab[:], self.AB_d[S:S + 256, :].rearrange("(tb p) c -> p tb c", p=128), w=["cab"])
            P.dma("sp", c256[:].rearrange("p a b -> p (a b)"), C["c256"][:, :], w=["c256"])
            P.dma("sp", s256[:].rearrange("p a b -> p (a b)"), C["s256n"][:, :], w=["s256"])
            for kb in range(2):
                bk, bkk = yb.next()
                specs = []
                for hh in range(2):
                    o = bk[:, hh * 128:(hh + 1) * 128]
                    for tb in range(2):
                        specs.append((o, c256[:, tb, kb * 128:(kb + 1) * 128], cab[:, tb, hh * 256:hh * 256 + 128], tb == 0, False))
                        specs.append((o, s256[:, tb, kb * 128:(kb + 1) * 128], cab[:, tb, hh * 256 + 128:hh * 256 + 256], False, tb == 1))
                self.mm(specs, r=["cab", "c256", "s256"], w=[bkk])

                def tok_of(j, cc, kb=kb):
                    return self.fourT[:, cc, S + kb * 128:S + (kb + 1) * 128]
                self.four_finish(bk[:, 0:256].rearrange("p (j c) -> p j c", j=1), bkk, 1, tok_of, ("C", kb))
        if "scratch_out" in self.debug and l == 0:
            P.barrier()
            P.dma("pool", self.fourT_d[:, :, :], self.fourT[:], w=["fourT_d"])
        P.barrier()
        P.pop()

    def phaseC(self, l):
        P = self.P
        I = self.I
        C = self.C
        P.push()
        self.woutb = P.sb("woutb", [128, 8, D], BF16)
        self.new_staging()
        self.woutk = self.load_weight(self.woutb, lambda c: I["w_outp"][l, c * 128:(c + 1) * 128, :], 8, D, "woutb")
        K = type("K", (), {})()
        self.K = K
        self.kT_all = P.sb("kT_all", [128, NTOK], BF16)
        self.vt_all = P.sb("vt_all", [128, 66, 128], BF16)
        P.dma("sp", self.kT_all[:], self.kT_d[:, :], w=["kT_all"])
        P.dma("sp", self.vt_all[:], self.vt_d[:, :, :], w=["vt_all"])
        K.qt = [P.sb("qt%d" % i, [128, 4, 512], BF16) for i in range(2)]
        K.sgt = [P.sb("sgtC%d" % i, [128, 2, 512], BF16) for i in range(2)]
        K.xt = [P.sb("xtC%d" % i, [128, D], F32) for i in range(4)]
        K.PT = [P.sb("PT%d" % i, [128, 2, 512], BF16) for i in range(6)]
        K.oc = [P.sb("oc%d" % i, [128, 512], F32) for i in range(2)]
        K.dd = [P.sb("dd%d" % i, [128, 512], F32) for i in range(2)]
        K.ar = [P.sb("ar%d" % i, [128, 512], F32) for i in range(2)]
        K.sqT = P.sb("sqT", [128, 4, 512], BF16)
        K.aT = P.sb("aT", [128, 4, 512], BF16)
        K.ys2 = [P.sb("ysC%d" % i, [128, D], F32) for i in range(2)]
        K.yc2 = [P.sb("ycC%d" % i, [128, D], F32) for i in range(2)]
        K.tmp = P.sb("tmpC", [128, D], F32)
        K.junk = P.sb("junkC", [128, D], BF16)
        K.ssa = P.sb("ssa", [128, 4], F32)
        K.rsa = P.sb("rsa", [128, 4], F32)
        K.ssy = P.sb("ssy", [128, 4], F32)
        K.rsy = P.sb("rsy", [128, 4], F32)
        K.G1 = P.sb("G1bc", [128, 2, D], F32)
        K.mnext = P.sb("mnext", [128, 128], BF16)
        K.mprev = P.sb("mprev", [128, 128], BF16)
        P.dma("sp", K.mnext[:], C["mask_next"][:, :], w=["mnext"])
        P.dma("sp", K.mprev[:], C["mask_prev"][:, :], w=["mprev"])
        for r_ in range(2):
            P.dma("sp", K.G1[:, r_, :], self.G_d[r_:r_ + 1, 0, :].partition_broadcast(128), w=[("G1", r_)])
        K.spairs = Ring([(self.psum[:, 2 * i:2 * i + 2, :], [("ps", 2 * i), ("ps", 2 * i + 1)]) for i in range(3)])
        K.tailpairs = Ring([(self.psum[:, 2 * i:2 * i + 2, :], [("ps", 2 * i), ("ps", 2 * i + 1)]) for i in range(4)])
        K.pti = 0
        K.xi = 0
        tiles = [(i * 512, 4, False) for i in range(S // 512)]
        if self.ctx_full:
            tiles = tiles + [(S, 2, True)]
        if "C_tiles" in self.debug:
            tiles = tiles[:2] + tiles[-1:]
        for it, t in enumerate(tiles):
            self.C_tile(l, it, *t)
        P.barrier()
        P.pop()

    def C_tile(self, l, it, tok0, nsub, is_ctx):
        P = self.P
        K = self.K
        T = nsub * 128
        sl = it % 2
        mi = 1 if is_ctx else 0
        qt = K.qt[sl]
        P.dma("sp", qt[:, :, 0:T], self.qT_d[:, :, tok0:tok0 + T].rearrange("c p t -> p c t"), w=[("qt", sl)])
        P.dma("sp", K.sgt[sl][:, :, 0:T], self.sguT_d[:, :, tok0:tok0 + T].rearrange("c p t -> p c t"), w=[("sgtC", sl)])
        xis = []
        for s in range(nsub):
            xi = K.xi % 4
            K.xi += 1
            xis.append(xi)
            P.dma("sp", K.xt[xi][:], self.src_rows(l, tok0 + s * 128, 128), w=[("xtC", xi)])
        items = []
        if not is_ctx:
            n0 = tok0 // 128
            for j in range(n0 - 1, n0 + nsub + 1):
                if j < 0 or j >= S // 128:
                    continue
                qlo = max(j - 1, n0) - n0
                qhi = min(j + 1, n0 + nsub - 1) - n0
                mnext = (j - 1 - n0) if (j - 1 >= n0) else None
                mprev = (j + 1 - n0) if (j + 1 <= n0 + nsub - 1) else None
                items.append((j, qlo * 128, (qhi + 1) * 128, mnext, mprev))
        items.append((64, 0, T, None, None))
        items.append((65, 0, T, None, None))
        LOOK = 2
        bo, ko = self.banks[6], ("ps", 6)
        bd, kd = self.banks[7], ("ps", 7)
        deferred = []

        def fin2():
            while deferred:
                c_ = deferred.pop(0)
                d_ = c_ % 2
                self.act(K.sqT[:, c_, 0:T], K.ar[d_][:, 0:T], AF.Square, r=[("ar", d_)], w=[("sqT", c_)])
                self.ts("pool", K.aT[:, c_, 0:T], K.ar[d_][:, 0:T], self.gmixT[:, c_:c_ + 1], 1.0, ALU.mult, ALU.mult,
                        r=[("ar", d_), "gmixT"], w=[("aT", c_)])

        for c in range(4):
            pend = []
            first = True
            for idx in range(len(items) + LOOK):
                if idx == 3:
                    fin2()
                if idx < len(items):
                    j, lo, hi, mn, mp = items[idx]
                    n = hi - lo
                    pair, pkeys = K.spairs.next()
                    kcols = slice(j * 128, (j + 1) * 128)
                    self.mm([(pair[:, 0, 0:n], self.kT_all[0:64, kcols], qt[0:64, c, lo:hi], True, True),
                             (pair[:, 1, 0:n], self.kT_all[64:128, kcols], qt[64:128, c, lo:hi], True, True)],
                            r=[("qt", sl), "kT_all"], w=pkeys)
                    pi = K.pti % len(K.PT)
                    K.pti += 1
                    pt = K.PT[pi]
                    pk = [("PT", pi)]
                    self.act(pt[:, :, 0:n], pair[:, :, 0:n], AF.Exp, r=pkeys, w=pk, scale=0.125)
                    if mn is not None:
                        o = mn * 128 - lo
                        self.tt("dve", pt[:, :, o:o + 128], pt[:, :, o:o + 128],
                                K.mnext[:, :].unsqueeze(1).to_broadcast([128, 2, 128]), ALU.mult, r=pk + ["mnext"], w=pk)
                    if mp is not None:
                        o = mp * 128 - lo
                        self.tt("dve", pt[:, :, o:o + 128], pt[:, :, o:o + 128],
                                K.mprev[:, :].unsqueeze(1).to_broadcast([128, 2, 128]), ALU.mult, r=pk + ["mprev"], w=pk)
                    pend.append((j, lo, hi, pt, pk))
                if idx >= LOOK:
                    j, lo, hi, pt, pk = pend.pop(0)
                    n = hi - lo
                    last = (idx == len(items) + LOOK - 1)
                    self.mm([(bo[0:64, lo:hi], self.vt_all[:, j, 0:64], pt[:, 0, 0:n], first, last),
                             (bo[64:128, lo:hi], self.vt_all[:, j, 64:128], pt[:, 1, 0:n], first, last),
                             (bd[0:64, lo:hi], self.ones[:, 0:64], pt[:, 0, 0:n], first, last),
                             (bd[64:128, lo:hi], self.ones[:, 0:64], pt[:, 1, 0:n], first, last)],
                            r=pk + ["ones", "vt_all"], w=[ko, kd])
                    first = False
            d2 = c % 2
            self.cp("act", K.oc[d2][:, 0:T], bo[:, 0:T], r=[ko], w=[("oc", d2)])
            self.ts("dve", K.dd[d2][:, 0:T], bd[:, 0:T], self.esink[:, c:c + 1], None, ALU.add, None, r=[kd, "esink"], w=[("dd", d2)])
            self.recip(K.dd[d2][:, 0:T], K.dd[d2][:, 0:T], r=[("dd", d2)], w=[("dd", d2)])
            self.tt("pool", K.ar[d2][:, 0:T], K.oc[d2][:, 0:T], K.dd[d2][:, 0:T], ALU.mult, r=[("oc", d2), ("dd", d2)], w=[("ar", d2)])
            deferred.append(c)
        fin2()
        pair, pkeys = K.spairs.next()
        bs_, ks_ = pair[:, 0, :], pkeys[0]
        specs = []
        for s in range(nsub):
            for c in range(4):
                specs.append((bs_[:, s:s + 1], K.sqT[:, c, s * 128:(s + 1) * 128], self.ones[:, 0:1], c == 0, c == 3))
        self.mm(specs, r=[("sqT", c) for c in range(4)] + ["ones"], w=[ks_])
        self.rstd(K.rsa[:, 0:nsub], bs_[:, 0:nsub], 512.0, [ks_], ["rsa"], "rsat")
        aTk = [("aT", c) for c in range(4)]
        def st1(s):
            xi = xis[s]
            cols = slice(s * 128, (s + 1) * 128)
            b2 = s % 2
            ys, yc = K.ys2[b2], K.yc2[b2]
            pa, pak = K.tailpairs.next()
            pb_, pbk = K.tailpairs.next()
            for half in range(2):
                self.mm([(pa[:, half, :], K.aT[:, c, cols], self.woutb[:, c, half * 512:(half + 1) * 512], c == 0, c == 3) for c in range(4)],
                        r=aTk + self.woutk, w=[pak[half]])
            for half in range(2):
                specs = []
                for cc in range(2):
                    specs.append((pb_[:, half, :], K.sgt[sl][:, cc, cols], self.woutb[:, 4 + cc, half * 512:(half + 1) * 512], cc == 0, False))
                for cc in range(2):
                    specs.append((pb_[:, half, :], self.fourT[:, cc, tok0 + s * 128: tok0 + (s + 1) * 128],
                                  self.woutb[:, 6 + cc, half * 512:(half + 1) * 512], False, cc == 1))
                self.mm(specs, r=[("sgtC", sl)] + self.woutk, w=[pbk[half]])
            self.cp("act", ys[:].rearrange("p (h n) -> p h n", h=2), pb_[:, :, :], r=pbk, w=[("ysC", b2)])
            self.stt(yc[:].rearrange("p (h n) -> p h n", h=2), pa[:, :, :], K.rsa[:, s:s + 1], ys[:].rearrange("p (h n) -> p h n", h=2),
                     ALU.mult, ALU.add, r=pak + ["rsa", ("ysC", b2)], w=[("ycC", b2)])

        def st2(s):
            xi = xis[s]
            b2 = s % 2
            yc = K.yc2[b2]
            yck = [("ycC", b2)]
            self.act(K.junk[:], yc[:], AF.Square, r=yck, w=["junkC", ("ssy", s)], accum_out=K.ssy[:, s:s + 1])
            self.rstd(K.rsy[:, s:s + 1], K.ssy[:, s:s + 1], float(D), [("ssy", s)], [("rsy", s)], ("rsyt", s))
            self.tt("pool", K.tmp[:], yc[:], K.G1[:, mi, :], ALU.mult, r=yck + [("G1", mi)], w=["tmpC"])
            self.stt(K.xt[xi][:], K.tmp[:], K.rsy[:, s:s + 1], K.xt[xi][:], ALU.mult, ALU.add,
                     r=["tmpC", ("rsy", s), ("xtC", xi)], w=[("xtC", xi)])
            P.dma("pool", self.xmid_d[tok0 + s * 128: tok0 + (s + 1) * 128, :], K.xt[xi][:], r=[("xtC", xi)], w=[("xmid", tok0, s)])

        st1(0)
        for s in range(1, nsub):
            st1(s)
            st2(s - 1)
        st2(nsub - 1)

    def phaseD(self, l):
        P = self.P
        I = self.I
        P.push()
        self.w1b = P.sb("w1b", [128, 8, DFF], BF16)
        self.w2b = P.sb("w2b", [128, 32, D], BF16)
        self.new_staging()
        self.w1k = self.load_weight(self.w1b, lambda c: I["w_ff1"][l, c * 128:(c + 1) * 128, :], 8, DFF, "w1b")
        self.w2k = self.load_weight(self.w2b, lambda c: I["w_ff2"][l, c * 128:(c + 1) * 128, :], 32, D, "w2b")
        Dd = type("Dd", (), {})()
        self.Dd = Dd
        Dd.xm = [P.sb("xm%d" % i, [128, D], F32) for i in range(4)]
        Dd.xn = [P.sb("xnD%d" % i, [128, D], BF16) for i in range(2)]
        Dd.hx = [P.sb("hx2T%d" % i, [128, 8, 256], BF16) for i in range(2)]
        Dd.hT = P.sb("hT", [128, 32, 256], BF16)
        Dd.rl = [P.sb("rl%d" % i, [128, 512], F32) for i in range(3)]
        Dd.tmp = P.sb("tmpD", [128, D], F32)
        Dd.junk = P.sb("junkD", [128, D], BF16)
        Dd.ss = P.sb("ssD", [128, 4], F32)
        Dd.rs = P.sb("rsD", [128, 4], F32)
        Dd.ssy = P.sb("ssyD", [128, 4], F32)
        Dd.rsy = P.sb("rsyD", [128, 2], F32)
        Dd.G2 = P.sb("G2bc", [128, 2, D], F32)
        for r_ in range(2):
            P.dma("sp", Dd.G2[:, r_, :], self.G_d[r_:r_ + 1, 1, :].partition_broadcast(128), w=[("G2", r_)])
        Dd.trbanks = Ring([(self.banks[i].bitcast(BF16), ("ps", i)) for i in range(2)])
        Dd.fbanks = Ring([(self.banks[i], ("ps", i)) for i in range(2, 5)])
        Dd.yring = Ring([(self.banks[i], ("ps", i)) for i in range(5, 8)])
        Dd.xi = 0
        Dd.rli = 0
        tiles = [(i * 256, False) for i in range(S // 256)]
        if self.ctx_full:
            tiles = tiles + [(S, True)]
        if "D_tiles" in self.debug:
            tiles = tiles[:2] + tiles[-1:]
        self.D_front(l, 0, *tiles[0])
        for it, t in enumerate(tiles):
            if it + 1 < len(tiles):
                self.D_front(l, it + 1, *tiles[it + 1])
            self.D_back(l, it, *t)
        P.barrier()
        P.pop()

    def D_front(self, l, it, tok0, is_ctx):
        P = self.P
        Dd = self.Dd
        mi = 1 if is_ctx else 0
        hx = Dd.hx[it % 2]
        xis = []
        for s in range(2):
            xi = Dd.xi % 4
            Dd.xi += 1
            xis.append(xi)
            P.dma("sp", Dd.xm[xi][:], self.xmid_d[tok0 + s * 128: tok0 + (s + 1) * 128, :], w=[("xm", xi)])
            self.act(Dd.junk[:], Dd.xm[xi][:], AF.Square, r=[("xm", xi)], w=["junkD", ("ssD", s)], accum_out=Dd.ss[:, s:s + 1])
            self.rstd(Dd.rs[:, s:s + 1], Dd.ss[:, s:s + 1], float(D), [("ssD", s)], [("rsD", s)], ("rsDt", s))
            self.ts("dve", Dd.xn[s][:], Dd.xm[xi][:], Dd.rs[:, s:s + 1], None, ALU.mult, None, r=[("xm", xi), ("rsD", s)], w=[("xnD", s)])
        Dd.cur_xis = getattr(Dd, "cur_xis", {})
        Dd.cur_xis[it] = xis
        for cq in range(2):
            tb, sk = Dd.trbanks.next()
            self.tr([(tb[:, j * 256 + s * 128: j * 256 + (s + 1) * 128], Dd.xn[s][:, (4 * cq + j) * 128:(4 * cq + j + 1) * 128])
                     for j in range(4) for s in range(2)], r=[("xnD", 0), ("xnD", 1), "ident"], w=[sk])
            for j in range(4):
                c = 4 * cq + j
                slot = tb[:, j * 256:(j + 1) * 256]
                if cq == 0:
                    self.act(hx[:, c, :], slot, AF.Identity, r=[sk, "modT"], w=[("hx2T", it % 2, c)],
                             scale=self.modT[:, 2, c, mi:mi + 1], bias=self.modT[:, 3, c, mi:mi + 1])
                else:
                    self.ts("dve", hx[:, c, :], slot, self.modT[:, 2, c, mi:mi + 1], self.modT[:, 3, c, mi:mi + 1], ALU.mult, ALU.add,
                            r=[sk, "modT"], w=[("hx2T", it % 2, c)])

    def D_back(self, l, it, tok0, is_ctx):
        P = self.P
        Dd = self.Dd
        mi = 1 if is_ctx else 0
        hx = Dd.hx[it % 2]
        hk = [("hx2T", it % 2, c) for c in range(8)]
        xis = Dd.cur_xis[it]
        for fp in range(16):
            fb, sk = Dd.fbanks.next()
            specs = []
            for j in range(2):
                fc = 2 * fp + j
                for dc in range(8):
                    specs.append((fb[:, j * 256:(j + 1) * 256], self.w1b[:, dc, fc * 128:(fc + 1) * 128], hx[:, dc, :], dc == 0, dc == 7))
            self.mm(specs, r=hk + self.w1k, w=[sk])
            ri = Dd.rli % 3
            Dd.rli += 1
            self.act(Dd.rl[ri][:], fb[:, :], AF.Relu, r=[sk], w=[("rl", ri)])
            self.tt("pool" if fp % 2 == 0 else "dve", Dd.hT[:, 2 * fp:2 * fp + 2, :].rearrange("p a t -> p (a t)"), Dd.rl[ri][:], Dd.rl[ri][:],
                    ALU.mult, r=[("rl", ri)], w=[("hT", 2 * fp), ("hT", 2 * fp + 1)])
        hTk = [("hT", fc) for fc in range(32)]
        for s in range(2):
            xi = xis[s]
            ybs = []
            for half in range(2):
                b_, k_ = Dd.yring.next()
                self.mm([(b_[:, :], Dd.hT[:, fc, s * 128:(s + 1) * 128], self.w2b[:, fc, half * 512:(half + 1) * 512], fc == 0, fc == 31)
                         for fc in range(32)], r=hTk + self.w2k, w=[k_])
                ybs.append((b_, k_))
            for half in range(2):
                self.act(Dd.junk[:, half * 512:(half + 1) * 512], ybs[half][0][:, :], AF.Square, r=[ybs[half][1]],
                         w=[("junkD", half), ("ssyD", s, half)], accum_out=Dd.ssy[:, s * 2 + half:s * 2 + half + 1])
            self.tt("dve", Dd.ssy[:, s * 2:s * 2 + 1], Dd.ssy[:, s * 2:s * 2 + 1], Dd.ssy[:, s * 2 + 1:s * 2 + 2], ALU.add,
                    r=[("ssyD", s, 0), ("ssyD", s, 1)], w=[("ssyDs", s)])
            self.rstd(Dd.rsy[:, s:s + 1], Dd.ssy[:, s * 2:s * 2 + 1], float(D), [("ssyDs", s)], [("rsyD", s)], ("rsyDt", s))
            for half in range(2):
                hc = slice(half * 512, (half + 1) * 512)
                self.tt("dve", Dd.tmp[:, hc], ybs[half][0][:, :], Dd.G2[:, mi, hc], ALU.mult, r=[ybs[half][1], ("G2", mi)], w=[("tmpD", half)])
            self.stt(Dd.xm[xi][:], Dd.tmp[:], Dd.rsy[:, s:s + 1], Dd.xm[xi][:], ALU.mult, ALU.add,
                     r=[("tmpD", 0), ("tmpD", 1), ("rsyD", s), ("xm", xi)], w=[("xm", xi)])
            if is_ctx and l == self.nlayers - 1:
                continue
            P.dma("pool", self.dst_rows(l, tok0 + s * 128, 128), Dd.xm[xi][:], r=[("xm", xi)], w=[("xout", tok0, s)])


def _prep_shared(inp):
    sh = {}
    f = lambda a: np.ascontiguousarray(a, dtype=np.float32)
    sh["w_ada"] = f(inp["w_ada"])
    sh["b_ada"] = f(inp["b_ada"])
    for nm in ("g_pre_mix", "g_post_mix", "g_pre_ff", "g_post_ff"):
        sh[nm] = f(inp[nm])
    sh["w_inp"] = f(inp["w_in"][:, :, _perm_cols()])
    sink = inp["sink"]
    sinkT = np.zeros((2, 128, 4), np.float32)
    for c in range(4):
        sinkT[:, 0:64, c] = sink[:, c][:, None]
        sinkT[:, 64:128, c] = sink[:, 4 + c][:, None]
    sh["sinkT"] = sinkT
    sh["w_sguT"] = f(np.transpose(inp["w_sgu"], (0, 3, 1, 2)).reshape(2, 128, 512))
    sh["b_sguT"] = f(np.transpose(inp["b_sgu"], (0, 2, 1)))
    sh["g_sgu"] = f(inp["g_sgu"].reshape(2, 256))
    rp = _attn_row_perm()
    gm = inp["g_mix"][:, rp]
    sh["g_mixT"] = f(np.transpose(gm.reshape(2, 8, 128), (0, 2, 1)))
    sh["w_outp"] = f(inp["w_out"][:, rp, :])
    sh["w_ff1"] = f(inp["w_ff1"])
    sh["w_ff2"] = f(inp["w_ff2"])
    return sh


def _core_inputs(inp, sh, consts, b):
    m = dict(sh)
    m["x"] = np.ascontiguousarray(inp["x"][b], dtype=np.float32)
    m["ctx"] = np.ascontiguousarray(inp["ctx"][b], dtype=np.float32)
    cs = np.stack([inp["c"][b], inp["c_ctx"]], axis=0).astype(np.float32)
    m["csT"] = np.ascontiguousarray(np.transpose(cs.reshape(2, 8, 128), (2, 1, 0)).reshape(128, 16))
    for k, v in consts.items():
        m["c_" + k] = v
    return m


_CACHE = {}


def kernel(**inputs):
    inp = {k: np.asarray(v) for k, v in inputs.items()}
    if "nc" not in _CACHE:
        _CACHE["nc"] = Builder().build()
        _CACHE["consts"] = _constants()
    nc = _CACHE["nc"]
    consts = _CACHE["consts"]
    sh = _prep_shared(inp)
    in_maps = [_core_inputs(inp, sh, consts, b) for b in range(8)]
    res = run_bass_kernel_spmd(nc, in_maps, core_ids=list(range(8)))
    out = np.stack([np.asarray(res.results[b]["out"], dtype=np.float32) for b in range(8)], axis=0)
    return out
```

```python
import contextlib
import numpy as np
import ml_dtypes
import concourse.bass as bass
import concourse.mybir as mybir
from concourse.bass_utils import run_bass_kernel_spmd

F32 = mybir.dt.float32
BF16 = mybir.dt.bfloat16
AF = mybir.ActivationFunctionType
ALU = mybir.AluOpType
AX = mybir.AxisListType
NPBF = ml_dtypes.bfloat16

COMPUTE = ("pe", "act", "dve", "pool")
NDMA_SEMS = 20

D = 1024
S = 8192
LCTX = 256
NTOK = S + LCTX
DFF = 4096
NCOL = 2176
EPS = 1e-6


class _Op:
    __slots__ = ("eng", "fn", "deps", "signal", "sigval", "is_dma", "sem_idx", "dma_val", "prev_same_sem")

    def __init__(self, eng, fn, is_dma):
        self.eng = eng
        self.fn = fn
        self.deps = []
        self.signal = False
        self.sigval = 0
        self.is_dma = is_dma
        self.sem_idx = None
        self.dma_val = 0
        self.prev_same_sem = None


class Prog:
    def __init__(self, nc):
        self.nc = nc
        self.ops = {e: [] for e in ("pe", "act", "dve", "pool", "sp")}
        self.last_w = {}
        self.readers = {}
        self.dma_count = {"sp": 0, "pool": 0, "act": 0}
        self.dma_last_on_sem = {}
        self.stacks = [contextlib.ExitStack()]
        self.last_op = {}

    def push(self):
        self.stacks.append(contextlib.ExitStack())

    def pop(self):
        self.stacks.pop().close()

    def sb(self, name, shape, dt):
        self.uid = getattr(self, "uid", 0) + 1
        return self.stacks[-1].enter_context(self.nc.sbuf_tensor("%s_u%d" % (name, self.uid), list(shape), dt))

    def ps(self, name, shape, dt):
        return self.stacks[-1].enter_context(self.nc.psum_tensor(name, list(shape), dt))

    def _track(self, op, r, w):
        ps_keys = [("ps", k[1]) for k in list(r) + list(w) if isinstance(k, tuple) and k[0] == "ps"]
        r = [k for k in r if not (isinstance(k, tuple) and k[0] == "ps")]
        w = [k for k in w if not (isinstance(k, tuple) and k[0] == "ps")] + list(dict.fromkeys(ps_keys))
        deps = set()
        for k in r:
            lw = self.last_w.get(k)
            if lw is not None:
                deps.add(lw)
        for k in w:
            lw = self.last_w.get(k)
            if lw is not None and (op.is_dma or lw.is_dma or lw.eng != op.eng):
                deps.add(lw)
            for rd in self.readers.get(k, ()):
                if rd is op:
                    continue
                if op.is_dma or rd.is_dma or rd.eng != op.eng:
                    deps.add(rd)
        for d in deps:
            if d is op:
                continue
            d.signal = True
            op.deps.append(d)
        for k in r:
            self.readers.setdefault(k, []).append(op)
        for k in w:
            self.last_w[k] = op
            self.readers[k] = []

    def op(self, eng, fn, r=(), w=()):
        o = _Op(eng, fn, False)
        self._track(o, r, w)
        self.ops[eng].append(o)
        self.last_op[eng] = o
        return o

    def wait_all(self, eng, r=()):
        o = _Op(eng, None, False)
        self._track(o, r, ())
        self.ops[eng].append(o)
        return o

    def dma(self, q, out, in_, r=(), w=(), **kw):
        o = _Op(q, (out, in_, kw), True)
        n = self.dma_count[q]
        self.dma_count[q] = n + 1
        nsem = NDMA_SEMS if q != "pool" else 6
        o.sem_idx = (q, n % nsem)
        prev = self.dma_last_on_sem.get(o.sem_idx)
        o.prev_same_sem = prev
        o.dma_val = (prev.dma_val if prev is not None else 0) + 16
        self.dma_last_on_sem[o.sem_idx] = o
        self._track(o, r, w)
        self.ops[q].append(o)
        return o

    def barrier(self):
        srcs = [o for o in self.last_op.values()] + list(self.dma_last_on_sem.values())
        for e in ("pe", "act", "dve", "pool", "sp"):
            o = _Op(e, None, False)
            for s_ in srcs:
                if (not s_.is_dma) and s_.eng == e:
                    continue
                s_.signal = True
                o.deps.append(s_)
            self.ops[e].append(o)
        self.last_w = {}
        self.readers = {}

    def emit(self):
        nc = self.nc
        st = self.stacks[0]
        esem = {e: st.enter_context(nc.semaphore("sem_" + e)) for e in COMPUTE}
        dsem = {}
        for q in ("sp", "pool", "act"):
            for i in range(min(NDMA_SEMS, self.dma_count[q])):
                dsem[(q, i)] = st.enter_context(nc.semaphore("dsem_%s_%d" % (q, i)))
        for e in COMPUTE:
            c = 0
            for o in self.ops[e]:
                if (not o.is_dma) and o.signal:
                    c += 1
                    o.sigval = c

        def src_of(d):
            if d.is_dma:
                return dsem[d.sem_idx], d.dma_val, ("d",) + d.sem_idx
            return esem[d.eng], d.sigval, ("e", d.eng)

        def run(engname, eng):
            seen = {}
            for o in self.ops[engname]:
                waits = {}
                deps = list(o.deps)
                if o.is_dma and o.prev_same_sem is not None:
                    deps.append(o.prev_same_sem)
                for d in deps:
                    sem, val, key = src_of(d)
                    if seen.get(key, 0) >= val:
                        continue
                    if key not in waits or waits[key][1] < val:
                        waits[key] = (sem, val)
                for key, (sem, val) in waits.items():
                    eng.wait_ge(sem, val)
                    seen[key] = val
                if o.is_dma:
                    out, in_, kw = o.fn
                    eng.dma_start(out=out, in_=in_, **kw).then_inc(dsem[o.sem_idx], 16)
                elif o.fn is None:
                    pass
                else:
                    ins = o.fn(eng)
                    if o.signal:
                        ins.then_inc(esem[engname], 1)

        with nc.Block() as block:
            @block.tensor
            def _(e):
                run("pe", e)

            @block.scalar
            def _(e):
                run("act", e)

            @block.vector
            def _(e):
                run("dve", e)

            @block.gpsimd
            def _(e):
                run("pool", e)

            @block.sync
            def _(e):
                run("sp", e)

    def close(self):
        while self.stacks:
            self.stacks.pop().close()


class Ring:
    def __init__(self, items):
        self.items = items
        self.i = 0

    def next(self):
        it = self.items[self.i % len(self.items)]
        self.i += 1
        return it


def _perm_cols():
    def rotp(d):
        if d < 16:
            return d + 16
        if d < 32:
            return d - 16
        if d < 48:
            return d + 16
        return d - 16
    cols = []
    for c in range(4):
        for h in (c, 4 + c):
            cols += [h * 64 + d for d in range(64)]
    for c in range(4):
        for h in (c, 4 + c):
            cols += [h * 64 + rotp(d) for d in range(64)]
    cols += [512 + j for j in range(128)]
    cols += [512 + h * 64 + rotp(d) for h in range(2) for d in range(64)]
    cols += [1280 + j for j in range(256)]
    cols += [640 + j for j in range(128)]
    cols += [768 + j for j in range(256)]
    cols += [1024 + j for j in range(256)]
    assert len(cols) == NCOL
    return np.array(cols)


def _attn_row_perm():
    rows = []
    for c in range(4):
        for h in (c, 4 + c):
            rows += [h * 64 + d for d in range(64)]
    return np.array(rows + list(range(512, 1024)))


def _constants():
    k = {}
    k["ident"] = np.eye(128, dtype=np.float32).astype(NPBF)
    n_freq = 16
    inv_freq = (np.float32(10000.0) ** (-np.arange(n_freq, dtype=np.float32) / np.float32(n_freq))).astype(np.float32)
    t = np.arange(S)
    row = (t // 64).astype(np.float32)
    col = (t % 64).astype(np.float32)
    ang = np.zeros((64, S), np.float32)
    ang[0:16] = row[None, :] * inv_freq[:, None]
    ang[16:32] = ang[0:16]
    ang[32:48] = col[None, :] * inv_freq[:, None]
    ang[48:64] = ang[32:48]
    cos = np.cos(ang).astype(np.float32)
    sin = np.sin(ang).astype(np.float32)
    sin[0:16] *= -1
    sin[32:48] *= -1
    cosT = np.ones((128, NTOK), np.float32)
    sinT = np.zeros((128, NTOK), np.float32)
    cosT[0:64, :S] = cos
    cosT[64:128, :S] = cos
    sinT[0:64, :S] = sin
    sinT[64:128, :S] = sin
    k["cosT"] = cosT
    k["sinT"] = sinT
    kk = np.arange(128)[:, None]
    qq = np.arange(128)[None, :]
    k["mask_next"] = (kk <= qq).astype(np.float32).astype(NPBF)
    k["mask_prev"] = (kk >= qq).astype(np.float32).astype(NPBF)
    k["ones"] = np.ones((128, 128), np.float32).astype(NPBF)
    cidx = np.arange(64)
    c64 = np.cos(2 * np.pi * np.outer(cidx, cidx) / 64.0) / 8.0
    s64 = np.sin(2 * np.pi * np.outer(cidx, cidx) / 64.0) / 8.0
    csblk = np.zeros((128, 256), np.float64)
    for g in range(2):
        csblk[g * 64:(g + 1) * 64, g * 64:(g + 1) * 64] = c64
        csblk[g * 64:(g + 1) * 64, 128 + g * 64:128 + (g + 1) * 64] = s64
    k["csblk"] = csblk.astype(np.float32).astype(NPBF)
    t1 = np.zeros((128, 128), np.float64)
    t1[0:64, 0:64] = c64
    t1[0:64, 64:128] = -s64
    t1[64:128, 0:64] = -s64
    t1[64:128, 64:128] = -c64
    k["t1tab"] = t1.astype(np.float32).astype(NPBF)
    t2 = np.arange(128)[:, None, None].astype(np.float64)
    k1 = np.arange(64)[None, :, None].astype(np.float64)
    k2 = np.arange(128)[None, None, :].astype(np.float64)
    phi = 2 * np.pi * ((t2 * (k1 + 64 * k2)) % 8192) / 8192.0
    k["mc"] = (np.cos(phi) / np.sqrt(128.0)).astype(np.float32).astype(NPBF).reshape(128, 64 * 128)
    k["ms"] = (np.sin(phi) / np.sqrt(128.0)).astype(np.float32).astype(NPBF).reshape(128, 64 * 128)
    tt = (np.arange(2)[None, :, None] * 128 + np.arange(128)[:, None, None]).astype(np.float64)
    kk2 = np.arange(256)[None, None, :].astype(np.float64)
    ph = 2 * np.pi * ((tt * kk2) % 256) / 256.0
    k["c256"] = (np.cos(ph) / 16.0).astype(np.float32).astype(NPBF).reshape(128, 512)
    k["s256n"] = (-np.sin(ph) / 16.0).astype(np.float32).astype(NPBF).reshape(128, 512)
    sel = np.zeros((2, 2, 128), np.float32)
    sel[0, 0, :] = 1
    sel[1, 1, :] = 1
    k["sel"] = sel.reshape(2, 256)
    k["i2"] = np.eye(2, dtype=np.float32)
    return k


CONST_SPECS = [
    ("ident", [128, 128], BF16), ("cosT", [128, NTOK], F32), ("sinT", [128, NTOK], F32),
    ("mask_next", [128, 128], BF16), ("mask_prev", [128, 128], BF16), ("ones", [128, 128], BF16),
    ("csblk", [128, 256], BF16), ("t1tab", [128, 128], BF16), ("mc", [128, 8192], BF16), ("ms", [128, 8192], BF16),
    ("c256", [128, 512], BF16), ("s256n", [128, 512], BF16), ("sel", [2, 256], F32), ("i2", [2, 2], F32),
]

IN_SPECS = [
    ("x", [S, D]), ("ctx", [LCTX, D]), ("csT", [128, 16]),
    ("w_ada", [2, D, 6 * D]), ("b_ada", [2, 6 * D]),
    ("g_pre_mix", [2, D]), ("g_post_mix", [2, D]), ("g_pre_ff", [2, D]), ("g_post_ff", [2, D]),
    ("w_inp", [2, D, NCOL]), ("sinkT", [2, 128, 4]), ("w_sguT", [2, 128, 512]), ("b_sguT", [2, 128, 4]),
    ("g_sgu", [2, 256]), ("g_mixT", [2, 128, 8]), ("w_outp", [2, D, D]),
    ("w_ff1", [2, D, DFF]), ("w_ff2", [2, DFF, D]),
]


class Builder:
    def __init__(self, nlayers=2, stop_after=None, debug=()):
        self.nlayers = nlayers
        self.stop_after = stop_after
        self.debug = set(debug)
        nc = bass.Bass("TRN2", target_bir_lowering=False)
        self.nc = nc
        self.I = {}
        for name, shape in IN_SPECS:
            self.I[name] = nc.dram_tensor(name, shape, F32, kind="ExternalInput").ap()
        self.C = {}
        for name, shape, dt in CONST_SPECS:
            self.C[name] = nc.dram_tensor("c_" + name, shape, dt, kind="ExternalInput").ap()
        self.out = nc.dram_tensor("out", [S, D], F32, kind="ExternalOutput").ap()
        kw = dict(kind="ExternalOutput") if "scratch_out" in self.debug else {}
        self.qT_d = nc.dram_tensor("qT_d", [4, 128, NTOK], BF16, **kw).ap()
        self.sguT_d = nc.dram_tensor("sguT_d", [2, 128, NTOK], BF16, **kw).ap()
        self.AB_d = nc.dram_tensor("AB_d", [NTOK, 512], BF16, **kw).ap()
        self.kT_d = nc.dram_tensor("kT_d", [128, NTOK], BF16, **kw).ap()
        self.vt_d = nc.dram_tensor("vt_d", [128, 66, 128], BF16, **kw).ap()
        self.fourT_d = nc.dram_tensor("fourT_d", [128, 2, NTOK], BF16, **kw).ap()
        self.xmid_d = nc.dram_tensor("xmid_d", [NTOK, D], F32, **kw).ap()
        self.x1_d = nc.dram_tensor("x1_d", [NTOK, D], F32, **kw).ap()
        self.G_d = nc.dram_tensor("G_d", [2, 4, D], F32, **kw).ap()
        self.modT_d = nc.dram_tensor("modT_d", [128, 64], F32, **kw).ap()
        self.dbg = {}
        self.P = Prog(nc)

    def act(self, out, in_, func, r, w, **kw):
        return self.P.op("act", lambda e: e.activation(out=out, in_=in_, func=func, **kw), r, w)

    def tt(self, eng, out, in0, in1, op, r, w):
        return self.P.op(eng, lambda e: e.tensor_tensor(out=out, in0=in0, in1=in1, op=op), r, w)

    def ts(self, eng, out, in0, s1, s2, op0, op1, r, w):
        if s2 is None:
            return self.P.op(eng, lambda e: e.tensor_scalar(out=out, in0=in0, scalar1=s1, scalar2=None, op0=op0), r, w)
        return self.P.op(eng, lambda e: e.tensor_scalar(out=out, in0=in0, scalar1=s1, scalar2=s2, op0=op0, op1=op1), r, w)

    def stt(self, out, in0, scalar, in1, op0, op1, r, w):
        return self.P.op("dve", lambda e: e.scalar_tensor_tensor(out=out, in0=in0, scalar=scalar, in1=in1, op0=op0, op1=op1), r, w)

    def cp(self, eng, out, in_, r, w):
        if eng == "act":
            return self.P.op("act", lambda e: e.activation(out=out, in_=in_, func=AF.Copy), r, w)
        return self.P.op(eng, lambda e: e.tensor_copy(out=out, in_=in_), r, w)

    def recip(self, out, in_, r, w):
        return self.P.op("dve", lambda e: e.reciprocal(out=out, in_=in_), r, w)

    def rstd(self, out, ss, n, key_in, key_out, tmpkey):
        self.act(out, ss, AF.Sqrt, r=key_in, w=[tmpkey], scale=1.0 / n, bias=self.eps_t[:, 0:1])
        self.recip(out, out, r=[tmpkey], w=key_out)

    def mm(self, specs, r, w):
        def fn(e):
            ins = None
            for sp_ in specs:
                ins = e.matmul(out=sp_[0], lhsT=sp_[1], rhs=sp_[2], start=sp_[3], stop=sp_[4], skip_group_check=True)
            return ins
        return self.P.op("pe", fn, r, w)

    def tr(self, specs, r, w):
        def fn(e):
            ins = None
            for o_, i_ in specs:
                ins = e.transpose(out=o_, in_=i_, identity=self.ident[:])
            return ins
        return self.P.op("pe", fn, r, w)

    def build(self):
        P = self.P
        nc = self.nc
        self.ident = P.sb("ident", [128, 128], BF16)
        self.ones = P.sb("ones", [128, 128], BF16)
        self.eps_t = P.sb("eps_t", [128, 1], F32)
        self.csil = P.sb("csil", [128, 8, 2], F32)
        self.modT = P.sb("modT", [128, 4, 8, 2], F32)
        self.psum = P.ps("psum", [128, 8, 512], F32)
        self.banks = [self.psum[:, i, :] for i in range(8)]
        P.dma("sp", self.ident[:], self.C["ident"][:, :], w=["ident"])
        P.dma("sp", self.ones[:], self.C["ones"][:, :], w=["ones"])
        P.op("dve", lambda e: e.memset(self.eps_t[:], EPS), w=["eps"])
        P.dma("sp", self.csil[:].rearrange("p c r -> p (c r)"), self.I["csT"][:, :], w=["csil"])
        self.act(self.csil[:], self.csil[:], AF.Silu, r=["csil"], w=["csil"])
        P.barrier()
        for l in range(self.nlayers):
            self.layer(l)
        P.barrier()
        P.emit()
        P.close()
        return nc

    def src_rows(self, l, tok0, n):
        if l == 0:
            if tok0 >= S:
                return self.I["ctx"][tok0 - S:tok0 - S + n, :]
            return self.I["x"][tok0:tok0 + n, :]
        return self.x1_d[tok0:tok0 + n, :]

    def dst_rows(self, l, tok0, n):
        if l == self.nlayers - 1 and self.nlayers == 2:
            assert tok0 < S
            return self.out[tok0:tok0 + n, :]
        if self.nlayers == 1 and tok0 < S:
            return self.out[tok0:tok0 + n, :]
        return self.x1_d[tok0:tok0 + n, :]

    def layer(self, l):
        P = self.P
        last = (l == 1)
        self.l = l
        self.ctx_full = not last
        P.push()
        self.gmixT = P.sb("gmixT", [128, 8], F32)
        self.esink = P.sb("esink", [128, 4], F32)
        self.bsT = P.sb("bsT", [128, 4], F32)
        self.fourT = P.sb("fourT", [128, 2, NTOK], BF16)
        P.dma("sp", self.gmixT[:], self.I["g_mixT"][l], w=["gmixT"])
        P.dma("sp", self.esink[:], self.I["sinkT"][l], w=["esink"])
        P.dma("sp", self.bsT[:], self.I["b_sguT"][l], w=["bsT"])
        self.act(self.esink[:], self.esink[:], AF.Exp, r=["esink"], w=["esink"])
        self.adaln(l)
        if self.stop_after == "adaln":
            P.pop()
            return
        self.phaseA(l)
        if self.stop_after == "A":
            P.pop()
            return
        self.phaseF(l)
        if self.stop_after == "F":
            P.pop()
            return
        self.phaseC(l)
        if self.stop_after == "C":
            P.pop()
            return
        P.pop()
        self.phaseD(l)

    def adaln(self, l):
        P = self.P
        I = self.I
        P.push()
        wblk = [P.sb("wada%d" % i, [128, 8, 512], F32) for i in range(2)]
        modrow = P.sb("modrow", [2, 6 * D], F32)
        brow = P.sb("brow", [2, 6 * D], F32)
        gains = P.sb("gains", [2, 4, D], F32)
        rows = P.sb("rows", [2, 6, D], F32)
        sel = P.sb("sel", [2, 2, 128], F32)
        i2 = P.sb("i2", [2, 2], F32)
        P.dma("sp", brow[:], I["b_ada"][l:l + 1, :].partition_broadcast(2), w=["brow"])
        for i, nm in enumerate(("g_pre_mix", "g_post_mix", "g_pre_ff", "g_post_ff")):
            P.dma("sp", gains[:, i, :], I[nm][l:l + 1, :].partition_broadcast(2), w=[("gains", i)])
        P.dma("sp", sel[:].rearrange("k r m -> k (r m)"), self.C["sel"][:, :], w=["sel"])
        P.dma("sp", i2[:], self.C["i2"][:, :], w=["i2"])
        for nb in range(12):
            wb = wblk[nb % 2]
            P.dma("sp", wb[:], I["w_ada"][l, :, nb * 512:(nb + 1) * 512].rearrange("(c p) n -> p c n", p=128),
                  w=[("wada", nb % 2)])
            bk = self.banks[nb % 2]
            self.mm([(bk[0:2, :], self.csil[:, c, :], wb[:, c, :], c == 0, c == 7) for c in range(8)],
                    r=[("wada", nb % 2), "csil"], w=[("ps", nb % 2)])
            self.tt("dve", modrow[:, nb * 512:(nb + 1) * 512], bk[0:2, :], brow[:, nb * 512:(nb + 1) * 512], ALU.add,
                    r=[("ps", nb % 2), "brow"], w=[("modrow", nb)])
        mr = lambda i: modrow[:, i * D:(i + 1) * D]
        mk = lambda i: [("modrow", 2 * i), ("modrow", 2 * i + 1)]
        self.stt(rows[:, 0, :], mr(1), 1.0, gains[:, 0, :], ALU.add, ALU.mult, r=mk(1) + [("gains", 0)], w=[("rows", 0)])
        self.cp("dve", rows[:, 1, :], mr(0), r=mk(0), w=[("rows", 1)])
        self.stt(rows[:, 2, :], mr(4), 1.0, gains[:, 2, :], ALU.add, ALU.mult, r=mk(4) + [("gains", 2)], w=[("rows", 2)])
        self.cp("dve", rows[:, 3, :], mr(3), r=mk(3), w=[("rows", 3)])
        self.tt("dve", rows[:, 4, :], mr(2), gains[:, 1, :], ALU.mult, r=mk(2) + [("gains", 1)], w=[("rows", 4)])
        self.tt("dve", rows[:, 5, :], mr(5), gains[:, 3, :], ALU.mult, r=mk(5) + [("gains", 3)], w=[("rows", 5)])
        bk = self.banks[2]
        specs = []
        for v in range(4):
            for c in range(8):
                col = (v * 8 + c) * 2
                specs.append((bk[:, col:col + 2], rows[:, v, c * 128:(c + 1) * 128], i2[:, :], True, True))
        self.mm(specs, r=[("rows", v) for v in range(4)] + ["i2"], w=[("ps", 2)])
        self.cp("dve", self.modT[:].rearrange("p v c r -> p (v c r)"), bk[:, 0:64], r=[("ps", 2)], w=["modT"])
        if "scratch_out" in self.debug:
            P.dma("pool", self.modT_d[:, :], self.modT[:].rearrange("p v c r -> p (v c r)"), r=["modT"], w=["modT_d"])
        P.dma("pool", self.G_d[:, 0, :], rows[:, 4, :], r=[("rows", 4)], w=["G_d0"])
        P.dma("pool", self.G_d[:, 1, :], rows[:, 5, :], r=[("rows", 5)], w=["G_d1"])
        P.barrier()
        P.pop()

    def new_staging(self, width=1024):
        self._stg = [self.P.sb("stg%d" % i, [128, width], F32) for i in range(2)]
        self._stg_w = width
        self._stg_n = 0

    def load_weight(self, dst, src_fn, nchunks, width, tag):
        P = self.P
        piece = self._stg_w
        keys = []
        for c in range(nchunks):
            for o in range(0, width, piece):
                wd = min(piece, width - o)
                n = self._stg_n
                self._stg_n += 1
                s_ = self._stg[n % 2]
                P.dma("sp", s_[:, 0:wd], src_fn(c)[:, o:o + wd], w=[("stg", n % 2)])
                eng = ("pool", "dve", "act")[n % 3]
                self.cp(eng, dst[:, c, o:o + wd], s_[:, 0:wd], r=[("stg", n % 2)], w=[(tag, c, o)])
                keys.append((tag, c, o))
        return keys

    def phaseA(self, l):
        P = self.P
        I = self.I
        C = self.C
        P.push()
        self.winb = P.sb("winb", [128, 8, NCOL], BF16)
        self.new_staging()
        self.wink = self.load_weight(self.winb, lambda c: I["w_inp"][l, c * 128:(c + 1) * 128, :], 8, NCOL, "winb")
        A = type("A", (), {})()
        self.A = A
        A.xt = [P.sb("xt%d" % i, [128, D], F32) for i in range(4)]
        A.junk = P.sb("junk", [128, D], BF16)
        A.ss = P.sb("ssA", [128, 8], F32)
        A.rs = P.sb("rsA", [128, 8], F32)
        A.xn = [P.sb("xn%d" % i, [128, D], BF16) for i in range(4)]
        A.hxT = [P.sb("hxT%d" % i, [128, 8, 512], BF16) for i in range(2)]
        A.cos = [P.sb("cos%d" % i, [128, 512], F32) for i in range(2)]
        A.sin = [P.sb("sin%d" % i, [128, 512], F32) for i in range(2)]
        A.t1 = [P.sb("t1_%d" % i, [128, 512], F32) for i in range(2)]
        A.t2 = [P.sb("t2_%d" % i, [128, 512], F32) for i in range(2)]
        A.qT = [P.sb("qTt%d" % i, [128, 4, 512], BF16) for i in range(2)]
        A.fT = P.sb("fTt", [128, 2, 512], BF16)
        A.kTt = [P.sb("kTt%d" % i, [128, 512], BF16) for i in range(2)]
        A.vtt = [P.sb("vtt%d" % i, [128, 4, 128], BF16) for i in range(2)]
        A.ABt = [P.sb("ABt%d" % i, [128, 4, 512], BF16) for i in range(2)]
        A.ug = P.sb("ug", [128, 4, 256], F32)
        A.gg = P.sb("gg", [128, 4, 256], F32)
        A.sq = P.sb("sqA", [128, 4, 256], F32)
        A.ssh = P.sb("ssh", [128, 16], F32)
        A.rsh = P.sb("rsh", [128, 16], F32)
        A.vc = P.sb("vc", [128, 4, 256], BF16)
        A.so = P.sb("so", [128, 4, 256], F32)
        A.ss2 = P.sb("ss2", [128, 4], F32)
        A.rs2 = P.sb("rs2", [128, 4], F32)
        A.son = P.sb("son", [128, 4, 256], BF16)
        A.sgt = [P.sb("sgt%d" % i, [128, 2, 512], BF16) for i in range(2)]
        A.gsg = P.sb("gsg", [128, 256], F32)
        A.wsT = P.sb("wsT", [128, 4, 128], BF16)
        A.wsf = P.sb("wsf", [128, 512], F32)
        A.csblk = P.sb("csblk", [128, 256], BF16)
        P.dma("sp", A.gsg[:], I["g_sgu"][l:l + 1, :].partition_broadcast(128), w=["gsg"])
        P.dma("sp", A.wsf[:], I["w_sguT"][l], w=["wsf"])
        self.cp("dve", A.wsT[:].rearrange("p h q -> p (h q)"), A.wsf[:], r=["wsf"], w=["wsT"])
        P.dma("sp", A.csblk[:], C["csblk"][:, :], w=["csblk"])
        A.trbanks = Ring([(self.banks[i].bitcast(BF16), ("ps", i)) for i in range(2)])
        A.pbanks = Ring([(self.banks[i], ("ps", i)) for i in range(2, 8)])
        A.xi = 0
        tiles = [(S, 2, True)] + [(i * 512, 4, False) for i in range(S // 512)]
        if "A_tiles" in self.debug:
            tiles = tiles[:3]
        self.A_front(l, 0, *tiles[0])
        for i, t in enumerate(tiles):
            if i + 1 < len(tiles):
                self.A_front(l, i + 1, *tiles[i + 1])
            self.A_back(l, i, *t)
        P.barrier()
        P.pop()

    def A_front(self, l, it, tok0, nsub, is_ctx):
        P = self.P
        A = self.A
        mi = 1 if is_ctx else 0
        T = nsub * 128
        hx = A.hxT[it % 2]
        hk = ("hxT", it % 2)
        for s in range(nsub):
            xi = A.xi % 4
            A.xi += 1
            P.dma("sp", A.xt[xi][:], self.src_rows(l, tok0 + s * 128, 128), w=[("xt", xi)])
            self.act(A.junk[:], A.xt[xi][:], AF.Square, r=[("xt", xi)], w=["junk", ("ssA", s)], accum_out=A.ss[:, s:s + 1])
            self.rstd(A.rs[:, s:s + 1], A.ss[:, s:s + 1], float(D), [("ssA", s)], [("rsA", s)], ("rsAt", s))
            self.ts("dve", A.xn[xi][:], A.xt[xi][:], A.rs[:, s:s + 1], None, ALU.mult, None,
                    r=[("xt", xi), ("rsA", s)], w=[("xn", xi)])
        xis = [(A.xi - nsub + s) % 4 for s in range(nsub)]
        for cp_ in range(4):
            tb, sk = A.trbanks.next()
            self.tr([(tb[:, j * 512 + s * 128: j * 512 + (s + 1) * 128], A.xn[xis[s]][:, (2 * cp_ + j) * 128:(2 * cp_ + j + 1) * 128])
                     for j in range(2) for s in range(nsub)],
                    r=[("xn", xi) for xi in xis] + ["ident"], w=[sk])
            for j in range(2):
                c = 2 * cp_ + j
                slot = tb[:, j * 512:(j + 1) * 512]
                if cp_ % 2 == 0:
                    self.act(hx[:, c, 0:T], slot[:, 0:T], AF.Identity, r=[sk, "modT"], w=[hk + (c,)],
                             scale=self.modT[:, 0, c, mi:mi + 1], bias=self.modT[:, 1, c, mi:mi + 1])
                else:
                    self.ts("dve", hx[:, c, 0:T], slot[:, 0:T], self.modT[:, 0, c, mi:mi + 1], self.modT[:, 1, c, mi:mi + 1],
                            ALU.mult, ALU.add, r=[sk, "modT"], w=[hk + (c,)])

    def A_back(self, l, it, tok0, nsub, is_ctx):
        P = self.P
        A = self.A
        C = self.C
        if "A_nob" in self.debug:
            return
        T = nsub * 128
        hx = A.hxT[it % 2]
        hkeys = [("hxT", it % 2, c) for c in range(8)]
        W = self.winb
        full = (not is_ctx) or self.ctx_full
        sl = it % 2
        P.dma("sp", A.cos[sl][:, 0:T], C["cosT"][:, tok0:tok0 + T], w=[("cos", sl)])
        P.dma("sp", A.sin[sl][:, 0:T], C["sinT"][:, tok0:tok0 + T], w=[("sin", sl)])

        def fm_chunk(fc):
            bk, bkk = A.pbanks.next()
            self.mm([(bk[:, 0:T], W[:, dc, fc * 128:(fc + 1) * 128], hx[:, dc, 0:T], dc == 0, dc == 7) for dc in range(8)],
                    r=hkeys + self.wink, w=[bkk])
            return bk, bkk

        def rope(fc, fcr, out_ap, okey, j):
            bq, kq = fm_chunk(fc)
            br, kr = fm_chunk(fcr)
            self.tt("dve", A.t1[j % 2][:, 0:T], bq[:, 0:T], A.cos[sl][:, 0:T], ALU.mult, r=[kq, ("cos", sl)], w=[("t1", j % 2)])
            self.tt("dve", A.t2[j % 2][:, 0:T], br[:, 0:T], A.sin[sl][:, 0:T], ALU.mult, r=[kr, ("sin", sl)], w=[("t2", j % 2)])
            self.tt("pool", out_ap, A.t1[j % 2][:, 0:T], A.t2[j % 2][:, 0:T], ALU.add, r=[("t1", j % 2), ("t2", j % 2)], w=okey)

        rope(8, 9, A.kTt[sl][:, 0:T], [("kTt", sl)], 0)
        P.dma("pool", self.kT_d[:, tok0:tok0 + T], A.kTt[sl][:, 0:T], r=[("kTt", sl)], w=[("kT_d", tok0)])
        if full:
            for c in range(4):
                rope(c, 4 + c, A.qT[sl][:, c, 0:T], [("qTt", sl, c)], c + 1)
            P.dma("pool", self.qT_d[:, :, tok0:tok0 + T].rearrange("c p t -> p c t"), A.qT[sl][:, :, 0:T],
                  r=[("qTt", sl, c) for c in range(4)], w=[("qT_d", tok0)])
            for cf in range(2):
                bk, bkk = fm_chunk(10 + cf)
                self.cp("act", A.fT[:, cf, 0:T], bk[:, 0:T], r=[bkk], w=[("fT", cf)])
        if "A_nob2" in self.debug:
            return
        for s in range(nsub):
            b1, k1 = A.pbanks.next()
            ncol = 384 if full else 128
            self.mm([(b1[:, 0:ncol], hx[:, dc, s * 128:(s + 1) * 128], W[:, dc, 1536:1536 + ncol], dc == 0, dc == 7) for dc in range(8)],
                    r=hkeys + self.wink, w=[k1])
            self.cp("act", A.vtt[sl][:, s, :], b1[:, 0:128], r=[k1], w=[("vtt", sl, s)])
            if full:
                self.act(A.ug[:, s, :], b1[:, 128:384], AF.Gelu_apprx_tanh, r=[k1], w=[("ug", s)])
                b2, k2 = A.pbanks.next()
                self.mm([(b2[:, 0:256], hx[:, dc, s * 128:(s + 1) * 128], W[:, dc, 1920:2176], dc == 0, dc == 7) for dc in range(8)],
                        r=hkeys + self.wink, w=[k2])
                self.act(A.gg[:, s, :], b2[:, 0:256], AF.Gelu_apprx_tanh, r=[k2], w=[("gg", s)])
        blk0 = tok0 // 128
        P.dma("pool", self.vt_d[:, blk0:blk0 + nsub, :], A.vtt[sl][:, 0:nsub, :], r=[("vtt", sl, s) for s in range(nsub)],
              w=[("vt_d", tok0)])
        if not full or "A_nob3" in self.debug:
            return
        ns = nsub
        ggk = [("gg", s) for s in range(ns)]
        ugk = [("ug", s) for s in range(ns)]
        for s in range(ns):
            bk, bkk = A.pbanks.next()
            specs = []
            for cf in range(2):
                specs.append((bk[:, cf * 256:(cf + 1) * 256], A.fT[:, cf, s * 128:(s + 1) * 128], A.csblk[:, :], True, True))
            self.mm(specs, r=[("fT", 0), ("fT", 1), "csblk"], w=[bkk])
            self.cp("act", A.ABt[sl][:, s, :], bk[:, :], r=[bkk], w=[("ABt", sl, s)])
        P.dma("pool", self.AB_d[tok0:tok0 + T, :].rearrange("(s p) c -> p s c", p=128), A.ABt[sl][:, 0:ns, :],
              r=[("ABt", sl, s) for s in range(ns)], w=[("AB_d", tok0)])
        if "A_nob4" in self.debug:
            return
        self.tt("pool", A.sq[:, 0:ns, :], A.gg[:, 0:ns, :], A.gg[:, 0:ns, :], ALU.mult, r=ggk, w=["sq"])
        P.op("dve", lambda e: e.tensor_reduce(out=A.ssh[:, 0:ns * 4], in_=A.sq[:, 0:ns, :].rearrange("p s (h d) -> p (s h) d", h=4),
                                              axis=AX.X, op=ALU.add), r=["sq"], w=["ssh"])
        self.rstd(A.rsh[:, 0:ns * 4], A.ssh[:, 0:ns * 4], 64.0, ["ssh"], ["rsh"], "rsht")
        self.tt("dve", A.sq[:, 0:ns, :].rearrange("p s (h d) -> p (s h) d", h=4),
                A.gg[:, 0:ns, :].rearrange("p s (h d) -> p (s h) d", h=4),
                A.rsh[:, 0:ns * 4].unsqueeze(2).to_broadcast([128, ns * 4, 64]), ALU.mult, r=ggk + ["rsh"], w=["sq"])
        self.tt("dve", A.vc[:, 0:ns, :], A.sq[:, 0:ns, :], A.gsg[:, :].unsqueeze(1).to_broadcast([128, ns, 256]), ALU.mult,
                r=["sq", "gsg"], w=["vc"])
        for hp in range(2):
            bk, bkk = A.pbanks.next()
            specs = []
            for hh in range(2):
                h = hp * 2 + hh
                o = bk[:, hh * 256:hh * 256 + ns * 64].rearrange("p (s d) -> p s d", d=64)
                specs.append((o, A.wsT[:, h, :], A.vc[:, 0:ns, h * 64:(h + 1) * 64], True, True))
            self.mm(specs, r=["vc", "wsT"], w=[bkk])
            for hh in range(2):
                h = hp * 2 + hh
                self.stt(A.so[:, 0:ns, h * 64:(h + 1) * 64], bk[:, hh * 256:hh * 256 + ns * 64].rearrange("p (s d) -> p s d", d=64),
                         self.bsT[:, h:h + 1], A.ug[:, 0:ns, h * 64:(h + 1) * 64], ALU.add, ALU.mult,
                         r=[bkk, "bsT"] + ugk, w=[("so", h)])
        sok = [("so", h) for h in range(4)]
        self.tt("pool", A.sq[:, 0:ns, :], A.so[:, 0:ns, :], A.so[:, 0:ns, :], ALU.mult, r=sok, w=["sq"])
        P.op("dve", lambda e: e.tensor_reduce(out=A.ss2[:, 0:ns], in_=A.sq[:, 0:ns, :], axis=AX.X, op=ALU.add), r=["sq"], w=["ss2"])
        self.rstd(A.rs2[:, 0:ns], A.ss2[:, 0:ns], 256.0, ["ss2"], ["rs2"], "rs2t")
        self.tt("dve", A.son[:, 0:ns, :], A.so[:, 0:ns, :], A.rs2[:, 0:ns].unsqueeze(2).to_broadcast([128, ns, 256]), ALU.mult,
                r=sok + ["rs2"], w=["son"])
        tb, sk = A.trbanks.next()
        self.tr([(tb[:, cc * 512 + s * 128: cc * 512 + (s + 1) * 128], A.son[:, s, cc * 128:(cc + 1) * 128])
                 for cc in range(2) for s in range(ns)], r=["son", "ident"], w=[sk])
        for cc in range(2):
            self.act(A.sgt[sl][:, cc, 0:T], tb[:, cc * 512: cc * 512 + T], AF.Identity, r=[sk, "gmixT"], w=[("sgt", sl, cc)],
                     scale=self.gmixT[:, 4 + cc:5 + cc])
        P.dma("pool", self.sguT_d[:, :, tok0:tok0 + T].rearrange("c p t -> p c t"), A.sgt[sl][:, :, 0:T],
              r=[("sgt", sl, 0), ("sgt", sl, 1)], w=[("sguT_d", tok0)])

    def four_finish(self, ybank_ap, ykey, nk, tok_of, g):
        P = self.P
        F = self.F
        i = F.fi % 2
        F.fi += 1
        ysb = F.ysb[i]
        self.cp("act", ysb[:, 0:nk, :], ybank_ap, r=[ykey], w=[("ysb", i)])
        self.tt("pool", F.ysq[:, 0:nk, :], ysb[:, 0:nk, :], ysb[:, 0:nk, :], ALU.mult, r=[("ysb", i)], w=["ysq"])
        P.op("dve", lambda e: e.tensor_reduce(out=F.yss[:, 0:nk], in_=F.ysq[:, 0:nk, :], axis=AX.X, op=ALU.add), r=["ysq"], w=["yss"])
        self.rstd(F.yrs[:, 0:nk], F.yss[:, 0:nk], 256.0, ["yss"], ["yrs"], "yrst")
        self.tt("dve", F.ynb[i][:, 0:nk, :], ysb[:, 0:nk, :], F.yrs[:, 0:nk].unsqueeze(2).to_broadcast([128, nk, 256]), ALU.mult,
                r=[("ysb", i), "yrs"], w=[("ynb", i)])
        tb, sk = F.trbanks.next()
        self.tr([(tb[:, (j * 2 + cc) * 128:(j * 2 + cc + 1) * 128], F.ynb[i][:, j, cc * 128:(cc + 1) * 128])
                 for j in range(nk) for cc in range(2)], r=[("ynb", i), "ident"], w=[sk])
        for j in range(nk):
            for cc in range(2):
                self.act(tok_of(j, cc), tb[:, (j * 2 + cc) * 128:(j * 2 + cc + 1) * 128], AF.Identity, r=[sk, "gmixT"],
                         w=[("fourT", g, j, cc)], scale=self.gmixT[:, 6 + cc:7 + cc])

    def phaseF(self, l):
        P = self.P
        C = self.C
        P.push()
        F = type("F", (), {})()
        self.F = F
        F.fi = 0
        F.ab1 = P.sb("ab1", [128, 128, 128], BF16)
        F.p1 = P.sb("p1", [128, 64, 2, 128], BF16)
        F.mc = P.sb("mc", [128, 64, 128], BF16)
        F.ms = P.sb("ms", [128, 64, 128], BF16)
        F.t1tab = P.sb("t1tab", [128, 128], BF16)
        F.ysb = [P.sb("ysb%d" % i, [128, 4, 256], F32) for i in range(2)]
        F.ysq = P.sb("ysq", [128, 4, 256], F32)
        F.yss = P.sb("yss", [128, 4], F32)
        F.yrs = P.sb("yrs", [128, 4], F32)
        F.ynb = [P.sb("ynb%d" % i, [128, 4, 256], BF16) for i in range(2)]
        F.trbanks = Ring([(self.banks[i].bitcast(BF16), ("ps", i)) for i in range(2)])
        P.dma("sp", F.mc[:].rearrange("p a b -> p (a b)"), C["mc"][:, :], w=["mc"])
        P.dma("sp", F.ms[:].rearrange("p a b -> p (a b)"), C["ms"][:, :], w=["ms"])
        P.dma("sp", F.t1tab[:], C["t1tab"][:, :], w=["t1tab"])
        pb = Ring([(self.banks[i], ("ps", i)) for i in range(2, 5)])
        yb = Ring([(self.banks[i], ("ps", i)) for i in range(5, 8)])
        F.p1b = P.sb("p1b", [128, 64, 2, 128], BF16)
        p1s = [F.p1, F.p1b]
        for hh in range(2):
            for ab in range(2):
                src = self.AB_d[0:S, hh * 256 + ab * 128: hh * 256 + ab * 128 + 128].rearrange("(t1 t2) c -> t1 t2 c", t2=128)
                P.dma("sp", F.ab1[ab * 64:(ab + 1) * 64, :, :], src, w=[("ab1", ab)])
            for c4 in range(32):
                bk, bkk = pb.next()
                specs = []
                for j in range(4):
                    ch = c4 * 4 + j
                    specs.append((bk[:, j * 128:(j + 1) * 128], F.ab1[:, :, ch], F.t1tab[:, :], True, True))
                self.mm(specs, r=[("ab1", 0), ("ab1", 1), "t1tab"], w=[bkk])
                o = p1s[hh][:, :, :, c4 * 4:(c4 + 1) * 4].rearrange("p k r c -> p c r k")
                i_ = bk[:, :].rearrange("p (c r k) -> p c r k", c=4, r=2)
                if c4 % 2 == 0:
                    self.cp("act", o, i_, r=[bkk], w=[("p1", hh, c4)])
                else:
                    self.cp("dve", o, i_, r=[bkk], w=[("p1", hh, c4)])
        p1keys = [("p1", hh, c4) for hh in range(2) for c4 in range(32)]
        for kg in range(16):
            bka, kka = yb.next()
            for half2 in range(2):
                if half2 == 1:
                    bka, kka = yb.next()
                specs = []
                for j in range(2):
                    k1 = kg * 4 + half2 * 2 + j
                    for hh in range(2):
                        o = bka[:, j * 256 + hh * 128: j * 256 + hh * 128 + 128]
                        specs.append((o, F.mc[:, k1, :], p1s[hh][:, k1, 0, :], True, False))
                        specs.append((o, F.ms[:, k1, :], p1s[hh][:, k1, 1, :], False, True))
                self.mm(specs, r=p1keys + ["mc", "ms"], w=[kka])
                k1base = kg * 4 + half2 * 2

                def tok_of(j, cc, k1base=k1base):
                    return self.fourT[:, cc, 0:S].rearrange("p (k2 k1) -> p k1 k2", k1=64)[:, k1base + j, :]
                self.four_finish(bka[:, :].rearrange("p (j c) -> p j c", j=2), kka, 2, tok_of, ("L", kg, half2))
        if self.ctx_full:
            cab = P.sb("cab", [128, 2, 512], BF16)
            c256 = P.sb("c256", [128, 2, 256], BF16)
            s256 = P.sb("s256n", [128, 2, 256], BF16)
            P.dma("sp", cab[:], self.AB_d[S:S + 256, :].rearrange("(tb p) c -> p tb c", p=128), w=["cab"])
            P.dma("sp", c256[:].rearrange("p a b -> p (a b)"), C["c256"][:, :], w=["c256"])
            P.dma("sp", s256[:].rearrange("p a b -> p (a b)"), C["s256n"][:, :], w=["s256"])
            for kb in range(2):
                bk, bkk = yb.next()
                specs = []
                for hh in range(2):
                    o = bk[:, hh * 128:(hh + 1) * 128]
                    for tb in range(2):
                        specs.append((o, c256[:, tb, kb * 128:(kb + 1) * 128], cab[:, tb, hh * 256:hh * 256 + 128], tb == 0, False))
                        specs.append((o, s256[:, tb, kb * 128:(kb + 1) * 128], cab[:, tb, hh * 256 + 128:hh * 256 + 256], False, tb == 1))
                self.mm(specs, r=["cab", "c256", "s256"], w=[bkk])

                def tok_of(j, cc, kb=kb):
                    return self.fourT[:, cc, S + kb * 128:S + (kb + 1) * 128]
                self.four_finish(bk[:, 0:256].rearrange("p (j c) -> p j c", j=1), bkk, 1, tok_of, ("C", kb))
        if "scratch_out" in self.debug and l == 0:
            P.barrier()
            P.dma("pool", self.fourT_d[:, :, :], self.fourT[:], w=["fourT_d"])
        P.barrier()
        P.pop()

    def phaseC(self, l):
        P = self.P
        I = self.I
        C = self.C
        P.push()
        self.woutb = P.sb("woutb", [128, 8, D], BF16)
        self.new_staging()
        self.woutk = self.load_weight(self.woutb, lambda c: I["w_outp"][l, c * 128:(c + 1) * 128, :], 8, D, "woutb")
        K = type("K", (), {})()
        self.K = K
        self.kT_all = P.sb("kT_all", [128, NTOK], BF16)
        self.vt_all = P.sb("vt_all", [128, 66, 128], BF16)
        P.dma("sp", self.kT_all[:], self.kT_d[:, :], w=["kT_all"])
        P.dma("sp", self.vt_all[:], self.vt_d[:, :, :], w=["vt_all"])
        K.qt = [P.sb("qt%d" % i, [128, 4, 512], BF16) for i in range(2)]
        K.sgt = [P.sb("sgtC%d" % i, [128, 2, 512], BF16) for i in range(2)]
        K.xt = [P.sb("xtC%d" % i, [128, D], F32) for i in range(4)]
        K.PT = [P.sb("PT%d" % i, [128, 2, 512], BF16) for i in range(6)]
        K.oc = [P.sb("oc%d" % i, [128, 512], F32) for i in range(2)]
        K.dd = [P.sb("dd%d" % i, [128, 512], F32) for i in range(2)]
        K.ar = [P.sb("ar%d" % i, [128, 512], F32) for i in range(2)]
        K.sqT = P.sb("sqT", [128, 4, 512], BF16)
        K.aT = P.sb("aT", [128, 4, 512], BF16)
        K.ys2 = [P.sb("ysC%d" % i, [128, D], F32) for i in range(2)]
        K.yc2 = [P.sb("ycC%d" % i, [128, D], F32) for i in range(2)]
        K.tmp = P.sb("tmpC", [128, D], F32)
        K.junk = P.sb("junkC", [128, D], BF16)
        K.ssa = P.sb("ssa", [128, 4], F32)
        K.rsa = P.sb("rsa", [128, 4], F32)
        K.ssy = P.sb("ssy", [128, 4], F32)
        K.rsy = P.sb("rsy", [128, 4], F32)
        K.G1 = P.sb("G1bc", [128, 2, D], F32)
        K.mnext = P.sb("mnext", [128, 128], BF16)
        K.mprev = P.sb("mprev", [128, 128], BF16)
        P.dma("sp", K.mnext[:], C["mask_next"][:, :], w=["mnext"])
        P.dma("sp", K.mprev[:], C["mask_prev"][:, :], w=["mprev"])
        for r_ in range(2):
            P.dma("sp", K.G1[:, r_, :], self.G_d[r_:r_ + 1, 0, :].partition_broadcast(128), w=[("G1", r_)])
        K.spairs = Ring([(self.psum[:, 2 * i:2 * i + 2, :], [("ps", 2 * i), ("ps", 2 * i + 1)]) for i in range(3)])
        K.tailpairs = Ring([(self.psum[:, 2 * i:2 * i + 2, :], [("ps", 2 * i), ("ps", 2 * i + 1)]) for i in range(4)])
        K.pti = 0
        K.xi = 0
        tiles = [(i * 512, 4, False) for i in range(S // 512)]
        if self.ctx_full:
            tiles = tiles + [(S, 2, True)]
        if "C_tiles" in self.debug:
            tiles = tiles[:2] + tiles[-1:]
        for it, t in enumerate(tiles):
            self.C_tile(l, it, *t)
        P.barrier()
        P.pop()

    def C_tile(self, l, it, tok0, nsub, is_ctx):
        P = self.P
        K = self.K
        T = nsub * 128
        sl = it % 2
        mi = 1 if is_ctx else 0
        qt = K.qt[sl]
        P.dma("sp", qt[:, :, 0:T], self.qT_d[:, :, tok0:tok0 + T].rearrange("c p t -> p c t"), w=[("qt", sl)])
        P.dma("sp", K.sgt[sl][:, :, 0:T], self.sguT_d[:, :, tok0:tok0 + T].rearrange("c p t -> p c t"), w=[("sgtC", sl)])
        xis = []
        for s in range(nsub):
            xi = K.xi % 4
            K.xi += 1
            xis.append(xi)
            P.dma("sp", K.xt[xi][:], self.src_rows(l, tok0 + s * 128, 128), w=[("xtC", xi)])
        items = []
        if not is_ctx:
            n0 = tok0 // 128
            for j in range(n0 - 1, n0 + nsub + 1):
                if j < 0 or j >= S // 128:
                    continue
                qlo = max(j - 1, n0) - n0
                qhi = min(j + 1, n0 + nsub - 1) - n0
                mnext = (j - 1 - n0) if (j - 1 >= n0) else None
                mprev = (j + 1 - n0) if (j + 1 <= n0 + nsub - 1) else None
                items.append((j, qlo * 128, (qhi + 1) * 128, mnext, mprev))
        items.append((64, 0, T, None, None))
        items.append((65, 0, T, None, None))
        LOOK = 2
        bo, ko = self.banks[6], ("ps", 6)
        bd, kd = self.banks[7], ("ps", 7)
        deferred = []

        def fin2():
            while deferred:
                c_ = deferred.pop(0)
                d_ = c_ % 2
                self.act(K.sqT[:, c_, 0:T], K.ar[d_][:, 0:T], AF.Square, r=[("ar", d_)], w=[("sqT", c_)])
                self.ts("pool", K.aT[:, c_, 0:T], K.ar[d_][:, 0:T], self.gmixT[:, c_:c_ + 1], 1.0, ALU.mult, ALU.mult,
                        r=[("ar", d_), "gmixT"], w=[("aT", c_)])

        for c in range(4):
            pend = []
            first = True
            for idx in range(len(items) + LOOK):
                if idx == min(5, len(items) + LOOK - 1):
                    fin2()
                if idx < len(items):
                    j, lo, hi, mn, mp = items[idx]
                    n = hi - lo
                    pair, pkeys = K.spairs.next()
                    kcols = slice(j * 128, (j + 1) * 128)
                    self.mm([(pair[:, 0, 0:n], self.kT_all[0:64, kcols], qt[0:64, c, lo:hi], True, True),
                             (pair[:, 1, 0:n], self.kT_all[64:128, kcols], qt[64:128, c, lo:hi], True, True)],
                            r=[("qt", sl), "kT_all"], w=pkeys)
                    pi = K.pti % len(K.PT)
                    K.pti += 1
                    pt = K.PT[pi]
                    pk = [("PT", pi)]
                    self.act(pt[:, :, 0:n], pair[:, :, 0:n], AF.Exp, r=pkeys, w=pk, scale=0.125)
                    if mn is not None:
                        o = mn * 128 - lo
                        self.tt("dve", pt[:, :, o:o + 128], pt[:, :, o:o + 128],
                                K.mnext[:, :].unsqueeze(1).to_broadcast([128, 2, 128]), ALU.mult, r=pk + ["mnext"], w=pk)
                    if mp is not None:
                        o = mp * 128 - lo
                        self.tt("dve", pt[:, :, o:o + 128], pt[:, :, o:o + 128],
                                K.mprev[:, :].unsqueeze(1).to_broadcast([128, 2, 128]), ALU.mult, r=pk + ["mprev"], w=pk)
                    pend.append((j, lo, hi, pt, pk))
                if idx >= LOOK:
                    j, lo, hi, pt, pk = pend.pop(0)
                    n = hi - lo
                    last = (idx == len(items) + LOOK - 1)
                    self.mm([(bo[0:64, lo:hi], self.vt_all[:, j, 0:64], pt[:, 0, 0:n], first, last),
                             (bo[64:128, lo:hi], self.vt_all[:, j, 64:128], pt[:, 1, 0:n], first, last),
                             (bd[0:64, lo:hi], self.ones[:, 0:64], pt[:, 0, 0:n], first, last),
                             (bd[64:128, lo:hi], self.ones[:, 0:64], pt[:, 1, 0:n], first, last)],
                            r=pk + ["ones", "vt_all"], w=[ko, kd])
                    first = False
            d2 = c % 2
            self.cp("act", K.oc[d2][:, 0:T], bo[:, 0:T], r=[ko], w=[("oc", d2)])
            self.ts("dve", K.dd[d2][:, 0:T], bd[:, 0:T], self.esink[:, c:c + 1], None, ALU.add, None, r=[kd, "esink"], w=[("dd", d2)])
            self.recip(K.dd[d2][:, 0:T], K.dd[d2][:, 0:T], r=[("dd", d2)], w=[("dd", d2)])
            self.tt("dve", K.ar[d2][:, 0:T], K.oc[d2][:, 0:T], K.dd[d2][:, 0:T], ALU.mult, r=[("oc", d2), ("dd", d2)], w=[("ar", d2)])
            deferred.append(c)
        fin2()
        pair, pkeys = K.spairs.next()
        bs_, ks_ = pair[:, 0, :], pkeys[0]
        specs = []
        for s in range(nsub):
            for c in range(4):
                specs.append((bs_[:, s:s + 1], K.sqT[:, c, s * 128:(s + 1) * 128], self.ones[:, 0:1], c == 0, c == 3))
        self.mm(specs, r=[("sqT", c) for c in range(4)] + ["ones"], w=[ks_])
        self.rstd(K.rsa[:, 0:nsub], bs_[:, 0:nsub], 512.0, [ks_], ["rsa"], "rsat")
        aTk = [("aT", c) for c in range(4)]
        def st1(s):
            xi = xis[s]
            cols = slice(s * 128, (s + 1) * 128)
            b2 = s % 2
            ys, yc = K.ys2[b2], K.yc2[b2]
            pa, pak = K.tailpairs.next()
            pb_, pbk = K.tailpairs.next()
            for half in range(2):
                self.mm([(pa[:, half, :], K.aT[:, c, cols], self.woutb[:, c, half * 512:(half + 1) * 512], c == 0, c == 3) for c in range(4)],
                        r=aTk + self.woutk, w=[pak[half]])
            for half in range(2):
                specs = []
                for cc in range(2):
                    specs.append((pb_[:, half, :], K.sgt[sl][:, cc, cols], self.woutb[:, 4 + cc, half * 512:(half + 1) * 512], cc == 0, False))
                for cc in range(2):
                    specs.append((pb_[:, half, :], self.fourT[:, cc, tok0 + s * 128: tok0 + (s + 1) * 128],
                                  self.woutb[:, 6 + cc, half * 512:(half + 1) * 512], False, cc == 1))
                self.mm(specs, r=[("sgtC", sl)] + self.woutk, w=[pbk[half]])
            self.cp("act", ys[:].rearrange("p (h n) -> p h n", h=2), pb_[:, :, :], r=pbk, w=[("ysC", b2)])
            self.stt(yc[:].rearrange("p (h n) -> p h n", h=2), pa[:, :, :], K.rsa[:, s:s + 1], ys[:].rearrange("p (h n) -> p h n", h=2),
                     ALU.mult, ALU.add, r=pak + ["rsa", ("ysC", b2)], w=[("ycC", b2)])

        def st2(s):
            xi = xis[s]
            b2 = s % 2
            yc = K.yc2[b2]
            yck = [("ycC", b2)]
            self.act(K.junk[:], yc[:], AF.Square, r=yck, w=["junkC", ("ssy", s)], accum_out=K.ssy[:, s:s + 1])
            self.rstd(K.rsy[:, s:s + 1], K.ssy[:, s:s + 1], float(D), [("ssy", s)], [("rsy", s)], ("rsyt", s))
            self.tt("pool", K.tmp[:], yc[:], K.G1[:, mi, :], ALU.mult, r=yck + [("G1", mi)], w=["tmpC"])
            self.stt(K.xt[xi][:], K.tmp[:], K.rsy[:, s:s + 1], K.xt[xi][:], ALU.mult, ALU.add,
                     r=["tmpC", ("rsy", s), ("xtC", xi)], w=[("xtC", xi)])
            P.dma("pool", self.xmid_d[tok0 + s * 128: tok0 + (s + 1) * 128, :], K.xt[xi][:], r=[("xtC", xi)], w=[("xmid", tok0, s)])

        st1(0)
        for s in range(1, nsub):
            st1(s)
            st2(s - 1)
        st2(nsub - 1)

    def phaseD(self, l):
        P = self.P
        I = self.I
        P.push()
        self.w1b = P.sb("w1b", [128, 8, DFF], BF16)
        self.w2b = P.sb("w2b", [128, 32, D], BF16)
        self.new_staging(512)
        self.w1k = self.load_weight(self.w1b, lambda c: I["w_ff1"][l, c * 128:(c + 1) * 128, :], 8, DFF, "w1b")
        self.w2k = self.load_weight(self.w2b, lambda c: I["w_ff2"][l, c * 128:(c + 1) * 128, :], 32, D, "w2b")
        Dd = type("Dd", (), {})()
        self.Dd = Dd
        Dd.xm = [P.sb("xm%d" % i, [128, D], F32) for i in range(6)]
        Dd.xn = [P.sb("xnD%d" % i, [128, D], BF16) for i in range(2)]
        Dd.hx = [P.sb("hx2T%d" % i, [128, 8, 256], BF16) for i in range(2)]
        Dd.hT = P.sb("hT", [128, 32, 256], BF16)
        Dd.rl = [P.sb("rl%d" % i, [128, 512], F32) for i in range(2)]
        Dd.tmp = P.sb("tmpD", [128, D], F32)
        Dd.junk = P.sb("junkD", [128, D], BF16)
        Dd.ss = P.sb("ssD", [128, 4], F32)
        Dd.rs = P.sb("rsD", [128, 4], F32)
        Dd.ssy = P.sb("ssyD", [128, 4], F32)
        Dd.rsy = P.sb("rsyD", [128, 2], F32)
        Dd.G2 = P.sb("G2bc", [128, 2, D], F32)
        for r_ in range(2):
            P.dma("sp", Dd.G2[:, r_, :], self.G_d[r_:r_ + 1, 1, :].partition_broadcast(128), w=[("G2", r_)])
        Dd.trbanks = Ring([(self.banks[i].bitcast(BF16), ("ps", i)) for i in range(2)])
        Dd.fbanks = Ring([(self.banks[i], ("ps", i)) for i in range(2, 5)])
        Dd.yring = Ring([(self.banks[i], ("ps", i)) for i in range(5, 8)])
        Dd.xi = 0
        Dd.rli = 0
        tiles = [(i * 256, False) for i in range(S // 256)]
        if self.ctx_full:
            tiles = tiles + [(S, True)]
        if "D_tiles" in self.debug:
            tiles = tiles[:2] + tiles[-1:]
        self.D_front(l, 0, *tiles[0])
        for it, t in enumerate(tiles):
            if it + 1 < len(tiles):
                self.D_front(l, it + 1, *tiles[it + 1])
            self.D_back(l, it, *t)
        P.barrier()
        P.pop()

    def D_front(self, l, it, tok0, is_ctx):
        P = self.P
        Dd = self.Dd
        mi = 1 if is_ctx else 0
        hx = Dd.hx[it % 2]
        xis = []
        for s in range(2):
            xi = Dd.xi % 6
            Dd.xi += 1
            xis.append(xi)
            P.dma("sp", Dd.xm[xi][:], self.xmid_d[tok0 + s * 128: tok0 + (s + 1) * 128, :], w=[("xm", xi)])
            self.act(Dd.junk[:], Dd.xm[xi][:], AF.Square, r=[("xm", xi)], w=["junkD", ("ssD", s)], accum_out=Dd.ss[:, s:s + 1])
            self.rstd(Dd.rs[:, s:s + 1], Dd.ss[:, s:s + 1], float(D), [("ssD", s)], [("rsD", s)], ("rsDt", s))
            self.ts("dve", Dd.xn[s][:], Dd.xm[xi][:], Dd.rs[:, s:s + 1], None, ALU.mult, None, r=[("xm", xi), ("rsD", s)], w=[("xnD", s)])
        Dd.cur_xis = getattr(Dd, "cur_xis", {})
        Dd.cur_xis[it] = xis
        for cq in range(2):
            tb, sk = Dd.trbanks.next()
            self.tr([(tb[:, j * 256 + s * 128: j * 256 + (s + 1) * 128], Dd.xn[s][:, (4 * cq + j) * 128:(4 * cq + j + 1) * 128])
                     for j in range(4) for s in range(2)], r=[("xnD", 0), ("xnD", 1), "ident"], w=[sk])
            for j in range(4):
                c = 4 * cq + j
                slot = tb[:, j * 256:(j + 1) * 256]
                if cq == 0:
                    self.act(hx[:, c, :], slot, AF.Identity, r=[sk, "modT"], w=[("hx2T", it % 2, c)],
                             scale=self.modT[:, 2, c, mi:mi + 1], bias=self.modT[:, 3, c, mi:mi + 1])
                else:
                    self.ts("dve", hx[:, c, :], slot, self.modT[:, 2, c, mi:mi + 1], self.modT[:, 3, c, mi:mi + 1], ALU.mult, ALU.add,
                            r=[sk, "modT"], w=[("hx2T", it % 2, c)])

    def D_back(self, l, it, tok0, is_ctx):
        P = self.P
        Dd = self.Dd
        mi = 1 if is_ctx else 0
        hx = Dd.hx[it % 2]
        hk = [("hx2T", it % 2, c) for c in range(8)]
        xis = Dd.cur_xis[it]
        for fp in range(16):
            fb, sk = Dd.fbanks.next()
            specs = []
            for j in range(2):
                fc = 2 * fp + j
                for dc in range(8):
                    specs.append((fb[:, j * 256:(j + 1) * 256], self.w1b[:, dc, fc * 128:(fc + 1) * 128], hx[:, dc, :], dc == 0, dc == 7))
            self.mm(specs, r=hk + self.w1k, w=[sk])
            ri = Dd.rli % 2
            Dd.rli += 1
            self.act(Dd.rl[ri][:], fb[:, :], AF.Relu, r=[sk], w=[("rl", ri)])
            self.tt("pool" if fp % 2 == 0 else "dve", Dd.hT[:, 2 * fp:2 * fp + 2, :].rearrange("p a t -> p (a t)"), Dd.rl[ri][:], Dd.rl[ri][:],
                    ALU.mult, r=[("rl", ri)], w=[("hT", 2 * fp), ("hT", 2 * fp + 1)])
        hTk = [("hT", fc) for fc in range(32)]
        for s in range(2):
            xi = xis[s]
            ybs = []
            for half in range(2):
                b_, k_ = Dd.yring.next()
                self.mm([(b_[:, :], Dd.hT[:, fc, s * 128:(s + 1) * 128], self.w2b[:, fc, half * 512:(half + 1) * 512], fc == 0, fc == 31)
                         for fc in range(32)], r=hTk + self.w2k, w=[k_])
                ybs.append((b_, k_))
            for half in range(2):
                self.act(Dd.junk[:, half * 512:(half + 1) * 512], ybs[half][0][:, :], AF.Square, r=[ybs[half][1]],
                         w=[("junkD", half), ("ssyD", s, half)], accum_out=Dd.ssy[:, s * 2 + half:s * 2 + half + 1])
            self.tt("dve", Dd.ssy[:, s * 2:s * 2 + 1], Dd.ssy[:, s * 2:s * 2 + 1], Dd.ssy[:, s * 2 + 1:s * 2 + 2], ALU.add,
                    r=[("ssyD", s, 0), ("ssyD", s, 1)], w=[("ssyDs", s)])
            self.rstd(Dd.rsy[:, s:s + 1], Dd.ssy[:, s * 2:s * 2 + 1], float(D), [("ssyDs", s)], [("rsyD", s)], ("rsyDt", s))
            for half in range(2):
                hc = slice(half * 512, (half + 1) * 512)
                self.tt("dve", Dd.tmp[:, hc], ybs[half][0][:, :], Dd.G2[:, mi, hc], ALU.mult, r=[ybs[half][1], ("G2", mi)], w=[("tmpD", half)])
            self.stt(Dd.xm[xi][:], Dd.tmp[:], Dd.rsy[:, s:s + 1], Dd.xm[xi][:], ALU.mult, ALU.add,
                     r=[("tmpD", 0), ("tmpD", 1), ("rsyD", s), ("xm", xi)], w=[("xm", xi)])
            if is_ctx and l == self.nlayers - 1:
                continue
            P.dma("pool", self.dst_rows(l, tok0 + s * 128, 128), Dd.xm[xi][:], r=[("xm", xi)], w=[("xout", tok0, s)])


def _prep_shared(inp):
    sh = {}
    f = lambda a: np.ascontiguousarray(a, dtype=np.float32)
    sh["w_ada"] = f(inp["w_ada"])
    sh["b_ada"] = f(inp["b_ada"])
    for nm in ("g_pre_mix", "g_post_mix", "g_pre_ff", "g_post_ff"):
        sh[nm] = f(inp[nm])
    sh["w_inp"] = f(inp["w_in"][:, :, _perm_cols()])
    sink = inp["sink"]
    sinkT = np.zeros((2, 128, 4), np.float32)
    for c in range(4):
        sinkT[:, 0:64, c] = sink[:, c][:, None]
        sinkT[:, 64:128, c] = sink[:, 4 + c][:, None]
    sh["sinkT"] = sinkT
    sh["w_sguT"] = f(np.transpose(inp["w_sgu"], (0, 3, 1, 2)).reshape(2, 128, 512))
    sh["b_sguT"] = f(np.transpose(inp["b_sgu"], (0, 2, 1)))
    sh["g_sgu"] = f(inp["g_sgu"].reshape(2, 256))
    rp = _attn_row_perm()
    gm = inp["g_mix"][:, rp]
    sh["g_mixT"] = f(np.transpose(gm.reshape(2, 8, 128), (0, 2, 1)))
    sh["w_outp"] = f(inp["w_out"][:, rp, :])
    sh["w_ff1"] = f(inp["w_ff1"])
    sh["w_ff2"] = f(inp["w_ff2"])
    return sh


def _core_inputs(inp, sh, consts, b):
    m = dict(sh)
    m["x"] = np.ascontiguousarray(inp["x"][b], dtype=np.float32)
    m["ctx"] = np.ascontiguousarray(inp["ctx"][b], dtype=np.float32)
    cs = np.stack([inp["c"][b], inp["c_ctx"]], axis=0).astype(np.float32)
    m["csT"] = np.ascontiguousarray(np.transpose(cs.reshape(2, 8, 128), (2, 1, 0)).reshape(128, 16))
    for k, v in consts.items():
        m["c_" + k] = v
    return m


_CACHE = {}


def kernel(**inputs):
    inp = {k: np.asarray(v) for k, v in inputs.items()}
    if "nc" not in _CACHE:
        _CACHE["nc"] = Builder().build()
        _CACHE["consts"] = _constants()
    nc = _CACHE["nc"]
    consts = _CACHE["consts"]
    sh = _prep_shared(inp)
    in_maps = [_core_inputs(inp, sh, consts, b) for b in range(8)]
    res = run_bass_kernel_spmd(nc, in_maps, core_ids=list(range(8)))
    out = np.stack([np.asarray(res.results[b]["out"], dtype=np.float32) for b in range(8)], axis=0)
    return out
```

```python
import contextlib
import numpy as np
import ml_dtypes
import concourse.bass as bass
import concourse.mybir as mybir
from concourse.bass_utils import run_bass_kernel_spmd

F32 = mybir.dt.float32
BF16 = mybir.dt.bfloat16
AF = mybir.ActivationFunctionType
ALU = mybir.AluOpType
AX = mybir.AxisListType
NPBF = ml_dtypes.bfloat16

COMPUTE = ("pe", "act", "dve", "pool")
NDMA_SEMS = 20

D = 1024
S = 8192
LCTX = 256
NTOK = S + LCTX
DFF = 4096
NCOL = 2176
EPS = 1e-6


class _Op:
    __slots__ = ("eng", "fn", "deps", "signal", "sigval", "is_dma", "sem_idx", "dma_val", "prev_same_sem")

    def __init__(self, eng, fn, is_dma):
        self.eng = eng
        self.fn = fn
        self.deps = []
        self.signal = False
        self.sigval = 0
        self.is_dma = is_dma
        self.sem_idx = None
        self.dma_val = 0
        self.prev_same_sem = None


class Prog:
    def __init__(self, nc):
        self.nc = nc
        self.ops = {e: [] for e in ("pe", "act", "dve", "pool", "sp")}
        self.last_w = {}
        self.readers = {}
        self.dma_count = {"sp": 0, "pool": 0, "act": 0}
        self.dma_last_on_sem = {}
        self.stacks = [contextlib.ExitStack()]
        self.last_op = {}

    def push(self):
        self.stacks.append(contextlib.ExitStack())

    def pop(self):
        self.stacks.pop().close()

    def sb(self, name, shape, dt):
        self.uid = getattr(self, "uid", 0) + 1
        return self.stacks[-1].enter_context(self.nc.sbuf_tensor("%s_u%d" % (name, self.uid), list(shape), dt))

    def ps(self, name, shape, dt):
        return self.stacks[-1].enter_context(self.nc.psum_tensor(name, list(shape), dt))

    def _track(self, op, r, w):
        ps_keys = [("ps", k[1]) for k in list(r) + list(w) if isinstance(k, tuple) and k[0] == "ps"]
        r = [k for k in r if not (isinstance(k, tuple) and k[0] == "ps")]
        w = [k for k in w if not (isinstance(k, tuple) and k[0] == "ps")] + list(dict.fromkeys(ps_keys))
        deps = set()
        for k in r:
            lw = self.last_w.get(k)
            if lw is not None:
                deps.add(lw)
        for k in w:
            lw = self.last_w.get(k)
            if lw is not None and (op.is_dma or lw.is_dma or lw.eng != op.eng):
                deps.add(lw)
            for rd in self.readers.get(k, ()):
                if rd is op:
                    continue
                if op.is_dma or rd.is_dma or rd.eng != op.eng:
                    deps.add(rd)
        for d in deps:
            if d is op:
                continue
            d.signal = True
            op.deps.append(d)
        for k in r:
            self.readers.setdefault(k, []).append(op)
        for k in w:
            self.last_w[k] = op
            self.readers[k] = []

    def op(self, eng, fn, r=(), w=()):
        o = _Op(eng, fn, False)
        self._track(o, r, w)
        self.ops[eng].append(o)
        self.last_op[eng] = o
        return o

    def wait_all(self, eng, r=()):
        o = _Op(eng, None, False)
        self._track(o, r, ())
        self.ops[eng].append(o)
        return o

    def dma(self, q, out, in_, r=(), w=(), **kw):
        o = _Op(q, (out, in_, kw), True)
        n = self.dma_count[q]
        self.dma_count[q] = n + 1
        nsem = NDMA_SEMS if q != "pool" else 6
        o.sem_idx = (q, n % nsem)
        prev = self.dma_last_on_sem.get(o.sem_idx)
        o.prev_same_sem = prev
        o.dma_val = (prev.dma_val if prev is not None else 0) + 16
        self.dma_last_on_sem[o.sem_idx] = o
        self._track(o, r, w)
        self.ops[q].append(o)
        return o

    def barrier(self):
        srcs = [o for o in self.last_op.values()] + list(self.dma_last_on_sem.values())
        for e in ("pe", "act", "dve", "pool", "sp"):
            o = _Op(e, None, False)
            for s_ in srcs:
                if (not s_.is_dma) and s_.eng == e:
                    continue
                s_.signal = True
                o.deps.append(s_)
            self.ops[e].append(o)
        self.last_w = {}
        self.readers = {}

    def emit(self):
        nc = self.nc
        st = self.stacks[0]
        esem = {e: st.enter_context(nc.semaphore("sem_" + e)) for e in COMPUTE}
        dsem = {}
        for q in ("sp", "pool", "act"):
            for i in range(min(NDMA_SEMS, self.dma_count[q])):
                dsem[(q, i)] = st.enter_context(nc.semaphore("dsem_%s_%d" % (q, i)))
        for e in COMPUTE:
            c = 0
            for o in self.ops[e]:
                if (not o.is_dma) and o.signal:
                    c += 1
                    o.sigval = c

        def src_of(d):
            if d.is_dma:
                return dsem[d.sem_idx], d.dma_val, ("d",) + d.sem_idx
            return esem[d.eng], d.sigval, ("e", d.eng)

        def run(engname, eng):
            seen = {}
            for o in self.ops[engname]:
                waits = {}
                deps = list(o.deps)
                if o.is_dma and o.prev_same_sem is not None:
                    deps.append(o.prev_same_sem)
                for d in deps:
                    sem, val, key = src_of(d)
                    if seen.get(key, 0) >= val:
                        continue
                    if key not in waits or waits[key][1] < val:
                        waits[key] = (sem, val)
                for key, (sem, val) in waits.items():
                    eng.wait_ge(sem, val)
                    seen[key] = val
                if o.is_dma:
                    out, in_, kw = o.fn
                    eng.dma_start(out=out, in_=in_, **kw).then_inc(dsem[o.sem_idx], 16)
                elif o.fn is None:
                    pass
                else:
                    ins = o.fn(eng)
                    if o.signal:
                        ins.then_inc(esem[engname], 1)

        with nc.Block() as block:
            @block.tensor
            def _(e):
                run("pe", e)

            @block.scalar
            def _(e):
                run("act", e)

            @block.vector
            def _(e):
                run("dve", e)

            @block.gpsimd
            def _(e):
                run("pool", e)

            @block.sync
            def _(e):
                run("sp", e)

    def close(self):
        while self.stacks:
            self.stacks.pop().close()


class Ring:
    def __init__(self, items):
        self.items = items
        self.i = 0

    def next(self):
        it = self.items[self.i % len(self.items)]
        self.i += 1
        return it


def _perm_cols():
    def rotp(d):
        if d < 16:
            return d + 16
        if d < 32:
            return d - 16
        if d < 48:
            return d + 16
        return d - 16
    cols = []
    for c in range(4):
        for h in (c, 4 + c):
            cols += [h * 64 + d for d in range(64)]
    for c in range(4):
        for h in (c, 4 + c):
            cols += [h * 64 + rotp(d) for d in range(64)]
    cols += [512 + j for j in range(128)]
    cols += [512 + h * 64 + rotp(d) for h in range(2) for d in range(64)]
    cols += [1280 + j for j in range(256)]
    cols += [640 + j for j in range(128)]
    cols += [768 + j for j in range(256)]
    cols += [1024 + j for j in range(256)]
    assert len(cols) == NCOL
    return np.array(cols)


def _attn_row_perm():
    rows = []
    for c in range(4):
        for h in (c, 4 + c):
            rows += [h * 64 + d for d in range(64)]
    return np.array(rows + list(range(512, 1024)))


def _constants():
    k = {}
    k["ident"] = np.eye(128, dtype=np.float32).astype(NPBF)
    n_freq = 16
    inv_freq = (np.float32(10000.0) ** (-np.arange(n_freq, dtype=np.float32) / np.float32(n_freq))).astype(np.float32)
    t = np.arange(S)
    row = (t // 64).astype(np.float32)
    col = (t % 64).astype(np.float32)
    ang = np.zeros((64, S), np.float32)
    ang[0:16] = row[None, :] * inv_freq[:, None]
    ang[16:32] = ang[0:16]
    ang[32:48] = col[None, :] * inv_freq[:, None]
    ang[48:64] = ang[32:48]
    cos = np.cos(ang).astype(np.float32)
    sin = np.sin(ang).astype(np.float32)
    sin[0:16] *= -1
    sin[32:48] *= -1
    cosT = np.ones((128, NTOK), np.float32)
    sinT = np.zeros((128, NTOK), np.float32)
    cosT[0:64, :S] = cos
    cosT[64:128, :S] = cos
    sinT[0:64, :S] = sin
    sinT[64:128, :S] = sin
    k["cosT"] = cosT
    k["sinT"] = sinT
    kk = np.arange(128)[:, None]
    qq = np.arange(128)[None, :]
    k["mask_next"] = (kk <= qq).astype(np.float32).astype(NPBF)
    k["mask_prev"] = (kk >= qq).astype(np.float32).astype(NPBF)
    k["ones"] = np.ones((128, 128), np.float32).astype(NPBF)
    cidx = np.arange(64)
    c64 = np.cos(2 * np.pi * np.outer(cidx, cidx) / 64.0) / 8.0
    s64 = np.sin(2 * np.pi * np.outer(cidx, cidx) / 64.0) / 8.0
    csblk = np.zeros((128, 256), np.float64)
    for g in range(2):
        csblk[g * 64:(g + 1) * 64, g * 64:(g + 1) * 64] = c64
        csblk[g * 64:(g + 1) * 64, 128 + g * 64:128 + (g + 1) * 64] = s64
    k["csblk"] = csblk.astype(np.float32).astype(NPBF)
    t1 = np.zeros((128, 128), np.float64)
    t1[0:64, 0:64] = c64
    t1[0:64, 64:128] = -s64
    t1[64:128, 0:64] = -s64
    t1[64:128, 64:128] = -c64
    k["t1tab"] = t1.astype(np.float32).astype(NPBF)
    t2 = np.arange(128)[:, None, None].astype(np.float64)
    k1 = np.arange(64)[None, :, None].astype(np.float64)
    k2 = np.arange(128)[None, None, :].astype(np.float64)
    phi = 2 * np.pi * ((t2 * (k1 + 64 * k2)) % 8192) / 8192.0
    k["mc"] = (np.cos(phi) / np.sqrt(128.0)).astype(np.float32).astype(NPBF).reshape(128, 64 * 128)
    k["ms"] = (np.sin(phi) / np.sqrt(128.0)).astype(np.float32).astype(NPBF).reshape(128, 64 * 128)
    tt = (np.arange(2)[None, :, None] * 128 + np.arange(128)[:, None, None]).astype(np.float64)
    kk2 = np.arange(256)[None, None, :].astype(np.float64)
    ph = 2 * np.pi * ((tt * kk2) % 256) / 256.0
    k["c256"] = (np.cos(ph) / 16.0).astype(np.float32).astype(NPBF).reshape(128, 512)
    k["s256n"] = (-np.sin(ph) / 16.0).astype(np.float32).astype(NPBF).reshape(128, 512)
    sel = np.zeros((2, 2, 128), np.float32)
    sel[0, 0, :] = 1
    sel[1, 1, :] = 1
    k["sel"] = sel.reshape(2, 256)
    k["i2"] = np.eye(2, dtype=np.float32)
    return k


CONST_SPECS = [
    ("ident", [128, 128], BF16), ("cosT", [128, NTOK], F32), ("sinT", [128, NTOK], F32),
    ("mask_next", [128, 128], BF16), ("mask_prev", [128, 128], BF16), ("ones", [128, 128], BF16),
    ("csblk", [128, 256], BF16), ("t1tab", [128, 128], BF16), ("mc", [128, 8192], BF16), ("ms", [128, 8192], BF16),
    ("c256", [128, 512], BF16), ("s256n", [128, 512], BF16), ("sel", [2, 256], F32), ("i2", [2, 2], F32),
]

IN_SPECS = [
    ("x", [S, D]), ("ctx", [LCTX, D]), ("csT", [128, 16]),
    ("w_ada", [2, D, 6 * D]), ("b_ada", [2, 6 * D]),
    ("g_pre_mix", [2, D]), ("g_post_mix", [2, D]), ("g_pre_ff", [2, D]), ("g_post_ff", [2, D]),
    ("w_inp", [2, D, NCOL]), ("sinkT", [2, 128, 4]), ("w_sguT", [2, 128, 512]), ("b_sguT", [2, 128, 4]),
    ("g_sgu", [2, 256]), ("g_mixT", [2, 128, 8]), ("w_outp", [2, D, D]),
    ("w_ff1", [2, D, DFF]), ("w_ff2", [2, DFF, D]),
]


class Builder:
    def __init__(self, nlayers=2, stop_after=None, debug=()):
        self.nlayers = nlayers
        self.stop_after = stop_after
        self.debug = set(debug)
        nc = bass.Bass("TRN2", target_bir_lowering=False)
        self.nc = nc
        self.I = {}
        for name, shape in IN_SPECS:
            self.I[name] = nc.dram_tensor(name, shape, F32, kind="ExternalInput").ap()
        self.C = {}
        for name, shape, dt in CONST_SPECS:
            self.C[name] = nc.dram_tensor("c_" + name, shape, dt, kind="ExternalInput").ap()
        self.out = nc.dram_tensor("out", [S, D], F32, kind="ExternalOutput").ap()
        kw = dict(kind="ExternalOutput") if "scratch_out" in self.debug else {}
        self.qT_d = nc.dram_tensor("qT_d", [4, 128, NTOK], BF16, **kw).ap()
        self.sguT_d = nc.dram_tensor("sguT_d", [2, 128, NTOK], BF16, **kw).ap()
        self.AB_d = nc.dram_tensor("AB_d", [NTOK, 512], BF16, **kw).ap()
        self.kT_d = nc.dram_tensor("kT_d", [128, NTOK], BF16, **kw).ap()
        self.vt_d = nc.dram_tensor("vt_d", [128, 66, 128], BF16, **kw).ap()
        self.fourT_d = nc.dram_tensor("fourT_d", [128, 2, NTOK], BF16, **kw).ap()
        self.xmid_d = nc.dram_tensor("xmid_d", [NTOK, D], F32, **kw).ap()
        self.x1_d = nc.dram_tensor("x1_d", [NTOK, D], F32, **kw).ap()
        self.G_d = nc.dram_tensor("G_d", [2, 4, D], F32, **kw).ap()
        self.modT_d = nc.dram_tensor("modT_d", [128, 64], F32, **kw).ap()
        self.dbg = {}
        self.P = Prog(nc)

    def act(self, out, in_, func, r, w, **kw):
        return self.P.op("act", lambda e: e.activation(out=out, in_=in_, func=func, **kw), r, w)

    def tt(self, eng, out, in0, in1, op, r, w):
        return self.P.op(eng, lambda e: e.tensor_tensor(out=out, in0=in0, in1=in1, op=op), r, w)

    def ts(self, eng, out, in0, s1, s2, op0, op1, r, w):
        if s2 is None:
            return self.P.op(eng, lambda e: e.tensor_scalar(out=out, in0=in0, scalar1=s1, scalar2=None, op0=op0), r, w)
        return self.P.op(eng, lambda e: e.tensor_scalar(out=out, in0=in0, scalar1=s1, scalar2=s2, op0=op0, op1=op1), r, w)

    def stt(self, out, in0, scalar, in1, op0, op1, r, w):
        return self.P.op("dve", lambda e: e.scalar_tensor_tensor(out=out, in0=in0, scalar=scalar, in1=in1, op0=op0, op1=op1), r, w)

    def cp(self, eng, out, in_, r, w):
        if eng == "act":
            return self.P.op("act", lambda e: e.activation(out=out, in_=in_, func=AF.Copy), r, w)
        return self.P.op(eng, lambda e: e.tensor_copy(out=out, in_=in_), r, w)

    def recip(self, out, in_, r, w):
        return self.P.op("dve", lambda e: e.reciprocal(out=out, in_=in_), r, w)

    def rstd(self, out, ss, n, key_in, key_out, tmpkey):
        self.act(out, ss, AF.Sqrt, r=key_in, w=[tmpkey], scale=1.0 / n, bias=self.eps_t[:, 0:1])
        self.recip(out, out, r=[tmpkey], w=key_out)

    def mm(self, specs, r, w):
        def fn(e):
            ins = None
            for sp_ in specs:
                ins = e.matmul(out=sp_[0], lhsT=sp_[1], rhs=sp_[2], start=sp_[3], stop=sp_[4], skip_group_check=True)
            return ins
        return self.P.op("pe", fn, r, w)

    def tr(self, specs, r, w):
        def fn(e):
            ins = None
            for o_, i_ in specs:
                ins = e.transpose(out=o_, in_=i_, identity=self.ident[:])
            return ins
        return self.P.op("pe", fn, r, w)

    def build(self):
        P = self.P
        nc = self.nc
        self.ident = P.sb("ident", [128, 128], BF16)
        self.ones = P.sb("ones", [128, 128], BF16)
        self.eps_t = P.sb("eps_t", [128, 1], F32)
        self.csil = P.sb("csil", [128, 8, 2], F32)
        self.modT = P.sb("modT", [128, 4, 8, 2], F32)
        self.psum = P.ps("psum", [128, 8, 512], F32)
        self.banks = [self.psum[:, i, :] for i in range(8)]
        P.dma("sp", self.ident[:], self.C["ident"][:, :], w=["ident"])
        P.dma("sp", self.ones[:], self.C["ones"][:, :], w=["ones"])
        P.op("dve", lambda e: e.memset(self.eps_t[:], EPS), w=["eps"])
        P.dma("sp", self.csil[:].rearrange("p c r -> p (c r)"), self.I["csT"][:, :], w=["csil"])
        self.act(self.csil[:], self.csil[:], AF.Silu, r=["csil"], w=["csil"])
        P.barrier()
        for l in range(self.nlayers):
            self.layer(l)
        P.barrier()
        P.emit()
        P.close()
        return nc

    def src_rows(self, l, tok0, n):
        if l == 0:
            if tok0 >= S:
                return self.I["ctx"][tok0 - S:tok0 - S + n, :]
            return self.I["x"][tok0:tok0 + n, :]
        return self.x1_d[tok0:tok0 + n, :]

    def dst_rows(self, l, tok0, n):
        if l == self.nlayers - 1 and self.nlayers == 2:
            assert tok0 < S
            return self.out[tok0:tok0 + n, :]
        if self.nlayers == 1 and tok0 < S:
            return self.out[tok0:tok0 + n, :]
        return self.x1_d[tok0:tok0 + n, :]

    def layer(self, l):
        P = self.P
        last = (l == 1)
        self.l = l
        self.ctx_full = not last
        P.push()
        self.gmixT = P.sb("gmixT", [128, 8], F32)
        self.esink = P.sb("esink", [128, 4], F32)
        self.bsT = P.sb("bsT", [128, 4], F32)
        self.fourT = P.sb("fourT", [128, 2, NTOK], BF16)
        P.dma("sp", self.gmixT[:], self.I["g_mixT"][l], w=["gmixT"])
        P.dma("sp", self.esink[:], self.I["sinkT"][l], w=["esink"])
        P.dma("sp", self.bsT[:], self.I["b_sguT"][l], w=["bsT"])
        self.act(self.esink[:], self.esink[:], AF.Exp, r=["esink"], w=["esink"])
        self.adaln(l)
        if self.stop_after == "adaln":
            P.pop()
            return
        self.phaseA(l)
        if self.stop_after == "A":
            P.pop()
            return
        self.phaseF(l)
        if self.stop_after == "F":
            P.pop()
            return
        self.phaseC(l)
        if self.stop_after == "C":
            P.pop()
            return
        P.pop()
        self.phaseD(l)

    def adaln(self, l):
        P = self.P
        I = self.I
        P.push()
        wblk = [P.sb("wada%d" % i, [128, 8, 512], F32) for i in range(2)]
        modrow = P.sb("modrow", [2, 6 * D], F32)
        brow = P.sb("brow", [2, 6 * D], F32)
        gains = P.sb("gains", [2, 4, D], F32)
        rows = P.sb("rows", [2, 6, D], F32)
        sel = P.sb("sel", [2, 2, 128], F32)
        i2 = P.sb("i2", [2, 2], F32)
        P.dma("sp", brow[:], I["b_ada"][l:l + 1, :].partition_broadcast(2), w=["brow"])
        for i, nm in enumerate(("g_pre_mix", "g_post_mix", "g_pre_ff", "g_post_ff")):
            P.dma("sp", gains[:, i, :], I[nm][l:l + 1, :].partition_broadcast(2), w=[("gains", i)])
        P.dma("sp", sel[:].rearrange("k r m -> k (r m)"), self.C["sel"][:, :], w=["sel"])
        P.dma("sp", i2[:], self.C["i2"][:, :], w=["i2"])
        for nb in range(12):
            wb = wblk[nb % 2]
            P.dma("sp", wb[:], I["w_ada"][l, :, nb * 512:(nb + 1) * 512].rearrange("(c p) n -> p c n", p=128),
                  w=[("wada", nb % 2)])
            bk = self.banks[nb % 2]
            self.mm([(bk[0:2, :], self.csil[:, c, :], wb[:, c, :], c == 0, c == 7) for c in range(8)],
                    r=[("wada", nb % 2), "csil"], w=[("ps", nb % 2)])
            self.tt("dve", modrow[:, nb * 512:(nb + 1) * 512], bk[0:2, :], brow[:, nb * 512:(nb + 1) * 512], ALU.add,
                    r=[("ps", nb % 2), "brow"], w=[("modrow", nb)])
        mr = lambda i: modrow[:, i * D:(i + 1) * D]
        mk = lambda i: [("modrow", 2 * i), ("modrow", 2 * i + 1)]
        self.stt(rows[:, 0, :], mr(1), 1.0, gains[:, 0, :], ALU.add, ALU.mult, r=mk(1) + [("gains", 0)], w=[("rows", 0)])
        self.cp("dve", rows[:, 1, :], mr(0), r=mk(0), w=[("rows", 1)])
        self.stt(rows[:, 2, :], mr(4), 1.0, gains[:, 2, :], ALU.add, ALU.mult, r=mk(4) + [("gains", 2)], w=[("rows", 2)])
        self.cp("dve", rows[:, 3, :], mr(3), r=mk(3), w=[("rows", 3)])
        self.tt("dve", rows[:, 4, :], mr(2), gains[:, 1, :], ALU.mult, r=mk(2) + [("gains", 1)], w=[("rows", 4)])
        self.tt("dve", rows[:, 5, :], mr(5), gains[:, 3, :], ALU.mult, r=mk(5) + [("gains", 3)], w=[("rows", 5)])
        bk = self.banks[2]
        specs = []
        for v in range(4):
            for c in range(8):
                col = (v * 8 + c) * 2
                specs.append((bk[:, col:col + 2], rows[:, v, c * 128:(c + 1) * 128], i2[:, :], True, True))
        self.mm(specs, r=[("rows", v) for v in range(4)] + ["i2"], w=[("ps", 2)])
        self.cp("dve", self.modT[:].rearrange("p v c r -> p (v c r)"), bk[:, 0:64], r=[("ps", 2)], w=["modT"])
        if "scratch_out" in self.debug:
            P.dma("pool", self.modT_d[:, :], self.modT[:].rearrange("p v c r -> p (v c r)"), r=["modT"], w=["modT_d"])
        P.dma("pool", self.G_d[:, 0, :], rows[:, 4, :], r=[("rows", 4)], w=["G_d0"])
        P.dma("pool", self.G_d[:, 1, :], rows[:, 5, :], r=[("rows", 5)], w=["G_d1"])
        P.barrier()
        P.pop()

    def new_staging(self, width=1024):
        self._stg = [self.P.sb("stg%d" % i, [128, width], F32) for i in range(2)]
        self._stg_w = width
        self._stg_n = 0

    def load_weight(self, dst, src_fn, nchunks, width, tag):
        P = self.P
        piece = self._stg_w
        keys = []
        for c in range(nchunks):
            for o in range(0, width, piece):
                wd = min(piece, width - o)
                n = self._stg_n
                self._stg_n += 1
                s_ = self._stg[n % 2]
                P.dma("sp", s_[:, 0:wd], src_fn(c)[:, o:o + wd], w=[("stg", n % 2)])
                eng = ("pool", "dve", "act")[n % 3]
                self.cp(eng, dst[:, c, o:o + wd], s_[:, 0:wd], r=[("stg", n % 2)], w=[(tag, c, o)])
                keys.append((tag, c, o))
        return keys

    def phaseA(self, l):
        P = self.P
        I = self.I
        C = self.C
        P.push()
        self.winb = P.sb("winb", [128, 8, NCOL], BF16)
        self.new_staging()
        self.wink = self.load_weight(self.winb, lambda c: I["w_inp"][l, c * 128:(c + 1) * 128, :], 8, NCOL, "winb")
        A = type("A", (), {})()
        self.A = A
        A.xt = [P.sb("xt%d" % i, [128, D], F32) for i in range(4)]
        A.junk = P.sb("junk", [128, D], BF16)
        A.ss = P.sb("ssA", [128, 8], F32)
        A.rs = P.sb("rsA", [128, 8], F32)
        A.xn = [P.sb("xn%d" % i, [128, D], BF16) for i in range(4)]
        A.hxT = [P.sb("hxT%d" % i, [128, 8, 512], BF16) for i in range(2)]
        A.cos = [P.sb("cos%d" % i, [128, 512], F32) for i in range(2)]
        A.sin = [P.sb("sin%d" % i, [128, 512], F32) for i in range(2)]
        A.t1 = [P.sb("t1_%d" % i, [128, 512], F32) for i in range(2)]
        A.t2 = [P.sb("t2_%d" % i, [128, 512], F32) for i in range(2)]
        A.qT = [P.sb("qTt%d" % i, [128, 4, 512], BF16) for i in range(2)]
        A.fT = P.sb("fTt", [128, 2, 512], BF16)
        A.kTt = [P.sb("kTt%d" % i, [128, 512], BF16) for i in range(2)]
        A.vtt = [P.sb("vtt%d" % i, [128, 4, 128], BF16) for i in range(2)]
        A.ABt = [P.sb("ABt%d" % i, [128, 4, 512], BF16) for i in range(2)]
        A.ug = P.sb("ug", [128, 4, 256], F32)
        A.gg = P.sb("gg", [128, 4, 256], F32)
        A.sq = P.sb("sqA", [128, 4, 256], F32)
        A.ssh = P.sb("ssh", [128, 16], F32)
        A.rsh = P.sb("rsh", [128, 16], F32)
        A.vc = P.sb("vc", [128, 4, 256], BF16)
        A.so = P.sb("so", [128, 4, 256], F32)
        A.ss2 = P.sb("ss2", [128, 4], F32)
        A.rs2 = P.sb("rs2", [128, 4], F32)
        A.son = P.sb("son", [128, 4, 256], BF16)
        A.sgt = [P.sb("sgt%d" % i, [128, 2, 512], BF16) for i in range(2)]
        A.gsg = P.sb("gsg", [128, 256], F32)
        A.wsT = P.sb("wsT", [128, 4, 128], BF16)
        A.wsf = P.sb("wsf", [128, 512], F32)
        A.csblk = P.sb("csblk", [128, 256], BF16)
        P.dma("sp", A.gsg[:], I["g_sgu"][l:l + 1, :].partition_broadcast(128), w=["gsg"])
        P.dma("sp", A.wsf[:], I["w_sguT"][l], w=["wsf"])
        self.cp("dve", A.wsT[:].rearrange("p h q -> p (h q)"), A.wsf[:], r=["wsf"], w=["wsT"])
        P.dma("sp", A.csblk[:], C["csblk"][:, :], w=["csblk"])
        A.trbanks = Ring([(self.banks[i].bitcast(BF16), ("ps", i)) for i in range(2)])
        A.pbanks = Ring([(self.banks[i], ("ps", i)) for i in range(2, 8)])
        A.xi = 0
        tiles = [(S, 2, True)] + [(i * 512, 4, False) for i in range(S // 512)]
        if "A_tiles" in self.debug:
            tiles = tiles[:3]
        self.A_front(l, 0, *tiles[0])
        for i, t in enumerate(tiles):
            if i + 1 < len(tiles):
                self.A_front(l, i + 1, *tiles[i + 1])
            self.A_back(l, i, *t)
        P.barrier()
        P.pop()

    def A_front(self, l, it, tok0, nsub, is_ctx):
        P = self.P
        A = self.A
        mi = 1 if is_ctx else 0
        T = nsub * 128
        hx = A.hxT[it % 2]
        hk = ("hxT", it % 2)
        for s in range(nsub):
            xi = A.xi % 4
            A.xi += 1
            P.dma("sp", A.xt[xi][:], self.src_rows(l, tok0 + s * 128, 128), w=[("xt", xi)])
            self.act(A.junk[:], A.xt[xi][:], AF.Square, r=[("xt", xi)], w=["junk", ("ssA", s)], accum_out=A.ss[:, s:s + 1])
            self.rstd(A.rs[:, s:s + 1], A.ss[:, s:s + 1], float(D), [("ssA", s)], [("rsA", s)], ("rsAt", s))
            self.ts("dve", A.xn[xi][:], A.xt[xi][:], A.rs[:, s:s + 1], None, ALU.mult, None,
                    r=[("xt", xi), ("rsA", s)], w=[("xn", xi)])
        xis = [(A.xi - nsub + s) % 4 for s in range(nsub)]
        for cp_ in range(4):
            tb, sk = A.trbanks.next()
            self.tr([(tb[:, j * 512 + s * 128: j * 512 + (s + 1) * 128], A.xn[xis[s]][:, (2 * cp_ + j) * 128:(2 * cp_ + j + 1) * 128])
                     for j in range(2) for s in range(nsub)],
                    r=[("xn", xi) for xi in xis] + ["ident"], w=[sk])
            for j in range(2):
                c = 2 * cp_ + j
                slot = tb[:, j * 512:(j + 1) * 512]
                if cp_ % 2 == 0:
                    self.act(hx[:, c, 0:T], slot[:, 0:T], AF.Identity, r=[sk, "modT"], w=[hk + (c,)],
                             scale=self.modT[:, 0, c, mi:mi + 1], bias=self.modT[:, 1, c, mi:mi + 1])
                else:
                    self.ts("dve", hx[:, c, 0:T], slot[:, 0:T], self.modT[:, 0, c, mi:mi + 1], self.modT[:, 1, c, mi:mi + 1],
                            ALU.mult, ALU.add, r=[sk, "modT"], w=[hk + (c,)])

    def A_back(self, l, it, tok0, nsub, is_ctx):
        P = self.P
        A = self.A
        C = self.C
        if "A_nob" in self.debug:
            return
        T = nsub * 128
        hx = A.hxT[it % 2]
        hkeys = [("hxT", it % 2, c) for c in range(8)]
        W = self.winb
        full = (not is_ctx) or self.ctx_full
        sl = it % 2
        P.dma("sp", A.cos[sl][:, 0:T], C["cosT"][:, tok0:tok0 + T], w=[("cos", sl)])
        P.dma("sp", A.sin[sl][:, 0:T], C["sinT"][:, tok0:tok0 + T], w=[("sin", sl)])

        def fm_chunk(fc):
            bk, bkk = A.pbanks.next()
            self.mm([(bk[:, 0:T], W[:, dc, fc * 128:(fc + 1) * 128], hx[:, dc, 0:T], dc == 0, dc == 7) for dc in range(8)],
                    r=hkeys + self.wink, w=[bkk])
            return bk, bkk

        def rope(fc, fcr, out_ap, okey, j):
            bq, kq = fm_chunk(fc)
            br, kr = fm_chunk(fcr)
            self.tt("dve", A.t1[j % 2][:, 0:T], bq[:, 0:T], A.cos[sl][:, 0:T], ALU.mult, r=[kq, ("cos", sl)], w=[("t1", j % 2)])
            self.tt("dve", A.t2[j % 2][:, 0:T], br[:, 0:T], A.sin[sl][:, 0:T], ALU.mult, r=[kr, ("sin", sl)], w=[("t2", j % 2)])
            self.tt("pool", out_ap, A.t1[j % 2][:, 0:T], A.t2[j % 2][:, 0:T], ALU.add, r=[("t1", j % 2), ("t2", j % 2)], w=okey)

        rope(8, 9, A.kTt[sl][:, 0:T], [("kTt", sl)], 0)
        P.dma("pool", self.kT_d[:, tok0:tok0 + T], A.kTt[sl][:, 0:T], r=[("kTt", sl)], w=[("kT_d", tok0)])
        if full:
            for c in range(4):
                rope(c, 4 + c, A.qT[sl][:, c, 0:T], [("qTt", sl, c)], c + 1)
            P.dma("pool", self.qT_d[:, :, tok0:tok0 + T].rearrange("c p t -> p c t"), A.qT[sl][:, :, 0:T],
                  r=[("qTt", sl, c) for c in range(4)], w=[("qT_d", tok0)])
            for cf in range(2):
                bk, bkk = fm_chunk(10 + cf)
                self.cp("act", A.fT[:, cf, 0:T], bk[:, 0:T], r=[bkk], w=[("fT", cf)])
        if "A_nob2" in self.debug:
            return
        for s in range(nsub):
            b1, k1 = A.pbanks.next()
            ncol = 384 if full else 128
            self.mm([(b1[:, 0:ncol], hx[:, dc, s * 128:(s + 1) * 128], W[:, dc, 1536:1536 + ncol], dc == 0, dc == 7) for dc in range(8)],
                    r=hkeys + self.wink, w=[k1])
            self.cp("act", A.vtt[sl][:, s, :], b1[:, 0:128], r=[k1], w=[("vtt", sl, s)])
            if full:
                self.act(A.ug[:, s, :], b1[:, 128:384], AF.Gelu_apprx_tanh, r=[k1], w=[("ug", s)])
                b2, k2 = A.pbanks.next()
                self.mm([(b2[:, 0:256], hx[:, dc, s * 128:(s + 1) * 128], W[:, dc, 1920:2176], dc == 0, dc == 7) for dc in range(8)],
                        r=hkeys + self.wink, w=[k2])
                self.act(A.gg[:, s, :], b2[:, 0:256], AF.Gelu_apprx_tanh, r=[k2], w=[("gg", s)])
        blk0 = tok0 // 128
        P.dma("pool", self.vt_d[:, blk0:blk0 + nsub, :], A.vtt[sl][:, 0:nsub, :], r=[("vtt", sl, s) for s in range(nsub)],
              w=[("vt_d", tok0)])
        if not full or "A_nob3" in self.debug:
            return
        ns = nsub
        ggk = [("gg", s) for s in range(ns)]
        ugk = [("ug", s) for s in range(ns)]
        for s in range(ns):
            bk, bkk = A.pbanks.next()
            specs = []
            for cf in range(2):
                specs.append((bk[:, cf * 256:(cf + 1) * 256], A.fT[:, cf, s * 128:(s + 1) * 128], A.csblk[:, :], True, True))
            self.mm(specs, r=[("fT", 0), ("fT", 1), "csblk"], w=[bkk])
            self.cp("act", A.ABt[sl][:, s, :], bk[:, :], r=[bkk], w=[("ABt", sl, s)])
        P.dma("pool", self.AB_d[tok0:tok0 + T, :].rearrange("(s p) c -> p s c", p=128), A.ABt[sl][:, 0:ns, :],
              r=[("ABt", sl, s) for s in range(ns)], w=[("AB_d", tok0)])
        if "A_nob4" in self.debug:
            return
        self.tt("pool", A.sq[:, 0:ns, :], A.gg[:, 0:ns, :], A.gg[:, 0:ns, :], ALU.mult, r=ggk, w=["sq"])
        P.op("dve", lambda e: e.tensor_reduce(out=A.ssh[:, 0:ns * 4], in_=A.sq[:, 0:ns, :].rearrange("p s (h d) -> p (s h) d", h=4),
                                              axis=AX.X, op=ALU.add), r=["sq"], w=["ssh"])
        self.rstd(A.rsh[:, 0:ns * 4], A.ssh[:, 0:ns * 4], 64.0, ["ssh"], ["rsh"], "rsht")
        self.tt("dve", A.sq[:, 0:ns, :].rearrange("p s (h d) -> p (s h) d", h=4),
                A.gg[:, 0:ns, :].rearrange("p s (h d) -> p (s h) d", h=4),
                A.rsh[:, 0:ns * 4].unsqueeze(2).to_broadcast([128, ns * 4, 64]), ALU.mult, r=ggk + ["rsh"], w=["sq"])
        self.tt("dve", A.vc[:, 0:ns, :], A.sq[:, 0:ns, :], A.gsg[:, :].unsqueeze(1).to_broadcast([128, ns, 256]), ALU.mult,
                r=["sq", "gsg"], w=["vc"])
        for hp in range(2):
            bk, bkk = A.pbanks.next()
            specs = []
            for hh in range(2):
                h = hp * 2 + hh
                o = bk[:, hh * 256:hh * 256 + ns * 64].rearrange("p (s d) -> p s d", d=64)
                specs.append((o, A.wsT[:, h, :], A.vc[:, 0:ns, h * 64:(h + 1) * 64], True, True))
            self.mm(specs, r=["vc", "wsT"], w=[bkk])
            for hh in range(2):
                h = hp * 2 + hh
                self.stt(A.so[:, 0:ns, h * 64:(h + 1) * 64], bk[:, hh * 256:hh * 256 + ns * 64].rearrange("p (s d) -> p s d", d=64),
                         self.bsT[:, h:h + 1], A.ug[:, 0:ns, h * 64:(h + 1) * 64], ALU.add, ALU.mult,
                         r=[bkk, "bsT"] + ugk, w=[("so", h)])
        sok = [("so", h) for h in range(4)]
        self.tt("pool", A.sq[:, 0:ns, :], A.so[:, 0:ns, :], A.so[:, 0:ns, :], ALU.mult, r=sok, w=["sq"])
        P.op("dve", lambda e: e.tensor_reduce(out=A.ss2[:, 0:ns], in_=A.sq[:, 0:ns, :], axis=AX.X, op=ALU.add), r=["sq"], w=["ss2"])
        self.rstd(A.rs2[:, 0:ns], A.ss2[:, 0:ns], 256.0, ["ss2"], ["rs2"], "rs2t")
        self.tt("dve", A.son[:, 0:ns, :], A.so[:, 0:ns, :], A.rs2[:, 0:ns].unsqueeze(2).to_broadcast([128, ns, 256]), ALU.mult,
                r=sok + ["rs2"], w=["son"])
        tb, sk = A.trbanks.next()
        self.tr([(tb[:, cc * 512 + s * 128: cc * 512 + (s + 1) * 128], A.son[:, s, cc * 128:(cc + 1) * 128])
                 for cc in range(2) for s in range(ns)], r=["son", "ident"], w=[sk])
        for cc in range(2):
            self.act(A.sgt[sl][:, cc, 0:T], tb[:, cc * 512: cc * 512 + T], AF.Identity, r=[sk, "gmixT"], w=[("sgt", sl, cc)],
                     scale=self.gmixT[:, 4 + cc:5 + cc])
        P.dma("pool", self.sguT_d[:, :, tok0:tok0 + T].rearrange("c p t -> p c t"), A.sgt[sl][:, :, 0:T],
              r=[("sgt", sl, 0), ("sgt", sl, 1)], w=[("sguT_d", tok0)])

    def four_finish(self, ybank_ap, ykey, nk, tok_of, g):
        P = self.P
        F = self.F
        i = F.fi % 2
        F.fi += 1
        ysb = F.ysb[i]
        self.cp("act", ysb[:, 0:nk, :], ybank_ap, r=[ykey], w=[("ysb", i)])
        self.tt("pool", F.ysq[:, 0:nk, :], ysb[:, 0:nk, :], ysb[:, 0:nk, :], ALU.mult, r=[("ysb", i)], w=["ysq"])
        P.op("dve", lambda e: e.tensor_reduce(out=F.yss[:, 0:nk], in_=F.ysq[:, 0:nk, :], axis=AX.X, op=ALU.add), r=["ysq"], w=["yss"])
        self.rstd(F.yrs[:, 0:nk], F.yss[:, 0:nk], 256.0, ["yss"], ["yrs"], "yrst")
        self.tt("dve", F.ynb[i][:, 0:nk, :], ysb[:, 0:nk, :], F.yrs[:, 0:nk].unsqueeze(2).to_broadcast([128, nk, 256]), ALU.mult,
                r=[("ysb", i), "yrs"], w=[("ynb", i)])
        tb, sk = F.trbanks.next()
        self.tr([(tb[:, (j * 2 + cc) * 128:(j * 2 + cc + 1) * 128], F.ynb[i][:, j, cc * 128:(cc + 1) * 128])
                 for j in range(nk) for cc in range(2)], r=[("ynb", i), "ident"], w=[sk])
        for j in range(nk):
            for cc in range(2):
                self.act(tok_of(j, cc), tb[:, (j * 2 + cc) * 128:(j * 2 + cc + 1) * 128], AF.Identity, r=[sk, "gmixT"],
                         w=[("fourT", g, j, cc)], scale=self.gmixT[:, 6 + cc:7 + cc])

    def phaseF(self, l):
        P = self.P
        C = self.C
        P.push()
        F = type("F", (), {})()
        self.F = F
        F.fi = 0
        F.ab1 = P.sb("ab1", [128, 128, 128], BF16)
        F.p1 = P.sb("p1", [128, 64, 2, 128], BF16)
        F.mc = P.sb("mc", [128, 64, 128], BF16)
        F.ms = P.sb("ms", [128, 64, 128], BF16)
        F.t1tab = P.sb("t1tab", [128, 128], BF16)
        F.ysb = [P.sb("ysb%d" % i, [128, 4, 256], F32) for i in range(2)]
        F.ysq = P.sb("ysq", [128, 4, 256], F32)
        F.yss = P.sb("yss", [128, 4], F32)
        F.yrs = P.sb("yrs", [128, 4], F32)
        F.ynb = [P.sb("ynb%d" % i, [128, 4, 256], BF16) for i in range(2)]
        F.trbanks = Ring([(self.banks[i].bitcast(BF16), ("ps", i)) for i in range(2)])
        P.dma("sp", F.mc[:].rearrange("p a b -> p (a b)"), C["mc"][:, :], w=["mc"])
        P.dma("sp", F.ms[:].rearrange("p a b -> p (a b)"), C["ms"][:, :], w=["ms"])
        P.dma("sp", F.t1tab[:], C["t1tab"][:, :], w=["t1tab"])
        pb = Ring([(self.banks[i], ("ps", i)) for i in range(2, 5)])
        yb = Ring([(self.banks[i], ("ps", i)) for i in range(5, 8)])
        F.p1b = P.sb("p1b", [128, 64, 2, 128], BF16)
        p1s = [F.p1, F.p1b]
        for hh in range(2):
            for ab in range(2):
                src = self.AB_d[0:S, hh * 256 + ab * 128: hh * 256 + ab * 128 + 128].rearrange("(t1 t2) c -> t1 t2 c", t2=128)
                P.dma("sp", F.ab1[ab * 64:(ab + 1) * 64, :, :], src, w=[("ab1", ab)])
            for c4 in range(32):
                bk, bkk = pb.next()
                specs = []
                for j in range(4):
                    ch = c4 * 4 + j
                    specs.append((bk[:, j * 128:(j + 1) * 128], F.ab1[:, :, ch], F.t1tab[:, :], True, True))
                self.mm(specs, r=[("ab1", 0), ("ab1", 1), "t1tab"], w=[bkk])
                o = p1s[hh][:, :, :, c4 * 4:(c4 + 1) * 4].rearrange("p k r c -> p c r k")
                i_ = bk[:, :].rearrange("p (c r k) -> p c r k", c=4, r=2)
                if c4 % 2 == 0:
                    self.cp("act", o, i_, r=[bkk], w=[("p1", hh, c4)])
                else:
                    self.cp("dve", o, i_, r=[bkk], w=[("p1", hh, c4)])
        p1keys = [("p1", hh, c4) for hh in range(2) for c4 in range(32)]
        for kg in range(16):
            bka, kka = yb.next()
            for half2 in range(2):
                if half2 == 1:
                    bka, kka = yb.next()
                specs = []
                for j in range(2):
                    k1 = kg * 4 + half2 * 2 + j
                    for hh in range(2):
                        o = bka[:, j * 256 + hh * 128: j * 256 + hh * 128 + 128]
                        specs.append((o, F.mc[:, k1, :], p1s[hh][:, k1, 0, :], True, False))
                        specs.append((o, F.ms[:, k1, :], p1s[hh][:, k1, 1, :], False, True))
                self.mm(specs, r=p1keys + ["mc", "ms"], w=[kka])
                k1base = kg * 4 + half2 * 2

                def tok_of(j, cc, k1base=k1base):
                    return self.fourT[:, cc, 0:S].rearrange("p (k2 k1) -> p k1 k2", k1=64)[:, k1base + j, :]
                self.four_finish(bka[:, :].rearrange("p (j c) -> p j c", j=2), kka, 2, tok_of, ("L", kg, half2))
        if self.ctx_full:
            cab = P.sb("cab", [128, 2, 512], BF16)
            c256 = P.sb("c256", [128, 2, 256], BF16)
            s256 = P.sb("s256n", [128, 2, 256], BF16)
            P.dma("sp", cab[:], self.AB_d[S:S + 256, :].rearrange("(tb p) c -> p tb c", p=128), w=["cab"])
            P.dma("sp", c256[:].rearrange("p a b -> p (a b)"), C["c256"][:, :], w=["c256"])
            P.dma("sp", s256[:].rearrange("p a b -> p (a b)"), C["s256n"][:, :], w=["s256"])
            for kb in range(2):
                bk, bkk = yb.next()
                specs = []
                for hh in range(2):
                    o = bk[:, hh * 128:(hh + 1) * 128]
                    for tb in range(2):
                        specs.append((o, c256[:, tb, kb * 128:(kb + 1) * 128], cab[:, tb, hh * 256:hh * 256 + 128], tb == 0, False))
                        specs.append((o, s256[:, tb, kb * 128:(kb + 1) * 128], cab[:, tb, hh * 256 + 128:hh * 256 + 256], False, tb == 1))
                self.mm(specs, r=["cab", "c256", "s256"], w=[bkk])

                def tok_of(j, cc, kb=kb):
                    return self.fourT[:, cc, S + kb * 128:S + (kb + 1) * 128]
                self.four_finish(bk[:, 0:256].rearrange("p (j c) -> p j c", j=1), bkk, 1, tok_of, ("C", kb))
        if "scratch_out" in self.debug and l == 0:
            P.barrier()
            P.dma("pool", self.fourT_d[:, :, :], self.fourT[:], w=["fourT_d"])
        P.barrier()
        P.pop()

    def phaseC(self, l):
        P = self.P
        I = self.I
        C = self.C
        P.push()
        self.woutb = P.sb("woutb", [128, 8, D], BF16)
        self.new_staging()
        self.woutk = self.load_weight(self.woutb, lambda c: I["w_outp"][l, c * 128:(c + 1) * 128, :], 8, D, "woutb")
        K = type("K", (), {})()
        self.K = K
        self.kT_all = P.sb("kT_all", [128, NTOK], BF16)
        self.vt_all = P.sb("vt_all", [128, 66, 128], BF16)
        P.dma("sp", self.kT_all[:], self.kT_d[:, :], w=["kT_all"])
        P.dma("sp", self.vt_all[:], self.vt_d[:, :, :], w=["vt_all"])
        K.qt = [P.sb("qt%d" % i, [128, 4, 512], BF16) for i in range(2)]
        K.sgt = [P.sb("sgtC%d" % i, [128, 2, 512], BF16) for i in range(2)]
        K.xt = [P.sb("xtC%d" % i, [128, D], F32) for i in range(4)]
        K.PT = [P.sb("PT%d" % i, [128, 2, 512], BF16) for i in range(6)]
        K.oc = [P.sb("oc%d" % i, [128, 512], F32) for i in range(2)]
        K.dd = [P.sb("dd%d" % i, [128, 512], F32) for i in range(2)]
        K.ar = [P.sb("ar%d" % i, [128, 512], F32) for i in range(2)]
        K.sqT = P.sb("sqT", [128, 4, 512], BF16)
        K.aT = P.sb("aT", [128, 4, 512], BF16)
        K.ys2 = [P.sb("ysC%d" % i, [128, D], F32) for i in range(2)]
        K.yc2 = [P.sb("ycC%d" % i, [128, D], F32) for i in range(2)]
        K.tmp = P.sb("tmpC", [128, D], F32)
        K.junk = P.sb("junkC", [128, D], BF16)
        K.ssa = P.sb("ssa", [128, 4], F32)
        K.rsa = P.sb("rsa", [128, 4], F32)
        K.ssy = P.sb("ssy", [128, 4], F32)
        K.rsy = P.sb("rsy", [128, 4], F32)
        K.G1 = P.sb("G1bc", [128, 2, D], F32)
        K.mnext = P.sb("mnext", [128, 128], BF16)
        K.mprev = P.sb("mprev", [128, 128], BF16)
        P.dma("sp", K.mnext[:], C["mask_next"][:, :], w=["mnext"])
        P.dma("sp", K.mprev[:], C["mask_prev"][:, :], w=["mprev"])
        for r_ in range(2):
            P.dma("sp", K.G1[:, r_, :], self.G_d[r_:r_ + 1, 0, :].partition_broadcast(128), w=[("G1", r_)])
        K.spairs = Ring([(self.psum[:, 2 * i:2 * i + 2, :], [("ps", 2 * i), ("ps", 2 * i + 1)]) for i in range(3)])
        K.tailpairs = Ring([(self.psum[:, 2 * i:2 * i + 2, :], [("ps", 2 * i), ("ps", 2 * i + 1)]) for i in range(4)])
        K.pti = 0
        K.xi = 0
        tiles = [(i * 512, 4, False) for i in range(S // 512)]
        if self.ctx_full:
            tiles = tiles + [(S, 2, True)]
        if "C_tiles" in self.debug:
            tiles = tiles[:2] + tiles[-1:]
        for it, t in enumerate(tiles):
            self.C_tile(l, it, *t)
        P.barrier()
        P.pop()

    def C_tile(self, l, it, tok0, nsub, is_ctx):
        P = self.P
        K = self.K
        T = nsub * 128
        sl = it % 2
        mi = 1 if is_ctx else 0
        qt = K.qt[sl]
        P.dma("sp", qt[:, :, 0:T], self.qT_d[:, :, tok0:tok0 + T].rearrange("c p t -> p c t"), w=[("qt", sl)])
        P.dma("sp", K.sgt[sl][:, :, 0:T], self.sguT_d[:, :, tok0:tok0 + T].rearrange("c p t -> p c t"), w=[("sgtC", sl)])
        xis = []
        for s in range(nsub):
            xi = K.xi % 4
            K.xi += 1
            xis.append(xi)
            P.dma("sp", K.xt[xi][:], self.src_rows(l, tok0 + s * 128, 128), w=[("xtC", xi)])
        items = []
        if not is_ctx:
            n0 = tok0 // 128
            for j in range(n0 - 1, n0 + nsub + 1):
                if j < 0 or j >= S // 128:
                    continue
                qlo = max(j - 1, n0) - n0
                qhi = min(j + 1, n0 + nsub - 1) - n0
                mnext = (j - 1 - n0) if (j - 1 >= n0) else None
                mprev = (j + 1 - n0) if (j + 1 <= n0 + nsub - 1) else None
                items.append((j, qlo * 128, (qhi + 1) * 128, mnext, mprev))
        items.append((64, 0, T, None, None))
        items.append((65, 0, T, None, None))
        LOOK = 2
        bo, ko = self.banks[6], ("ps", 6)
        bd, kd = self.banks[7], ("ps", 7)
        deferred = []

        def fin2():
            while deferred:
                c_ = deferred.pop(0)
                d_ = c_ % 2
                self.act(K.sqT[:, c_, 0:T], K.ar[d_][:, 0:T], AF.Square, r=[("ar", d_)], w=[("sqT", c_)])
                self.ts("pool", K.aT[:, c_, 0:T], K.ar[d_][:, 0:T], self.gmixT[:, c_:c_ + 1], 1.0, ALU.mult, ALU.mult,
                        r=[("ar", d_), "gmixT"], w=[("aT", c_)])

        for c in range(4):
            pend = []
            first = True
            for idx in range(len(items) + LOOK):
                if idx == min(5, len(items) + LOOK - 1):
                    fin2()
                if idx < len(items):
                    j, lo, hi, mn, mp = items[idx]
                    n = hi - lo
                    pair, pkeys = K.spairs.next()
                    kcols = slice(j * 128, (j + 1) * 128)
                    self.mm([(pair[:, 0, 0:n], self.kT_all[0:64, kcols], qt[0:64, c, lo:hi], True, True),
                             (pair[:, 1, 0:n], self.kT_all[64:128, kcols], qt[64:128, c, lo:hi], True, True)],
                            r=[("qt", sl), "kT_all"], w=pkeys)
                    pi = K.pti % len(K.PT)
                    K.pti += 1
                    pt = K.PT[pi]
                    pk = [("PT", pi)]
                    self.act(pt[:, :, 0:n], pair[:, :, 0:n], AF.Exp, r=pkeys, w=pk, scale=0.125)
                    if mn is not None:
                        o = mn * 128 - lo
                        self.tt("dve", pt[:, :, o:o + 128], pt[:, :, o:o + 128],
                                K.mnext[:, :].unsqueeze(1).to_broadcast([128, 2, 128]), ALU.mult, r=pk + ["mnext"], w=pk)
                    if mp is not None:
                        o = mp * 128 - lo
                        self.tt("dve", pt[:, :, o:o + 128], pt[:, :, o:o + 128],
                                K.mprev[:, :].unsqueeze(1).to_broadcast([128, 2, 128]), ALU.mult, r=pk + ["mprev"], w=pk)
                    pend.append((j, lo, hi, pt, pk))
                if idx >= LOOK:
                    j, lo, hi, pt, pk = pend.pop(0)
                    n = hi - lo
                    last = (idx == len(items) + LOOK - 1)
                    self.mm([(bo[0:64, lo:hi], self.vt_all[:, j, 0:64], pt[:, 0, 0:n], first, last),
                             (bo[64:128, lo:hi], self.vt_all[:, j, 64:128], pt[:, 1, 0:n], first, last),
                             (bd[0:64, lo:hi], self.ones[:, 0:64], pt[:, 0, 0:n], first, last),
                             (bd[64:128, lo:hi], self.ones[:, 0:64], pt[:, 1, 0:n], first, last)],
                            r=pk + ["ones", "vt_all"], w=[ko, kd])
                    first = False
            d2 = c % 2
            self.cp("act", K.oc[d2][:, 0:T], bo[:, 0:T], r=[ko], w=[("oc", d2)])
            self.ts("dve", K.dd[d2][:, 0:T], bd[:, 0:T], self.esink[:, c:c + 1], None, ALU.add, None, r=[kd, "esink"], w=[("dd", d2)])
            self.recip(K.dd[d2][:, 0:T], K.dd[d2][:, 0:T], r=[("dd", d2)], w=[("dd", d2)])
            self.tt("dve", K.ar[d2][:, 0:T], K.oc[d2][:, 0:T], K.dd[d2][:, 0:T], ALU.mult, r=[("oc", d2), ("dd", d2)], w=[("ar", d2)])
            deferred.append(c)
        fin2()
        pair, pkeys = K.spairs.next()
        bs_, ks_ = pair[:, 0, :], pkeys[0]
        specs = []
        for s in range(nsub):
            for c in range(4):
                specs.append((bs_[:, s:s + 1], K.sqT[:, c, s * 128:(s + 1) * 128], self.ones[:, 0:1], c == 0, c == 3))
        self.mm(specs, r=[("sqT", c) for c in range(4)] + ["ones"], w=[ks_])
        self.rstd(K.rsa[:, 0:nsub], bs_[:, 0:nsub], 512.0, [ks_], ["rsa"], "rsat")
        aTk = [("aT", c) for c in range(4)]
        def st1(s):
            xi = xis[s]
            cols = slice(s * 128, (s + 1) * 128)
            b2 = s % 2
            ys, yc = K.ys2[b2], K.yc2[b2]
            pa, pak = K.tailpairs.next()
            pb_, pbk = K.tailpairs.next()
            for half in range(2):
                self.mm([(pa[:, half, :], K.aT[:, c, cols], self.woutb[:, c, half * 512:(half + 1) * 512], c == 0, c == 3) for c in range(4)],
                        r=aTk + self.woutk, w=[pak[half]])
            for half in range(2):
                specs = []
                for cc in range(2):
                    specs.append((pb_[:, half, :], K.sgt[sl][:, cc, cols], self.woutb[:, 4 + cc, half * 512:(half + 1) * 512], cc == 0, False))
                for cc in range(2):
                    specs.append((pb_[:, half, :], self.fourT[:, cc, tok0 + s * 128: tok0 + (s + 1) * 128],
                                  self.woutb[:, 6 + cc, half * 512:(half + 1) * 512], False, cc == 1))
                self.mm(specs, r=[("sgtC", sl)] + self.woutk, w=[pbk[half]])
            self.cp("act", ys[:].rearrange("p (h n) -> p h n", h=2), pb_[:, :, :], r=pbk, w=[("ysC", b2)])
            self.stt(yc[:].rearrange("p (h n) -> p h n", h=2), pa[:, :, :], K.rsa[:, s:s + 1], ys[:].rearrange("p (h n) -> p h n", h=2),
                     ALU.mult, ALU.add, r=pak + ["rsa", ("ysC", b2)], w=[("ycC", b2)])

        def st2(s):
            xi = xis[s]
            b2 = s % 2
            yc = K.yc2[b2]
            yck = [("ycC", b2)]
            self.act(K.junk[:], yc[:], AF.Square, r=yck, w=["junkC", ("ssy", s)], accum_out=K.ssy[:, s:s + 1])
            self.rstd(K.rsy[:, s:s + 1], K.ssy[:, s:s + 1], float(D), [("ssy", s)], [("rsy", s)], ("rsyt", s))
            self.tt("pool", K.tmp[:], yc[:], K.G1[:, mi, :], ALU.mult, r=yck + [("G1", mi)], w=["tmpC"])
            self.stt(K.xt[xi][:], K.tmp[:], K.rsy[:, s:s + 1], K.xt[xi][:], ALU.mult, ALU.add,
                     r=["tmpC", ("rsy", s), ("xtC", xi)], w=[("xtC", xi)])
            P.dma("pool", self.xmid_d[tok0 + s * 128: tok0 + (s + 1) * 128, :], K.xt[xi][:], r=[("xtC", xi)], w=[("xmid", tok0, s)])

        st1(0)
        for s in range(1, nsub):
            st1(s)
            st2(s - 1)
        st2(nsub - 1)

    def phaseD(self, l):
        P = self.P
        I = self.I
        P.push()
        self.w1b = P.sb("w1b", [128, 8, DFF], BF16)
        self.w2b = P.sb("w2b", [128, 32, D], BF16)
        self.new_staging(512)
        self.w1k = self.load_weight(self.w1b, lambda c: I["w_ff1"][l, c * 128:(c + 1) * 128, :], 8, DFF, "w1b")
        self.w2k = self.load_weight(self.w2b, lambda c: I["w_ff2"][l, c * 128:(c + 1) * 128, :], 32, D, "w2b")
        Dd = type("Dd", (), {})()
        self.Dd = Dd
        Dd.xm = [P.sb("xm%d" % i, [128, D], F32) for i in range(6)]
        Dd.xn = [P.sb("xnD%d" % i, [128, D], BF16) for i in range(2)]
        Dd.hx = [P.sb("hx2T%d" % i, [128, 8, 256], BF16) for i in range(2)]
        Dd.hT = P.sb("hT", [128, 32, 256], BF16)
        Dd.rl = [P.sb("rl%d" % i, [128, 512], F32) for i in range(2)]
        Dd.tmp = P.sb("tmpD", [128, D], F32)
        Dd.junk = P.sb("junkD", [128, D], BF16)
        Dd.ss = P.sb("ssD", [128, 4], F32)
        Dd.rs = P.sb("rsD", [128, 4], F32)
        Dd.ssy = P.sb("ssyD", [128, 4], F32)
        Dd.rsy = P.sb("rsyD", [128, 2], F32)
        Dd.G2 = P.sb("G2bc", [128, 2, D], F32)
        for r_ in range(2):
            P.dma("sp", Dd.G2[:, r_, :], self.G_d[r_:r_ + 1, 1, :].partition_broadcast(128), w=[("G2", r_)])
        Dd.trbanks = Ring([(self.banks[i].bitcast(BF16), ("ps", i)) for i in range(2)])
        Dd.fbanks = Ring([(self.banks[i], ("ps", i)) for i in range(2, 5)])
        Dd.yring = Ring([(self.banks[i], ("ps", i)) for i in range(5, 8)])
        Dd.xi = 0
        Dd.rli = 0
        tiles = [(i * 256, False) for i in range(S // 256)]
        if self.ctx_full:
            tiles = tiles + [(S, True)]
        if "D_tiles" in self.debug:
            tiles = tiles[:2] + tiles[-1:]
        self.D_front(l, 0, *tiles[0])
        if len(tiles) > 1:
            self.D_front(l, 1, *tiles[1])
        for it, t in enumerate(tiles):
            self.D_back(l, it, t[0], t[1], part=1)
            if it + 2 < len(tiles):
                self.D_front(l, it + 2, *tiles[it + 2])
            self.D_back(l, it, t[0], t[1], part=2)
        P.barrier()
        P.pop()

    def D_front(self, l, it, tok0, is_ctx):
        P = self.P
        Dd = self.Dd
        mi = 1 if is_ctx else 0
        hx = Dd.hx[it % 2]
        xis = []
        for s in range(2):
            xi = Dd.xi % 6
            Dd.xi += 1
            xis.append(xi)
            P.dma("sp", Dd.xm[xi][:], self.xmid_d[tok0 + s * 128: tok0 + (s + 1) * 128, :], w=[("xm", xi)])
            self.act(Dd.junk[:], Dd.xm[xi][:], AF.Square, r=[("xm", xi)], w=["junkD", ("ssD", s)], accum_out=Dd.ss[:, s:s + 1])
            self.rstd(Dd.rs[:, s:s + 1], Dd.ss[:, s:s + 1], float(D), [("ssD", s)], [("rsD", s)], ("rsDt", s))
            self.ts("dve", Dd.xn[s][:], Dd.xm[xi][:], Dd.rs[:, s:s + 1], None, ALU.mult, None, r=[("xm", xi), ("rsD", s)], w=[("xnD", s)])
        Dd.cur_xis = getattr(Dd, "cur_xis", {})
        Dd.cur_xis[it] = xis
        for cq in range(2):
            tb, sk = Dd.trbanks.next()
            self.tr([(tb[:, j * 256 + s * 128: j * 256 + (s + 1) * 128], Dd.xn[s][:, (4 * cq + j) * 128:(4 * cq + j + 1) * 128])
                     for j in range(4) for s in range(2)], r=[("xnD", 0), ("xnD", 1), "ident"], w=[sk])
            for j in range(4):
                c = 4 * cq + j
                slot = tb[:, j * 256:(j + 1) * 256]
                if cq == 0:
                    self.act(hx[:, c, :], slot, AF.Identity, r=[sk, "modT"], w=[("hx2T", it % 2, c)],
                             scale=self.modT[:, 2, c, mi:mi + 1], bias=self.modT[:, 3, c, mi:mi + 1])
                else:
                    self.ts("dve", hx[:, c, :], slot, self.modT[:, 2, c, mi:mi + 1], self.modT[:, 3, c, mi:mi + 1], ALU.mult, ALU.add,
                            r=[sk, "modT"], w=[("hx2T", it % 2, c)])

    def D_back(self, l, it, tok0, is_ctx, part):
        P = self.P
        Dd = self.Dd
        mi = 1 if is_ctx else 0
        hx = Dd.hx[it % 2]
        hk = [("hx2T", it % 2, c) for c in range(8)]
        xis = Dd.cur_xis[it]
        for fp in range(16 if part == 1 else 0):
            fb, sk = Dd.fbanks.next()
            specs = []
            for j in range(2):
                fc = 2 * fp + j
                for dc in range(8):
                    specs.append((fb[:, j * 256:(j + 1) * 256], self.w1b[:, dc, fc * 128:(fc + 1) * 128], hx[:, dc, :], dc == 0, dc == 7))
            self.mm(specs, r=hk + self.w1k, w=[sk])
            ri = Dd.rli % 2
            Dd.rli += 1
            self.act(Dd.rl[ri][:], fb[:, :], AF.Relu, r=[sk], w=[("rl", ri)])
            self.tt("pool" if fp % 2 == 0 else "dve", Dd.hT[:, 2 * fp:2 * fp + 2, :].rearrange("p a t -> p (a t)"), Dd.rl[ri][:], Dd.rl[ri][:],
                    ALU.mult, r=[("rl", ri)], w=[("hT", 2 * fp), ("hT", 2 * fp + 1)])
        if part == 1:
            return
        hTk = [("hT", fc) for fc in range(32)]
        for s in range(2):
            xi = xis[s]
            ybs = []
            for half in range(2):
                b_, k_ = Dd.yring.next()
                self.mm([(b_[:, :], Dd.hT[:, fc, s * 128:(s + 1) * 128], self.w2b[:, fc, half * 512:(half + 1) * 512], fc == 0, fc == 31)
                         for fc in range(32)], r=hTk + self.w2k, w=[k_])
                ybs.append((b_, k_))
            for half in range(2):
                self.act(Dd.junk[:, half * 512:(half + 1) * 512], ybs[half][0][:, :], AF.Square, r=[ybs[half][1]],
                         w=[("junkD", half), ("ssyD", s, half)], accum_out=Dd.ssy[:, s * 2 + half:s * 2 + half + 1])
            self.tt("dve", Dd.ssy[:, s * 2:s * 2 + 1], Dd.ssy[:, s * 2:s * 2 + 1], Dd.ssy[:, s * 2 + 1:s * 2 + 2], ALU.add,
                    r=[("ssyD", s, 0), ("ssyD", s, 1)], w=[("ssyDs", s)])
            self.rstd(Dd.rsy[:, s:s + 1], Dd.ssy[:, s * 2:s * 2 + 1], float(D), [("ssyDs", s)], [("rsyD", s)], ("rsyDt", s))
            for half in range(2):
                hc = slice(half * 512, (half + 1) * 512)
                self.tt("dve", Dd.tmp[:, hc], ybs[half][0][:, :], Dd.G2[:, mi, hc], ALU.mult, r=[ybs[half][1], ("G2", mi)], w=[("tmpD", half)])
            self.stt(Dd.xm[xi][:], Dd.tmp[:], Dd.rsy[:, s:s + 1], Dd.xm[xi][:], ALU.mult, ALU.add,
                     r=[("tmpD", 0), ("tmpD", 1), ("rsyD", s), ("xm", xi)], w=[("xm", xi)])
            if is_ctx and l == self.nlayers - 1:
                continue
            P.dma("pool", self.dst_rows(l, tok0 + s * 128, 128), Dd.xm[xi][:], r=[("xm", xi)], w=[("xout", tok0, s)])


def _prep_shared(inp):
    sh = {}
    f = lambda a: np.ascontiguousarray(a, dtype=np.float32)
    sh["w_ada"] = f(inp["w_ada"])
    sh["b_ada"] = f(inp["b_ada"])
    for nm in ("g_pre_mix", "g_post_mix", "g_pre_ff", "g_post_ff"):
        sh[nm] = f(inp[nm])
    sh["w_inp"] = f(inp["w_in"][:, :, _perm_cols()])
    sink = inp["sink"]
    sinkT = np.zeros((2, 128, 4), np.float32)
    for c in range(4):
        sinkT[:, 0:64, c] = sink[:, c][:, None]
        sinkT[:, 64:128, c] = sink[:, 4 + c][:, None]
    sh["sinkT"] = sinkT
    sh["w_sguT"] = f(np.transpose(inp["w_sgu"], (0, 3, 1, 2)).reshape(2, 128, 512))
    sh["b_sguT"] = f(np.transpose(inp["b_sgu"], (0, 2, 1)))
    sh["g_sgu"] = f(inp["g_sgu"].reshape(2, 256))
    rp = _attn_row_perm()
    gm = inp["g_mix"][:, rp]
    sh["g_mixT"] = f(np.transpose(gm.reshape(2, 8, 128), (0, 2, 1)))
    sh["w_outp"] = f(inp["w_out"][:, rp, :])
    sh["w_ff1"] = f(inp["w_ff1"])
    sh["w_ff2"] = f(inp["w_ff2"])
    return sh


def _core_inputs(inp, sh, consts, b):
    m = dict(sh)
    m["x"] = np.ascontiguousarray(inp["x"][b], dtype=np.float32)
    m["ctx"] = np.ascontiguousarray(inp["ctx"][b], dtype=np.float32)
    cs = np.stack([inp["c"][b], inp["c_ctx"]], axis=0).astype(np.float32)
    m["csT"] = np.ascontiguousarray(np.transpose(cs.reshape(2, 8, 128), (2, 1, 0)).reshape(128, 16))
    for k, v in consts.items():
        m["c_" + k] = v
    return m


_CACHE = {}


def kernel(**inputs):
    inp = {k: np.asarray(v) for k, v in inputs.items()}
    if "nc" not in _CACHE:
        _CACHE["nc"] = Builder().build()
        _CACHE["consts"] = _constants()
    nc = _CACHE["nc"]
    consts = _CACHE["consts"]
    sh = _prep_shared(inp)
    in_maps = [_core_inputs(inp, sh, consts, b) for b in range(8)]
    res = run_bass_kernel_spmd(nc, in_maps, core_ids=list(range(8)))
    out = np.stack([np.asarray(res.results[b]["out"], dtype=np.float32) for b in range(8)], axis=0)
    return out
```

```python
import contextlib
import numpy as np
import ml_dtypes
import concourse.bass as bass
import concourse.mybir as mybir
from concourse.bass_utils import run_bass_kernel_spmd

F32 = mybir.dt.float32
BF16 = mybir.dt.bfloat16
AF = mybir.ActivationFunctionType
ALU = mybir.AluOpType
AX = mybir.AxisListType
NPBF = ml_dtypes.bfloat16

COMPUTE = ("pe", "act", "dve", "pool")
NDMA_SEMS = 20

D = 1024
S = 8192
LCTX = 256
NTOK = S + LCTX
DFF = 4096
NCOL = 2176
EPS = 1e-6


class _Op:
    __slots__ = ("eng", "fn", "deps", "signal", "sigval", "is_dma", "sem_idx", "dma_val", "prev_same_sem")

    def __init__(self, eng, fn, is_dma):
        self.eng = eng
        self.fn = fn
        self.deps = []
        self.signal = False
        self.sigval = 0
        self.is_dma = is_dma
        self.sem_idx = None
        self.dma_val = 0
        self.prev_same_sem = None


class Prog:
    def __init__(self, nc):
        self.nc = nc
        self.ops = {e: [] for e in ("pe", "act", "dve", "pool", "sp")}
        self.last_w = {}
        self.readers = {}
        self.dma_count = {"sp": 0, "pool": 0, "act": 0}
        self.dma_last_on_sem = {}
        self.stacks = [contextlib.ExitStack()]
        self.last_op = {}

    def push(self):
        self.stacks.append(contextlib.ExitStack())

    def pop(self):
        self.stacks.pop().close()

    def sb(self, name, shape, dt):
        self.uid = getattr(self, "uid", 0) + 1
        return self.stacks[-1].enter_context(self.nc.sbuf_tensor("%s_u%d" % (name, self.uid), list(shape), dt))

    def ps(self, name, shape, dt):
        return self.stacks[-1].enter_context(self.nc.psum_tensor(name, list(shape), dt))

    def _track(self, op, r, w):
        ps_keys = [("ps", k[1]) for k in list(r) + list(w) if isinstance(k, tuple) and k[0] == "ps"]
        r = [k for k in r if not (isinstance(k, tuple) and k[0] == "ps")]
        w = [k for k in w if not (isinstance(k, tuple) and k[0] == "ps")] + list(dict.fromkeys(ps_keys))
        deps = set()
        for k in r:
            lw = self.last_w.get(k)
            if lw is not None:
                deps.add(lw)
        for k in w:
            lw = self.last_w.get(k)
            if lw is not None and (op.is_dma or lw.is_dma or lw.eng != op.eng):
                deps.add(lw)
            for rd in self.readers.get(k, ()):
                if rd is op:
                    continue
                if op.is_dma or rd.is_dma or rd.eng != op.eng:
                    deps.add(rd)
        for d in deps:
            if d is op:
                continue
            d.signal = True
            op.deps.append(d)
        for k in r:
            self.readers.setdefault(k, []).append(op)
        for k in w:
            self.last_w[k] = op
            self.readers[k] = []

    def op(self, eng, fn, r=(), w=()):
        o = _Op(eng, fn, False)
        self._track(o, r, w)
        self.ops[eng].append(o)
        self.last_op[eng] = o
        return o

    def wait_all(self, eng, r=()):
        o = _Op(eng, None, False)
        self._track(o, r, ())
        self.ops[eng].append(o)
        return o

    def dma(self, q, out, in_, r=(), w=(), **kw):
        o = _Op(q, (out, in_, kw), True)
        n = self.dma_count[q]
        self.dma_count[q] = n + 1
        nsem = NDMA_SEMS if q != "pool" else 6
        o.sem_idx = (q, n % nsem)
        prev = self.dma_last_on_sem.get(o.sem_idx)
        o.prev_same_sem = prev
        o.dma_val = (prev.dma_val if prev is not None else 0) + 16
        self.dma_last_on_sem[o.sem_idx] = o
        self._track(o, r, w)
        self.ops[q].append(o)
        return o

    def barrier(self):
        srcs = [o for o in self.last_op.values()] + list(self.dma_last_on_sem.values())
        for e in ("pe", "act", "dve", "pool", "sp"):
            o = _Op(e, None, False)
            for s_ in srcs:
                if (not s_.is_dma) and s_.eng == e:
                    continue
                s_.signal = True
                o.deps.append(s_)
            self.ops[e].append(o)
        self.last_w = {}
        self.readers = {}

    def emit(self):
        nc = self.nc
        st = self.stacks[0]
        esem = {e: st.enter_context(nc.semaphore("sem_" + e)) for e in COMPUTE}
        dsem = {}
        for q in ("sp", "pool", "act"):
            for i in range(min(NDMA_SEMS, self.dma_count[q])):
                dsem[(q, i)] = st.enter_context(nc.semaphore("dsem_%s_%d" % (q, i)))
        for e in COMPUTE:
            c = 0
            for o in self.ops[e]:
                if (not o.is_dma) and o.signal:
                    c += 1
                    o.sigval = c

        def src_of(d):
            if d.is_dma:
                return dsem[d.sem_idx], d.dma_val, ("d",) + d.sem_idx
            return esem[d.eng], d.sigval, ("e", d.eng)

        def run(engname, eng):
            seen = {}
            for o in self.ops[engname]:
                waits = {}
                deps = list(o.deps)
                if o.is_dma and o.prev_same_sem is not None:
                    deps.append(o.prev_same_sem)
                for d in deps:
                    sem, val, key = src_of(d)
                    if seen.get(key, 0) >= val:
                        continue
                    if key not in waits or waits[key][1] < val:
                        waits[key] = (sem, val)
                for key, (sem, val) in waits.items():
                    eng.wait_ge(sem, val)
                    seen[key] = val
                if o.is_dma:
                    out, in_, kw = o.fn
                    eng.dma_start(out=out, in_=in_, **kw).then_inc(dsem[o.sem_idx], 16)
                elif o.fn is None:
                    pass
                else:
                    ins = o.fn(eng)
                    if o.signal:
                        ins.then_inc(esem[engname], 1)

        with nc.Block() as block:
            @block.tensor
            def _(e):
                run("pe", e)

            @block.scalar
            def _(e):
                run("act", e)

            @block.vector
            def _(e):
                run("dve", e)

            @block.gpsimd
            def _(e):
                run("pool", e)

            @block.sync
            def _(e):
                run("sp", e)

    def close(self):
        while self.stacks:
            self.stacks.pop().close()


class Ring:
    def __init__(self, items):
        self.items = items
        self.i = 0

    def next(self):
        it = self.items[self.i % len(self.items)]
        self.i += 1
        return it


def _perm_cols():
    def rotp(d):
        if d < 16:
            return d + 16
        if d < 32:
            return d - 16
        if d < 48:
            return d + 16
        return d - 16
    cols = []
    for c in range(4):
        for h in (c, 4 + c):
            cols += [h * 64 + d for d in range(64)]
    for c in range(4):
        for h in (c, 4 + c):
            cols += [h * 64 + rotp(d) for d in range(64)]
    cols += [512 + j for j in range(128)]
    cols += [512 + h * 64 + rotp(d) for h in range(2) for d in range(64)]
    cols += [1280 + j for j in range(256)]
    cols += [640 + j for j in range(128)]
    cols += [768 + j for j in range(256)]
    cols += [1024 + j for j in range(256)]
    assert len(cols) == NCOL
    return np.array(cols)


def _attn_row_perm():
    rows = []
    for c in range(4):
        for h in (c, 4 + c):
            rows += [h * 64 + d for d in range(64)]
    return np.array(rows + list(range(512, 1024)))


def _constants():
    k = {}
    k["ident"] = np.eye(128, dtype=np.float32).astype(NPBF)
    n_freq = 16
    inv_freq = (np.float32(10000.0) ** (-np.arange(n_freq, dtype=np.float32) / np.float32(n_freq))).astype(np.float32)
    t = np.arange(S)
    row = (t // 64).astype(np.float32)
    col = (t % 64).astype(np.float32)
    ang = np.zeros((64, S), np.float32)
    ang[0:16] = row[None, :] * inv_freq[:, None]
    ang[16:32] = ang[0:16]
    ang[32:48] = col[None, :] * inv_freq[:, None]
    ang[48:64] = ang[32:48]
    cos = np.cos(ang).astype(np.float32)
    sin = np.sin(ang).astype(np.float32)
    sin[0:16] *= -1
    sin[32:48] *= -1
    cosT = np.ones((128, NTOK), np.float32)
    sinT = np.zeros((128, NTOK), np.float32)
    cosT[0:64, :S] = cos
    cosT[64:128, :S] = cos
    sinT[0:64, :S] = sin
    sinT[64:128, :S] = sin
    k["cosT"] = cosT
    k["sinT"] = sinT
    kk = np.arange(128)[:, None]
    qq = np.arange(128)[None, :]
    k["mask_next"] = (kk <= qq).astype(np.float32).astype(NPBF)
    k["mask_prev"] = (kk >= qq).astype(np.float32).astype(NPBF)
    k["ones"] = np.ones((128, 128), np.float32).astype(NPBF)
    cidx = np.arange(64)
    c64 = np.cos(2 * np.pi * np.outer(cidx, cidx) / 64.0) / 8.0
    s64 = np.sin(2 * np.pi * np.outer(cidx, cidx) / 64.0) / 8.0
    csblk = np.zeros((128, 256), np.float64)
    for g in range(2):
        csblk[g * 64:(g + 1) * 64, g * 64:(g + 1) * 64] = c64
        csblk[g * 64:(g + 1) * 64, 128 + g * 64:128 + (g + 1) * 64] = s64
    k["csblk"] = csblk.astype(np.float32).astype(NPBF)
    t1 = np.zeros((128, 128), np.float64)
    t1[0:64, 0:64] = c64
    t1[0:64, 64:128] = -s64
    t1[64:128, 0:64] = -s64
    t1[64:128, 64:128] = -c64
    k["t1tab"] = t1.astype(np.float32).astype(NPBF)
    t2 = np.arange(128)[:, None, None].astype(np.float64)
    k1 = np.arange(64)[None, :, None].astype(np.float64)
    k2 = np.arange(128)[None, None, :].astype(np.float64)
    phi = 2 * np.pi * ((t2 * (k1 + 64 * k2)) % 8192) / 8192.0
    k["mc"] = (np.cos(phi) / np.sqrt(128.0)).astype(np.float32).astype(NPBF).reshape(128, 64 * 128)
    k["ms"] = (np.sin(phi) / np.sqrt(128.0)).astype(np.float32).astype(NPBF).reshape(128, 64 * 128)
    tt = (np.arange(2)[None, :, None] * 128 + np.arange(128)[:, None, None]).astype(np.float64)
    kk2 = np.arange(256)[None, None, :].astype(np.float64)
    ph = 2 * np.pi * ((tt * kk2) % 256) / 256.0
    k["c256"] = (np.cos(ph) / 16.0).astype(np.float32).astype(NPBF).reshape(128, 512)
    k["s256n"] = (-np.sin(ph) / 16.0).astype(np.float32).astype(NPBF).reshape(128, 512)
    sel = np.zeros((2, 2, 128), np.float32)
    sel[0, 0, :] = 1
    sel[1, 1, :] = 1
    k["sel"] = sel.reshape(2, 256)
    k["i2"] = np.eye(2, dtype=np.float32)
    return k


CONST_SPECS = [
    ("ident", [128, 128], BF16), ("cosT", [128, NTOK], F32), ("sinT", [128, NTOK], F32),
    ("mask_next", [128, 128], BF16), ("mask_prev", [128, 128], BF16), ("ones", [128, 128], BF16),
    ("csblk", [128, 256], BF16), ("t1tab", [128, 128], BF16), ("mc", [128, 8192], BF16), ("ms", [128, 8192], BF16),
    ("c256", [128, 512], BF16), ("s256n", [128, 512], BF16), ("sel", [2, 256], F32), ("i2", [2, 2], F32),
]

IN_SPECS = [
    ("x", [S, D]), ("ctx", [LCTX, D]), ("csT", [128, 16]),
    ("w_ada", [2, D, 6 * D]), ("b_ada", [2, 6 * D]),
    ("g_pre_mix", [2, D]), ("g_post_mix", [2, D]), ("g_pre_ff", [2, D]), ("g_post_ff", [2, D]),
    ("w_inp", [2, D, NCOL]), ("sinkT", [2, 128, 4]), ("w_sguT", [2, 128, 512]), ("b_sguT", [2, 128, 4]),
    ("g_sgu", [2, 256]), ("g_mixT", [2, 128, 8]), ("w_outp", [2, D, D]),
    ("w_ff1", [2, D, DFF]), ("w_ff2", [2, DFF, D]),
]


class Builder:
    def __init__(self, nlayers=2, stop_after=None, debug=()):
        self.nlayers = nlayers
        self.stop_after = stop_after
        self.debug = set(debug)
        nc = bass.Bass("TRN2", target_bir_lowering=False)
        self.nc = nc
        self.I = {}
        for name, shape in IN_SPECS:
            self.I[name] = nc.dram_tensor(name, shape, F32, kind="ExternalInput").ap()
        self.C = {}
        for name, shape, dt in CONST_SPECS:
            self.C[name] = nc.dram_tensor("c_" + name, shape, dt, kind="ExternalInput").ap()
        self.out = nc.dram_tensor("out", [S, D], F32, kind="ExternalOutput").ap()
        kw = dict(kind="ExternalOutput") if "scratch_out" in self.debug else {}
        self.qT_d = nc.dram_tensor("qT_d", [4, 128, NTOK], BF16, **kw).ap()
        self.sguT_d = nc.dram_tensor("sguT_d", [2, 128, NTOK], BF16, **kw).ap()
        self.AB_d = nc.dram_tensor("AB_d", [NTOK, 512], BF16, **kw).ap()
        self.kT_d = nc.dram_tensor("kT_d", [128, NTOK], BF16, **kw).ap()
        self.vt_d = nc.dram_tensor("vt_d", [128, 66, 128], BF16, **kw).ap()
        self.fourT_d = nc.dram_tensor("fourT_d", [128, 2, NTOK], BF16, **kw).ap()
        self.xmid_d = nc.dram_tensor("xmid_d", [NTOK, D], F32, **kw).ap()
        self.x1_d = nc.dram_tensor("x1_d", [NTOK, D], F32, **kw).ap()
        self.G_d = nc.dram_tensor("G_d", [2, 4, D], F32, **kw).ap()
        self.modT_d = nc.dram_tensor("modT_d", [128, 64], F32, **kw).ap()
        self.dbg = {}
        self.P = Prog(nc)

    def act(self, out, in_, func, r, w, **kw):
        return self.P.op("act", lambda e: e.activation(out=out, in_=in_, func=func, **kw), r, w)

    def tt(self, eng, out, in0, in1, op, r, w):
        return self.P.op(eng, lambda e: e.tensor_tensor(out=out, in0=in0, in1=in1, op=op), r, w)

    def ts(self, eng, out, in0, s1, s2, op0, op1, r, w):
        if s2 is None:
            return self.P.op(eng, lambda e: e.tensor_scalar(out=out, in0=in0, scalar1=s1, scalar2=None, op0=op0), r, w)
        return self.P.op(eng, lambda e: e.tensor_scalar(out=out, in0=in0, scalar1=s1, scalar2=s2, op0=op0, op1=op1), r, w)

    def stt(self, out, in0, scalar, in1, op0, op1, r, w):
        return self.P.op("dve", lambda e: e.scalar_tensor_tensor(out=out, in0=in0, scalar=scalar, in1=in1, op0=op0, op1=op1), r, w)

    def cp(self, eng, out, in_, r, w):
        if eng == "act":
            return self.P.op("act", lambda e: e.activation(out=out, in_=in_, func=AF.Copy), r, w)
        return self.P.op(eng, lambda e: e.tensor_copy(out=out, in_=in_), r, w)

    def recip(self, out, in_, r, w):
        return self.P.op("dve", lambda e: e.reciprocal(out=out, in_=in_), r, w)

    def rstd(self, out, ss, n, key_in, key_out, tmpkey):
        self.act(out, ss, AF.Sqrt, r=key_in, w=[tmpkey], scale=1.0 / n, bias=self.eps_t[:, 0:1])
        self.recip(out, out, r=[tmpkey], w=key_out)

    def mm(self, specs, r, w):
        def fn(e):
            ins = None
            for sp_ in specs:
                ins = e.matmul(out=sp_[0], lhsT=sp_[1], rhs=sp_[2], start=sp_[3], stop=sp_[4], skip_group_check=True)
            return ins
        return self.P.op("pe", fn, r, w)

    def tr(self, specs, r, w):
        def fn(e):
            ins = None
            for o_, i_ in specs:
                ins = e.transpose(out=o_, in_=i_, identity=self.ident[:])
            return ins
        return self.P.op("pe", fn, r, w)

    def build(self):
        P = self.P
        nc = self.nc
        self.ident = P.sb("ident", [128, 128], BF16)
        self.ones = P.sb("ones", [128, 128], BF16)
        self.eps_t = P.sb("eps_t", [128, 1], F32)
        self.csil = P.sb("csil", [128, 8, 2], F32)
        self.modT = P.sb("modT", [128, 4, 8, 2], F32)
        self.psum = P.ps("psum", [128, 8, 512], F32)
        self.banks = [self.psum[:, i, :] for i in range(8)]
        P.dma("sp", self.ident[:], self.C["ident"][:, :], w=["ident"])
        P.dma("sp", self.ones[:], self.C["ones"][:, :], w=["ones"])
        P.op("dve", lambda e: e.memset(self.eps_t[:], EPS), w=["eps"])
        P.dma("sp", self.csil[:].rearrange("p c r -> p (c r)"), self.I["csT"][:, :], w=["csil"])
        self.act(self.csil[:], self.csil[:], AF.Silu, r=["csil"], w=["csil"])
        P.barrier()
        for l in range(self.nlayers):
            self.layer(l)
        P.barrier()
        P.emit()
        P.close()
        return nc

    def src_rows(self, l, tok0, n):
        if l == 0:
            if tok0 >= S:
                return self.I["ctx"][tok0 - S:tok0 - S + n, :]
            return self.I["x"][tok0:tok0 + n, :]
        return self.x1_d[tok0:tok0 + n, :]

    def dst_rows(self, l, tok0, n):
        if l == self.nlayers - 1 and self.nlayers == 2:
            assert tok0 < S
            return self.out[tok0:tok0 + n, :]
        if self.nlayers == 1 and tok0 < S:
            return self.out[tok0:tok0 + n, :]
        return self.x1_d[tok0:tok0 + n, :]

    def layer(self, l):
        P = self.P
        last = (l == 1)
        self.l = l
        self.ctx_full = not last
        P.push()
        self.gmixT = P.sb("gmixT", [128, 8], F32)
        self.esink = P.sb("esink", [128, 4], F32)
        self.bsT = P.sb("bsT", [128, 4], F32)
        self.fourT = P.sb("fourT", [128, 2, NTOK], BF16)
        P.dma("sp", self.gmixT[:], self.I["g_mixT"][l], w=["gmixT"])
        P.dma("sp", self.esink[:], self.I["sinkT"][l], w=["esink"])
        P.dma("sp", self.bsT[:], self.I["b_sguT"][l], w=["bsT"])
        self.act(self.esink[:], self.esink[:], AF.Exp, r=["esink"], w=["esink"])
        self.adaln(l)
        if self.stop_after == "adaln":
            P.pop()
            return
        self.phaseA(l)
        if self.stop_after == "A":
            P.pop()
            return
        self.phaseF(l)
        if self.stop_after == "F":
            P.pop()
            return
        self.phaseC(l)
        if self.stop_after == "C":
            P.pop()
            return
        P.pop()
        self.phaseD(l)

    def adaln(self, l):
        P = self.P
        I = self.I
        P.push()
        wblk = [P.sb("wada%d" % i, [128, 8, 512], F32) for i in range(2)]
        modrow = P.sb("modrow", [2, 6 * D], F32)
        brow = P.sb("brow", [2, 6 * D], F32)
        gains = P.sb("gains", [2, 4, D], F32)
        rows = P.sb("rows", [2, 6, D], F32)
        sel = P.sb("sel", [2, 2, 128], F32)
        i2 = P.sb("i2", [2, 2], F32)
        P.dma("sp", brow[:], I["b_ada"][l:l + 1, :].partition_broadcast(2), w=["brow"])
        for i, nm in enumerate(("g_pre_mix", "g_post_mix", "g_pre_ff", "g_post_ff")):
            P.dma("sp", gains[:, i, :], I[nm][l:l + 1, :].partition_broadcast(2), w=[("gains", i)])
        P.dma("sp", sel[:].rearrange("k r m -> k (r m)"), self.C["sel"][:, :], w=["sel"])
        P.dma("sp", i2[:], self.C["i2"][:, :], w=["i2"])
        for nb in range(12):
            wb = wblk[nb % 2]
            P.dma("sp", wb[:], I["w_ada"][l, :, nb * 512:(nb + 1) * 512].rearrange("(c p) n -> p c n", p=128),
                  w=[("wada", nb % 2)])
            bk = self.banks[nb % 2]
            self.mm([(bk[0:2, :], self.csil[:, c, :], wb[:, c, :], c == 0, c == 7) for c in range(8)],
                    r=[("wada", nb % 2), "csil"], w=[("ps", nb % 2)])
            self.tt("dve", modrow[:, nb * 512:(nb + 1) * 512], bk[0:2, :], brow[:, nb * 512:(nb + 1) * 512], ALU.add,
                    r=[("ps", nb % 2), "brow"], w=[("modrow", nb)])
        mr = lambda i: modrow[:, i * D:(i + 1) * D]
        mk = lambda i: [("modrow", 2 * i), ("modrow", 2 * i + 1)]
        self.stt(rows[:, 0, :], mr(1), 1.0, gains[:, 0, :], ALU.add, ALU.mult, r=mk(1) + [("gains", 0)], w=[("rows", 0)])
        self.cp("dve", rows[:, 1, :], mr(0), r=mk(0), w=[("rows", 1)])
        self.stt(rows[:, 2, :], mr(4), 1.0, gains[:, 2, :], ALU.add, ALU.mult, r=mk(4) + [("gains", 2)], w=[("rows", 2)])
        self.cp("dve", rows[:, 3, :], mr(3), r=mk(3), w=[("rows", 3)])
        self.tt("dve", rows[:, 4, :], mr(2), gains[:, 1, :], ALU.mult, r=mk(2) + [("gains", 1)], w=[("rows", 4)])
        self.tt("dve", rows[:, 5, :], mr(5), gains[:, 3, :], ALU.mult, r=mk(5) + [("gains", 3)], w=[("rows", 5)])
        bk = self.banks[2]
        specs = []
        for v in range(4):
            for c in range(8):
                col = (v * 8 + c) * 2
                specs.append((bk[:, col:col + 2], rows[:, v, c * 128:(c + 1) * 128], i2[:, :], True, True))
        self.mm(specs, r=[("rows", v) for v in range(4)] + ["i2"], w=[("ps", 2)])
        self.cp("dve", self.modT[:].rearrange("p v c r -> p (v c r)"), bk[:, 0:64], r=[("ps", 2)], w=["modT"])
        if "scratch_out" in self.debug:
            P.dma("pool", self.modT_d[:, :], self.modT[:].rearrange("p v c r -> p (v c r)"), r=["modT"], w=["modT_d"])
        P.dma("pool", self.G_d[:, 0, :], rows[:, 4, :], r=[("rows", 4)], w=["G_d0"])
        P.dma("pool", self.G_d[:, 1, :], rows[:, 5, :], r=[("rows", 5)], w=["G_d1"])
        P.barrier()
        P.pop()

    def new_staging(self, width=1024):
        self._stg = [self.P.sb("stg%d" % i, [128, width], F32) for i in range(2)]
        self._stg_w = width
        self._stg_n = 0

    def load_weight(self, dst, src_fn, nchunks, width, tag):
        P = self.P
        piece = self._stg_w
        keys = []
        for c in range(nchunks):
            for o in range(0, width, piece):
                wd = min(piece, width - o)
                n = self._stg_n
                self._stg_n += 1
                s_ = self._stg[n % 2]
                P.dma("sp", s_[:, 0:wd], src_fn(c)[:, o:o + wd], w=[("stg", n % 2)])
                eng = ("pool", "dve", "act")[n % 3]
                self.cp(eng, dst[:, c, o:o + wd], s_[:, 0:wd], r=[("stg", n % 2)], w=[(tag, c, o)])
                keys.append((tag, c, o))
        return keys

    def phaseA(self, l):
        P = self.P
        I = self.I
        C = self.C
        P.push()
        self.winb = P.sb("winb", [128, 8, NCOL], BF16)
        self.new_staging()
        self.wink = self.load_weight(self.winb, lambda c: I["w_inp"][l, c * 128:(c + 1) * 128, :], 8, NCOL, "winb")
        A = type("A", (), {})()
        self.A = A
        A.xt = [P.sb("xt%d" % i, [128, D], F32) for i in range(4)]
        A.junk = P.sb("junk", [128, D], BF16)
        A.ss = P.sb("ssA", [128, 8], F32)
        A.rs = P.sb("rsA", [128, 8], F32)
        A.xn = [P.sb("xn%d" % i, [128, D], BF16) for i in range(4)]
        A.hxT = [P.sb("hxT%d" % i, [128, 8, 512], BF16) for i in range(2)]
        A.cos = [P.sb("cos%d" % i, [128, 512], F32) for i in range(2)]
        A.sin = [P.sb("sin%d" % i, [128, 512], F32) for i in range(2)]
        A.t1 = [P.sb("t1_%d" % i, [128, 512], F32) for i in range(2)]
        A.t2 = [P.sb("t2_%d" % i, [128, 512], F32) for i in range(2)]
        A.qT = [P.sb("qTt%d" % i, [128, 4, 512], BF16) for i in range(2)]
        A.fT = P.sb("fTt", [128, 2, 512], BF16)
        A.kTt = [P.sb("kTt%d" % i, [128, 512], BF16) for i in range(2)]
        A.vtt = [P.sb("vtt%d" % i, [128, 4, 128], BF16) for i in range(2)]
        A.ABt = [P.sb("ABt%d" % i, [128, 4, 512], BF16) for i in range(2)]
        A.ug = [P.sb("ug%d" % i, [128, 4, 256], F32) for i in range(2)]
        A.gg = [P.sb("gg%d" % i, [128, 4, 256], F32) for i in range(2)]
        A.sq = P.sb("sqA", [128, 4, 256], F32)
        A.ssh = P.sb("ssh", [128, 16], F32)
        A.rsh = P.sb("rsh", [128, 16], F32)
        A.vc = P.sb("vc", [128, 4, 256], BF16)
        A.so = P.sb("so", [128, 4, 256], F32)
        A.ss2 = P.sb("ss2", [128, 4], F32)
        A.rs2 = P.sb("rs2", [128, 4], F32)
        A.son = P.sb("son", [128, 4, 256], BF16)
        A.sgt = [P.sb("sgt%d" % i, [128, 2, 512], BF16) for i in range(2)]
        A.gsg = P.sb("gsg", [128, 256], F32)
        A.wsT = P.sb("wsT", [128, 4, 128], BF16)
        A.wsf = P.sb("wsf", [128, 512], F32)
        A.csblk = P.sb("csblk", [128, 256], BF16)
        P.dma("sp", A.gsg[:], I["g_sgu"][l:l + 1, :].partition_broadcast(128), w=["gsg"])
        P.dma("sp", A.wsf[:], I["w_sguT"][l], w=["wsf"])
        self.cp("dve", A.wsT[:].rearrange("p h q -> p (h q)"), A.wsf[:], r=["wsf"], w=["wsT"])
        P.dma("sp", A.csblk[:], C["csblk"][:, :], w=["csblk"])
        A.trbanks = Ring([(self.banks[i].bitcast(BF16), ("ps", i)) for i in range(2)])
        A.pbanks = Ring([(self.banks[i], ("ps", i)) for i in range(2, 8)])
        A.xi = 0
        tiles = [(S, 2, True)] + [(i * 512, 4, False) for i in range(S // 512)]
        if "A_tiles" in self.debug:
            tiles = tiles[:3]
        A.sgu_gen = None
        self.A_front(l, 0, *tiles[0])
        for i, t in enumerate(tiles):
            if i + 1 < len(tiles):
                self.A_front(l, i + 1, *tiles[i + 1])
            self.A_back(l, i, *t)
            if A.sgu_gen is not None:
                for _ in A.sgu_gen:
                    pass
            A.sgu_gen = self.A_sgu_gen(l, i, *t)
        for _ in A.sgu_gen:
            pass
        P.barrier()
        P.pop()

    def A_front(self, l, it, tok0, nsub, is_ctx):
        P = self.P
        A = self.A
        mi = 1 if is_ctx else 0
        T = nsub * 128
        hx = A.hxT[it % 2]
        hk = ("hxT", it % 2)
        for s in range(nsub):
            xi = A.xi % 4
            A.xi += 1
            P.dma("sp", A.xt[xi][:], self.src_rows(l, tok0 + s * 128, 128), w=[("xt", xi)])
            self.act(A.junk[:], A.xt[xi][:], AF.Square, r=[("xt", xi)], w=["junk", ("ssA", s)], accum_out=A.ss[:, s:s + 1])
            self.rstd(A.rs[:, s:s + 1], A.ss[:, s:s + 1], float(D), [("ssA", s)], [("rsA", s)], ("rsAt", s))
            self.ts("dve", A.xn[xi][:], A.xt[xi][:], A.rs[:, s:s + 1], None, ALU.mult, None,
                    r=[("xt", xi), ("rsA", s)], w=[("xn", xi)])
        xis = [(A.xi - nsub + s) % 4 for s in range(nsub)]
        for cp_ in range(4):
            tb, sk = A.trbanks.next()
            self.tr([(tb[:, j * 512 + s * 128: j * 512 + (s + 1) * 128], A.xn[xis[s]][:, (2 * cp_ + j) * 128:(2 * cp_ + j + 1) * 128])
                     for j in range(2) for s in range(nsub)],
                    r=[("xn", xi) for xi in xis] + ["ident"], w=[sk])
            for j in range(2):
                c = 2 * cp_ + j
                slot = tb[:, j * 512:(j + 1) * 512]
                if cp_ % 2 == 0:
                    self.act(hx[:, c, 0:T], slot[:, 0:T], AF.Identity, r=[sk, "modT"], w=[hk + (c,)],
                             scale=self.modT[:, 0, c, mi:mi + 1], bias=self.modT[:, 1, c, mi:mi + 1])
                else:
                    self.ts("dve", hx[:, c, 0:T], slot[:, 0:T], self.modT[:, 0, c, mi:mi + 1], self.modT[:, 1, c, mi:mi + 1],
                            ALU.mult, ALU.add, r=[sk, "modT"], w=[hk + (c,)])

    def A_back(self, l, it, tok0, nsub, is_ctx):
        P = self.P
        A = self.A
        C = self.C
        if "A_nob" in self.debug:
            return
        T = nsub * 128
        hx = A.hxT[it % 2]
        hkeys = [("hxT", it % 2, c) for c in range(8)]
        W = self.winb
        full = (not is_ctx) or self.ctx_full
        sl = it % 2
        P.dma("sp", A.cos[sl][:, 0:T], C["cosT"][:, tok0:tok0 + T], w=[("cos", sl)])
        P.dma("sp", A.sin[sl][:, 0:T], C["sinT"][:, tok0:tok0 + T], w=[("sin", sl)])

        def fm_chunk(fc):
            bk, bkk = A.pbanks.next()
            self.mm([(bk[:, 0:T], W[:, dc, fc * 128:(fc + 1) * 128], hx[:, dc, 0:T], dc == 0, dc == 7) for dc in range(8)],
                    r=hkeys + self.wink, w=[bkk])
            return bk, bkk

        def step():
            g_ = getattr(A, "sgu_gen", None)
            if g_ is not None:
                if next(g_, "done") == "done":
                    A.sgu_gen = None

        def rope(fc, fcr, out_ap, okey, j):
            bq, kq = fm_chunk(fc)
            br, kr = fm_chunk(fcr)
            self.tt("dve", A.t1[j % 2][:, 0:T], bq[:, 0:T], A.cos[sl][:, 0:T], ALU.mult, r=[kq, ("cos", sl)], w=[("t1", j % 2)])
            self.tt("dve", A.t2[j % 2][:, 0:T], br[:, 0:T], A.sin[sl][:, 0:T], ALU.mult, r=[kr, ("sin", sl)], w=[("t2", j % 2)])
            self.tt("pool", out_ap, A.t1[j % 2][:, 0:T], A.t2[j % 2][:, 0:T], ALU.add, r=[("t1", j % 2), ("t2", j % 2)], w=okey)
            step()

        rope(8, 9, A.kTt[sl][:, 0:T], [("kTt", sl)], 0)
        P.dma("pool", self.kT_d[:, tok0:tok0 + T], A.kTt[sl][:, 0:T], r=[("kTt", sl)], w=[("kT_d", tok0)])
        if full:
            for c in range(4):
                rope(c, 4 + c, A.qT[sl][:, c, 0:T], [("qTt", sl, c)], c + 1)
            P.dma("pool", self.qT_d[:, :, tok0:tok0 + T].rearrange("c p t -> p c t"), A.qT[sl][:, :, 0:T],
                  r=[("qTt", sl, c) for c in range(4)], w=[("qT_d", tok0)])
            for cf in range(2):
                bk, bkk = fm_chunk(10 + cf)
                self.cp("act", A.fT[:, cf, 0:T], bk[:, 0:T], r=[bkk], w=[("fT", cf)])
                step()
        if "A_nob2" in self.debug:
            return
        for s in range(nsub):
            b1, k1 = A.pbanks.next()
            ncol = 384 if full else 128
            self.mm([(b1[:, 0:ncol], hx[:, dc, s * 128:(s + 1) * 128], W[:, dc, 1536:1536 + ncol], dc == 0, dc == 7) for dc in range(8)],
                    r=hkeys + self.wink, w=[k1])
            self.cp("act", A.vtt[sl][:, s, :], b1[:, 0:128], r=[k1], w=[("vtt", sl, s)])
            if full:
                self.act(A.ug[sl][:, s, :], b1[:, 128:384], AF.Gelu_apprx_tanh, r=[k1], w=[("ug", sl, s)])
                b2, k2 = A.pbanks.next()
                self.mm([(b2[:, 0:256], hx[:, dc, s * 128:(s + 1) * 128], W[:, dc, 1920:2176], dc == 0, dc == 7) for dc in range(8)],
                        r=hkeys + self.wink, w=[k2])
                self.act(A.gg[sl][:, s, :], b2[:, 0:256], AF.Gelu_apprx_tanh, r=[k2], w=[("gg", sl, s)])
        blk0 = tok0 // 128
        P.dma("pool", self.vt_d[:, blk0:blk0 + nsub, :], A.vtt[sl][:, 0:nsub, :], r=[("vtt", sl, s) for s in range(nsub)],
              w=[("vt_d", tok0)])
        if not full or "A_nob3" in self.debug:
            return
        ns = nsub
        for s in range(ns):
            bk, bkk = A.pbanks.next()
            specs = []
            for cf in range(2):
                specs.append((bk[:, cf * 256:(cf + 1) * 256], A.fT[:, cf, s * 128:(s + 1) * 128], A.csblk[:, :], True, True))
            self.mm(specs, r=[("fT", 0), ("fT", 1), "csblk"], w=[bkk])
            self.cp("act", A.ABt[sl][:, s, :], bk[:, :], r=[bkk], w=[("ABt", sl, s)])
        P.dma("pool", self.AB_d[tok0:tok0 + T, :].rearrange("(s p) c -> p s c", p=128), A.ABt[sl][:, 0:ns, :],
              r=[("ABt", sl, s) for s in range(ns)], w=[("AB_d", tok0)])
        return

    def A_sgu_gen(self, l, it, tok0, nsub, is_ctx):
        P = self.P
        A = self.A
        full = (not is_ctx) or self.ctx_full
        if not full or "A_nob" in self.debug or "A_nob2" in self.debug or "A_nob3" in self.debug:
            return
        ns = nsub
        T = nsub * 128
        sl = it % 2
        ggk = [("gg", sl, s) for s in range(ns)]
        ugk = [("ug", sl, s) for s in range(ns)]
        self.tt("pool", A.sq[:, 0:ns, :], A.gg[sl][:, 0:ns, :], A.gg[sl][:, 0:ns, :], ALU.mult, r=ggk, w=["sq"])
        P.op("dve", lambda e: e.tensor_reduce(out=A.ssh[:, 0:ns * 4], in_=A.sq[:, 0:ns, :].rearrange("p s (h d) -> p (s h) d", h=4),
                                              axis=AX.X, op=ALU.add), r=["sq"], w=["ssh"])
        yield
        self.rstd(A.rsh[:, 0:ns * 4], A.ssh[:, 0:ns * 4], 64.0, ["ssh"], ["rsh"], "rsht")
        yield
        self.tt("dve", A.sq[:, 0:ns, :].rearrange("p s (h d) -> p (s h) d", h=4),
                A.gg[sl][:, 0:ns, :].rearrange("p s (h d) -> p (s h) d", h=4),
                A.rsh[:, 0:ns * 4].unsqueeze(2).to_broadcast([128, ns * 4, 64]), ALU.mult, r=ggk + ["rsh"], w=["sq"])
        self.tt("dve", A.vc[:, 0:ns, :], A.sq[:, 0:ns, :], A.gsg[:, :].unsqueeze(1).to_broadcast([128, ns, 256]), ALU.mult,
                r=["sq", "gsg"], w=["vc"])
        yield
        for hp in range(2):
            bk, bkk = A.pbanks.next()
            specs = []
            for hh in range(2):
                h = hp * 2 + hh
                o = bk[:, hh * 256:hh * 256 + ns * 64].rearrange("p (s d) -> p s d", d=64)
                specs.append((o, A.wsT[:, h, :], A.vc[:, 0:ns, h * 64:(h + 1) * 64], True, True))
            self.mm(specs, r=["vc", "wsT"], w=[bkk])
            for hh in range(2):
                h = hp * 2 + hh
                self.stt(A.so[:, 0:ns, h * 64:(h + 1) * 64], bk[:, hh * 256:hh * 256 + ns * 64].rearrange("p (s d) -> p s d", d=64),
                         self.bsT[:, h:h + 1], A.ug[sl][:, 0:ns, h * 64:(h + 1) * 64], ALU.add, ALU.mult,
                         r=[bkk, "bsT"] + ugk, w=[("so", h)])
        sok = [("so", h) for h in range(4)]
        yield
        self.tt("pool", A.sq[:, 0:ns, :], A.so[:, 0:ns, :], A.so[:, 0:ns, :], ALU.mult, r=sok, w=["sq"])
        P.op("dve", lambda e: e.tensor_reduce(out=A.ss2[:, 0:ns], in_=A.sq[:, 0:ns, :], axis=AX.X, op=ALU.add), r=["sq"], w=["ss2"])
        self.rstd(A.rs2[:, 0:ns], A.ss2[:, 0:ns], 256.0, ["ss2"], ["rs2"], "rs2t")
        yield
        self.tt("dve", A.son[:, 0:ns, :], A.so[:, 0:ns, :], A.rs2[:, 0:ns].unsqueeze(2).to_broadcast([128, ns, 256]), ALU.mult,
                r=sok + ["rs2"], w=["son"])
        tb, sk = A.trbanks.next()
        self.tr([(tb[:, cc * 512 + s * 128: cc * 512 + (s + 1) * 128], A.son[:, s, cc * 128:(cc + 1) * 128])
                 for cc in range(2) for s in range(ns)], r=["son", "ident"], w=[sk])
        for cc in range(2):
            self.act(A.sgt[sl][:, cc, 0:T], tb[:, cc * 512: cc * 512 + T], AF.Identity, r=[sk, "gmixT"], w=[("sgt", sl, cc)],
                     scale=self.gmixT[:, 4 + cc:5 + cc])
        P.dma("pool", self.sguT_d[:, :, tok0:tok0 + T].rearrange("c p t -> p c t"), A.sgt[sl][:, :, 0:T],
              r=[("sgt", sl, 0), ("sgt", sl, 1)], w=[("sguT_d", tok0)])

    def four_finish(self, ybank_ap, ykey, nk, tok_of, g):
        P = self.P
        F = self.F
        i = F.fi % 2
        F.fi += 1
        ysb = F.ysb[i]
        self.cp("act", ysb[:, 0:nk, :], ybank_ap, r=[ykey], w=[("ysb", i)])
        self.tt("pool", F.ysq[:, 0:nk, :], ysb[:, 0:nk, :], ysb[:, 0:nk, :], ALU.mult, r=[("ysb", i)], w=["ysq"])
        P.op("dve", lambda e: e.tensor_reduce(out=F.yss[:, 0:nk], in_=F.ysq[:, 0:nk, :], axis=AX.X, op=ALU.add), r=["ysq"], w=["yss"])
        self.rstd(F.yrs[:, 0:nk], F.yss[:, 0:nk], 256.0, ["yss"], ["yrs"], "yrst")
        self.tt("dve", F.ynb[i][:, 0:nk, :], ysb[:, 0:nk, :], F.yrs[:, 0:nk].unsqueeze(2).to_broadcast([128, nk, 256]), ALU.mult,
                r=[("ysb", i), "yrs"], w=[("ynb", i)])
        tb, sk = F.trbanks.next()
        self.tr([(tb[:, (j * 2 + cc) * 128:(j * 2 + cc + 1) * 128], F.ynb[i][:, j, cc * 128:(cc + 1) * 128])
                 for j in range(nk) for cc in range(2)], r=[("ynb", i), "ident"], w=[sk])
        for j in range(nk):
            for cc in range(2):
                self.act(tok_of(j, cc), tb[:, (j * 2 + cc) * 128:(j * 2 + cc + 1) * 128], AF.Identity, r=[sk, "gmixT"],
                         w=[("fourT", g, j, cc)], scale=self.gmixT[:, 6 + cc:7 + cc])

    def phaseF(self, l):
        P = self.P
        C = self.C
        P.push()
        F = type("F", (), {})()
        self.F = F
        F.fi = 0
        F.ab1 = P.sb("ab1", [128, 128, 128], BF16)
        F.p1 = P.sb("p1", [128, 64, 2, 128], BF16)
        F.mc = P.sb("mc", [128, 64, 128], BF16)
        F.ms = P.sb("ms", [128, 64, 128], BF16)
        F.t1tab = P.sb("t1tab", [128, 128], BF16)
        F.ysb = [P.sb("ysb%d" % i, [128, 4, 256], F32) for i in range(2)]
        F.ysq = P.sb("ysq", [128, 4, 256], F32)
        F.yss = P.sb("yss", [128, 4], F32)
        F.yrs = P.sb("yrs", [128, 4], F32)
        F.ynb = [P.sb("ynb%d" % i, [128, 4, 256], BF16) for i in range(2)]
        F.trbanks = Ring([(self.banks[i].bitcast(BF16), ("ps", i)) for i in range(2)])
        P.dma("sp", F.mc[:].rearrange("p a b -> p (a b)"), C["mc"][:, :], w=["mc"])
        P.dma("sp", F.ms[:].rearrange("p a b -> p (a b)"), C["ms"][:, :], w=["ms"])
        P.dma("sp", F.t1tab[:], C["t1tab"][:, :], w=["t1tab"])
        pb = Ring([(self.banks[i], ("ps", i)) for i in range(2, 5)])
        yb = Ring([(self.banks[i], ("ps", i)) for i in range(5, 8)])
        F.p1b = P.sb("p1b", [128, 64, 2, 128], BF16)
        p1s = [F.p1, F.p1b]
        for hh in range(2):
            for ab in range(2):
                src = self.AB_d[0:S, hh * 256 + ab * 128: hh * 256 + ab * 128 + 128].rearrange("(t1 t2) c -> t1 t2 c", t2=128)
                P.dma("sp", F.ab1[ab * 64:(ab + 1) * 64, :, :], src, w=[("ab1", ab)])
            for c4 in range(32):
                bk, bkk = pb.next()
                specs = []
                for j in range(4):
                    ch = c4 * 4 + j
                    specs.append((bk[:, j * 128:(j + 1) * 128], F.ab1[:, :, ch], F.t1tab[:, :], True, True))
                self.mm(specs, r=[("ab1", 0), ("ab1", 1), "t1tab"], w=[bkk])
                o = p1s[hh][:, :, :, c4 * 4:(c4 + 1) * 4].rearrange("p k r c -> p c r k")
                i_ = bk[:, :].rearrange("p (c r k) -> p c r k", c=4, r=2)
                if c4 % 2 == 0:
                    self.cp("act", o, i_, r=[bkk], w=[("p1", hh, c4)])
                else:
                    self.cp("dve", o, i_, r=[bkk], w=[("p1", hh, c4)])
        p1keys = [("p1", hh, c4) for hh in range(2) for c4 in range(32)]
        for kg in range(16):
            bka, kka = yb.next()
            for half2 in range(2):
                if half2 == 1:
                    bka, kka = yb.next()
                specs = []
                for j in range(2):
                    k1 = kg * 4 + half2 * 2 + j
                    for hh in range(2):
                        o = bka[:, j * 256 + hh * 128: j * 256 + hh * 128 + 128]
                        specs.append((o, F.mc[:, k1, :], p1s[hh][:, k1, 0, :], True, False))
                        specs.append((o, F.ms[:, k1, :], p1s[hh][:, k1, 1, :], False, True))
                self.mm(specs, r=p1keys + ["mc", "ms"], w=[kka])
                k1base = kg * 4 + half2 * 2

                def tok_of(j, cc, k1base=k1base):
                    return self.fourT[:, cc, 0:S].rearrange("p (k2 k1) -> p k1 k2", k1=64)[:, k1base + j, :]
                self.four_finish(bka[:, :].rearrange("p (j c) -> p j c", j=2), kka, 2, tok_of, ("L", kg, half2))
        if self.ctx_full:
            cab = P.sb("cab", [128, 2, 512], BF16)
            c256 = P.sb("c256", [128, 2, 256], BF16)
            s256 = P.sb("s256n", [128, 2, 256], BF16)
            P.dma("sp", cab[:], self.AB_d[S:S + 256, :].rearrange("(tb p) c -> p tb c", p=128), w=["cab"])
            P.dma("sp", c256[:].rearrange("p a b -> p (a b)"), C["c256"][:, :], w=["c256"])
            P.dma("sp", s256[:].rearrange("p a b -> p (a b)"), C["s256n"][:, :], w=["s256"])
            for kb in range(2):
                bk, bkk = yb.next()
                specs = []
                for hh in range(2):
                    o = bk[:, hh * 128:(hh + 1) * 128]
                    for tb in range(2):
                        specs.append((o, c256[:, tb, kb * 128:(kb + 1) * 128], cab[:, tb, hh * 256:hh * 256 + 128], tb == 0, False))
                        specs.append((o, s256[:, tb, kb * 128:(kb + 1) * 128], cab[:, tb, hh * 256 + 128:hh * 256 + 256], False, tb == 1))
                self.mm(specs, r=["cab", "c256", "s256"], w=[bkk])

                def tok_of(j, cc, kb=kb):
                    return self.fourT[:, cc, S + kb * 128:S + (kb + 1) * 128]
                self.four_finish(bk[:, 0:256].rearrange("p (j c) -> p j c", j=1), bkk, 1, tok_of, ("C", kb))
        if "scratch_out" in self.debug and l == 0:
            P.barrier()
            P.dma("pool", self.fourT_d[:, :, :], self.fourT[:], w=["fourT_d"])
        P.barrier()
        P.pop()

    def phaseC(self, l):
        P = self.P
        I = self.I
        C = self.C
        P.push()
        self.woutb = P.sb("woutb", [128, 8, D], BF16)
        self.new_staging()
        self.woutk = self.load_weight(self.woutb, lambda c: I["w_outp"][l, c * 128:(c + 1) * 128, :], 8, D, "woutb")
        K = type("K", (), {})()
        self.K = K
        self.kT_all = P.sb("kT_all", [128, NTOK], BF16)
        self.vt_all = P.sb("vt_all", [128, 66, 128], BF16)
        P.dma("sp", self.kT_all[:], self.kT_d[:, :], w=["kT_all"])
        P.dma("sp", self.vt_all[:], self.vt_d[:, :, :], w=["vt_all"])
        K.qt = [P.sb("qt%d" % i, [128, 4, 512], BF16) for i in range(2)]
        K.sgt = [P.sb("sgtC%d" % i, [128, 2, 512], BF16) for i in range(2)]
        K.xt = [P.sb("xtC%d" % i, [128, D], F32) for i in range(4)]
        K.PT = [P.sb("PT%d" % i, [128, 2, 512], BF16) for i in range(6)]
        K.oc = [P.sb("oc%d" % i, [128, 512], F32) for i in range(2)]
        K.dd = [P.sb("dd%d" % i, [128, 512], F32) for i in range(2)]
        K.ar = [P.sb("ar%d" % i, [128, 512], F32) for i in range(2)]
        K.sqT = P.sb("sqT", [128, 4, 512], BF16)
        K.aT = P.sb("aT", [128, 4, 512], BF16)
        K.ys2 = [P.sb("ysC%d" % i, [128, D], F32) for i in range(2)]
        K.yc2 = [P.sb("ycC%d" % i, [128, D], F32) for i in range(2)]
        K.tmp = P.sb("tmpC", [128, D], F32)
        K.junk = P.sb("junkC", [128, D], BF16)
        K.ssa = P.sb("ssa", [128, 4], F32)
        K.rsa = P.sb("rsa", [128, 4], F32)
        K.ssy = P.sb("ssy", [128, 4], F32)
        K.rsy = P.sb("rsy", [128, 4], F32)
        K.G1 = P.sb("G1bc", [128, 2, D], F32)
        K.mnext = P.sb("mnext", [128, 128], BF16)
        K.mprev = P.sb("mprev", [128, 128], BF16)
        P.dma("sp", K.mnext[:], C["mask_next"][:, :], w=["mnext"])
        P.dma("sp", K.mprev[:], C["mask_prev"][:, :], w=["mprev"])
        for r_ in range(2):
            P.dma("sp", K.G1[:, r_, :], self.G_d[r_:r_ + 1, 0, :].partition_broadcast(128), w=[("G1", r_)])
        K.spairs = Ring([(self.psum[:, 2 * i:2 * i + 2, :], [("ps", 2 * i), ("ps", 2 * i + 1)]) for i in range(3)])
        K.tailpairs = Ring([(self.psum[:, 2 * i:2 * i + 2, :], [("ps", 2 * i), ("ps", 2 * i + 1)]) for i in range(4)])
        K.pti = 0
        K.xi = 0
        tiles = [(i * 512, 4, False) for i in range(S // 512)]
        if self.ctx_full:
            tiles = tiles + [(S, 2, True)]
        if "C_tiles" in self.debug:
            tiles = tiles[:2] + tiles[-1:]
        for it, t in enumerate(tiles):
            self.C_tile(l, it, *t)
        P.barrier()
        P.pop()

    def C_tile(self, l, it, tok0, nsub, is_ctx):
        P = self.P
        K = self.K
        T = nsub * 128
        sl = it % 2
        mi = 1 if is_ctx else 0
        qt = K.qt[sl]
        P.dma("sp", qt[:, :, 0:T], self.qT_d[:, :, tok0:tok0 + T].rearrange("c p t -> p c t"), w=[("qt", sl)])
        P.dma("sp", K.sgt[sl][:, :, 0:T], self.sguT_d[:, :, tok0:tok0 + T].rearrange("c p t -> p c t"), w=[("sgtC", sl)])
        xis = []
        for s in range(nsub):
            xi = K.xi % 4
            K.xi += 1
            xis.append(xi)
            P.dma("sp", K.xt[xi][:], self.src_rows(l, tok0 + s * 128, 128), w=[("xtC", xi)])
        items = []
        if not is_ctx:
            n0 = tok0 // 128
            for j in range(n0 - 1, n0 + nsub + 1):
                if j < 0 or j >= S // 128:
                    continue
                qlo = max(j - 1, n0) - n0
                qhi = min(j + 1, n0 + nsub - 1) - n0
                mnext = (j - 1 - n0) if (j - 1 >= n0) else None
                mprev = (j + 1 - n0) if (j + 1 <= n0 + nsub - 1) else None
                items.append((j, qlo * 128, (qhi + 1) * 128, mnext, mprev))
        items.append((64, 0, T, None, None))
        items.append((65, 0, T, None, None))
        LOOK = 2
        bo, ko = self.banks[6], ("ps", 6)
        bd, kd = self.banks[7], ("ps", 7)
        deferred = []

        def fin2():
            while deferred:
                c_ = deferred.pop(0)
                d_ = c_ % 2
                self.act(K.sqT[:, c_, 0:T], K.ar[d_][:, 0:T], AF.Square, r=[("ar", d_)], w=[("sqT", c_)])
                self.ts("pool", K.aT[:, c_, 0:T], K.ar[d_][:, 0:T], self.gmixT[:, c_:c_ + 1], 1.0, ALU.mult, ALU.mult,
                        r=[("ar", d_), "gmixT"], w=[("aT", c_)])

        for c in range(4):
            pend = []
            first = True
            for idx in range(len(items) + LOOK):
                if idx == min(5, len(items) + LOOK - 1):
                    fin2()
                if idx < len(items):
                    j, lo, hi, mn, mp = items[idx]
                    n = hi - lo
                    pair, pkeys = K.spairs.next()
                    kcols = slice(j * 128, (j + 1) * 128)
                    self.mm([(pair[:, 0, 0:n], self.kT_all[0:64, kcols], qt[0:64, c, lo:hi], True, True),
                             (pair[:, 1, 0:n], self.kT_all[64:128, kcols], qt[64:128, c, lo:hi], True, True)],
                            r=[("qt", sl), "kT_all"], w=pkeys)
                    pi = K.pti % len(K.PT)
                    K.pti += 1
                    pt = K.PT[pi]
                    pk = [("PT", pi)]
                    self.act(pt[:, :, 0:n], pair[:, :, 0:n], AF.Exp, r=pkeys, w=pk, scale=0.125)
                    if mn is not None:
                        o = mn * 128 - lo
                        self.tt("dve", pt[:, :, o:o + 128], pt[:, :, o:o + 128],
                                K.mnext[:, :].unsqueeze(1).to_broadcast([128, 2, 128]), ALU.mult, r=pk + ["mnext"], w=pk)
                    if mp is not None:
                        o = mp * 128 - lo
                        self.tt("dve", pt[:, :, o:o + 128], pt[:, :, o:o + 128],
                                K.mprev[:, :].unsqueeze(1).to_broadcast([128, 2, 128]), ALU.mult, r=pk + ["mprev"], w=pk)
                    pend.append((j, lo, hi, pt, pk))
                if idx >= LOOK:
                    j, lo, hi, pt, pk = pend.pop(0)
                    n = hi - lo
                    last = (idx == len(items) + LOOK - 1)
                    self.mm([(bo[0:64, lo:hi], self.vt_all[:, j, 0:64], pt[:, 0, 0:n], first, last),
                             (bo[64:128, lo:hi], self.vt_all[:, j, 64:128], pt[:, 1, 0:n], first, last),
                             (bd[0:64, lo:hi], self.ones[:, 0:64], pt[:, 0, 0:n], first, last),
                             (bd[64:128, lo:hi], self.ones[:, 0:64], pt[:, 1, 0:n], first, last)],
                            r=pk + ["ones", "vt_all"], w=[ko, kd])
                    first = False
            d2 = c % 2
            self.cp("act", K.oc[d2][:, 0:T], bo[:, 0:T], r=[ko], w=[("oc", d2)])
            self.ts("dve", K.dd[d2][:, 0:T], bd[:, 0:T], self.esink[:, c:c + 1], None, ALU.add, None, r=[kd, "esink"], w=[("dd", d2)])
            self.recip(K.dd[d2][:, 0:T], K.dd[d2][:, 0:T], r=[("dd", d2)], w=[("dd", d2)])
            self.tt("dve", K.ar[d2][:, 0:T], K.oc[d2][:, 0:T], K.dd[d2][:, 0:T], ALU.mult, r=[("oc", d2), ("dd", d2)], w=[("ar", d2)])
            deferred.append(c)
        fin2()
        pair, pkeys = K.spairs.next()
        bs_, ks_ = pair[:, 0, :], pkeys[0]
        specs = []
        for s in range(nsub):
            for c in range(4):
                specs.append((bs_[:, s:s + 1], K.sqT[:, c, s * 128:(s + 1) * 128], self.ones[:, 0:1], c == 0, c == 3))
        self.mm(specs, r=[("sqT", c) for c in range(4)] + ["ones"], w=[ks_])
        self.rstd(K.rsa[:, 0:nsub], bs_[:, 0:nsub], 512.0, [ks_], ["rsa"], "rsat")
        aTk = [("aT", c) for c in range(4)]
        def st1(s):
            xi = xis[s]
            cols = slice(s * 128, (s + 1) * 128)
            b2 = s % 2
            ys, yc = K.ys2[b2], K.yc2[b2]
            pa, pak = K.tailpairs.next()
            pb_, pbk = K.tailpairs.next()
            for half in range(2):
                self.mm([(pa[:, half, :], K.aT[:, c, cols], self.woutb[:, c, half * 512:(half + 1) * 512], c == 0, c == 3) for c in range(4)],
                        r=aTk + self.woutk, w=[pak[half]])
            for half in range(2):
                specs = []
                for cc in range(2):
                    specs.append((pb_[:, half, :], K.sgt[sl][:, cc, cols], self.woutb[:, 4 + cc, half * 512:(half + 1) * 512], cc == 0, False))
                for cc in range(2):
                    specs.append((pb_[:, half, :], self.fourT[:, cc, tok0 + s * 128: tok0 + (s + 1) * 128],
                                  self.woutb[:, 6 + cc, half * 512:(half + 1) * 512], False, cc == 1))
                self.mm(specs, r=[("sgtC", sl)] + self.woutk, w=[pbk[half]])
            self.cp("act", ys[:].rearrange("p (h n) -> p h n", h=2), pb_[:, :, :], r=pbk, w=[("ysC", b2)])
            self.stt(yc[:].rearrange("p (h n) -> p h n", h=2), pa[:, :, :], K.rsa[:, s:s + 1], ys[:].rearrange("p (h n) -> p h n", h=2),
                     ALU.mult, ALU.add, r=pak + ["rsa", ("ysC", b2)], w=[("ycC", b2)])

        def st2(s):
            xi = xis[s]
            b2 = s % 2
            yc = K.yc2[b2]
            yck = [("ycC", b2)]
            self.act(K.junk[:], yc[:], AF.Square, r=yck, w=["junkC", ("ssy", s)], accum_out=K.ssy[:, s:s + 1])
            self.rstd(K.rsy[:, s:s + 1], K.ssy[:, s:s + 1], float(D), [("ssy", s)], [("rsy", s)], ("rsyt", s))
            self.tt("pool", K.tmp[:], yc[:], K.G1[:, mi, :], ALU.mult, r=yck + [("G1", mi)], w=["tmpC"])
            self.stt(K.xt[xi][:], K.tmp[:], K.rsy[:, s:s + 1], K.xt[xi][:], ALU.mult, ALU.add,
                     r=["tmpC", ("rsy", s), ("xtC", xi)], w=[("xtC", xi)])
            P.dma("pool", self.xmid_d[tok0 + s * 128: tok0 + (s + 1) * 128, :], K.xt[xi][:], r=[("xtC", xi)], w=[("xmid", tok0, s)])

        st1(0)
        for s in range(1, nsub):
            st1(s)
            st2(s - 1)
        st2(nsub - 1)

    def phaseD(self, l):
        P = self.P
        I = self.I
        P.push()
        self.w1b = P.sb("w1b", [128, 8, DFF], BF16)
        self.w2b = P.sb("w2b", [128, 32, D], BF16)
        self.new_staging(512)
        self.w1k = self.load_weight(self.w1b, lambda c: I["w_ff1"][l, c * 128:(c + 1) * 128, :], 8, DFF, "w1b")
        self.w2k = self.load_weight(self.w2b, lambda c: I["w_ff2"][l, c * 128:(c + 1) * 128, :], 32, D, "w2b")
        Dd = type("Dd", (), {})()
        self.Dd = Dd
        Dd.xm = [P.sb("xm%d" % i, [128, D], F32) for i in range(6)]
        Dd.xn = [P.sb("xnD%d" % i, [128, D], BF16) for i in range(2)]
        Dd.hx = [P.sb("hx2T%d" % i, [128, 8, 256], BF16) for i in range(2)]
        Dd.hT = P.sb("hT", [128, 32, 256], BF16)
        Dd.rl = [P.sb("rl%d" % i, [128, 512], F32) for i in range(2)]
        Dd.tmp = P.sb("tmpD", [128, D], F32)
        Dd.junk = P.sb("junkD", [128, D], BF16)
        Dd.ss = P.sb("ssD", [128, 4], F32)
        Dd.rs = P.sb("rsD", [128, 4], F32)
        Dd.ssy = P.sb("ssyD", [128, 4], F32)
        Dd.rsy = P.sb("rsyD", [128, 2], F32)
        Dd.G2 = P.sb("G2bc", [128, 2, D], F32)
        for r_ in range(2):
            P.dma("sp", Dd.G2[:, r_, :], self.G_d[r_:r_ + 1, 1, :].partition_broadcast(128), w=[("G2", r_)])
        Dd.trbanks = Ring([(self.banks[i].bitcast(BF16), ("ps", i)) for i in range(2)])
        Dd.fbanks = Ring([(self.banks[i], ("ps", i)) for i in range(2, 5)])
        Dd.yring = Ring([(self.banks[i], ("ps", i)) for i in range(5, 8)])
        Dd.xi = 0
        Dd.rli = 0
        tiles = [(i * 256, False) for i in range(S // 256)]
        if self.ctx_full:
            tiles = tiles + [(S, True)]
        if "D_tiles" in self.debug:
            tiles = tiles[:2] + tiles[-1:]
        self.D_front(l, 0, *tiles[0])
        if len(tiles) > 1:
            self.D_front(l, 1, *tiles[1])
        for it, t in enumerate(tiles):
            self.D_back(l, it, t[0], t[1], part=1)
            if it + 2 < len(tiles):
                self.D_front(l, it + 2, *tiles[it + 2])
            self.D_back(l, it, t[0], t[1], part=2)
        P.barrier()
        P.pop()

    def D_front(self, l, it, tok0, is_ctx):
        P = self.P
        Dd = self.Dd
        mi = 1 if is_ctx else 0
        hx = Dd.hx[it % 2]
        xis = []
        for s in range(2):
            xi = Dd.xi % 6
            Dd.xi += 1
            xis.append(xi)
            P.dma("sp", Dd.xm[xi][:], self.xmid_d[tok0 + s * 128: tok0 + (s + 1) * 128, :], w=[("xm", xi)])
            self.act(Dd.junk[:], Dd.xm[xi][:], AF.Square, r=[("xm", xi)], w=["junkD", ("ssD", s)], accum_out=Dd.ss[:, s:s + 1])
            self.rstd(Dd.rs[:, s:s + 1], Dd.ss[:, s:s + 1], float(D), [("ssD", s)], [("rsD", s)], ("rsDt", s))
            self.ts("dve", Dd.xn[s][:], Dd.xm[xi][:], Dd.rs[:, s:s + 1], None, ALU.mult, None, r=[("xm", xi), ("rsD", s)], w=[("xnD", s)])
        Dd.cur_xis = getattr(Dd, "cur_xis", {})
        Dd.cur_xis[it] = xis
        for cq in range(2):
            tb, sk = Dd.trbanks.next()
            self.tr([(tb[:, j * 256 + s * 128: j * 256 + (s + 1) * 128], Dd.xn[s][:, (4 * cq + j) * 128:(4 * cq + j + 1) * 128])
                     for j in range(4) for s in range(2)], r=[("xnD", 0), ("xnD", 1), "ident"], w=[sk])
            for j in range(4):
                c = 4 * cq + j
                slot = tb[:, j * 256:(j + 1) * 256]
                if cq == 0:
                    self.act(hx[:, c, :], slot, AF.Identity, r=[sk, "modT"], w=[("hx2T", it % 2, c)],
                             scale=self.modT[:, 2, c, mi:mi + 1], bias=self.modT[:, 3, c, mi:mi + 1])
                else:
                    self.ts("dve", hx[:, c, :], slot, self.modT[:, 2, c, mi:mi + 1], self.modT[:, 3, c, mi:mi + 1], ALU.mult, ALU.add,
                            r=[sk, "modT"], w=[("hx2T", it % 2, c)])

    def D_back(self, l, it, tok0, is_ctx, part):
        P = self.P
        Dd = self.Dd
        mi = 1 if is_ctx else 0
        hx = Dd.hx[it % 2]
        hk = [("hx2T", it % 2, c) for c in range(8)]
        xis = Dd.cur_xis[it]
        for fp in range(16 if part == 1 else 0):
            fb, sk = Dd.fbanks.next()
            specs = []
            for j in range(2):
                fc = 2 * fp + j
                for dc in range(8):
                    specs.append((fb[:, j * 256:(j + 1) * 256], self.w1b[:, dc, fc * 128:(fc + 1) * 128], hx[:, dc, :], dc == 0, dc == 7))
            self.mm(specs, r=hk + self.w1k, w=[sk])
            ri = Dd.rli % 2
            Dd.rli += 1
            self.act(Dd.rl[ri][:], fb[:, :], AF.Relu, r=[sk], w=[("rl", ri)])
            self.tt("pool" if fp % 2 == 0 else "dve", Dd.hT[:, 2 * fp:2 * fp + 2, :].rearrange("p a t -> p (a t)"), Dd.rl[ri][:], Dd.rl[ri][:],
                    ALU.mult, r=[("rl", ri)], w=[("hT", 2 * fp), ("hT", 2 * fp + 1)])
        if part == 1:
            return
        hTk = [("hT", fc) for fc in range(32)]
        for s in range(2):
            xi = xis[s]
            ybs = []
            for half in range(2):
                b_, k_ = Dd.yring.next()
                self.mm([(b_[:, :], Dd.hT[:, fc, s * 128:(s + 1) * 128], self.w2b[:, fc, half * 512:(half + 1) * 512], fc == 0, fc == 31)
                         for fc in range(32)], r=hTk + self.w2k, w=[k_])
                ybs.append((b_, k_))
            for half in range(2):
                self.act(Dd.junk[:, half * 512:(half + 1) * 512], ybs[half][0][:, :], AF.Square, r=[ybs[half][1]],
                         w=[("junkD", half), ("ssyD", s, half)], accum_out=Dd.ssy[:, s * 2 + half:s * 2 + half + 1])
            self.tt("dve", Dd.ssy[:, s * 2:s * 2 + 1], Dd.ssy[:, s * 2:s * 2 + 1], Dd.ssy[:, s * 2 + 1:s * 2 + 2], ALU.add,
                    r=[("ssyD", s, 0), ("ssyD", s, 1)], w=[("ssyDs", s)])
            self.rstd(Dd.rsy[:, s:s + 1], Dd.ssy[:, s * 2:s * 2 + 1], float(D), [("ssyDs", s)], [("rsyD", s)], ("rsyDt", s))
            for half in range(2):
                hc = slice(half * 512, (half + 1) * 512)
                self.tt("dve", Dd.tmp[:, hc], ybs[half][0][:, :], Dd.G2[:, mi, hc], ALU.mult, r=[ybs[half][1], ("G2", mi)], w=[("tmpD", half)])
            self.stt(Dd.xm[xi][:], Dd.tmp[:], Dd.rsy[:, s:s + 1], Dd.xm[xi][:], ALU.mult, ALU.add,
                     r=[("tmpD", 0), ("tmpD", 1), ("rsyD", s), ("xm", xi)], w=[("xm", xi)])
            if is_ctx and l == self.nlayers - 1:
                continue
            P.dma("pool", self.dst_rows(l, tok0 + s * 128, 128), Dd.xm[xi][:], r=[("xm", xi)], w=[("xout", tok0, s)])


def _prep_shared(inp):
    sh = {}
    f = lambda a: np.ascontiguousarray(a, dtype=np.float32)
    sh["w_ada"] = f(inp["w_ada"])
    sh["b_ada"] = f(inp["b_ada"])
    for nm in ("g_pre_mix", "g_post_mix", "g_pre_ff", "g_post_ff"):
        sh[nm] = f(inp[nm])
    sh["w_inp"] = f(inp["w_in"][:, :, _perm_cols()])
    sink = inp["sink"]
    sinkT = np.zeros((2, 128, 4), np.float32)
    for c in range(4):
        sinkT[:, 0:64, c] = sink[:, c][:, None]
        sinkT[:, 64:128, c] = sink[:, 4 + c][:, None]
    sh["sinkT"] = sinkT
    sh["w_sguT"] = f(np.transpose(inp["w_sgu"], (0, 3, 1, 2)).reshape(2, 128, 512))
    sh["b_sguT"] = f(np.transpose(inp["b_sgu"], (0, 2, 1)))
    sh["g_sgu"] = f(inp["g_sgu"].reshape(2, 256))
    rp = _attn_row_perm()
    gm = inp["g_mix"][:, rp]
    sh["g_mixT"] = f(np.transpose(gm.reshape(2, 8, 128), (0, 2, 1)))
    sh["w_outp"] = f(inp["w_out"][:, rp, :])
    sh["w_ff1"] = f(inp["w_ff1"])
    sh["w_ff2"] = f(inp["w_ff2"])
    return sh


def _core_inputs(inp, sh, consts, b):
    m = dict(sh)
    m["x"] = np.ascontiguousarray(inp["x"][b], dtype=np.float32)
    m["ctx"] = np.ascontiguousarray(inp["ctx"][b], dtype=np.float32)
    cs = np.stack([inp["c"][b], inp["c_ctx"]], axis=0).astype(np.float32)
    m["csT"] = np.ascontiguousarray(np.transpose(cs.reshape(2, 8, 128), (2, 1, 0)).reshape(128, 16))
    for k, v in consts.items():
        m["c_" + k] = v
    return m


_CACHE = {}


def kernel(**inputs):
    inp = {k: np.asarray(v) for k, v in inputs.items()}
    if "nc" not in _CACHE:
        _CACHE["nc"] = Builder().build()
        _CACHE["consts"] = _constants()
    nc = _CACHE["nc"]
    consts = _CACHE["consts"]
    sh = _prep_shared(inp)
    in_maps = [_core_inputs(inp, sh, consts, b) for b in range(8)]
    res = run_bass_kernel_spmd(nc, in_maps, core_ids=list(range(8)))
    out = np.stack([np.asarray(res.results[b]["out"], dtype=np.float32) for b in range(8)], axis=0)
    return out
```

```python
import contextlib
import numpy as np
import ml_dtypes
import concourse.bass as bass
import concourse.mybir as mybir
from concourse.bass_utils import run_bass_kernel_spmd

F32 = mybir.dt.float32
BF16 = mybir.dt.bfloat16
AF = mybir.ActivationFunctionType
ALU = mybir.AluOpType
AX = mybir.AxisListType
NPBF = ml_dtypes.bfloat16

COMPUTE = ("pe", "act", "dve", "pool")
NDMA_SEMS = 20

D = 1024
S = 8192
LCTX = 256
NTOK = S + LCTX
DFF = 4096
NCOL = 2176
EPS = 1e-6


class _Op:
    __slots__ = ("eng", "fn", "deps", "signal", "sigval", "is_dma", "sem_idx", "dma_val", "prev_same_sem")

    def __init__(self, eng, fn, is_dma):
        self.eng = eng
        self.fn = fn
        self.deps = []
        self.signal = False
        self.sigval = 0
        self.is_dma = is_dma
        self.sem_idx = None
        self.dma_val = 0
        self.prev_same_sem = None


class Prog:
    def __init__(self, nc):
        self.nc = nc
        self.ops = {e: [] for e in ("pe", "act", "dve", "pool", "sp")}
        self.last_w = {}
        self.readers = {}
        self.dma_count = {"sp": 0, "pool": 0, "act": 0}
        self.dma_last_on_sem = {}
        self.stacks = [contextlib.ExitStack()]
        self.last_op = {}

    def push(self):
        self.stacks.append(contextlib.ExitStack())

    def pop(self):
        self.stacks.pop().close()

    def sb(self, name, shape, dt):
        self.uid = getattr(self, "uid", 0) + 1
        return self.stacks[-1].enter_context(self.nc.sbuf_tensor("%s_u%d" % (name, self.uid), list(shape), dt))

    def ps(self, name, shape, dt):
        return self.stacks[-1].enter_context(self.nc.psum_tensor(name, list(shape), dt))

    def _track(self, op, r, w):
        ps_keys = [("ps", k[1]) for k in list(r) + list(w) if isinstance(k, tuple) and k[0] == "ps"]
        r = [k for k in r if not (isinstance(k, tuple) and k[0] == "ps")]
        w = [k for k in w if not (isinstance(k, tuple) and k[0] == "ps")] + list(dict.fromkeys(ps_keys))
        deps = set()
        for k in r:
            lw = self.last_w.get(k)
            if lw is not None:
                deps.add(lw)
        for k in w:
            lw = self.last_w.get(k)
            if lw is not None and (op.is_dma or lw.is_dma or lw.eng != op.eng):
                deps.add(lw)
            for rd in self.readers.get(k, ()):
                if rd is op:
                    continue
                if op.is_dma or rd.is_dma or rd.eng != op.eng:
                    deps.add(rd)
        for d in deps:
            if d is op:
                continue
            d.signal = True
            op.deps.append(d)
        for k in r:
            self.readers.setdefault(k, []).append(op)
        for k in w:
            self.last_w[k] = op
            self.readers[k] = []

    def op(self, eng, fn, r=(), w=()):
        o = _Op(eng, fn, False)
        self._track(o, r, w)
        self.ops[eng].append(o)
        self.last_op[eng] = o
        return o

    def wait_all(self, eng, r=()):
        o = _Op(eng, None, False)
        self._track(o, r, ())
        self.ops[eng].append(o)
        return o

    def dma(self, q, out, in_, r=(), w=(), **kw):
        o = _Op(q, (out, in_, kw), True)
        n = self.dma_count[q]
        self.dma_count[q] = n + 1
        nsem = NDMA_SEMS if q != "pool" else 6
        o.sem_idx = (q, n % nsem)
        prev = self.dma_last_on_sem.get(o.sem_idx)
        o.prev_same_sem = prev
        o.dma_val = (prev.dma_val if prev is not None else 0) + 16
        self.dma_last_on_sem[o.sem_idx] = o
        self._track(o, r, w)
        self.ops[q].append(o)
        return o

    def barrier(self):
        srcs = [o for o in self.last_op.values()] + list(self.dma_last_on_sem.values())
        for e in ("pe", "act", "dve", "pool", "sp"):
            o = _Op(e, None, False)
            for s_ in srcs:
                if (not s_.is_dma) and s_.eng == e:
                    continue
                s_.signal = True
                o.deps.append(s_)
            self.ops[e].append(o)
        self.last_w = {}
        self.readers = {}

    def emit(self):
        nc = self.nc
        st = self.stacks[0]
        esem = {e: st.enter_context(nc.semaphore("sem_" + e)) for e in COMPUTE}
        dsem = {}
        for q in ("sp", "pool", "act"):
            for i in range(min(NDMA_SEMS, self.dma_count[q])):
                dsem[(q, i)] = st.enter_context(nc.semaphore("dsem_%s_%d" % (q, i)))
        for e in COMPUTE:
            c = 0
            for o in self.ops[e]:
                if (not o.is_dma) and o.signal:
                    c += 1
                    o.sigval = c

        def src_of(d):
            if d.is_dma:
                return dsem[d.sem_idx], d.dma_val, ("d",) + d.sem_idx
            return esem[d.eng], d.sigval, ("e", d.eng)

        def run(engname, eng):
            seen = {}
            for o in self.ops[engname]:
                waits = {}
                deps = list(o.deps)
                if o.is_dma and o.prev_same_sem is not None:
                    deps.append(o.prev_same_sem)
                for d in deps:
                    sem, val, key = src_of(d)
                    if seen.get(key, 0) >= val:
                        continue
                    if key not in waits or waits[key][1] < val:
                        waits[key] = (sem, val)
                for key, (sem, val) in waits.items():
                    eng.wait_ge(sem, val)
                    seen[key] = val
                if o.is_dma:
                    out, in_, kw = o.fn
                    eng.dma_start(out=out, in_=in_, **kw).then_inc(dsem[o.sem_idx], 16)
                elif o.fn is None:
                    pass
                else:
                    ins = o.fn(eng)
                    if o.signal:
                        ins.then_inc(esem[engname], 1)

        with nc.Block() as block:
            @block.tensor
            def _(e):
                run("pe", e)

            @block.scalar
            def _(e):
                run("act", e)

            @block.vector
            def _(e):
                run("dve", e)

            @block.gpsimd
            def _(e):
                run("pool", e)

            @block.sync
            def _(e):
                run("sp", e)

    def close(self):
        while self.stacks:
            self.stacks.pop().close()


class Ring:
    def __init__(self, items):
        self.items = items
        self.i = 0

    def next(self):
        it = self.items[self.i % len(self.items)]
        self.i += 1
        return it


def _perm_cols():
    def rotp(d):
        if d < 16:
            return d + 16
        if d < 32:
            return d - 16
        if d < 48:
            return d + 16
        return d - 16
    cols = []
    for c in range(4):
        for h in (c, 4 + c):
            cols += [h * 64 + d for d in range(64)]
    for c in range(4):
        for h in (c, 4 + c):
            cols += [h * 64 + rotp(d) for d in range(64)]
    cols += [512 + j for j in range(128)]
    cols += [512 + h * 64 + rotp(d) for h in range(2) for d in range(64)]
    cols += [1280 + j for j in range(256)]
    cols += [640 + j for j in range(128)]
    cols += [768 + j for j in range(256)]
    cols += [1024 + j for j in range(256)]
    assert len(cols) == NCOL
    return np.array(cols)


def _attn_row_perm():
    rows = []
    for c in range(4):
        for h in (c, 4 + c):
            rows += [h * 64 + d for d in range(64)]
    return np.array(rows + list(range(512, 1024)))


def _constants():
    k = {}
    k["ident"] = np.eye(128, dtype=np.float32).astype(NPBF)
    n_freq = 16
    inv_freq = (np.float32(10000.0) ** (-np.arange(n_freq, dtype=np.float32) / np.float32(n_freq))).astype(np.float32)
    t = np.arange(S)
    row = (t // 64).astype(np.float32)
    col = (t % 64).astype(np.float32)
    ang = np.zeros((64, S), np.float32)
    ang[0:16] = row[None, :] * inv_freq[:, None]
    ang[16:32] = ang[0:16]
    ang[32:48] = col[None, :] * inv_freq[:, None]
    ang[48:64] = ang[32:48]
    cos = np.cos(ang).astype(np.float32)
    sin = np.sin(ang).astype(np.float32)
    sin[0:16] *= -1
    sin[32:48] *= -1
    cosT = np.ones((128, NTOK), np.float32)
    sinT = np.zeros((128, NTOK), np.float32)
    cosT[0:64, :S] = cos
    cosT[64:128, :S] = cos
    sinT[0:64, :S] = sin
    sinT[64:128, :S] = sin
    k["cosT"] = cosT
    k["sinT"] = sinT
    kk = np.arange(128)[:, None]
    qq = np.arange(128)[None, :]
    k["mask_next"] = (kk <= qq).astype(np.float32).astype(NPBF)
    k["mask_prev"] = (kk >= qq).astype(np.float32).astype(NPBF)
    k["ones"] = np.ones((128, 128), np.float32).astype(NPBF)
    cidx = np.arange(64)
    c64 = np.cos(2 * np.pi * np.outer(cidx, cidx) / 64.0) / 8.0
    s64 = np.sin(2 * np.pi * np.outer(cidx, cidx) / 64.0) / 8.0
    csblk = np.zeros((128, 256), np.float64)
    for g in range(2):
        csblk[g * 64:(g + 1) * 64, g * 64:(g + 1) * 64] = c64
        csblk[g * 64:(g + 1) * 64, 128 + g * 64:128 + (g + 1) * 64] = s64
    k["csblk"] = csblk.astype(np.float32).astype(NPBF)
    t1 = np.zeros((128, 128), np.float64)
    t1[0:64, 0:64] = c64
    t1[0:64, 64:128] = -s64
    t1[64:128, 0:64] = -s64
    t1[64:128, 64:128] = -c64
    k["t1tab"] = t1.astype(np.float32).astype(NPBF)
    t2 = np.arange(128)[:, None, None].astype(np.float64)
    k1 = np.arange(64)[None, :, None].astype(np.float64)
    k2 = np.arange(128)[None, None, :].astype(np.float64)
    phi = 2 * np.pi * ((t2 * (k1 + 64 * k2)) % 8192) / 8192.0
    k["mc"] = (np.cos(phi) / np.sqrt(128.0)).astype(np.float32).astype(NPBF).reshape(128, 64 * 128)
    k["ms"] = (np.sin(phi) / np.sqrt(128.0)).astype(np.float32).astype(NPBF).reshape(128, 64 * 128)
    tt = (np.arange(2)[None, :, None] * 128 + np.arange(128)[:, None, None]).astype(np.float64)
    kk2 = np.arange(256)[None, None, :].astype(np.float64)
    ph = 2 * np.pi * ((tt * kk2) % 256) / 256.0
    k["c256"] = (np.cos(ph) / 16.0).astype(np.float32).astype(NPBF).reshape(128, 512)
    k["s256n"] = (-np.sin(ph) / 16.0).astype(np.float32).astype(NPBF).reshape(128, 512)
    sel = np.zeros((2, 2, 128), np.float32)
    sel[0, 0, :] = 1
    sel[1, 1, :] = 1
    k["sel"] = sel.reshape(2, 256)
    k["i2"] = np.eye(2, dtype=np.float32)
    return k


CONST_SPECS = [
    ("ident", [128, 128], BF16), ("cosT", [128, NTOK], F32), ("sinT", [128, NTOK], F32),
    ("mask_next", [128, 128], BF16), ("mask_prev", [128, 128], BF16), ("ones", [128, 128], BF16),
    ("csblk", [128, 256], BF16), ("t1tab", [128, 128], BF16), ("mc", [128, 8192], BF16), ("ms", [128, 8192], BF16),
    ("c256", [128, 512], BF16), ("s256n", [128, 512], BF16), ("sel", [2, 256], F32), ("i2", [2, 2], F32),
]

IN_SPECS = [
    ("x", [S, D]), ("ctx", [LCTX, D]), ("csT", [128, 16]),
    ("w_ada", [2, D, 6 * D]), ("b_ada", [2, 6 * D]),
    ("g_pre_mix", [2, D]), ("g_post_mix", [2, D]), ("g_pre_ff", [2, D]), ("g_post_ff", [2, D]),
    ("w_inp", [2, D, NCOL]), ("sinkT", [2, 128, 4]), ("w_sguT", [2, 128, 512]), ("b_sguT", [2, 128, 4]),
    ("g_sgu", [2, 256]), ("g_mixT", [2, 128, 8]), ("w_outp", [2, D, D]),
    ("w_ff1", [2, D, DFF]), ("w_ff2", [2, DFF, D]),
]


class Builder:
    def __init__(self, nlayers=2, stop_after=None, debug=()):
        self.nlayers = nlayers
        self.stop_after = stop_after
        self.debug = set(debug)
        nc = bass.Bass("TRN2", target_bir_lowering=False)
        self.nc = nc
        self.I = {}
        for name, shape in IN_SPECS:
            self.I[name] = nc.dram_tensor(name, shape, F32, kind="ExternalInput").ap()
        self.C = {}
        for name, shape, dt in CONST_SPECS:
            self.C[name] = nc.dram_tensor("c_" + name, shape, dt, kind="ExternalInput").ap()
        self.out = nc.dram_tensor("out", [S, D], F32, kind="ExternalOutput").ap()
        kw = dict(kind="ExternalOutput") if "scratch_out" in self.debug else {}
        self.qT_d = nc.dram_tensor("qT_d", [4, 128, NTOK], BF16, **kw).ap()
        self.sguT_d = nc.dram_tensor("sguT_d", [2, 128, NTOK], BF16, **kw).ap()
        self.AB_d = nc.dram_tensor("AB_d", [NTOK, 512], BF16, **kw).ap()
        self.kT_d = nc.dram_tensor("kT_d", [128, NTOK], BF16, **kw).ap()
        self.vt_d = nc.dram_tensor("vt_d", [128, 66, 128], BF16, **kw).ap()
        self.fourT_d = nc.dram_tensor("fourT_d", [128, 2, NTOK], BF16, **kw).ap()
        self.xmid_d = nc.dram_tensor("xmid_d", [NTOK, D], F32, **kw).ap()
        self.x1_d = nc.dram_tensor("x1_d", [NTOK, D], F32, **kw).ap()
        self.G_d = nc.dram_tensor("G_d", [2, 4, D], F32, **kw).ap()
        self.modT_d = nc.dram_tensor("modT_d", [128, 64], F32, **kw).ap()
        self.dbg = {}
        self.P = Prog(nc)

    def act(self, out, in_, func, r, w, **kw):
        return self.P.op("act", lambda e: e.activation(out=out, in_=in_, func=func, **kw), r, w)

    def tt(self, eng, out, in0, in1, op, r, w):
        return self.P.op(eng, lambda e: e.tensor_tensor(out=out, in0=in0, in1=in1, op=op), r, w)

    def ts(self, eng, out, in0, s1, s2, op0, op1, r, w):
        if s2 is None:
            return self.P.op(eng, lambda e: e.tensor_scalar(out=out, in0=in0, scalar1=s1, scalar2=None, op0=op0), r, w)
        return self.P.op(eng, lambda e: e.tensor_scalar(out=out, in0=in0, scalar1=s1, scalar2=s2, op0=op0, op1=op1), r, w)

    def stt(self, out, in0, scalar, in1, op0, op1, r, w):
        return self.P.op("dve", lambda e: e.scalar_tensor_tensor(out=out, in0=in0, scalar=scalar, in1=in1, op0=op0, op1=op1), r, w)

    def cp(self, eng, out, in_, r, w):
        if eng == "act":
            return self.P.op("act", lambda e: e.activation(out=out, in_=in_, func=AF.Copy), r, w)
        return self.P.op(eng, lambda e: e.tensor_copy(out=out, in_=in_), r, w)

    def recip(self, out, in_, r, w):
        return self.P.op("dve", lambda e: e.reciprocal(out=out, in_=in_), r, w)

    def rstd(self, out, ss, n, key_in, key_out, tmpkey):
        self.act(out, ss, AF.Sqrt, r=key_in, w=[tmpkey], scale=1.0 / n, bias=self.eps_t[:, 0:1])
        self.recip(out, out, r=[tmpkey], w=key_out)

    def mm(self, specs, r, w):
        def fn(e):
            ins = None
            for sp_ in specs:
                ins = e.matmul(out=sp_[0], lhsT=sp_[1], rhs=sp_[2], start=sp_[3], stop=sp_[4], skip_group_check=True)
            return ins
        return self.P.op("pe", fn, r, w)

    def tr(self, specs, r, w):
        def fn(e):
            ins = None
            for o_, i_ in specs:
                ins = e.transpose(out=o_, in_=i_, identity=self.ident[:])
            return ins
        return self.P.op("pe", fn, r, w)

    def build(self):
        P = self.P
        nc = self.nc
        self.ident = P.sb("ident", [128, 128], BF16)
        self.ones = P.sb("ones", [128, 128], BF16)
        self.eps_t = P.sb("eps_t", [128, 1], F32)
        self.csil = P.sb("csil", [128, 8, 2], F32)
        self.modT = P.sb("modT", [128, 4, 8, 2], F32)
        self.psum = P.ps("psum", [128, 8, 512], F32)
        self.banks = [self.psum[:, i, :] for i in range(8)]
        P.dma("sp", self.ident[:], self.C["ident"][:, :], w=["ident"])
        P.dma("sp", self.ones[:], self.C["ones"][:, :], w=["ones"])
        P.op("dve", lambda e: e.memset(self.eps_t[:], EPS), w=["eps"])
        P.dma("sp", self.csil[:].rearrange("p c r -> p (c r)"), self.I["csT"][:, :], w=["csil"])
        self.act(self.csil[:], self.csil[:], AF.Silu, r=["csil"], w=["csil"])
        P.barrier()
        for l in range(self.nlayers):
            self.layer(l)
        P.barrier()
        P.emit()
        P.close()
        return nc

    def src_rows(self, l, tok0, n):
        if l == 0:
            if tok0 >= S:
                return self.I["ctx"][tok0 - S:tok0 - S + n, :]
            return self.I["x"][tok0:tok0 + n, :]
        return self.x1_d[tok0:tok0 + n, :]

    def dst_rows(self, l, tok0, n):
        if l == self.nlayers - 1 and self.nlayers == 2:
            assert tok0 < S
            return self.out[tok0:tok0 + n, :]
        if self.nlayers == 1 and tok0 < S:
            return self.out[tok0:tok0 + n, :]
        return self.x1_d[tok0:tok0 + n, :]

    def layer(self, l):
        P = self.P
        last = (l == 1)
        self.l = l
        self.ctx_full = not last
        P.push()
        self.gmixT = P.sb("gmixT", [128, 8], F32)
        self.esink = P.sb("esink", [128, 4], F32)
        self.bsT = P.sb("bsT", [128, 4], F32)
        self.fourT = P.sb("fourT", [128, 2, NTOK], BF16)
        P.dma("sp", self.gmixT[:], self.I["g_mixT"][l], w=["gmixT"])
        P.dma("sp", self.esink[:], self.I["sinkT"][l], w=["esink"])
        P.dma("sp", self.bsT[:], self.I["b_sguT"][l], w=["bsT"])
        self.act(self.esink[:], self.esink[:], AF.Exp, r=["esink"], w=["esink"])
        self.adaln(l)
        if self.stop_after == "adaln":
            P.pop()
            return
        self.phaseA(l)
        if self.stop_after == "A":
            P.pop()
            return
        self.phaseF(l)
        if self.stop_after == "F":
            P.pop()
            return
        self.phaseC(l)
        if self.stop_after == "C":
            P.pop()
            return
        P.pop()
        self.phaseD(l)

    def adaln(self, l):
        P = self.P
        I = self.I
        P.push()
        wblk = [P.sb("wada%d" % i, [128, 8, 512], F32) for i in range(2)]
        modrow = P.sb("modrow", [2, 6 * D], F32)
        brow = P.sb("brow", [2, 6 * D], F32)
        gains = P.sb("gains", [2, 4, D], F32)
        rows = P.sb("rows", [2, 6, D], F32)
        sel = P.sb("sel", [2, 2, 128], F32)
        i2 = P.sb("i2", [2, 2], F32)
        P.dma("sp", brow[:], I["b_ada"][l:l + 1, :].partition_broadcast(2), w=["brow"])
        for i, nm in enumerate(("g_pre_mix", "g_post_mix", "g_pre_ff", "g_post_ff")):
            P.dma("sp", gains[:, i, :], I[nm][l:l + 1, :].partition_broadcast(2), w=[("gains", i)])
        P.dma("sp", sel[:].rearrange("k r m -> k (r m)"), self.C["sel"][:, :], w=["sel"])
        P.dma("sp", i2[:], self.C["i2"][:, :], w=["i2"])
        for nb in range(12):
            wb = wblk[nb % 2]
            P.dma("sp", wb[:], I["w_ada"][l, :, nb * 512:(nb + 1) * 512].rearrange("(c p) n -> p c n", p=128),
                  w=[("wada", nb % 2)])
            bk = self.banks[nb % 2]
            self.mm([(bk[0:2, :], self.csil[:, c, :], wb[:, c, :], c == 0, c == 7) for c in range(8)],
                    r=[("wada", nb % 2), "csil"], w=[("ps", nb % 2)])
            self.tt("dve", modrow[:, nb * 512:(nb + 1) * 512], bk[0:2, :], brow[:, nb * 512:(nb + 1) * 512], ALU.add,
                    r=[("ps", nb % 2), "brow"], w=[("modrow", nb)])
        mr = lambda i: modrow[:, i * D:(i + 1) * D]
        mk = lambda i: [("modrow", 2 * i), ("modrow", 2 * i + 1)]
        self.stt(rows[:, 0, :], mr(1), 1.0, gains[:, 0, :], ALU.add, ALU.mult, r=mk(1) + [("gains", 0)], w=[("rows", 0)])
        self.cp("dve", rows[:, 1, :], mr(0), r=mk(0), w=[("rows", 1)])
        self.stt(rows[:, 2, :], mr(4), 1.0, gains[:, 2, :], ALU.add, ALU.mult, r=mk(4) + [("gains", 2)], w=[("rows", 2)])
        self.cp("dve", rows[:, 3, :], mr(3), r=mk(3), w=[("rows", 3)])
        self.tt("dve", rows[:, 4, :], mr(2), gains[:, 1, :], ALU.mult, r=mk(2) + [("gains", 1)], w=[("rows", 4)])
        self.tt("dve", rows[:, 5, :], mr(5), gains[:, 3, :], ALU.mult, r=mk(5) + [("gains", 3)], w=[("rows", 5)])
        bk = self.banks[2]
        specs = []
        for v in range(4):
            for c in range(8):
                col = (v * 8 + c) * 2
                specs.append((bk[:, col:col + 2], rows[:, v, c * 128:(c + 1) * 128], i2[:, :], True, True))
        self.mm(specs, r=[("rows", v) for v in range(4)] + ["i2"], w=[("ps", 2)])
        self.cp("dve", self.modT[:].rearrange("p v c r -> p (v c r)"), bk[:, 0:64], r=[("ps", 2)], w=["modT"])
        if "scratch_out" in self.debug:
            P.dma("pool", self.modT_d[:, :], self.modT[:].rearrange("p v c r -> p (v c r)"), r=["modT"], w=["modT_d"])
        P.dma("pool", self.G_d[:, 0, :], rows[:, 4, :], r=[("rows", 4)], w=["G_d0"])
        P.dma("pool", self.G_d[:, 1, :], rows[:, 5, :], r=[("rows", 5)], w=["G_d1"])
        P.barrier()
        P.pop()

    def new_staging(self, width=1024):
        self._stg = [self.P.sb("stg%d" % i, [128, width], F32) for i in range(2)]
        self._stg_w = width
        self._stg_n = 0

    def load_weight(self, dst, src_fn, nchunks, width, tag):
        P = self.P
        piece = self._stg_w
        keys = []
        for c in range(nchunks):
            for o in range(0, width, piece):
                wd = min(piece, width - o)
                n = self._stg_n
                self._stg_n += 1
                s_ = self._stg[n % 2]
                P.dma("sp", s_[:, 0:wd], src_fn(c)[:, o:o + wd], w=[("stg", n % 2)])
                eng = ("pool", "dve", "act")[n % 3]
                self.cp(eng, dst[:, c, o:o + wd], s_[:, 0:wd], r=[("stg", n % 2)], w=[(tag, c, o)])
                keys.append((tag, c, o))
        return keys

    def phaseA(self, l):
        P = self.P
        I = self.I
        C = self.C
        P.push()
        self.winb = P.sb("winb", [128, 8, NCOL], BF16)
        self.new_staging()
        self.wink = self.load_weight(self.winb, lambda c: I["w_inp"][l, c * 128:(c + 1) * 128, :], 8, NCOL, "winb")
        A = type("A", (), {})()
        self.A = A
        A.xt = [P.sb("xt%d" % i, [128, D], F32) for i in range(4)]
        A.junk = P.sb("junk", [128, D], BF16)
        A.ss = P.sb("ssA", [128, 8], F32)
        A.rs = P.sb("rsA", [128, 8], F32)
        A.xn = [P.sb("xn%d" % i, [128, D], BF16) for i in range(4)]
        A.hxT = [P.sb("hxT%d" % i, [128, 8, 512], BF16) for i in range(2)]
        A.cos = [P.sb("cos%d" % i, [128, 512], F32) for i in range(2)]
        A.sin = [P.sb("sin%d" % i, [128, 512], F32) for i in range(2)]
        A.t1 = [P.sb("t1_%d" % i, [128, 512], F32) for i in range(2)]
        A.t2 = [P.sb("t2_%d" % i, [128, 512], F32) for i in range(2)]
        A.qT = [P.sb("qTt%d" % i, [128, 4, 512], BF16) for i in range(2)]
        A.fT = P.sb("fTt", [128, 2, 512], BF16)
        A.kTt = [P.sb("kTt%d" % i, [128, 512], BF16) for i in range(2)]
        A.vtt = [P.sb("vtt%d" % i, [128, 4, 128], BF16) for i in range(2)]
        A.ABt = [P.sb("ABt%d" % i, [128, 4, 512], BF16) for i in range(2)]
        A.ug = [P.sb("ug%d" % i, [128, 4, 256], F32) for i in range(2)]
        A.gg = [P.sb("gg%d" % i, [128, 4, 256], F32) for i in range(2)]
        A.sq = P.sb("sqA", [128, 4, 256], F32)
        A.ssh = P.sb("ssh", [128, 16], F32)
        A.rsh = P.sb("rsh", [128, 16], F32)
        A.vc = P.sb("vc", [128, 4, 256], BF16)
        A.so = P.sb("so", [128, 4, 256], F32)
        A.ss2 = P.sb("ss2", [128, 4], F32)
        A.rs2 = P.sb("rs2", [128, 4], F32)
        A.son = P.sb("son", [128, 4, 256], BF16)
        A.sgt = [P.sb("sgt%d" % i, [128, 2, 512], BF16) for i in range(2)]
        A.gsg = P.sb("gsg", [128, 256], F32)
        A.wsT = P.sb("wsT", [128, 4, 128], BF16)
        A.wsf = P.sb("wsf", [128, 512], F32)
        A.csblk = P.sb("csblk", [128, 256], BF16)
        P.dma("sp", A.gsg[:], I["g_sgu"][l:l + 1, :].partition_broadcast(128), w=["gsg"])
        P.dma("sp", A.wsf[:], I["w_sguT"][l], w=["wsf"])
        self.cp("dve", A.wsT[:].rearrange("p h q -> p (h q)"), A.wsf[:], r=["wsf"], w=["wsT"])
        P.dma("sp", A.csblk[:], C["csblk"][:, :], w=["csblk"])
        A.trbanks = Ring([(self.banks[i].bitcast(BF16), ("ps", i)) for i in range(2)])
        A.pbanks = Ring([(self.banks[i], ("ps", i)) for i in range(2, 8)])
        A.xi = 0
        tiles = [(S, 2, True)] + [(i * 512, 4, False) for i in range(S // 512)]
        if "A_tiles" in self.debug:
            tiles = tiles[:3]
        A.sgu_gen = None
        self.A_front(l, 0, *tiles[0])
        for i, t in enumerate(tiles):
            if i + 1 < len(tiles):
                self.A_front(l, i + 1, *tiles[i + 1])
            self.A_back(l, i, *t)
            if A.sgu_gen is not None:
                for _ in A.sgu_gen:
                    pass
            A.sgu_gen = self.A_sgu_gen(l, i, *t)
        for _ in A.sgu_gen:
            pass
        P.barrier()
        P.pop()

    def A_front(self, l, it, tok0, nsub, is_ctx):
        P = self.P
        A = self.A
        mi = 1 if is_ctx else 0
        T = nsub * 128
        hx = A.hxT[it % 2]
        hk = ("hxT", it % 2)
        for s in range(nsub):
            xi = A.xi % 4
            A.xi += 1
            P.dma("sp", A.xt[xi][:], self.src_rows(l, tok0 + s * 128, 128), w=[("xt", xi)])
            self.act(A.junk[:], A.xt[xi][:], AF.Square, r=[("xt", xi)], w=["junk", ("ssA", s)], accum_out=A.ss[:, s:s + 1])
            self.rstd(A.rs[:, s:s + 1], A.ss[:, s:s + 1], float(D), [("ssA", s)], [("rsA", s)], ("rsAt", s))
            self.ts("dve", A.xn[xi][:], A.xt[xi][:], A.rs[:, s:s + 1], None, ALU.mult, None,
                    r=[("xt", xi), ("rsA", s)], w=[("xn", xi)])
        xis = [(A.xi - nsub + s) % 4 for s in range(nsub)]
        for cp_ in range(4):
            tb, sk = A.trbanks.next()
            self.tr([(tb[:, j * 512 + s * 128: j * 512 + (s + 1) * 128], A.xn[xis[s]][:, (2 * cp_ + j) * 128:(2 * cp_ + j + 1) * 128])
                     for j in range(2) for s in range(nsub)],
                    r=[("xn", xi) for xi in xis] + ["ident"], w=[sk])
            for j in range(2):
                c = 2 * cp_ + j
                slot = tb[:, j * 512:(j + 1) * 512]
                if cp_ % 2 == 0:
                    self.act(hx[:, c, 0:T], slot[:, 0:T], AF.Identity, r=[sk, "modT"], w=[hk + (c,)],
                             scale=self.modT[:, 0, c, mi:mi + 1], bias=self.modT[:, 1, c, mi:mi + 1])
                else:
                    self.ts("dve", hx[:, c, 0:T], slot[:, 0:T], self.modT[:, 0, c, mi:mi + 1], self.modT[:, 1, c, mi:mi + 1],
                            ALU.mult, ALU.add, r=[sk, "modT"], w=[hk + (c,)])

    def A_back(self, l, it, tok0, nsub, is_ctx):
        P = self.P
        A = self.A
        C = self.C
        if "A_nob" in self.debug:
            return
        T = nsub * 128
        hx = A.hxT[it % 2]
        hkeys = [("hxT", it % 2, c) for c in range(8)]
        W = self.winb
        full = (not is_ctx) or self.ctx_full
        sl = it % 2
        P.dma("sp", A.cos[sl][:, 0:T], C["cosT"][:, tok0:tok0 + T], w=[("cos", sl)])
        P.dma("sp", A.sin[sl][:, 0:T], C["sinT"][:, tok0:tok0 + T], w=[("sin", sl)])

        def fm_chunk(fc):
            bk, bkk = A.pbanks.next()
            self.mm([(bk[:, 0:T], W[:, dc, fc * 128:(fc + 1) * 128], hx[:, dc, 0:T], dc == 0, dc == 7) for dc in range(8)],
                    r=hkeys + self.wink, w=[bkk])
            return bk, bkk

        def step():
            g_ = getattr(A, "sgu_gen", None)
            if g_ is not None:
                if next(g_, "done") == "done":
                    A.sgu_gen = None

        def rope(fc, fcr, out_ap, okey, j):
            bq, kq = fm_chunk(fc)
            br, kr = fm_chunk(fcr)
            self.tt("dve", A.t1[j % 2][:, 0:T], bq[:, 0:T], A.cos[sl][:, 0:T], ALU.mult, r=[kq, ("cos", sl)], w=[("t1", j % 2)])
            self.tt("dve", A.t2[j % 2][:, 0:T], br[:, 0:T], A.sin[sl][:, 0:T], ALU.mult, r=[kr, ("sin", sl)], w=[("t2", j % 2)])
            self.tt("pool", out_ap, A.t1[j % 2][:, 0:T], A.t2[j % 2][:, 0:T], ALU.add, r=[("t1", j % 2), ("t2", j % 2)], w=okey)
            step()

        rope(8, 9, A.kTt[sl][:, 0:T], [("kTt", sl)], 0)
        P.dma("pool", self.kT_d[:, tok0:tok0 + T], A.kTt[sl][:, 0:T], r=[("kTt", sl)], w=[("kT_d", tok0)])
        if full:
            for c in range(4):
                rope(c, 4 + c, A.qT[sl][:, c, 0:T], [("qTt", sl, c)], c + 1)
            P.dma("pool", self.qT_d[:, :, tok0:tok0 + T].rearrange("c p t -> p c t"), A.qT[sl][:, :, 0:T],
                  r=[("qTt", sl, c) for c in range(4)], w=[("qT_d", tok0)])
            for cf in range(2):
                bk, bkk = fm_chunk(10 + cf)
                self.cp("act", A.fT[:, cf, 0:T], bk[:, 0:T], r=[bkk], w=[("fT", cf)])
                step()
        if "A_nob2" in self.debug:
            return
        for s in range(nsub):
            b1, k1 = A.pbanks.next()
            ncol = 384 if full else 128
            self.mm([(b1[:, 0:ncol], hx[:, dc, s * 128:(s + 1) * 128], W[:, dc, 1536:1536 + ncol], dc == 0, dc == 7) for dc in range(8)],
                    r=hkeys + self.wink, w=[k1])
            self.cp("act", A.vtt[sl][:, s, :], b1[:, 0:128], r=[k1], w=[("vtt", sl, s)])
            if full:
                self.act(A.ug[sl][:, s, :], b1[:, 128:384], AF.Gelu_apprx_tanh, r=[k1], w=[("ug", sl, s)])
                b2, k2 = A.pbanks.next()
                self.mm([(b2[:, 0:256], hx[:, dc, s * 128:(s + 1) * 128], W[:, dc, 1920:2176], dc == 0, dc == 7) for dc in range(8)],
                        r=hkeys + self.wink, w=[k2])
                self.act(A.gg[sl][:, s, :], b2[:, 0:256], AF.Gelu_apprx_tanh, r=[k2], w=[("gg", sl, s)])
        blk0 = tok0 // 128
        P.dma("pool", self.vt_d[:, blk0:blk0 + nsub, :], A.vtt[sl][:, 0:nsub, :], r=[("vtt", sl, s) for s in range(nsub)],
              w=[("vt_d", tok0)])
        if not full or "A_nob3" in self.debug:
            return
        ns = nsub
        for s in range(ns):
            bk, bkk = A.pbanks.next()
            specs = []
            for cf in range(2):
                specs.append((bk[:, cf * 256:(cf + 1) * 256], A.fT[:, cf, s * 128:(s + 1) * 128], A.csblk[:, :], True, True))
            self.mm(specs, r=[("fT", 0), ("fT", 1), "csblk"], w=[bkk])
            self.cp("act", A.ABt[sl][:, s, :], bk[:, :], r=[bkk], w=[("ABt", sl, s)])
        P.dma("pool", self.AB_d[tok0:tok0 + T, :].rearrange("(s p) c -> p s c", p=128), A.ABt[sl][:, 0:ns, :],
              r=[("ABt", sl, s) for s in range(ns)], w=[("AB_d", tok0)])
        return

    def A_sgu_gen(self, l, it, tok0, nsub, is_ctx):
        P = self.P
        A = self.A
        full = (not is_ctx) or self.ctx_full
        if not full or "A_nob" in self.debug or "A_nob2" in self.debug or "A_nob3" in self.debug:
            return
        ns = nsub
        T = nsub * 128
        sl = it % 2
        ggk = [("gg", sl, s) for s in range(ns)]
        ugk = [("ug", sl, s) for s in range(ns)]
        self.tt("pool", A.sq[:, 0:ns, :], A.gg[sl][:, 0:ns, :], A.gg[sl][:, 0:ns, :], ALU.mult, r=ggk, w=["sq"])
        P.op("dve", lambda e: e.tensor_reduce(out=A.ssh[:, 0:ns * 4], in_=A.sq[:, 0:ns, :].rearrange("p s (h d) -> p (s h) d", h=4),
                                              axis=AX.X, op=ALU.add), r=["sq"], w=["ssh"])
        yield
        self.rstd(A.rsh[:, 0:ns * 4], A.ssh[:, 0:ns * 4], 64.0, ["ssh"], ["rsh"], "rsht")
        yield
        self.tt("dve", A.sq[:, 0:ns, :].rearrange("p s (h d) -> p (s h) d", h=4),
                A.gg[sl][:, 0:ns, :].rearrange("p s (h d) -> p (s h) d", h=4),
                A.rsh[:, 0:ns * 4].unsqueeze(2).to_broadcast([128, ns * 4, 64]), ALU.mult, r=ggk + ["rsh"], w=["sq"])
        self.tt("dve", A.vc[:, 0:ns, :], A.sq[:, 0:ns, :], A.gsg[:, :].unsqueeze(1).to_broadcast([128, ns, 256]), ALU.mult,
                r=["sq", "gsg"], w=["vc"])
        yield
        for hp in range(2):
            bk, bkk = A.pbanks.next()
            specs = []
            for hh in range(2):
                h = hp * 2 + hh
                o = bk[:, hh * 256:hh * 256 + ns * 64].rearrange("p (s d) -> p s d", d=64)
                specs.append((o, A.wsT[:, h, :], A.vc[:, 0:ns, h * 64:(h + 1) * 64], True, True))
            self.mm(specs, r=["vc", "wsT"], w=[bkk])
            for hh in range(2):
                h = hp * 2 + hh
                self.stt(A.so[:, 0:ns, h * 64:(h + 1) * 64], bk[:, hh * 256:hh * 256 + ns * 64].rearrange("p (s d) -> p s d", d=64),
                         self.bsT[:, h:h + 1], A.ug[sl][:, 0:ns, h * 64:(h + 1) * 64], ALU.add, ALU.mult,
                         r=[bkk, "bsT"] + ugk, w=[("so", h)])
        sok = [("so", h) for h in range(4)]
        yield
        self.tt("pool", A.sq[:, 0:ns, :], A.so[:, 0:ns, :], A.so[:, 0:ns, :], ALU.mult, r=sok, w=["sq"])
        P.op("dve", lambda e: e.tensor_reduce(out=A.ss2[:, 0:ns], in_=A.sq[:, 0:ns, :], axis=AX.X, op=ALU.add), r=["sq"], w=["ss2"])
        self.rstd(A.rs2[:, 0:ns], A.ss2[:, 0:ns], 256.0, ["ss2"], ["rs2"], "rs2t")
        yield
        self.tt("dve", A.son[:, 0:ns, :], A.so[:, 0:ns, :], A.rs2[:, 0:ns].unsqueeze(2).to_broadcast([128, ns, 256]), ALU.mult,
                r=sok + ["rs2"], w=["son"])
        tb, sk = A.trbanks.next()
        self.tr([(tb[:, cc * 512 + s * 128: cc * 512 + (s + 1) * 128], A.son[:, s, cc * 128:(cc + 1) * 128])
                 for cc in range(2) for s in range(ns)], r=["son", "ident"], w=[sk])
        for cc in range(2):
            self.act(A.sgt[sl][:, cc, 0:T], tb[:, cc * 512: cc * 512 + T], AF.Identity, r=[sk, "gmixT"], w=[("sgt", sl, cc)],
                     scale=self.gmixT[:, 4 + cc:5 + cc])
        P.dma("pool", self.sguT_d[:, :, tok0:tok0 + T].rearrange("c p t -> p c t"), A.sgt[sl][:, :, 0:T],
              r=[("sgt", sl, 0), ("sgt", sl, 1)], w=[("sguT_d", tok0)])

    def four_finish(self, ybank_ap, ykey, nk, tok_of, g):
        P = self.P
        F = self.F
        i = F.fi % 2
        F.fi += 1
        ysb = F.ysb[i]
        self.cp("act", ysb[:, 0:nk, :], ybank_ap, r=[ykey], w=[("ysb", i)])
        self.tt("pool", F.ysq[:, 0:nk, :], ysb[:, 0:nk, :], ysb[:, 0:nk, :], ALU.mult, r=[("ysb", i)], w=["ysq"])
        P.op("dve", lambda e: e.tensor_reduce(out=F.yss[:, 0:nk], in_=F.ysq[:, 0:nk, :], axis=AX.X, op=ALU.add), r=["ysq"], w=["yss"])
        self.rstd(F.yrs[:, 0:nk], F.yss[:, 0:nk], 256.0, ["yss"], ["yrs"], "yrst")
        self.tt("dve", F.ynb[i][:, 0:nk, :], ysb[:, 0:nk, :], F.yrs[:, 0:nk].unsqueeze(2).to_broadcast([128, nk, 256]), ALU.mult,
                r=[("ysb", i), "yrs"], w=[("ynb", i)])
        def tail():
            tb, sk = F.trbanks.next()
            self.tr([(tb[:, (j * 2 + cc) * 128:(j * 2 + cc + 1) * 128], F.ynb[i][:, j, cc * 128:(cc + 1) * 128])
                     for j in range(nk) for cc in range(2)], r=[("ynb", i), "ident"], w=[sk])
            for j in range(nk):
                for cc in range(2):
                    self.act(tok_of(j, cc), tb[:, (j * 2 + cc) * 128:(j * 2 + cc + 1) * 128], AF.Identity, r=[sk, "gmixT"],
                             w=[("fourT", g, j, cc)], scale=self.gmixT[:, 6 + cc:7 + cc])
        prev_tail = getattr(F, "pending_tail", None)
        F.pending_tail = tail
        if prev_tail is not None:
            prev_tail()

    def phaseF(self, l):
        P = self.P
        C = self.C
        P.push()
        F = type("F", (), {})()
        self.F = F
        F.fi = 0
        F.ab1 = P.sb("ab1", [128, 128, 128], BF16)
        F.p1 = P.sb("p1", [128, 64, 2, 128], BF16)
        F.mc = P.sb("mc", [128, 64, 128], BF16)
        F.ms = P.sb("ms", [128, 64, 128], BF16)
        F.t1tab = P.sb("t1tab", [128, 128], BF16)
        F.ysb = [P.sb("ysb%d" % i, [128, 4, 256], F32) for i in range(2)]
        F.ysq = P.sb("ysq", [128, 4, 256], F32)
        F.yss = P.sb("yss", [128, 4], F32)
        F.yrs = P.sb("yrs", [128, 4], F32)
        F.ynb = [P.sb("ynb%d" % i, [128, 4, 256], BF16) for i in range(2)]
        F.trbanks = Ring([(self.banks[i].bitcast(BF16), ("ps", i)) for i in range(2)])
        P.dma("sp", F.mc[:].rearrange("p a b -> p (a b)"), C["mc"][:, :], w=["mc"])
        P.dma("sp", F.ms[:].rearrange("p a b -> p (a b)"), C["ms"][:, :], w=["ms"])
        P.dma("sp", F.t1tab[:], C["t1tab"][:, :], w=["t1tab"])
        pb = Ring([(self.banks[i], ("ps", i)) for i in range(2, 5)])
        yb = Ring([(self.banks[i], ("ps", i)) for i in range(5, 8)])
        F.p1b = P.sb("p1b", [128, 64, 2, 128], BF16)
        p1s = [F.p1, F.p1b]
        for hh in range(2):
            for ab in range(2):
                src = self.AB_d[0:S, hh * 256 + ab * 128: hh * 256 + ab * 128 + 128].rearrange("(t1 t2) c -> t1 t2 c", t2=128)
                P.dma("sp", F.ab1[ab * 64:(ab + 1) * 64, :, :], src, w=[("ab1", ab)])
            for c4 in range(32):
                bk, bkk = pb.next()
                specs = []
                for j in range(4):
                    ch = c4 * 4 + j
                    specs.append((bk[:, j * 128:(j + 1) * 128], F.ab1[:, :, ch], F.t1tab[:, :], True, True))
                self.mm(specs, r=[("ab1", 0), ("ab1", 1), "t1tab"], w=[bkk])
                o = p1s[hh][:, :, :, c4 * 4:(c4 + 1) * 4].rearrange("p k r c -> p c r k")
                i_ = bk[:, :].rearrange("p (c r k) -> p c r k", c=4, r=2)
                if c4 % 2 == 0:
                    self.cp("act", o, i_, r=[bkk], w=[("p1", hh, c4)])
                else:
                    self.cp("dve", o, i_, r=[bkk], w=[("p1", hh, c4)])
        p1keys = [("p1", hh, c4) for hh in range(2) for c4 in range(32)]
        for kg in range(16):
            bka, kka = yb.next()
            for half2 in range(2):
                if half2 == 1:
                    bka, kka = yb.next()
                specs = []
                for j in range(2):
                    k1 = kg * 4 + half2 * 2 + j
                    for hh in range(2):
                        o = bka[:, j * 256 + hh * 128: j * 256 + hh * 128 + 128]
                        specs.append((o, F.mc[:, k1, :], p1s[hh][:, k1, 0, :], True, False))
                        specs.append((o, F.ms[:, k1, :], p1s[hh][:, k1, 1, :], False, True))
                self.mm(specs, r=p1keys + ["mc", "ms"], w=[kka])
                k1base = kg * 4 + half2 * 2

                def tok_of(j, cc, k1base=k1base):
                    return self.fourT[:, cc, 0:S].rearrange("p (k2 k1) -> p k1 k2", k1=64)[:, k1base + j, :]
                self.four_finish(bka[:, :].rearrange("p (j c) -> p j c", j=2), kka, 2, tok_of, ("L", kg, half2))
        if self.ctx_full:
            cab = P.sb("cab", [128, 2, 512], BF16)
            c256 = P.sb("c256", [128, 2, 256], BF16)
            s256 = P.sb("s256n", [128, 2, 256], BF16)
            P.dma("sp", cab[:], self.AB_d[S:S + 256, :].rearrange("(tb p) c -> p tb c", p=128), w=["cab"])
            P.dma("sp", c256[:].rearrange("p a b -> p (a b)"), C["c256"][:, :], w=["c256"])
            P.dma("sp", s256[:].rearrange("p a b -> p (a b)"), C["s256n"][:, :], w=["s256"])
            for kb in range(2):
                bk, bkk = yb.next()
                specs = []
                for hh in range(2):
                    o = bk[:, hh * 128:(hh + 1) * 128]
                    for tb in range(2):
                        specs.append((o, c256[:, tb, kb * 128:(kb + 1) * 128], cab[:, tb, hh * 256:hh * 256 + 128], tb == 0, False))
                        specs.append((o, s256[:, tb, kb * 128:(kb + 1) * 128], cab[:, tb, hh * 256 + 128:hh * 256 + 256], False, tb == 1))
                self.mm(specs, r=["cab", "c256", "s256"], w=[bkk])

                def tok_of(j, cc, kb=kb):
                    return self.fourT[:, cc, S + kb * 128:S + (kb + 1) * 128]
                self.four_finish(bk[:, 0:256].rearrange("p (j c) -> p j c", j=1), bkk, 1, tok_of, ("C", kb))
        if getattr(F, "pending_tail", None) is not None:
            F.pending_tail()
            F.pending_tail = None
        if "scratch_out" in self.debug and l == 0:
            P.barrier()
            P.dma("pool", self.fourT_d[:, :, :], self.fourT[:], w=["fourT_d"])
        P.barrier()
        P.pop()

    def phaseC(self, l):
        P = self.P
        I = self.I
        C = self.C
        P.push()
        self.woutb = P.sb("woutb", [128, 8, D], BF16)
        self.new_staging()
        self.woutk = self.load_weight(self.woutb, lambda c: I["w_outp"][l, c * 128:(c + 1) * 128, :], 8, D, "woutb")
        K = type("K", (), {})()
        self.K = K
        self.kT_all = P.sb("kT_all", [128, NTOK], BF16)
        self.vt_all = P.sb("vt_all", [128, 66, 128], BF16)
        P.dma("sp", self.kT_all[:], self.kT_d[:, :], w=["kT_all"])
        P.dma("sp", self.vt_all[:], self.vt_d[:, :, :], w=["vt_all"])
        K.qt = [P.sb("qt%d" % i, [128, 4, 512], BF16) for i in range(2)]
        K.sgt = [P.sb("sgtC%d" % i, [128, 2, 512], BF16) for i in range(2)]
        K.xt = [P.sb("xtC%d" % i, [128, D], F32) for i in range(4)]
        K.PT = [P.sb("PT%d" % i, [128, 2, 512], BF16) for i in range(6)]
        K.oc = [P.sb("oc%d" % i, [128, 512], F32) for i in range(2)]
        K.dd = [P.sb("dd%d" % i, [128, 512], F32) for i in range(2)]
        K.ar = [P.sb("ar%d" % i, [128, 512], F32) for i in range(2)]
        K.sqT = P.sb("sqT", [128, 4, 512], BF16)
        K.aT = P.sb("aT", [128, 4, 512], BF16)
        K.ys2 = [P.sb("ysC%d" % i, [128, D], F32) for i in range(2)]
        K.yc2 = [P.sb("ycC%d" % i, [128, D], F32) for i in range(2)]
        K.tmp = P.sb("tmpC", [128, D], F32)
        K.junk = P.sb("junkC", [128, D], BF16)
        K.ssa = P.sb("ssa", [128, 4], F32)
        K.rsa = P.sb("rsa", [128, 4], F32)
        K.ssy = P.sb("ssy", [128, 4], F32)
        K.rsy = P.sb("rsy", [128, 4], F32)
        K.G1 = P.sb("G1bc", [128, 2, D], F32)
        K.mnext = P.sb("mnext", [128, 128], BF16)
        K.mprev = P.sb("mprev", [128, 128], BF16)
        P.dma("sp", K.mnext[:], C["mask_next"][:, :], w=["mnext"])
        P.dma("sp", K.mprev[:], C["mask_prev"][:, :], w=["mprev"])
        for r_ in range(2):
            P.dma("sp", K.G1[:, r_, :], self.G_d[r_:r_ + 1, 0, :].partition_broadcast(128), w=[("G1", r_)])
        K.spairs = Ring([(self.psum[:, 2 * i:2 * i + 2, :], [("ps", 2 * i), ("ps", 2 * i + 1)]) for i in range(3)])
        K.tailpairs = Ring([(self.psum[:, 2 * i:2 * i + 2, :], [("ps", 2 * i), ("ps", 2 * i + 1)]) for i in range(4)])
        K.pti = 0
        K.xi = 0
        tiles = [(i * 512, 4, False) for i in range(S // 512)]
        if self.ctx_full:
            tiles = tiles + [(S, 2, True)]
        if "C_tiles" in self.debug:
            tiles = tiles[:2] + tiles[-1:]
        for it, t in enumerate(tiles):
            self.C_tile(l, it, *t)
        P.barrier()
        P.pop()

    def C_tile(self, l, it, tok0, nsub, is_ctx):
        P = self.P
        K = self.K
        T = nsub * 128
        sl = it % 2
        mi = 1 if is_ctx else 0
        qt = K.qt[sl]
        P.dma("sp", qt[:, :, 0:T], self.qT_d[:, :, tok0:tok0 + T].rearrange("c p t -> p c t"), w=[("qt", sl)])
        P.dma("sp", K.sgt[sl][:, :, 0:T], self.sguT_d[:, :, tok0:tok0 + T].rearrange("c p t -> p c t"), w=[("sgtC", sl)])
        xis = []
        for s in range(nsub):
            xi = K.xi % 4
            K.xi += 1
            xis.append(xi)
            P.dma("sp", K.xt[xi][:], self.src_rows(l, tok0 + s * 128, 128), w=[("xtC", xi)])
        items = []
        if not is_ctx:
            n0 = tok0 // 128
            for j in range(n0 - 1, n0 + nsub + 1):
                if j < 0 or j >= S // 128:
                    continue
                qlo = max(j - 1, n0) - n0
                qhi = min(j + 1, n0 + nsub - 1) - n0
                mnext = (j - 1 - n0) if (j - 1 >= n0) else None
                mprev = (j + 1 - n0) if (j + 1 <= n0 + nsub - 1) else None
                items.append((j, qlo * 128, (qhi + 1) * 128, mnext, mprev))
        items.append((64, 0, T, None, None))
        items.append((65, 0, T, None, None))
        LOOK = 2
        bo, ko = self.banks[6], ("ps", 6)
        bd, kd = self.banks[7], ("ps", 7)
        deferred = []

        def fin2():
            while deferred:
                c_ = deferred.pop(0)
                d_ = c_ % 2
                self.act(K.sqT[:, c_, 0:T], K.ar[d_][:, 0:T], AF.Square, r=[("ar", d_)], w=[("sqT", c_)])
                self.ts("pool", K.aT[:, c_, 0:T], K.ar[d_][:, 0:T], self.gmixT[:, c_:c_ + 1], 1.0, ALU.mult, ALU.mult,
                        r=[("ar", d_), "gmixT"], w=[("aT", c_)])

        for c in range(4):
            pend = []
            first = True
            for idx in range(len(items) + LOOK):
                if idx == min(5, len(items) + LOOK - 1):
                    fin2()
                if idx < len(items):
                    j, lo, hi, mn, mp = items[idx]
                    n = hi - lo
                    pair, pkeys = K.spairs.next()
                    kcols = slice(j * 128, (j + 1) * 128)
                    self.mm([(pair[:, 0, 0:n], self.kT_all[0:64, kcols], qt[0:64, c, lo:hi], True, True),
                             (pair[:, 1, 0:n], self.kT_all[64:128, kcols], qt[64:128, c, lo:hi], True, True)],
                            r=[("qt", sl), "kT_all"], w=pkeys)
                    pi = K.pti % len(K.PT)
                    K.pti += 1
                    pt = K.PT[pi]
                    pk = [("PT", pi)]
                    self.act(pt[:, :, 0:n], pair[:, :, 0:n], AF.Exp, r=pkeys, w=pk, scale=0.125)
                    if mn is not None:
                        o = mn * 128 - lo
                        self.tt("dve", pt[:, :, o:o + 128], pt[:, :, o:o + 128],
                                K.mnext[:, :].unsqueeze(1).to_broadcast([128, 2, 128]), ALU.mult, r=pk + ["mnext"], w=pk)
                    if mp is not None:
                        o = mp * 128 - lo
                        self.tt("dve", pt[:, :, o:o + 128], pt[:, :, o:o + 128],
                                K.mprev[:, :].unsqueeze(1).to_broadcast([128, 2, 128]), ALU.mult, r=pk + ["mprev"], w=pk)
                    pend.append((j, lo, hi, pt, pk))
                if idx >= LOOK:
                    j, lo, hi, pt, pk = pend.pop(0)
                    n = hi - lo
                    last = (idx == len(items) + LOOK - 1)
                    self.mm([(bo[0:64, lo:hi], self.vt_all[:, j, 0:64], pt[:, 0, 0:n], first, last),
                             (bo[64:128, lo:hi], self.vt_all[:, j, 64:128], pt[:, 1, 0:n], first, last),
                             (bd[0:64, lo:hi], self.ones[:, 0:64], pt[:, 0, 0:n], first, last),
                             (bd[64:128, lo:hi], self.ones[:, 0:64], pt[:, 1, 0:n], first, last)],
                            r=pk + ["ones", "vt_all"], w=[ko, kd])
                    first = False
            d2 = c % 2
            self.cp("act", K.oc[d2][:, 0:T], bo[:, 0:T], r=[ko], w=[("oc", d2)])
            self.ts("dve", K.dd[d2][:, 0:T], bd[:, 0:T], self.esink[:, c:c + 1], None, ALU.add, None, r=[kd, "esink"], w=[("dd", d2)])
            self.recip(K.dd[d2][:, 0:T], K.dd[d2][:, 0:T], r=[("dd", d2)], w=[("dd", d2)])
            self.tt("dve", K.ar[d2][:, 0:T], K.oc[d2][:, 0:T], K.dd[d2][:, 0:T], ALU.mult, r=[("oc", d2), ("dd", d2)], w=[("ar", d2)])
            deferred.append(c)
        fin2()
        pair, pkeys = K.spairs.next()
        bs_, ks_ = pair[:, 0, :], pkeys[0]
        specs = []
        for s in range(nsub):
            for c in range(4):
                specs.append((bs_[:, s:s + 1], K.sqT[:, c, s * 128:(s + 1) * 128], self.ones[:, 0:1], c == 0, c == 3))
        self.mm(specs, r=[("sqT", c) for c in range(4)] + ["ones"], w=[ks_])
        self.rstd(K.rsa[:, 0:nsub], bs_[:, 0:nsub], 512.0, [ks_], ["rsa"], "rsat")
        aTk = [("aT", c) for c in range(4)]
        def st1(s):
            xi = xis[s]
            cols = slice(s * 128, (s + 1) * 128)
            b2 = s % 2
            ys, yc = K.ys2[b2], K.yc2[b2]
            pa, pak = K.tailpairs.next()
            pb_, pbk = K.tailpairs.next()
            for half in range(2):
                self.mm([(pa[:, half, :], K.aT[:, c, cols], self.woutb[:, c, half * 512:(half + 1) * 512], c == 0, c == 3) for c in range(4)],
                        r=aTk + self.woutk, w=[pak[half]])
            for half in range(2):
                specs = []
                for cc in range(2):
                    specs.append((pb_[:, half, :], K.sgt[sl][:, cc, cols], self.woutb[:, 4 + cc, half * 512:(half + 1) * 512], cc == 0, False))
                for cc in range(2):
                    specs.append((pb_[:, half, :], self.fourT[:, cc, tok0 + s * 128: tok0 + (s + 1) * 128],
                                  self.woutb[:, 6 + cc, half * 512:(half + 1) * 512], False, cc == 1))
                self.mm(specs, r=[("sgtC", sl)] + self.woutk, w=[pbk[half]])
            self.cp("act", ys[:].rearrange("p (h n) -> p h n", h=2), pb_[:, :, :], r=pbk, w=[("ysC", b2)])
            self.stt(yc[:].rearrange("p (h n) -> p h n", h=2), pa[:, :, :], K.rsa[:, s:s + 1], ys[:].rearrange("p (h n) -> p h n", h=2),
                     ALU.mult, ALU.add, r=pak + ["rsa", ("ysC", b2)], w=[("ycC", b2)])

        def st2(s):
            xi = xis[s]
            b2 = s % 2
            yc = K.yc2[b2]
            yck = [("ycC", b2)]
            self.act(K.junk[:], yc[:], AF.Square, r=yck, w=["junkC", ("ssy", s)], accum_out=K.ssy[:, s:s + 1])
            self.rstd(K.rsy[:, s:s + 1], K.ssy[:, s:s + 1], float(D), [("ssy", s)], [("rsy", s)], ("rsyt", s))
            self.tt("pool", K.tmp[:], yc[:], K.G1[:, mi, :], ALU.mult, r=yck + [("G1", mi)], w=["tmpC"])
            self.stt(K.xt[xi][:], K.tmp[:], K.rsy[:, s:s + 1], K.xt[xi][:], ALU.mult, ALU.add,
                     r=["tmpC", ("rsy", s), ("xtC", xi)], w=[("xtC", xi)])
            P.dma("pool", self.xmid_d[tok0 + s * 128: tok0 + (s + 1) * 128, :], K.xt[xi][:], r=[("xtC", xi)], w=[("xmid", tok0, s)])

        st1(0)
        for s in range(1, nsub):
            st1(s)
            st2(s - 1)
        st2(nsub - 1)

    def phaseD(self, l):
        P = self.P
        I = self.I
        P.push()
        self.w1b = P.sb("w1b", [128, 8, DFF], BF16)
        self.w2b = P.sb("w2b", [128, 32, D], BF16)
        self.new_staging(512)
        self.w1k = self.load_weight(self.w1b, lambda c: I["w_ff1"][l, c * 128:(c + 1) * 128, :], 8, DFF, "w1b")
        self.w2k = self.load_weight(self.w2b, lambda c: I["w_ff2"][l, c * 128:(c + 1) * 128, :], 32, D, "w2b")
        Dd = type("Dd", (), {})()
        self.Dd = Dd
        Dd.xm = [P.sb("xm%d" % i, [128, D], F32) for i in range(6)]
        Dd.xn = [P.sb("xnD%d" % i, [128, D], BF16) for i in range(2)]
        Dd.hx = [P.sb("hx2T%d" % i, [128, 8, 256], BF16) for i in range(2)]
        Dd.hT = P.sb("hT", [128, 32, 256], BF16)
        Dd.rl = [P.sb("rl%d" % i, [128, 512], F32) for i in range(2)]
        Dd.tmp = P.sb("tmpD", [128, D], F32)
        Dd.junk = P.sb("junkD", [128, D], BF16)
        Dd.ss = P.sb("ssD", [128, 4], F32)
        Dd.rs = P.sb("rsD", [128, 4], F32)
        Dd.ssy = P.sb("ssyD", [128, 4], F32)
        Dd.rsy = P.sb("rsyD", [128, 2], F32)
        Dd.G2 = P.sb("G2bc", [128, 2, D], F32)
        for r_ in range(2):
            P.dma("sp", Dd.G2[:, r_, :], self.G_d[r_:r_ + 1, 1, :].partition_broadcast(128), w=[("G2", r_)])
        Dd.trbanks = Ring([(self.banks[i].bitcast(BF16), ("ps", i)) for i in range(2)])
        Dd.fbanks = Ring([(self.banks[i], ("ps", i)) for i in range(2, 5)])
        Dd.yring = Ring([(self.banks[i], ("ps", i)) for i in range(5, 8)])
        Dd.xi = 0
        Dd.rli = 0
        tiles = [(i * 256, False) for i in range(S // 256)]
        if self.ctx_full:
            tiles = tiles + [(S, True)]
        if "D_tiles" in self.debug:
            tiles = tiles[:2] + tiles[-1:]
        self.D_front(l, 0, *tiles[0])
        if len(tiles) > 1:
            self.D_front(l, 1, *tiles[1])
        for it, t in enumerate(tiles):
            self.D_back(l, it, t[0], t[1], part=1)
            if it + 2 < len(tiles):
                self.D_front(l, it + 2, *tiles[it + 2])
            self.D_back(l, it, t[0], t[1], part=2)
        P.barrier()
        P.pop()

    def D_front(self, l, it, tok0, is_ctx):
        P = self.P
        Dd = self.Dd
        mi = 1 if is_ctx else 0
        hx = Dd.hx[it % 2]
        xis = []
        for s in range(2):
            xi = Dd.xi % 6
            Dd.xi += 1
            xis.append(xi)
            P.dma("sp", Dd.xm[xi][:], self.xmid_d[tok0 + s * 128: tok0 + (s + 1) * 128, :], w=[("xm", xi)])
            self.act(Dd.junk[:], Dd.xm[xi][:], AF.Square, r=[("xm", xi)], w=["junkD", ("ssD", s)], accum_out=Dd.ss[:, s:s + 1])
            self.rstd(Dd.rs[:, s:s + 1], Dd.ss[:, s:s + 1], float(D), [("ssD", s)], [("rsD", s)], ("rsDt", s))
            self.ts("dve", Dd.xn[s][:], Dd.xm[xi][:], Dd.rs[:, s:s + 1], None, ALU.mult, None, r=[("xm", xi), ("rsD", s)], w=[("xnD", s)])
        Dd.cur_xis = getattr(Dd, "cur_xis", {})
        Dd.cur_xis[it] = xis
        for cq in range(2):
            tb, sk = Dd.trbanks.next()
            self.tr([(tb[:, j * 256 + s * 128: j * 256 + (s + 1) * 128], Dd.xn[s][:, (4 * cq + j) * 128:(4 * cq + j + 1) * 128])
                     for j in range(4) for s in range(2)], r=[("xnD", 0), ("xnD", 1), "ident"], w=[sk])
            for j in range(4):
                c = 4 * cq + j
                slot = tb[:, j * 256:(j + 1) * 256]
                if cq == 0:
                    self.act(hx[:, c, :], slot, AF.Identity, r=[sk, "modT"], w=[("hx2T", it % 2, c)],
                             scale=self.modT[:, 2, c, mi:mi + 1], bias=self.modT[:, 3, c, mi:mi + 1])
                else:
                    self.ts("dve", hx[:, c, :], slot, self.modT[:, 2, c, mi:mi + 1], self.modT[:, 3, c, mi:mi + 1], ALU.mult, ALU.add,
                            r=[sk, "modT"], w=[("hx2T", it % 2, c)])

    def D_back(self, l, it, tok0, is_ctx, part):
        P = self.P
        Dd = self.Dd
        mi = 1 if is_ctx else 0
        hx = Dd.hx[it % 2]
        hk = [("hx2T", it % 2, c) for c in range(8)]
        xis = Dd.cur_xis[it]
        for fp in range(16 if part == 1 else 0):
            fb, sk = Dd.fbanks.next()
            specs = []
            for j in range(2):
                fc = 2 * fp + j
                for dc in range(8):
                    specs.append((fb[:, j * 256:(j + 1) * 256], self.w1b[:, dc, fc * 128:(fc + 1) * 128], hx[:, dc, :], dc == 0, dc == 7))
            self.mm(specs, r=hk + self.w1k, w=[sk])
            ri = Dd.rli % 2
            Dd.rli += 1
            self.act(Dd.rl[ri][:], fb[:, :], AF.Relu, r=[sk], w=[("rl", ri)])
            self.tt("pool" if fp % 2 == 0 else "dve", Dd.hT[:, 2 * fp:2 * fp + 2, :].rearrange("p a t -> p (a t)"), Dd.rl[ri][:], Dd.rl[ri][:],
                    ALU.mult, r=[("rl", ri)], w=[("hT", 2 * fp), ("hT", 2 * fp + 1)])
        if part == 1:
            return
        hTk = [("hT", fc) for fc in range(32)]
        for s in range(2):
            xi = xis[s]
            ybs = []
            for half in range(2):
                b_, k_ = Dd.yring.next()
                self.mm([(b_[:, :], Dd.hT[:, fc, s * 128:(s + 1) * 128], self.w2b[:, fc, half * 512:(half + 1) * 512], fc == 0, fc == 31)
                         for fc in range(32)], r=hTk + self.w2k, w=[k_])
                ybs.append((b_, k_))
            for half in range(2):
                self.act(Dd.junk[:, half * 512:(half + 1) * 512], ybs[half][0][:, :], AF.Square, r=[ybs[half][1]],
                         w=[("junkD", half), ("ssyD", s, half)], accum_out=Dd.ssy[:, s * 2 + half:s * 2 + half + 1])
            self.tt("dve", Dd.ssy[:, s * 2:s * 2 + 1], Dd.ssy[:, s * 2:s * 2 + 1], Dd.ssy[:, s * 2 + 1:s * 2 + 2], ALU.add,
                    r=[("ssyD", s, 0), ("ssyD", s, 1)], w=[("ssyDs", s)])
            self.rstd(Dd.rsy[:, s:s + 1], Dd.ssy[:, s * 2:s * 2 + 1], float(D), [("ssyDs", s)], [("rsyD", s)], ("rsyDt", s))
            for half in range(2):
                hc = slice(half * 512, (half + 1) * 512)
                self.tt("dve", Dd.tmp[:, hc], ybs[half][0][:, :], Dd.G2[:, mi, hc], ALU.mult, r=[ybs[half][1], ("G2", mi)], w=[("tmpD", half)])
            self.stt(Dd.xm[xi][:], Dd.tmp[:], Dd.rsy[:, s:s + 1], Dd.xm[xi][:], ALU.mult, ALU.add,
                     r=[("tmpD", 0), ("tmpD", 1), ("rsyD", s), ("xm", xi)], w=[("xm", xi)])
            if is_ctx and l == self.nlayers - 1:
                continue
            P.dma("pool", self.dst_rows(l, tok0 + s * 128, 128), Dd.xm[xi][:], r=[("xm", xi)], w=[("xout", tok0, s)])


def _prep_shared(inp):
    sh = {}
    f = lambda a: np.ascontiguousarray(a, dtype=np.float32)
    sh["w_ada"] = f(inp["w_ada"])
    sh["b_ada"] = f(inp["b_ada"])
    for nm in ("g_pre_mix", "g_post_mix", "g_pre_ff", "g_post_ff"):
        sh[nm] = f(inp[nm])
    sh["w_inp"] = f(inp["w_in"][:, :, _perm_cols()])
    sink = inp["sink"]
    sinkT = np.zeros((2, 128, 4), np.float32)
    for c in range(4):
        sinkT[:, 0:64, c] = sink[:, c][:, None]
        sinkT[:, 64:128, c] = sink[:, 4 + c][:, None]
    sh["sinkT"] = sinkT
    sh["w_sguT"] = f(np.transpose(inp["w_sgu"], (0, 3, 1, 2)).reshape(2, 128, 512))
    sh["b_sguT"] = f(np.transpose(inp["b_sgu"], (0, 2, 1)))
    sh["g_sgu"] = f(inp["g_sgu"].reshape(2, 256))
    rp = _attn_row_perm()
    gm = inp["g_mix"][:, rp]
    sh["g_mixT"] = f(np.transpose(gm.reshape(2, 8, 128), (0, 2, 1)))
    sh["w_outp"] = f(inp["w_out"][:, rp, :])
    sh["w_ff1"] = f(inp["w_ff1"])
    sh["w_ff2"] = f(inp["w_ff2"])
    return sh


def _core_inputs(inp, sh, consts, b):
    m = dict(sh)
    m["x"] = np.ascontiguousarray(inp["x"][b], dtype=np.float32)
    m["ctx"] = np.ascontiguousarray(inp["ctx"][b], dtype=np.float32)
    cs = np.stack([inp["c"][b], inp["c_ctx"]], axis=0).astype(np.float32)
    m["csT"] = np.ascontiguousarray(np.transpose(cs.reshape(2, 8, 128), (2, 1, 0)).reshape(128, 16))
    for k, v in consts.items():
        m["c_" + k] = v
    return m


_CACHE = {}


def kernel(**inputs):
    inp = {k: np.asarray(v) for k, v in inputs.items()}
    if "nc" not in _CACHE:
        _CACHE["nc"] = Builder().build()
        _CACHE["consts"] = _constants()
    nc = _CACHE["nc"]
    consts = _CACHE["consts"]
    sh = _prep_shared(inp)
    in_maps = [_core_inputs(inp, sh, consts, b) for b in range(8)]
    res = run_bass_kernel_spmd(nc, in_maps, core_ids=list(range(8)))
    out = np.stack([np.asarray(res.results[b]["out"], dtype=np.float32) for b in range(8)], axis=0)
    return out
```
